# Optimizing a Trainium2 kernel written in Bass

```python
import math
import jax, jax.numpy as jnp
from jax import lax
import numpy as np

D_MODEL = 4096
BATCH = 1
SEQ = 8192
DEPTH = 2

N_MIXERS = 2
N_LAYERS_A = (DEPTH + 1) // 2
N_LAYERS_B = DEPTH // 2
DEEPNORM_ALPHA = (2.0 * DEPTH) ** 0.25
DEEPNORM_BETA = (8.0 * DEPTH) ** -0.25
MOD_INIT = 0.25

GDN_HEADS = D_MODEL // 128
GDN_DK = 128
GDN_DV = 128
GDN_QK_WIDTH = GDN_HEADS * GDN_DK
GDN_V_WIDTH = GDN_HEADS * GDN_DV
GDN_CONV = 4
GDN_CHUNK = 64
GDN_CONV_CH = 2 * GDN_QK_WIDTH + GDN_V_WIDTH
GDN_IN_WIDTH = GDN_CONV_CH + GDN_V_WIDTH + 2 * GDN_HEADS

MLA_HEADS = D_MODEL // 128
MLA_Q_RANK = 896
MLA_KV_RANK = 512
MLA_NOPE = 128
MLA_ROPE = 64
MLA_V = 128
MLA_QK = MLA_NOPE + MLA_ROPE
MLA_V_WIDTH = MLA_HEADS * MLA_V
MLA_IN_WIDTH = MLA_Q_RANK + MLA_KV_RANK + MLA_ROPE + MLA_V_WIDTH
ROPE_THETA = 10000.0
Q_BLOCK = 128

RMS_EPS = 1e-6
LN_EPS = 1e-5

kernel_name = "hybrid_gdn_mla_deepnorm_adaln"


def rms_norm(x, g, eps=RMS_EPS):
    xf = x.astype(jnp.float32)
    y = xf * lax.rsqrt(jnp.mean(xf * xf, axis=-1, keepdims=True) + eps)
    return (y * g.astype(jnp.float32)).astype(x.dtype)


def layer_norm(x, g, b, eps=LN_EPS):
    xf = x.astype(jnp.float32)
    mu = jnp.mean(xf, axis=-1, keepdims=True)
    var = jnp.mean(jnp.square(xf - mu), axis=-1, keepdims=True)
    y = (xf - mu) * lax.rsqrt(var + eps) * g.astype(jnp.float32) + b.astype(jnp.float32)
    return y.astype(x.dtype)


def l2_normalize(x, eps=RMS_EPS):
    xf = x.astype(jnp.float32)
    return xf * lax.rsqrt(jnp.sum(xf * xf, axis=-1, keepdims=True) + eps)


def causal_depthwise_conv(x, w):
    k_width, ch = w.shape
    return lax.conv_general_dilated(
        x, w.astype(x.dtype)[:, None, :], window_strides=(1,), padding=[(k_width - 1, 0)],
        dimension_numbers=("NWC", "WIO", "NWC"), feature_group_count=ch)


def gated_delta_rule_chunked(q, k, v, g, beta):
    bsz, seq, heads, dk = q.shape
    dv = v.shape[-1]
    c = GDN_CHUNK
    n = seq // c
    q = q * (dk ** -0.5)

    def chunks(t):
        return t.reshape(bsz, n, c, heads, -1).transpose(1, 0, 3, 2, 4)

    qc, kc, vc = chunks(q), chunks(k), chunks(v)
    gc = g.reshape(bsz, n, c, heads).transpose(1, 0, 3, 2)
    bc = beta.reshape(bsz, n, c, heads).transpose(1, 0, 3, 2)
    gcum = jnp.cumsum(gc, axis=-1)
    incl = jnp.tril(jnp.ones((c, c), dtype=bool))
    strict = jnp.tril(jnp.ones((c, c), dtype=bool), -1)
    diff = gcum[..., :, None] - gcum[..., None, :]
    decay_mat = jnp.where(incl, jnp.exp(jnp.where(incl, diff, 0.0)), 0.0)

    kb = kc * bc[..., None]
    a_mat = jnp.where(strict, jnp.einsum("nbhcd,nbhsd->nbhcs", kb, kc) * decay_mat, 0.0)
    rhs = jnp.concatenate([vc * bc[..., None], kb * jnp.exp(gcum)[..., None]], axis=-1)
    sol = lax.linalg.triangular_solve(a_mat + jnp.eye(c, dtype=a_mat.dtype), rhs,
                                      left_side=True, lower=True, unit_diagonal=True)
    u, w = sol[..., :dv], sol[..., dv:]
    qk = jnp.where(incl, jnp.einsum("nbhcd,nbhsd->nbhcs", qc, kc) * decay_mat, 0.0)
    q_dec = qc * jnp.exp(gcum)[..., None]
    k_dec = kc * jnp.exp(gcum[..., -1:] - gcum)[..., None]
    g_last = jnp.exp(gcum[..., -1])

    def step(state, inp):
        u_i, w_i, qk_i, qd_i, kd_i, gl_i = inp
        v_new = u_i - jnp.einsum("bhcd,bhde->bhce", w_i, state)
        o_i = jnp.einsum("bhcd,bhde->bhce", qd_i, state) + jnp.einsum("bhcs,bhse->bhce", qk_i, v_new)
        state = state * gl_i[..., None, None] + jnp.einsum("bhcd,bhce->bhde", kd_i, v_new)
        return state, o_i

    s0 = jnp.zeros((bsz, heads, dk, dv), jnp.float32)
    _, o = lax.scan(step, s0, (u, w, qk, q_dec, k_dec, g_last))
    return o.transpose(1, 0, 3, 2, 4).reshape(bsz, seq, heads, dv)


def gated_deltanet_branch(h, w_in, w_conv, a_log, dt_bias, norm_g, w_out):
    bsz, seq, _ = h.shape
    proj = h @ w_in
    qkv, z, b_raw, a_raw = jnp.split(
        proj, [GDN_CONV_CH, GDN_CONV_CH + GDN_V_WIDTH, GDN_CONV_CH + GDN_V_WIDTH + GDN_HEADS], axis=-1)
    qkv = jax.nn.silu(causal_depthwise_conv(qkv, w_conv))
    q, k, v = jnp.split(qkv, [GDN_QK_WIDTH, 2 * GDN_QK_WIDTH], axis=-1)
    q = l2_normalize(q.reshape(bsz, seq, GDN_HEADS, GDN_DK))
    k = l2_normalize(k.reshape(bsz, seq, GDN_HEADS, GDN_DK))
    v = v.reshape(bsz, seq, GDN_HEADS, GDN_DV).astype(jnp.float32)
    beta = jax.nn.sigmoid(b_raw.astype(jnp.float32))
    g = -jnp.exp(a_log.astype(jnp.float32)) * jax.nn.softplus(
        a_raw.astype(jnp.float32) + dt_bias.astype(jnp.float32))
    o = gated_delta_rule_chunked(q, k, v, g, beta)
    o = rms_norm(o, norm_g).astype(h.dtype).reshape(bsz, seq, GDN_V_WIDTH)
    return (o * jax.nn.silu(z)) @ w_out


def apply_rope(x, positions):
    half = x.shape[-1] // 2
    inv_freq = ROPE_THETA ** (-jnp.arange(0, half, dtype=jnp.float32) / half)
    ang = positions.astype(jnp.float32)[..., None] * inv_freq
    cos = jnp.cos(ang)[:, :, None, :]
    sin = jnp.sin(ang)[:, :, None, :]
    xf = x.astype(jnp.float32)
    x1, x2 = xf[..., :half], xf[..., half:]
    return jnp.concatenate([x1 * cos - x2 * sin, x2 * cos + x1 * sin], axis=-1).astype(x.dtype)


def causal_attention_blocked(q, k, v):
    bsz, seq, heads, dqk = q.shape
    dv = v.shape[-1]
    nb = seq // Q_BLOCK
    scale = dqk ** -0.5
    qb = q.reshape(bsz, nb, Q_BLOCK, heads, dqk).transpose(1, 0, 2, 3, 4)
    kpos = jnp.arange(seq)

    def one_block(args):
        q_blk, start = args
        s = jnp.einsum("bqhd,bkhd->bhqk", q_blk, k, preferred_element_type=jnp.float32) * scale
        qpos = start + jnp.arange(Q_BLOCK)
        s = jnp.where(kpos[None, :] <= qpos[:, None], s, -jnp.inf)
        p = jax.nn.softmax(s, axis=-1).astype(v.dtype)
        return jnp.einsum("bhqk,bkhd->bqhd", p, v)

    out = lax.map(one_block, (qb, jnp.arange(nb) * Q_BLOCK))
    return out.transpose(1, 0, 2, 3, 4).reshape(bsz, seq, heads, dv)


def mla_branch(h, positions, w_in, q_norm_g, w_qb, kv_norm_g, w_kvb, w_out):
    bsz, seq, _ = h.shape
    proj = h @ w_in
    cq, ckv, k_rope, z = jnp.split(
        proj, [MLA_Q_RANK, MLA_Q_RANK + MLA_KV_RANK, MLA_Q_RANK + MLA_KV_RANK + MLA_ROPE], axis=-1)
    q = (rms_norm(cq, q_norm_g) @ w_qb).reshape(bsz, seq, MLA_HEADS, MLA_QK)
    kv = (rms_norm(ckv, kv_norm_g) @ w_kvb).reshape(bsz, seq, MLA_HEADS, MLA_NOPE + MLA_V)
    q_nope, q_rope = q[..., :MLA_NOPE], q[..., MLA_NOPE:]
    k_nope, v = kv[..., :MLA_NOPE], kv[..., MLA_NOPE:]
    q_rope = apply_rope(q_rope, positions)
    k_rope = apply_rope(k_rope[:, :, None, :], positions)
    q = jnp.concatenate([q_nope, q_rope], axis=-1)
    k = jnp.concatenate([k_nope, jnp.broadcast_to(k_rope, (bsz, seq, MLA_HEADS, MLA_ROPE))], axis=-1)
    o = causal_attention_blocked(q, k, v).reshape(bsz, seq, MLA_V_WIDTH)
    return (o * jax.nn.silu(z)) @ w_out


def setup_inputs(seed: int = 0) -> dict:
    key = jax.random.key(seed)
    ks = jax.random.split(key, 24)
    f32 = jnp.float32
    nrm = lambda k, shape, s: jax.random.normal(k, shape, f32) * s
    x = jax.random.normal(ks[0], (BATCH, SEQ, D_MODEL), f32)
    c = jax.random.normal(ks[1], (BATCH, D_MODEL), f32)
    positions = jnp.broadcast_to(jnp.arange(SEQ, dtype=jnp.int32), (BATCH, SEQ))
    w_mod = nrm(ks[2], (DEPTH, D_MODEL, 3 * D_MODEL), MOD_INIT * D_MODEL ** -0.5)
    b_mod = nrm(ks[3], (DEPTH, 3 * D_MODEL), 0.01)
    ln_g = 1.0 + nrm(ks[4], (DEPTH, D_MODEL), 0.01)
    ln_b = nrm(ks[5], (DEPTH, D_MODEL), 0.01)
    a_w_in = nrm(ks[6], (N_LAYERS_A, D_MODEL, GDN_IN_WIDTH), D_MODEL ** -0.5)
    a_w_conv = nrm(ks[7], (N_LAYERS_A, GDN_CONV, GDN_CONV_CH), GDN_CONV ** -0.5)
    a_a_log = jnp.log(jax.random.uniform(ks[8], (N_LAYERS_A, GDN_HEADS), f32, 1.0, 16.0))
    dt = jnp.exp(jax.random.uniform(ks[9], (N_LAYERS_A, GDN_HEADS), f32, math.log(1e-3), math.log(1e-1)))
    a_dt_bias = dt + jnp.log(-jnp.expm1(-dt))
    a_norm_g = 1.0 + nrm(ks[10], (N_LAYERS_A, GDN_DV), 0.01)
    a_w_out = nrm(ks[11], (N_LAYERS_A, GDN_V_WIDTH, D_MODEL), DEEPNORM_BETA * GDN_V_WIDTH ** -0.5)
    b_w_in = nrm(ks[12], (N_LAYERS_B, D_MODEL, MLA_IN_WIDTH), D_MODEL ** -0.5)
    b_q_norm_g = 1.0 + nrm(ks[13], (N_LAYERS_B, MLA_Q_RANK), 0.01)
    b_w_qb = nrm(ks[14], (N_LAYERS_B, MLA_Q_RANK, MLA_HEADS * MLA_QK), MLA_Q_RANK ** -0.5)
    b_kv_norm_g = 1.0 + nrm(ks[15], (N_LAYERS_B, MLA_KV_RANK), 0.01)
    b_w_kvb = nrm(ks[16], (N_LAYERS_B, MLA_KV_RANK, MLA_HEADS * (MLA_NOPE + MLA_V)), MLA_KV_RANK ** -0.5)
    b_w_out = nrm(ks[17], (N_LAYERS_B, MLA_V_WIDTH, D_MODEL), DEEPNORM_BETA * MLA_V_WIDTH ** -0.5)
    return {"x": x, "c": c, "positions": positions, "w_mod": w_mod, "b_mod": b_mod,
            "ln_g": ln_g, "ln_b": ln_b, "a_w_in": a_w_in, "a_w_conv": a_w_conv,
            "a_a_log": a_a_log, "a_dt_bias": a_dt_bias, "a_norm_g": a_norm_g, "a_w_out": a_w_out,
            "b_w_in": b_w_in, "b_q_norm_g": b_q_norm_g, "b_w_qb": b_w_qb,
            "b_kv_norm_g": b_kv_norm_g, "b_w_kvb": b_w_kvb, "b_w_out": b_w_out}


def reference(x, c, positions, w_mod, b_mod, ln_g, ln_b, a_w_in, a_w_conv, a_a_log, a_dt_bias,
              a_norm_g, a_w_out, b_w_in, b_q_norm_g, b_w_qb, b_kv_norm_g, b_w_kvb, b_w_out):
    c_act = jax.nn.silu(c)
    for i in range(DEPTH):
        mod = c_act @ w_mod[i] + b_mod[i]
        shift, scale, gate = jnp.split(mod, 3, axis=-1)
        h = x * (1.0 + scale[:, None, :]) + shift[:, None, :]
        j = i // N_MIXERS
        if i % N_MIXERS == 0:
            y = gated_deltanet_branch(h, a_w_in[j], a_w_conv[j], a_a_log[j], a_dt_bias[j],
                                      a_norm_g[j], a_w_out[j])
        else:
            y = mla_branch(h, positions, b_w_in[j], b_q_norm_g[j], b_w_qb[j],
                           b_kv_norm_g[j], b_w_kvb[j], b_w_out[j])
        x = layer_norm(DEEPNORM_ALPHA * x + (1.0 + gate[:, None, :]) * y, ln_g[i], ln_b[i])
    return x
```

```python
import numpy as np
from contextlib import ExitStack
import concourse.bass as bass
import concourse.mybir as mybir
from concourse.bass_utils import run_bass_kernel_spmd

F32 = mybir.dt.float32
BF16 = mybir.dt.bfloat16
I32 = mybir.dt.int32
AF = mybir.ActivationFunctionType
ALU = mybir.AluOpType
AX = mybir.AxisListType

NCORES = 8
SEQ = 8192
D = 4096
TPC = SEQ // NCORES
ALPHA = (2.0 * 2) ** 0.25
RMS_EPS = 1e-6
LN_EPS = 1e-5
NEG = -30000.0

ENGS = ("pe", "act", "dve", "pool", "sp")
SEM_CHUNK = 4000
STQ = "sp"
HT_ACT = False


class Sched:
    def __init__(self, nc):
        self.nc = nc
        self.ops = {e: [] for e in ENGS}
        self.writers = {}
        self.readers = {}
        self.dmacnt = {}
        self.nops = 0
        self.excl = set()

    def _collect(self, eng, reads, writes):
        deps = []
        for k in reads:
            for t in self.writers.get(k, ()):
                deps.append((t, "raw"))
            if k in self.excl:
                for t in self.readers.get(k, ()):
                    deps.append((t, "war"))
        for k in writes:
            for t in self.writers.get(k, ()):
                deps.append((t, "waw"))
            for t in self.readers.get(k, ()):
                deps.append((t, "war"))
        return deps

    def _commit(self, tok, reads, writes, partial):
        for k in writes:
            if partial:
                self.writers.setdefault(k, []).append(tok)
            else:
                self.writers[k] = [tok]
            self.readers[k] = []
        for k in reads:
            self.readers.setdefault(k, []).append(tok)

    def op(self, eng, fn, reads=(), writes=(), partial=False):
        deps = self._collect(eng, reads, writes)
        tok = ("E", eng, self.nops)
        self.ops[eng].append(dict(fn=fn, deps=deps, tok=tok, dma=None))
        self._commit(tok, reads, writes, partial)
        self.nops += 1
        return tok

    def dma(self, eng, out, in_, reads=(), writes=(), sem=None, partial=False, **kw):
        semname = sem or (writes[0] if writes else reads[0])
        deps = self._collect(eng, reads, writes)
        self.dmacnt[semname] = self.dmacnt.get(semname, 0) + 1
        tok = ("D", semname, self.dmacnt[semname] * 16)
        fn = lambda e, out=out, in_=in_, kw=kw: e.dma_start(out=out, in_=in_, **kw)
        self.ops[eng].append(dict(fn=fn, deps=deps, tok=tok, dma=semname))
        self._commit(tok, reads, writes, partial)
        self.nops += 1
        return tok

    def flush(self, es, final_waits=(), barrier=True):
        nc = self.nc
        if not hasattr(self, "sems"):
            self.sems = {}
            self.nflag = {e: 0 for e in ENGS}
        sems = self.sems
        flagged = set()
        for e in ENGS:
            for o in self.ops[e]:
                keep = []
                for (t, kind) in o["deps"]:
                    if t[0] == "E" and t[1] == e and o["dma"] is None:
                        if e == "pe" or kind == "war":
                            continue
                    keep.append(t)
                    if t[0] == "E":
                        flagged.add(t)
                o["deps"] = keep
        lasts = []
        if barrier:
            for e in ENGS:
                for o in reversed(self.ops[e]):
                    if o["dma"] is None:
                        flagged.add(o["tok"])
                        lasts.append(o["tok"])
                        break
        for t in final_waits:
            if t[0] == "E":
                flagged.add(t)
        tokval = {}

        def getsem(key):
            if key not in sems:
                sems[key] = es.enter_context(nc.semaphore("s_" + "_".join(str(x) for x in key)))
            return sems[key]

        for e in ENGS:
            for o in self.ops[e]:
                if o["tok"] in flagged:
                    n = self.nflag[e]
                    tokval[o["tok"]] = ((e, n // SEM_CHUNK), n % SEM_CHUNK + 1)
                    self.nflag[e] = n + 1
                    getsem((e, n // SEM_CHUNK))
        for name in self.dmacnt:
            getsem(("D", name))

        def resolve(t):
            if t[0] == "E":
                return tokval[t]
            return (("D", t[1]), t[2])

        endw = [resolve(t) for t in list(final_waits) + lasts]
        if barrier:
            endw += [(("D", name), c * 16) for name, c in self.dmacnt.items()]
        ops = self.ops
        with nc.Block() as block:
            engobj = {"pe": block.tensor, "act": block.scalar, "dve": block.vector, "pool": block.gpsimd,
                      "sp": block.sync}

            def make(e):
                def body(eng):
                    waited = {}
                    for o in ops[e]:
                        need = {}
                        for t in o["deps"]:
                            s, v = resolve(t)
                            if waited.get(s, 0) >= v:
                                continue
                            need[s] = max(need.get(s, 0), v)
                        for s, v in need.items():
                            eng.wait_ge(sems[s], v)
                            waited[s] = v
                        ins = o["fn"](eng)
                        if o["dma"] is not None:
                            ins.then_inc(sems[("D", o["dma"])], 16)
                        elif o["tok"] in tokval:
                            ins.then_inc(sems[tokval[o["tok"]][0]], 1)
                    for s, v in endw:
                        if waited.get(s, 0) < v:
                            eng.wait_ge(sems[s], v)
                            waited[s] = v
                return body

            for e in ENGS:
                engobj[e](make(e))
        self.ops = {e: [] for e in ENGS}
        self.writers = {}
        self.readers = {}


class Builder:
    def __init__(self, nc, es):
        self.nc = nc
        self.es = es
        self.S = Sched(nc)
        self.finals = []

    def sb(self, st, name, shape, dt):
        self.uid = getattr(self, "uid", 0) + 1
        return st.enter_context(self.nc.sbuf_tensor(f"{name}_u{self.uid}", list(shape), dt))

    def ps(self, st, name, shape, dt=F32):
        self.uid = getattr(self, "uid", 0) + 1
        return st.enter_context(self.nc.psum_tensor(f"{name}_u{self.uid}", list(shape), dt))


def make_consts(B, st):
    S = B.S
    C = {}
    C["idf"] = B.sb(st, "c_idf", [128, 128], F32)
    C["idb"] = B.sb(st, "c_idb", [128, 128], BF16)
    C["onesf"] = B.sb(st, "c_onesf", [128, 128], F32)
    C["onesb"] = B.sb(st, "c_onesb", [128, 128], BF16)
    S.op("pool", lambda e: e.memset(C["idf"][:], 0.0), writes=["c_idf"])
    S.op("pool", lambda e: e.affine_select(out=C["idf"][:], in_=C["idf"][:], pattern=[[-1, 128]],
                                           compare_op=ALU.not_equal, fill=1.0, base=0, channel_multiplier=1),
         reads=["c_idf"], writes=["c_idf"])
    S.op("pool", lambda e: e.tensor_copy(out=C["idb"][:], in_=C["idf"][:]), reads=["c_idf"], writes=["c_idb"])
    S.op("pool", lambda e: e.memset(C["onesf"][:], 1.0), writes=["c_onesf"])
    S.op("pool", lambda e: e.memset(C["onesb"][:], 1.0), writes=["c_onesb"])
    return C


def phase_mod(B, c_ap, wm_ap, bm_ap, out_ap):
    S, nc = B.S, B.nc
    with ExitStack() as st:
        C = make_consts(B, st)
        ct = B.sb(st, "m_ct", [32, 128], F32)
        cact = B.sb(st, "m_cact", [128, 32], F32)
        bmt = B.sb(st, "m_bm", [1, 3072], F32)
        rowt = B.sb(st, "m_row", [1, 3072], F32)
        wt = [B.sb(st, f"m_w{i}", [128, 3072], F32) for i in range(3)]
        pT = B.ps(st, "m_pT", [128, 512], F32)
        pr = [B.ps(st, f"m_pr{i}", [128, 512], F32) for i in range(6)]
        S.dma("sp", ct[:], c_ap.rearrange("o (kc p) -> (o kc) p", p=128), writes=["m_ct"])
        S.dma("sp", bmt[:], bm_ap, writes=["m_bm"])
        S.op("pe", lambda e: e.transpose(pT[:, 0:32], ct[:], C["idf"][0:32, 0:32]), reads=["m_ct", "c_idf"], writes=["m_pT"])
        S.op("act", lambda e: e.activation(out=cact[:], in_=pT[:, 0:32], func=AF.Silu), reads=["m_pT"], writes=["m_cact"])
        for kc in range(32):
            w = wt[kc % 3]
            S.dma("sp" if kc % 2 == 0 else "act", w[:], wm_ap[kc * 128:(kc + 1) * 128, :], writes=[f"m_w{kc % 3}"])
            for nb in range(6):
                S.op("pe", lambda e, w=w, nb=nb, kc=kc: e.matmul(pr[nb][0:1, :], cact[:, kc:kc + 1], w[:, nb * 512:(nb + 1) * 512],
                                                                 start=(kc == 0), stop=(kc == 31)),
                     reads=[f"m_w{kc % 3}", "m_cact"], writes=[f"m_pr{nb}"])
        for nb in range(6):
            S.op("dve", lambda e, nb=nb: e.tensor_tensor(out=rowt[0:1, nb * 512:(nb + 1) * 512], in0=pr[nb][0:1, :],
                                                         in1=bmt[0:1, nb * 512:(nb + 1) * 512], op=ALU.add),
                 reads=[f"m_pr{nb}", "m_bm"], writes=["m_row"])
        t = S.dma("sp", out_ap, rowt[:], reads=["m_row"], writes=["m_out"])
        S.flush(B.es, final_waits=[t])


def load_mod_fm(B, st, C, mod_ap, layer, pt, ptkey):
    S = B.S
    rows = B.sb(st, f"md_rows{layer}", [64, 128], F32)
    sh = B.sb(st, f"md_sh{layer}", [128, 32], F32)
    sc = B.sb(st, f"md_sc{layer}", [128, 32], F32)
    S.dma("sp", rows[:], mod_ap[layer:layer + 1, 0:8192].rearrange("o (kc p) -> (o kc) p", p=128), writes=[f"md_rows{layer}"])
    S.op("pe", lambda e: e.transpose(pt[:, 0:64], rows[:], C["idf"][0:64, 0:64]), reads=[f"md_rows{layer}", "c_idf"], writes=[ptkey])
    S.op("dve", lambda e: e.tensor_copy(out=sh[:], in_=pt[:, 0:32]), reads=[ptkey], writes=[f"md_sh{layer}"])
    S.op("dve", lambda e: e.tensor_scalar(out=sc[:], in0=pt[:, 32:64], scalar1=1.0, scalar2=None, op0=ALU.add),
         reads=[ptkey], writes=[f"md_sc{layer}"])
    return sh, sc


def make_hT(B, C, xs, xkey, hT, hkey, st_idx, sh, sc, shk, sck, ptr, ptk, cnt):
    S = B.S
    for g in range(8):
        pt = ptr[cnt[0] % len(ptr)]
        pk = ptk[cnt[0] % len(ptr)]
        cnt[0] += 1
        for j in range(4):
            kc = g * 4 + j
            S.op("pe", lambda e, pt=pt, j=j, kc=kc: e.transpose(pt[:, j, :], xs[:, kc * 128:(kc + 1) * 128], C["idf"][:]),
                 reads=[xkey, "c_idf"], writes=[pk])
        for j in range(4):
            kc = g * 4 + j
            dst = hT[:, kc, st_idx * 128:(st_idx + 1) * 128]
            if HT_ACT and kc % 2 == 0:
                S.op("act", lambda e, pt=pt, j=j, kc=kc, dst=dst: e.activation(out=dst, in_=pt[:, j, :], func=AF.Identity,
                                                                               bias=sh[:, kc:kc + 1], scale=sc[:, kc:kc + 1]),
                     reads=[pk, shk, sck], writes=[f"{hkey}{kc}"])
            else:
                S.op("dve", lambda e, pt=pt, j=j, kc=kc, dst=dst: e.scalar_tensor_tensor(out=dst, in0=pt[:, j, :], scalar=sc[:, kc:kc + 1],
                                                                                         in1=sh[:, kc:kc + 1].broadcast_to([128, 128]),
                                                                                         op0=ALU.mult, op1=ALU.add),
                     reads=[pk, shk, sck], writes=[f"{hkey}{kc}"])


def phase_a1(B, x_ap, mod_ap, w4_ap, wab_ap, cw_ap, alog_ap, dtb_ap, scr, ntiles=SEQ // 512, stage=9):
    S, nc = B.S, B.nc
    with ExitStack() as st:
        C = make_consts(B, st)
        ptr = [B.ps(st, f"a_pt{i}", [128, 4, 128], F32) for i in range(2)]
        ptk = [f"a_pt{i}" for i in range(2)]
        psp = [B.ps(st, f"a_pp{i}", [128, 512], F32) for i in range(4)]
        pgb = B.ps(st, "a_pgb", [128, 512], F32)
        pss = B.ps(st, "a_pss", [128, 512], F32)
        S.excl.update(["a_pt0", "a_pt1", "a_pp0", "a_pp1", "a_pp2", "a_pp3", "a_pgb", "a_pss"])
        sh, sc = load_mod_fm(B, st, C, mod_ap, 0, pgb, "a_pgb")
        xs = [B.sb(st, f"a_xs{i}", [128, D], F32) for i in range(2)]
        hT = B.sb(st, "a_hT", [128, 32, 512], BF16)
        Wt = [B.sb(st, f"a_W{i}", [128, 16, 512], BF16) for i in range(3)]
        pc = B.sb(st, "a_pc", [128, 12, 515], F32)
        tmp = [B.sb(st, f"a_tmp{i}", [128, 512], F32) for i in range(2)]
        sl = [B.sb(st, f"a_sl{i}", [128, 512], F32) for i in range(2)]
        sqb = B.sb(st, "a_sqb", [128, 512], BF16)
        rn = B.sb(st, "a_rn", [128, 512], F32)
        outs = {n: B.sb(st, f"a_o{n}", [128, 4, 512], BF16) for n in "qkvz"}
        wabf = B.sb(st, "a_wabf", [128, 32, 8], F32)
        wab = B.sb(st, "a_wab", [128, 32, 8], BF16)
        cwr = B.sb(st, "a_cwr", [48, 128], F32)
        cwT = B.sb(st, "a_cwT", [128, 48], F32)
        alb = B.sb(st, "a_alb", [128, 4], F32)
        negA = B.sb(st, "a_negA", [128, 4], F32)
        dtb = B.sb(st, "a_dtb", [128, 4], F32)
        gbt = B.sb(st, "a_gbt", [128, 4, 8], F32)
        t4 = [B.sb(st, f"a_t4{i}", [128, 4], F32) for i in range(2)]
        S.dma("sp", wabf[:], wab_ap.rearrange("(kc p) n -> p kc n", p=128), writes=["a_wabf"])
        S.op("dve", lambda e: e.tensor_copy(out=wab[:], in_=wabf[:]), reads=["a_wabf"], writes=["a_wab"])
        S.dma("sp", cwr[:], cw_ap.rearrange("j (b p) -> (j b) p", p=128), writes=["a_cwr"])
        S.op("pe", lambda e: e.transpose(pss[:, 0:48], cwr[:], C["idf"][0:48, 0:48]), reads=["a_cwr", "c_idf"], writes=["a_pss"])
        S.op("dve", lambda e: e.tensor_copy(out=cwT[:], in_=pss[:, 0:48]), reads=["a_pss"], writes=["a_cwT"])
        S.dma("sp", alb[:], alog_ap.partition_broadcast(128), writes=["a_alb"])
        S.dma("sp", dtb[:], dtb_ap.partition_broadcast(128), writes=["a_dtb"])
        S.op("act", lambda e: e.activation(out=negA[:], in_=alb[:], func=AF.Exp), reads=["a_alb"], writes=["a_negA"])
        S.op("dve", lambda e: e.tensor_scalar(out=negA[:], in0=negA[:], scalar1=-1.0, scalar2=None, op0=ALU.mult),
             reads=["a_negA"], writes=["a_negA"])
        S.op("pool", lambda e: e.memset(pc[:], 0.0), writes=[f"a_pc{b}" for b in range(12)])
        xv = x_ap.rearrange("(n p) d -> n p d", p=128)
        w4v = w4_ap.rearrange("(kc p) n -> p kc n", p=128)
        cnt = [0]
        wcnt = 0
        pend = []
        qk_scale = 128.0 ** -0.5
        for T in range(ntiles if stage > 0 else 0):
            for stx in range(4):
                n = T * 4 + stx
                xb = xs[n % 2]
                S.dma("sp", xb[:], xv[n], writes=[f"a_xs{n % 2}"])
                make_hT(B, C, xb, f"a_xs{n % 2}", hT, "a_hT", stx, sh, sc, "md_sh0", "md_sc0", ptr, ptk, cnt)
            for stx in range(4 if stage > 1 else 0):
                for kc in range(32):
                    S.op("pe", lambda e, kc=kc, stx=stx: e.matmul(pgb[:, 0:8], hT[:, kc, stx * 128:(stx + 1) * 128], wab[:, kc, :],
                                                                    start=(kc == 0), stop=(kc == 31)),
                         reads=[f"a_hT{kc}", "a_wab"], writes=["a_pgb"])
                S.op("act", lambda e: e.activation(out=t4[0][:], in_=pgb[:, 0:4], func=AF.Exp, scale=-1.0), reads=["a_pgb"], writes=["a_t40"])
                S.op("dve", lambda e: e.tensor_tensor(out=t4[1][:], in0=pgb[:, 4:8], in1=dtb[:], op=ALU.add), reads=["a_pgb", "a_dtb"], writes=["a_t41"])
                S.op("dve", lambda e: e.tensor_scalar(out=t4[0][:], in0=t4[0][:], scalar1=1.0, scalar2=None, op0=ALU.add), reads=["a_t40"], writes=["a_t40"])
                S.op("dve", lambda e, stx=stx: e.reciprocal(out=gbt[:, stx, 4:8], in_=t4[0][:]), reads=["a_t40"], writes=["a_gbt"])
                S.op("act", lambda e: e.activation(out=t4[1][:], in_=t4[1][:], func=AF.Exp), reads=["a_t41"], writes=["a_t41"])
                S.op("act", lambda e: e.activation(out=t4[1][:], in_=t4[1][:], func=AF.Ln, bias=1.0, scale=1.0), reads=["a_t41"], writes=["a_t41"])
                S.op("dve", lambda e, stx=stx: e.tensor_tensor(out=gbt[:, stx, 0:4], in0=t4[1][:], in1=negA[:], op=ALU.mult),
                     reads=["a_t41", "a_negA"], writes=["a_gbt"])
            if stage > 1:
                S.dma("sp", scr["gbs"].rearrange("(n p) e -> p n e", p=128)[:, T * 4:(T + 1) * 4, :], gbt[:], reads=["a_gbt"], writes=["scr_gb"])
            for cb in range(4 if stage > 2 else 0):
                for half in range(2):
                    slot = wcnt % 3
                    wcnt += 1
                    S.dma("pool", Wt[slot][:], w4v[:, half * 16:(half + 1) * 16, cb * 512:(cb + 1) * 512], writes=[f"a_W{slot}"])
                    if half == 0 and pend:
                        pend.pop()()
                    for sbk in range(4):
                        for k16 in range(16):
                            kc = half * 16 + k16
                            S.op("pe", lambda e, slot=slot, sbk=sbk, k16=k16, kc=kc, half=half: e.matmul(
                                psp[sbk][:], Wt[slot][:, k16, sbk * 128:(sbk + 1) * 128], hT[:, kc, :],
                                start=(kc == 0), stop=(kc == 31)),
                                reads=[f"a_W{slot}", f"a_hT{kc}"], writes=[f"a_pp{sbk}"])
                for sbk in range(4):
                    b = cb * 4 + sbk
                    if cb == 3:
                        S.op("act", lambda e, sbk=sbk: e.activation(out=outs["z"][:, sbk, :], in_=psp[sbk][:], func=AF.Silu),
                             reads=[f"a_pp{sbk}"], writes=["a_oz"])
                        continue
                    ev = "act" if sbk % 2 == 0 else "dve"
                    if ev == "act":
                        S.op("act", lambda e, b=b, sbk=sbk: e.copy(out=pc[:, b, 3:515], in_=psp[sbk][:]), reads=[f"a_pp{sbk}"], writes=[f"a_pc{b}"])
                    else:
                        S.op("dve", lambda e, b=b, sbk=sbk: e.tensor_copy(out=pc[:, b, 3:515], in_=psp[sbk][:]), reads=[f"a_pp{sbk}"], writes=[f"a_pc{b}"])
                    if stage < 4:
                        continue
                    ce = "dve"
                    tb = tmp[b % 2]
                    tk = f"a_tmp{b % 2}"
                    S.op(ce, lambda e, b=b, tb=tb: e.tensor_scalar(out=tb[:], in0=pc[:, b, 3:515], scalar1=cwT[:, 3 * 12 + b:3 * 12 + b + 1],
                                                                   scalar2=None, op0=ALU.mult), reads=[f"a_pc{b}", "a_cwT"], writes=[tk])
                    for j in (2, 1, 0):
                        S.op(ce, lambda e, b=b, tb=tb, j=j: e.scalar_tensor_tensor(out=tb[:], in0=pc[:, b, j:j + 512],
                                                                                 scalar=cwT[:, j * 12 + b:j * 12 + b + 1], in1=tb[:],
                                                                                 op0=ALU.mult, op1=ALU.add),
                             reads=[f"a_pc{b}", "a_cwT", tk], writes=[tk])
                    S.op(ce, lambda e, b=b: e.tensor_copy(out=pc[:, b, 0:3], in_=pc[:, b, 512:515]), reads=[f"a_pc{b}"], writes=[f"a_pc{b}"])
                    if stage < 5:
                        continue
                    if cb == 2:
                        S.op("act", lambda e, tb=tb, sbk=sbk: e.activation(out=outs["v"][:, sbk, :], in_=tb[:], func=AF.Silu),
                             reads=[tk], writes=["a_ov"])
                        continue
                    slb = sl[b % 2]
                    sk = f"a_sl{b % 2}"
                    S.op("act", lambda e, tb=tb, slb=slb: e.activation(out=slb[:], in_=tb[:], func=AF.Silu), reads=[tk], writes=[sk])
                    S.op("dve", lambda e, slb=slb: e.tensor_tensor(out=sqb[:], in0=slb[:], in1=slb[:], op=ALU.mult), reads=[sk], writes=["a_sqb"])
                    S.op("pe", lambda e: e.matmul(pss[:], C["onesb"][:], sqb[:], start=True, stop=True), reads=["c_onesb", "a_sqb"], writes=["a_pss"])
                    S.op("act", lambda e: e.activation(out=rn[:], in_=pss[:], func=AF.Sqrt, bias=RMS_EPS, scale=1.0), reads=["a_pss"], writes=["a_rn"])
                    S.op("dve", lambda e: e.reciprocal(out=rn[:], in_=rn[:]), reads=["a_rn"], writes=["a_rn"])
                    nm = "q" if cb == 0 else "k"
                    S.op("dve", lambda e, slb=slb, nm=nm, sbk=sbk, cb=cb: e.scalar_tensor_tensor(
                        out=outs[nm][:, sbk, :], in0=slb[:], scalar=(qk_scale if cb == 0 else 1.0), in1=rn[:], op0=ALU.mult, op1=ALU.mult),
                        reads=[sk, "a_rn"], writes=[f"a_o{nm}"])
                nm = "qkvz"[cb]
                if stage < 6:
                    continue
                pend.append(lambda nm=nm, T=T: S.dma(STQ, scr[nm + "s"].rearrange("h d t -> d h t")[:, :, T * 512:(T + 1) * 512], outs[nm][:],
                                                     reads=[f"a_o{nm}"], writes=[f"scr_{nm}"]))
        while pend:
            pend.pop()()
        S.flush(B.es)


def fl(ap):
    return ap.rearrange("p a b -> p (a b)")


def phase_a2(B, scr, ng_ap, y_ap, ntiles=SEQ // 512, NL=6, stage=9):
    S, nc = B.S, B.nc
    with ExitStack() as st:
        C = make_consts(B, st)
        banks = [B.ps(st, f"g_ps{i}", [128, 512], F32) for i in range(7)]
        pbt = B.ps(st, "g_pbt", [128, 8, 128], BF16)
        S.excl.update([f"g_ps{i}" for i in range(7)] + ["g_pbt"])
        bcnt = [0]

        def nb():
            i = bcnt[0] % 7
            bcnt[0] += 1
            return banks[i], f"g_ps{i}"

        def v3(t):
            return t[:].rearrange("p (a b) -> p a b", a=4)

        def T3(name, dt=F32):
            return B.sb(st, name, [128, 4, 128], dt)

        triu = B.sb(st, "g_triu", [128, 128], F32)
        su4, nm4, i4 = T3("g_su4"), T3("g_nm4"), T3("g_i4")
        ng = B.sb(st, "g_ng", [128, 1], F32)
        S.op("pool", lambda e: e.memset(triu[:], 1.0), writes=["g_triu"])
        S.op("pool", lambda e: e.affine_select(out=triu[:], in_=triu[:], pattern=[[1, 128]], compare_op=ALU.is_ge, fill=0.0,
                                               base=0, channel_multiplier=-1), reads=["g_triu"], writes=["g_triu"])
        S.op("pool", lambda e: e.memset(su4[:], 1.0), writes=["g_su4"])
        S.op("pool", lambda e: e.affine_select(out=su4[:], in_=su4[:], pattern=[[0, 4], [1, 128]], compare_op=ALU.is_ge, fill=0.0,
                                               base=-1, channel_multiplier=-1), reads=["g_su4"], writes=["g_su4"])
        S.op("pool", lambda e: e.memset(nm4[:], 0.0), writes=["g_nm4"])
        S.op("pool", lambda e: e.affine_select(out=nm4[:], in_=nm4[:], pattern=[[0, 4], [1, 128]], compare_op=ALU.is_ge, fill=NEG,
                                               base=0, channel_multiplier=-1), reads=["g_nm4"], writes=["g_nm4"])
        S.op("pool", lambda e: e.memset(i4[:], 0.0), writes=["g_i4"])
        S.op("pool", lambda e: e.affine_select(out=i4[:], in_=i4[:], pattern=[[0, 4], [-1, 128]], compare_op=ALU.not_equal, fill=1.0,
                                               base=0, channel_multiplier=1), reads=["g_i4"], writes=["g_i4"])
        S.dma("sp", ng[:], ng_ap.rearrange("o p -> p o"), writes=["g_ng"])
        S.op("dve", lambda e: e.tensor_scalar(out=ng[:], in0=ng[:], scalar1=float(128.0 ** 0.5), scalar2=None, op0=ALU.mult),
             reads=["g_ng"], writes=["g_ng"])
        S32 = T3("g_S32")
        Sbf = T3("g_Sbf", BF16)
        S.op("pool", lambda e: e.memset(S32[:], 0.0), writes=["g_S32"])
        S.op("pool", lambda e: e.memset(Sbf[:], 0.0), writes=["g_Sbf"])
        inb = {n: [B.sb(st, f"g_in{n}{i}", [128, 4, 512], BF16) for i in range(2)] for n in "qkvz"}
        gbb = [B.sb(st, f"g_gb{i}", [128, 4, 8], F32) for i in range(2)]
        ybuf = [B.sb(st, f"g_y{i}", [128, 4, 512], BF16) for i in range(2)]
        kvtm = B.sb(st, "g_kvtm", [128, 8, 128], BF16)
        gc = B.sb(st, "g_gc", [128, 2, 4], F32)
        ngc = B.sb(st, "g_ngc", [128, 4], F32)
        ed = B.sb(st, "g_ed", [128, 4], F32)
        gm, E, GM, DT, DTS, U, UT = T3("g_gm"), T3("g_E"), T3("g_GM"), T3("g_DT"), T3("g_DTS"), T3("g_U"), T3("g_UT")
        Wb = [T3(f"g_W{i}") for i in range(2)]
        WTb = [T3(f"g_WT{i}") for i in range(2)]
        Pb = [T3(f"g_P{i}") for i in range(2)]
        TpT, qkT, kgT, qdT, kdec, R, vnew = (T3(n, BF16) for n in ("g_TpT", "g_qkT", "g_kgT", "g_qdT", "g_kdec", "g_R", "g_vnew"))
        sq = T3("g_sq", BF16)
        rn, yt = T3("g_rn"), T3("g_yt")

        def bc(ap2):
            return ap2.unsqueeze(2).broadcast_to([128, 4, 128])

        for T in range(ntiles):
            bi = T % 2
            for n in "qkvz":
                S.dma("sp", inb[n][bi][:], scr[n + "s"].rearrange("h d t -> d h t")[:, :, T * 512:(T + 1) * 512],
                      reads=[f"scr_{n}"], writes=[f"g_in{n}{bi}"])
            S.dma("sp", gbb[bi][:], scr["gbs"].rearrange("(n p) e -> p n e", p=128)[:, T * 4:(T + 1) * 4, :],
                  reads=["scr_gb"], writes=[f"g_gb{bi}"])
            qT, kT, vT, zs, gb = inb["q"][bi], inb["k"][bi], inb["v"][bi], inb["z"][bi], gbb[bi]
            kq, kk, kv, kz, kgb = (f"g_inq{bi}", f"g_ink{bi}", f"g_inv{bi}", f"g_inz{bi}", f"g_gb{bi}")
            for c in range(4):
                cs = slice(c * 128, (c + 1) * 128)
                if stage < 1:
                    continue
                for h in range(4):
                    S.op("pe", lambda e, h=h, cs=cs, kT=kT: e.transpose(pbt[:, h, :], kT[:, h, cs], C["idb"][:]), reads=[kk, "c_idb"], writes=["g_pbt"])
                for h in range(4):
                    S.op("pe", lambda e, h=h, cs=cs, vT=vT: e.transpose(pbt[:, 4 + h, :], vT[:, h, cs], C["idb"][:]), reads=[kv, "c_idb"], writes=["g_pbt"])
                S.op("act", lambda e: e.copy(out=kvtm[:], in_=pbt[:]), reads=["g_pbt"], writes=["g_kvtm"])
                if stage < 2:
                    continue
                pa, pak = nb()
                S.op("pe", lambda e, pa=pa, gb=gb, c=c: e.matmul(pa[:, 0:4], triu[:], gb[:, c, 0:4], start=True, stop=True), reads=["g_triu", kgb], writes=[pak])
                S.op("pe", lambda e, pa=pa, gb=gb, c=c: e.matmul(pa[:, 4:8], C["onesf"][:], gb[:, c, 0:4], start=True, stop=True), reads=["c_onesf", kgb], writes=[pak])
                S.op("dve", lambda e, pa=pa: e.tensor_copy(out=gc[:].rearrange("p a b -> p (a b)"), in_=pa[:, 0:8]), reads=[pak], writes=["g_gc"])
                S.op("dve", lambda e: e.tensor_scalar(out=ngc[:], in0=gc[:, 0, :], scalar1=-1.0, scalar2=None, op0=ALU.mult), reads=["g_gc"], writes=["g_ngc"])
                S.op("dve", lambda e: e.tensor_tensor(out=ed[:], in0=gc[:, 1, :], in1=gc[:, 0, :], op=ALU.subtract), reads=["g_gc"], writes=["g_ed"])
                S.op("act", lambda e: e.activation(out=ed[:], in_=ed[:], func=AF.Exp), reads=["g_ed"], writes=["g_ed"])
                if stage < 3:
                    continue
                S.op("pool", lambda e, gb=gb, c=c: e.tensor_tensor(out=gm[:], in0=triu[:].unsqueeze(1).broadcast_to([128, 4, 128]),
                                                                    in1=bc(gb[:, c, 0:4]), op=ALU.mult), reads=["g_triu", kgb], writes=["g_gm"])
                pg, pgk = nb()
                S.op("pe", lambda e, pg=pg: e.matmul(pg[:], C["onesf"][:], fl(gm[:]), start=True, stop=True), reads=["c_onesf", "g_gm"], writes=[pgk])
                if stage < 3.2:
                    continue
                S.op("act", lambda e, pg=pg: e.activation(out=fl(E[:]), in_=pg[:], func=AF.Exp), reads=[pgk], writes=["g_E"])
                S.op("dve", lambda e, pg=pg: e.tensor_tensor(out=fl(GM[:]), in0=pg[:], in1=fl(nm4[:]), op=ALU.add), reads=[pgk, "g_nm4"], writes=["g_GM"])
                if stage < 3.3:
                    continue
                for h in range(4):
                    S.op("act", lambda e, h=h: e.activation(out=DT[:, h, :], in_=GM[:, h, :], func=AF.Exp, bias=ngc[:, h:h + 1], scale=1.0),
                         reads=["g_GM", "g_ngc"], writes=["g_DT"])
                if stage < 3.4:
                    continue
                S.op("pool", lambda e: e.tensor_tensor(out=DTS[:], in0=DT[:], in1=su4[:], op=ALU.mult), reads=["g_DT", "g_su4"], writes=["g_DTS"])
                S.op("pool", lambda e, gb=gb, c=c: e.tensor_tensor(out=DTS[:], in0=DTS[:], in1=bc(gb[:, c, 4:8]), op=ALU.mult), reads=["g_DTS", kgb], writes=["g_DTS"])
                if stage < 4:
                    continue
                pk, pkk = nb()
                for h in range(4):
                    S.op("pe", lambda e, h=h, cs=cs, kT=kT, pk=pk: e.matmul(v3(pk)[:, h, :], kT[:, h, cs], kT[:, h, cs], start=True, stop=True), reads=[kk], writes=[pkk])
                pq, pqk = nb()
                for h in range(4):
                    S.op("pe", lambda e, h=h, cs=cs, kT=kT, qT=qT, pq=pq: e.matmul(v3(pq)[:, h, :], kT[:, h, cs], qT[:, h, cs], start=True, stop=True), reads=[kk, kq], writes=[pqk])
                S.op("dve", lambda e, pk=pk: e.tensor_tensor(out=fl(U[:]), in0=pk[:], in1=fl(DTS[:]), op=ALU.mult), reads=[pkk, "g_DTS"], writes=["g_U"])
                S.op("dve", lambda e, pq=pq: e.tensor_tensor(out=fl(qkT[:]), in0=pq[:], in1=fl(DT[:]), op=ALU.mult), reads=[pqk, "g_DT"], writes=["g_qkT"])
                if stage < 5:
                    continue
                pu, puk = nb()
                for h in range(4):
                    S.op("pe", lambda e, h=h, pu=pu: e.transpose(v3(pu)[:, h, :], U[:, h, :], C["idf"][:]), reads=["g_U", "c_idf"], writes=[puk])
                S.op("act", lambda e, pu=pu: e.copy(out=fl(UT[:]), in_=pu[:]), reads=[puk], writes=["g_UT"])
                S.op("pool", lambda e: e.tensor_tensor(out=Pb[0][:], in0=i4[:], in1=U[:], op=ALU.subtract), reads=["g_i4", "g_U"], writes=["g_P0"])
                if stage < 6:
                    continue
                W, WT, P = U, UT, Pb[0]
                Wk, WTk, Pk = "g_U", "g_UT", "g_P0"
                for l in range(1, NL + 1):
                    Wn, WTn, Pn = Wb[l % 2], WTb[l % 2], Pb[l % 2]
                    Wnk, WTnk, Pnk = f"g_W{l % 2}", f"g_WT{l % 2}", f"g_P{l % 2}"
                    if l < NL:
                        pw, pwk = nb()
                        for h in range(4):
                            S.op("pe", lambda e, h=h, pw=pw, W=W, WT=WT: e.matmul(v3(pw)[:, h, :], WT[:, h, :], W[:, h, :], start=True, stop=True), reads=[Wk, WTk], writes=[pwk])
                    pwt, pwtk = nb()
                    for h in range(4):
                        S.op("pe", lambda e, h=h, pwt=pwt, W=W, WT=WT: e.matmul(v3(pwt)[:, h, :], W[:, h, :], WT[:, h, :], start=True, stop=True), reads=[Wk, WTk], writes=[pwtk])
                    if l < NL:
                        S.op("act", lambda e, pw=pw, Wn=Wn: e.copy(out=fl(Wn[:]), in_=pw[:]), reads=[pwk], writes=[Wnk])
                    S.op("dve", lambda e, pwt=pwt, WTn=WTn: e.tensor_copy(out=fl(WTn[:]), in_=pwt[:]), reads=[pwtk], writes=[WTnk])
                    pp, ppk = nb()
                    for h in range(4):
                        S.op("pe", lambda e, h=h, pp=pp, WTn=WTn, P=P: e.matmul(v3(pp)[:, h, :], WTn[:, h, :], P[:, h, :], start=True, stop=True), reads=[WTnk, Pk], writes=[ppk])
                    if l < NL:
                        S.op("dve", lambda e, pp=pp, P=P, Pn=Pn: e.tensor_tensor(out=fl(Pn[:]), in0=pp[:], in1=fl(P[:]), op=ALU.add), reads=[ppk, Pk], writes=[Pnk])
                    else:
                        S.op("dve", lambda e, pp=pp, P=P: e.tensor_tensor(out=fl(TpT[:]), in0=pp[:], in1=fl(P[:]), op=ALU.add), reads=[ppk, Pk], writes=["g_TpT"])
                    W, WT, P, Wk, WTk, Pk = Wn, WTn, Pn, Wnk, WTnk, Pnk
                if stage < 7:
                    continue
                S.op("pool", lambda e, kT=kT, cs=cs: e.tensor_tensor(out=kgT[:], in0=kT[:, :, cs], in1=E[:], op=ALU.mult), reads=[kk, "g_E"], writes=["g_kgT"])
                S.op("pool", lambda e, qT=qT, cs=cs: e.tensor_tensor(out=qdT[:], in0=qT[:, :, cs], in1=E[:], op=ALU.mult), reads=[kq, "g_E"], writes=["g_qdT"])
                S.op("pool", lambda e: e.tensor_tensor(out=kdec[:], in0=kvtm[:, 0:4, :], in1=bc(ed[:]), op=ALU.mult), reads=["g_kvtm", "g_ed"], writes=["g_kdec"])
                if stage < 8:
                    continue
                p1, p1k = nb()
                for h in range(4):
                    S.op("pe", lambda e, h=h, p1=p1: e.matmul(v3(p1)[:, h, :], kgT[:, h, :], Sbf[:, h, :], start=True, stop=True), reads=["g_kgT", "g_Sbf"], writes=[p1k])
                S.op("dve", lambda e, p1=p1: e.tensor_tensor(out=R[:], in0=kvtm[:, 4:8, :], in1=v3(p1), op=ALU.subtract), reads=["g_kvtm", p1k], writes=["g_R"])
                p2, p2k = nb()
                for h in range(4):
                    S.op("pe", lambda e, h=h, p2=p2: e.matmul(v3(p2)[:, h, :], TpT[:, h, :], R[:, h, :], start=True, stop=True), reads=["g_TpT", "g_R"], writes=[p2k])
                S.op("dve", lambda e, p2=p2, gb=gb, c=c: e.tensor_tensor(out=vnew[:], in0=v3(p2), in1=bc(gb[:, c, 4:8]), op=ALU.mult), reads=[p2k, kgb], writes=["g_vnew"])
                po, pok = nb()
                for h in range(4):
                    S.op("pe", lambda e, h=h, po=po: e.matmul(v3(po)[:, h, :], Sbf[:, h, :], qdT[:, h, :], start=True, stop=False), reads=["g_Sbf", "g_qdT"], writes=[pok])
                    S.op("pe", lambda e, h=h, po=po: e.matmul(v3(po)[:, h, :], vnew[:, h, :], qkT[:, h, :], start=False, stop=True), reads=["g_vnew", "g_qkT"], writes=[pok])
                p3, p3k = nb()
                for h in range(4):
                    S.op("pe", lambda e, h=h, p3=p3: e.matmul(v3(p3)[:, h, :], kdec[:, h, :], vnew[:, h, :], start=True, stop=True), reads=["g_kdec", "g_vnew"], writes=[p3k])
                for h in range(4):
                    S.op("dve", lambda e, h=h, p3=p3: e.scalar_tensor_tensor(out=S32[:, h, :], in0=S32[:, h, :], scalar=E[:, h, 127:128], in1=v3(p3)[:, h, :],
                                                                           op0=ALU.mult, op1=ALU.add), reads=["g_S32", "g_E", p3k], writes=["g_S32"])
                S.op("act", lambda e: e.copy(out=Sbf[:], in_=S32[:]), reads=["g_S32"], writes=["g_Sbf"])
                if stage < 9:
                    continue
                S.op("act", lambda e, po=po: e.activation(out=fl(sq[:]), in_=po[:], func=AF.Square), reads=[pok], writes=["g_sq"])
                pss, pssk = nb()
                S.op("pe", lambda e, pss=pss: e.matmul(pss[:], C["onesb"][:], fl(sq[:]), start=True, stop=True), reads=["c_onesb", "g_sq"], writes=[pssk])
                S.op("act", lambda e, pss=pss: e.activation(out=fl(rn[:]), in_=pss[:], func=AF.Sqrt, bias=128.0 * RMS_EPS, scale=1.0), reads=[pssk], writes=["g_rn"])
                S.op("dve", lambda e: e.reciprocal(out=rn[:], in_=rn[:]), reads=["g_rn"], writes=["g_rn"])
                S.op("dve", lambda e, po=po: e.scalar_tensor_tensor(out=fl(yt[:]), in0=po[:], scalar=ng[:, 0:1], in1=fl(rn[:]), op0=ALU.mult, op1=ALU.mult),
                     reads=[pok, "g_ng", "g_rn"], writes=["g_yt"])
                S.op("pool", lambda e, zs=zs, cs=cs, bi=bi: e.tensor_tensor(out=ybuf[bi][:, :, cs], in0=yt[:], in1=zs[:, :, cs], op=ALU.mult),
                     reads=["g_yt", kz], writes=[f"g_y{bi}"])
            B.finals.append(S.dma(STQ, y_ap.rearrange("(h d) t -> d h t", d=128)[:, :, T * 512:(T + 1) * 512], ybuf[bi][:],
                                  reads=[f"g_y{bi}"], writes=["y_out"]))
        S.flush(B.es)


def phase_out(B, yT_ap, zT_ap, w_ap, xres_ap, mod_ap, layer, lng_ap, lnb_ap, out_ap, ntok=TPC):
    S, nc = B.S, B.nc
    with ExitStack() as st:
        pp = [B.ps(st, f"o_pp{i}", [128, 512], F32) for i in range(4)]
        S.excl.update([f"o_pp{i}" for i in range(4)])
        yT = B.sb(st, "o_yT", [128, 32, 512], BF16)
        zt = B.sb(st, "o_zt", [128, 8, 512], BF16)
        gb_, lg_, lb_ = (B.sb(st, n, [128, D], F32) for n in ("o_gate", "o_lng", "o_lnb"))
        Wt = [B.sb(st, f"o_W{i}", [128, 16, 512], BF16) for i in range(3)]
        xr = B.sb(st, "o_xr", [128, D], F32)
        r = B.sb(st, "o_r", [128, D], F32)
        sm = B.sb(st, "o_sm", [128, 8], F32)
        S.dma("sp", gb_[:], mod_ap[layer:layer + 1, 8192:12288].partition_broadcast(128), writes=["o_gate"])
        S.dma("sp", lg_[:], lng_ap.partition_broadcast(128), writes=["o_lng"])
        S.dma("sp", lb_[:], lnb_ap.partition_broadcast(128), writes=["o_lnb"])
        S.op("dve", lambda e: e.tensor_scalar(out=gb_[:], in0=gb_[:], scalar1=1.0, scalar2=None, op0=ALU.add), reads=["o_gate"], writes=["o_gate"])
        yv = yT_ap.rearrange("(kc p) t -> p kc t", p=128)
        wv = w_ap.rearrange("(kc p) n -> p kc n", p=128)
        xv = xres_ap.rearrange("(n p) d -> n p d", p=128)
        ov = out_ap.rearrange("(n p) d -> n p d", p=128)
        wcnt = 0
        pcnt = 0
        for hf in range(ntok // 512):
            S.dma("sp", yT[:], yv[:, :, hf * 512:(hf + 1) * 512], writes=["o_yT"])
            if zT_ap is not None:
                zv = zT_ap.rearrange("(kc p) t -> p kc t", p=128)
                for g in range(4):
                    S.dma("sp", zt[:], zv[:, g * 8:(g + 1) * 8, hf * 512:(hf + 1) * 512], writes=["o_zt"])
                    S.op("pool", lambda e, g=g: e.tensor_tensor(out=yT[:, g * 8:(g + 1) * 8, :], in0=yT[:, g * 8:(g + 1) * 8, :], in1=zt[:], op=ALU.mult),
                         reads=["o_yT", "o_zt"], writes=["o_yT"])
            for ts in range(4):
                n = hf * 4 + ts
                S.dma("sp", xr[:], xv[n], writes=["o_xr"])
                for nb in range(8):
                    pb = pp[pcnt % 4]
                    pk = f"o_pp{pcnt % 4}"
                    pcnt += 1
                    for half in range(2):
                        slot = wcnt % 3
                        wcnt += 1
                        S.dma("pool", Wt[slot][:], wv[:, half * 16:(half + 1) * 16, nb * 512:(nb + 1) * 512], writes=[f"o_W{slot}"])
                        for k16 in range(16):
                            kc = half * 16 + k16
                            S.op("pe", lambda e, pb=pb, slot=slot, k16=k16, kc=kc, ts=ts: e.matmul(
                                pb[:], yT[:, kc, ts * 128:(ts + 1) * 128], Wt[slot][:, k16, :], start=(kc == 0), stop=(kc == 31)),
                                reads=["o_yT", f"o_W{slot}"], writes=[pk])
                    cs = slice(nb * 512, (nb + 1) * 512)
                    S.op("dve", lambda e, pb=pb, cs=cs: e.tensor_tensor(out=r[:, cs], in0=pb[:], in1=gb_[:, cs], op=ALU.mult), reads=[pk, "o_gate"], writes=["o_r"])
                    S.op("dve", lambda e, cs=cs: e.scalar_tensor_tensor(out=r[:, cs], in0=xr[:, cs], scalar=float(ALPHA), in1=r[:, cs], op0=ALU.mult, op1=ALU.add),
                         reads=["o_xr", "o_r"], writes=["o_r"])
                S.op("dve", lambda e: e.reduce_sum(out=sm[:, 0:1], in_=r[:], axis=AX.X), reads=["o_r"], writes=["o_sm"])
                S.op("pool", lambda e: e.tensor_tensor(out=xr[:], in0=r[:], in1=r[:], op=ALU.mult), reads=["o_r", "o_xr"], writes=["o_xr"])
                S.op("dve", lambda e: e.reduce_sum(out=sm[:, 1:2], in_=xr[:], axis=AX.X), reads=["o_xr", "o_sm"], writes=["o_sm"])
                S.op("dve", lambda e: e.tensor_scalar(out=sm[:, 0:2], in0=sm[:, 0:2], scalar1=1.0 / D, scalar2=None, op0=ALU.mult), reads=["o_sm"], writes=["o_sm"])
                S.op("dve", lambda e: e.tensor_tensor(out=sm[:, 2:3], in0=sm[:, 0:1], in1=sm[:, 0:1], op=ALU.mult), reads=["o_sm"], writes=["o_sm"])
                S.op("dve", lambda e: e.tensor_tensor(out=sm[:, 3:4], in0=sm[:, 1:2], in1=sm[:, 2:3], op=ALU.subtract), reads=["o_sm"], writes=["o_sm"])
                S.op("act", lambda e: e.activation(out=sm[:, 4:5], in_=sm[:, 3:4], func=AF.Sqrt, bias=LN_EPS, scale=1.0), reads=["o_sm"], writes=["o_sm"])
                S.op("dve", lambda e: e.reciprocal(out=sm[:, 5:6], in_=sm[:, 4:5]), reads=["o_sm"], writes=["o_sm"])
                S.op("dve", lambda e: e.tensor_scalar(out=r[:], in0=r[:], scalar1=sm[:, 0:1], scalar2=None, op0=ALU.subtract), reads=["o_r", "o_sm"], writes=["o_r"])
                S.op("dve", lambda e: e.scalar_tensor_tensor(out=r[:], in0=r[:], scalar=sm[:, 5:6], in1=lg_[:], op0=ALU.mult, op1=ALU.mult),
                     reads=["o_r", "o_sm", "o_lng"], writes=["o_r"])
                S.op("pool", lambda e: e.tensor_tensor(out=r[:], in0=r[:], in1=lb_[:], op=ALU.add), reads=["o_r", "o_lnb"], writes=["o_r"])
                B.finals.append(S.dma("sp", ov[n], r[:], reads=["o_r"], writes=["o_out"]))
        S.flush(B.es)


def phase_b2(B, x1_ap, mod_ap, win_ap, qg_ap, kvg_ap, latT_ap, zsT_ap):
    S, nc = B.S, B.nc
    with ExitStack() as st:
        C = make_consts(B, st)
        ptr = [B.ps(st, f"b_pt{i}", [128, 4, 128], F32) for i in range(2)]
        ptk = [f"b_pt{i}" for i in range(2)]
        psp = [B.ps(st, f"b_pp{i}", [128, 512], F32) for i in range(4)]
        pmd = B.ps(st, "b_pmd", [128, 512], F32)
        pbt = B.ps(st, "b_pbt", [128, 8, 128], BF16)
        S.excl.update(ptk + [f"b_pp{i}" for i in range(4)] + ["b_pmd", "b_pbt"])
        sh, sc = load_mod_fm(B, st, C, mod_ap, 1, pmd, "b_pmd")
        xs = B.sb(st, "b_xs", [128, D], F32)
        hT = B.sb(st, "b_hT", [128, 32, TPC], BF16)
        Wt = [B.sb(st, f"b_W{i}", [128, 16, 512], BF16) for i in range(4)]
        zo = B.sb(st, "b_zo", [128, 4, 512], BF16)
        lt = B.sb(st, "b_lt", [128, 1472], F32)
        lsq = B.sb(st, "b_lsq", [128, 896], F32)
        lb = B.sb(st, "b_lb", [128, 13, 128], BF16)
        ltT = B.sb(st, "b_ltT", [128, 13, 128], BF16)
        qg = B.sb(st, "b_qg", [128, 896], F32)
        kvg = B.sb(st, "b_kvg", [128, 512], F32)
        sm = B.sb(st, "b_sm", [128, 8], F32)
        S.dma("sp", qg[:], qg_ap.partition_broadcast(128), writes=["b_qg"])
        S.dma("sp", kvg[:], kvg_ap.partition_broadcast(128), writes=["b_kvg"])
        S.op("pool", lambda e: e.memset(lb[:], 0.0), writes=["b_lb"])
        xv = x1_ap.rearrange("(n p) d -> n p d", p=128)
        wv = win_ap.rearrange("(kc p) n -> p kc n", p=128)
        cnt = [0]
        for ts in range(8):
            S.dma("sp", xs[:], xv[ts], writes=["b_xs"])
            make_hT(B, C, xs, "b_xs", hT, "b_hT", ts, sh, sc, "md_sh1", "md_sc1", ptr, ptk, cnt)
        hkeys = [f"b_hT{kc}" for kc in range(32)]
        wcnt = 0
        zv = zsT_ap.rearrange("(cb s p) t -> cb p s t", s=4, p=128)
        for cb in range(8):
            slots = []
            for half in range(2):
                slot = wcnt % 4
                wcnt += 1
                slots.append(slot)
                S.dma("pool", Wt[slot][:], wv[:, half * 16:(half + 1) * 16, 1472 + cb * 512:1472 + (cb + 1) * 512], writes=[f"b_W{slot}"])
            for th in range(2):
                for sbk in range(4):
                    for kc in range(32):
                        slot = slots[kc // 16]
                        S.op("pe", lambda e, slot=slot, sbk=sbk, kc=kc, th=th: e.matmul(
                            psp[sbk][:], Wt[slot][:, kc % 16, sbk * 128:(sbk + 1) * 128], hT[:, kc, th * 512:(th + 1) * 512],
                            start=(kc == 0), stop=(kc == 31)), reads=[f"b_W{slot}", hkeys[kc]], writes=[f"b_pp{sbk}"])
                    S.op("act", lambda e, sbk=sbk: e.activation(out=zo[:, sbk, :], in_=psp[sbk][:], func=AF.Silu), reads=[f"b_pp{sbk}"], writes=["b_zo"])
                S.dma("sp", zv[cb][:, :, th * 512:(th + 1) * 512], zo[:], reads=["b_zo"], writes=["b_zs"])
        lv = latT_ap.rearrange("(j p) t -> p j t", p=128)
        for ts in range(8):
            for nbk, (c0, c1) in enumerate(((0, 512), (512, 1024), (1024, 1472))):
                slots = []
                for half in range(2):
                    slot = wcnt % 4
                    wcnt += 1
                    slots.append(slot)
                    S.dma("pool", Wt[slot][:, :, 0:c1 - c0], wv[:, half * 16:(half + 1) * 16, c0:c1], writes=[f"b_W{slot}"])
                pb = psp[nbk]
                for kc in range(32):
                    slot = slots[kc // 16]
                    S.op("pe", lambda e, slot=slot, kc=kc, ts=ts, pb=pb, c0=c0, c1=c1: e.matmul(
                        pb[:, 0:c1 - c0], hT[:, kc, ts * 128:(ts + 1) * 128], Wt[slot][:, kc % 16, 0:c1 - c0],
                        start=(kc == 0), stop=(kc == 31)), reads=[f"b_W{slot}", hkeys[kc]], writes=[f"b_pp{nbk}"])
                S.op("act", lambda e, pb=pb, c0=c0, c1=c1: e.copy(out=lt[:, c0:c1], in_=pb[:, 0:c1 - c0]), reads=[f"b_pp{nbk}"], writes=["b_lt"])
            for (a0, a1, gt, gk, col) in ((0, 896, qg, "b_qg", 0), (896, 1408, kvg, "b_kvg", 1)):
                n = a1 - a0
                S.op("pool", lambda e, a0=a0, a1=a1, n=n: e.tensor_tensor(out=lsq[:, 0:n], in0=lt[:, a0:a1], in1=lt[:, a0:a1], op=ALU.mult), reads=["b_lt"], writes=["b_lsq"])
                S.op("dve", lambda e, n=n, col=col: e.reduce_sum(out=sm[:, col:col + 1], in_=lsq[:, 0:n], axis=AX.X), reads=["b_lsq"], writes=["b_sm"])
                S.op("act", lambda e, n=n, col=col: e.activation(out=sm[:, col + 2:col + 3], in_=sm[:, col:col + 1], func=AF.Sqrt, bias=RMS_EPS, scale=1.0 / n),
                     reads=["b_sm"], writes=["b_sm"])
                S.op("dve", lambda e, col=col: e.reciprocal(out=sm[:, col + 4:col + 5], in_=sm[:, col + 2:col + 3]), reads=["b_sm"], writes=["b_sm"])
                S.op("dve", lambda e, a0=a0, a1=a1, n=n, gt=gt, col=col: e.scalar_tensor_tensor(
                    out=lb[:].rearrange("p a b -> p (a b)")[:, a0:a1], in0=lt[:, a0:a1], scalar=sm[:, col + 4:col + 5], in1=gt[:, 0:n], op0=ALU.mult, op1=ALU.mult),
                    reads=["b_lt", "b_sm", gk], writes=["b_lb"])
            S.op("dve", lambda e: e.tensor_copy(out=lb[:, 11, 0:64], in_=lt[:, 1408:1472]), reads=["b_lt"], writes=["b_lb"])
            S.op("dve", lambda e: e.tensor_copy(out=lb[:, 12, 0:32], in_=lt[:, 1440:1472]), reads=["b_lt"], writes=["b_lb"])
            S.op("dve", lambda e: e.tensor_copy(out=lb[:, 12, 32:64], in_=lt[:, 1408:1440]), reads=["b_lt"], writes=["b_lb"])
            for j0 in (0, 8):
                nj = min(8, 13 - j0)
                for j in range(nj):
                    S.op("pe", lambda e, j=j, j0=j0: e.transpose(pbt[:, j, :], lb[:, j0 + j, :], C["idb"][:]), reads=["b_lb", "c_idb"], writes=["b_pbt"])
                S.op("act", lambda e, j0=j0, nj=nj: e.copy(out=ltT[:, j0:j0 + nj, :], in_=pbt[:, 0:nj, :]), reads=["b_pbt"], writes=["b_ltT"])
            S.dma("sp", lv[:, :, ts * 128:(ts + 1) * 128], ltT[:], reads=["b_ltT"], writes=["b_lat"])
        S.flush(B.es)


def phase_c(B, lat_ap, wq_ap, wkv_ap, pos_ap, invf_ap, sgn_ap, oT_ap, nq=SEQ // 512):
    S, nc = B.S, B.nc
    TWO_PI = float(2 * np.pi)
    with ExitStack() as st:
        C = make_consts(B, st)
        acc = [B.ps(st, f"c_acc{i}", [128, 512], F32) for i in range(4)]
        pst = [B.ps(st, f"c_st{i}", [128, 512], F32) for i in range(2)]
        pm = [B.ps(st, f"c_pm{i}", [128, 512], F32) for i in range(2)]
        S.excl.update([f"c_acc{i}" for i in range(4)] + ["c_st0", "c_st1", "c_pm0", "c_pm1"])
        Wq = B.sb(st, "c_Wq", [128, 7, 1024], BF16)
        Wkv = B.sb(st, "c_Wkv", [128, 4, 1024], BF16)
        S.dma("pool", Wq[:], wq_ap.rearrange("(kc p) n -> p kc n", p=128), writes=["c_Wq"])
        S.dma("pool", Wkv[:], wkv_ap.rearrange("(kc p) n -> p kc n", p=128), writes=["c_Wkv"])
        invf = B.sb(st, "c_invf", [64, 1], F32)
        sgn = B.sb(st, "c_sgn", [64, 1], F32)
        S.dma("sp", invf[:], invf_ap, writes=["c_invf"])
        S.dma("sp", sgn[:], sgn_ap, writes=["c_sgn"])
        tril = B.sb(st, "c_tril", [128, 128], BF16)
        S.op("pool", lambda e: e.memset(tril[:], 1.0), writes=["c_tril"])
        S.op("pool", lambda e: e.affine_select(out=tril[:], in_=tril[:], pattern=[[1, 128]], compare_op=ALU.is_ge, fill=0.0,
                                               base=0, channel_multiplier=-1), reads=["c_tril"], writes=["c_tril"])
        qn = B.sb(st, "c_qn", [128, SEQ], BF16)
        kn = B.sb(st, "c_kn", [128, SEQ], BF16)
        qr = B.sb(st, "c_qr", [64, SEQ], BF16)
        kr = B.sb(st, "c_kr", [64, SEQ], BF16)
        vt = B.sb(st, "c_vt", [128, SEQ // 128, 132], BF16)
        S.op("pool", lambda e: e.memset(vt[:], 1.0), writes=["c_vt"])
        lat = [B.sb(st, f"c_lat{i}", [128, 13, 512], BF16) for i in range(2)]
        posi = B.sb(st, "c_posi", [64, 512], I32)
        ang = B.sb(st, "c_ang", [64, 512], F32)
        cs_ = B.sb(st, "c_cos", [64, 512], F32)
        sn_ = B.sb(st, "c_sin", [64, 512], F32)
        t1 = B.sb(st, "c_t1", [64, 512], F32)
        t2 = B.sb(st, "c_t2", [64, 512], F32)
        pt = [B.sb(st, f"c_p{i}", [128, 512], BF16) for i in range(2)]
        rs = B.sb(st, "c_rs", [128, 4], F32)
        on = B.sb(st, "c_on", [128, 4, 128], BF16)
        oT = B.sb(st, "c_oT", [128, 512], BF16)
        pbt = pm[1]
        lv = lat_ap.rearrange("(j p) t -> p j t", p=128)
        scale = float(192.0 ** -0.5)

        def rope(dst, dkey, pa, pak, pb, pbk, sl):
            S.op("dve", lambda e: e.tensor_tensor(out=t1[:], in0=pa[0:64, :], in1=cs_[:], op=ALU.mult), reads=[pak, "c_cos"], writes=["c_t1"])
            S.op("dve", lambda e: e.tensor_tensor(out=t2[:], in0=pb[0:64, :], in1=sn_[:], op=ALU.mult), reads=[pbk, "c_sin"], writes=["c_t2"])
            S.op("pool", lambda e: e.tensor_tensor(out=dst[0:64, sl], in0=t1[:], in1=t2[:], op=ALU.add), reads=["c_t1", "c_t2"], writes=[dkey])

        for h in range(4):
            for tt in range(SEQ // 512):
                sl = slice(tt * 512, (tt + 1) * 512)
                lt_ = lat[tt % 2]
                lk = f"c_lat{tt % 2}"
                S.dma("sp", lt_[:], lv[:, :, sl], writes=[lk])
                S.dma("sp", posi[:], pos_ap[0:1, sl].partition_broadcast(64), writes=["c_posi"])
                S.op("dve", lambda e: e.tensor_copy(out=ang[:], in_=posi[:]), reads=["c_posi"], writes=["c_ang"])
                S.op("dve", lambda e: e.tensor_scalar(out=ang[:], in0=ang[:], scalar1=invf[:, 0:1], scalar2=None, op0=ALU.mult), reads=["c_ang", "c_invf"], writes=["c_ang"])
                for (dst, dkey, addc) in ((sn_, "c_sin", 0.0), (cs_, "c_cos", float(0.5 * np.pi))):
                    S.op("dve", lambda e, addc=addc: e.tensor_scalar(out=t1[:], in0=ang[:], scalar1=addc, scalar2=None, op0=ALU.add), reads=["c_ang"], writes=["c_t1"])
                    S.op("dve", lambda e: e.tensor_scalar(out=t2[:], in0=t1[:], scalar1=1.0 / TWO_PI, scalar2=None, op0=ALU.mult), reads=["c_t1"], writes=["c_t2"])
                    S.op("dve", lambda e: e.tensor_copy(out=posi[:], in_=t2[:]), reads=["c_t2"], writes=["c_posi"])
                    S.op("dve", lambda e: e.tensor_copy(out=t2[:], in_=posi[:]), reads=["c_posi"], writes=["c_t2"])
                    S.op("dve", lambda e: e.scalar_tensor_tensor(out=t1[:], in0=t2[:], scalar=-TWO_PI, in1=t1[:], op0=ALU.mult, op1=ALU.add), reads=["c_t1", "c_t2"], writes=["c_t1"])
                    S.op("dve", lambda e: e.tensor_scalar(out=t2[:], in0=t1[:], scalar1=float(np.pi), scalar2=None, op0=ALU.is_gt), reads=["c_t1"], writes=["c_t2"])
                    S.op("dve", lambda e: e.scalar_tensor_tensor(out=t1[:], in0=t2[:], scalar=-TWO_PI, in1=t1[:], op0=ALU.mult, op1=ALU.add), reads=["c_t1", "c_t2"], writes=["c_t1"])
                    S.op("dve", lambda e: e.tensor_scalar(out=t2[:], in0=t1[:], scalar1=float(-np.pi), scalar2=None, op0=ALU.is_lt), reads=["c_t1"], writes=["c_t2"])
                    S.op("dve", lambda e: e.scalar_tensor_tensor(out=t1[:], in0=t2[:], scalar=TWO_PI, in1=t1[:], op0=ALU.mult, op1=ALU.add), reads=["c_t1", "c_t2"], writes=["c_t1"])
                    S.op("act", lambda e, dst=dst: e.activation(out=dst[:], in_=t1[:], func=AF.Sin), reads=["c_t1"], writes=[dkey])
                S.op("dve", lambda e: e.tensor_scalar(out=sn_[:], in0=sn_[:], scalar1=sgn[:, 0:1], scalar2=None, op0=ALU.mult), reads=["c_sin", "c_sgn"], writes=["c_sin"])
                for kc in range(7):
                    S.op("pe", lambda e, kc=kc, lt_=lt_, h=h: e.matmul(pm[0][:], Wq[:, kc, h * 256:h * 256 + 128], lt_[:, kc, :], start=(kc == 0), stop=(kc == 6)),
                         reads=["c_Wq", lk], writes=["c_pm0"])
                S.op("act", lambda e, sl=sl: e.copy(out=qn[:, sl], in_=pm[0][:]), reads=["c_pm0"], writes=["c_qn"])
                for kc in range(7):
                    S.op("pe", lambda e, kc=kc, lt_=lt_, h=h: e.matmul(pst[0][0:64, :], Wq[:, kc, h * 256 + 128:h * 256 + 192], lt_[:, kc, :], start=(kc == 0), stop=(kc == 6)),
                         reads=["c_Wq", lk], writes=["c_st0"])
                for kc in range(7):
                    S.op("pe", lambda e, kc=kc, lt_=lt_, h=h: e.matmul(pst[1][0:64, :], Wq[:, kc, h * 256 + 192:h * 256 + 256], lt_[:, kc, :], start=(kc == 0), stop=(kc == 6)),
                         reads=["c_Wq", lk], writes=["c_st1"])
                rope(qr, "c_qr", pst[0], "c_st0", pst[1], "c_st1", sl)
                if h == 0:
                    S.op("pe", lambda e, lt_=lt_: e.matmul(pst[0][0:64, :], C["idb"][0:64, 0:64], lt_[0:64, 11, :], start=True, stop=True), reads=["c_idb", lk], writes=["c_st0"])
                    S.op("pe", lambda e, lt_=lt_: e.matmul(pst[1][0:64, :], C["idb"][0:64, 0:64], lt_[0:64, 12, :], start=True, stop=True), reads=["c_idb", lk], writes=["c_st1"])
                    rope(kr, "c_kr", pst[0], "c_st0", pst[1], "c_st1", sl)
                for kc in range(4):
                    S.op("pe", lambda e, kc=kc, lt_=lt_, h=h: e.matmul(pm[0][:], Wkv[:, kc, h * 256:h * 256 + 128], lt_[:, 7 + kc, :], start=(kc == 0), stop=(kc == 3)),
                         reads=["c_Wkv", lk], writes=["c_pm0"])
                S.op("act", lambda e, sl=sl: e.copy(out=kn[:, sl], in_=pm[0][:]), reads=["c_pm0"], writes=["c_kn"])
                for sub in range(4):
                    for kc in range(4):
                        S.op("pe", lambda e, kc=kc, lt_=lt_, h=h, sub=sub: e.matmul(pm[1][:, sub * 128:(sub + 1) * 128], lt_[:, 7 + kc, sub * 128:(sub + 1) * 128],
                                                                                   Wkv[:, kc, h * 256 + 128:h * 256 + 256], start=(kc == 0), stop=(kc == 3)),
                             reads=["c_Wkv", lk], writes=["c_pm1"])
                S.op("dve", lambda e, tt=tt: e.tensor_copy(out=vt[:, tt * 4:(tt + 1) * 4, 0:128], in_=pm[1][:].rearrange("p (a b) -> p a b", a=4)), reads=["c_pm1"], writes=["c_vt"])
            step = 0
            for qb in range(nq):
                qsl = slice(qb * 512, (qb + 1) * 512)
                nkt = 4 * qb + 4
                for kt in range(nkt):
                    ksl = slice(kt * 128, (kt + 1) * 128)
                    ps_ = pst[step % 2]
                    psk = f"c_st{step % 2}"
                    pb_ = pt[step % 2]
                    pbk_ = f"c_p{step % 2}"
                    step += 1
                    S.op("pe", lambda e, ps_=ps_, ksl=ksl, qsl=qsl: e.matmul(ps_[:], kn[:, ksl], qn[:, qsl], start=True, stop=False), reads=["c_kn", "c_qn"], writes=[psk])
                    S.op("pe", lambda e, ps_=ps_, ksl=ksl, qsl=qsl: e.matmul(ps_[:], kr[0:64, ksl], qr[0:64, qsl], start=False, stop=True), reads=["c_kr", "c_qr"], writes=[psk])
                    S.op("act", lambda e, ps_=ps_, pb_=pb_: e.activation(out=pb_[:], in_=ps_[:], func=AF.Exp, scale=scale), reads=[psk], writes=[pbk_])
                    j = kt - 4 * qb
                    if j >= 0:
                        S.op("pool", lambda e, pb_=pb_, j=j: e.tensor_tensor(out=pb_[:, j * 128:(j + 1) * 128], in0=pb_[:, j * 128:(j + 1) * 128], in1=tril[:], op=ALU.mult),
                             reads=[pbk_, "c_tril"], writes=[pbk_])
                    for i in range(4):
                        if j > i:
                            continue
                        last = (kt == 4 * qb + i)
                        S.op("pe", lambda e, pb_=pb_, i=i, kt=kt, last=last: e.matmul(acc[i][:, 0:129], pb_[:, i * 128:(i + 1) * 128], vt[:, kt, 0:129],
                                                                                       start=(kt == 0), stop=last), reads=[pbk_, "c_vt"], writes=[f"c_acc{i}"])
                for i in range(4):
                    S.op("dve", lambda e, i=i: e.reciprocal(out=rs[:, i:i + 1], in_=acc[i][:, 128:129]), reads=[f"c_acc{i}"], writes=["c_rs"])
                    S.op("dve", lambda e, i=i: e.tensor_scalar(out=on[:, i, :], in0=acc[i][:, 0:128], scalar1=rs[:, i:i + 1], scalar2=None, op0=ALU.mult),
                         reads=[f"c_acc{i}", "c_rs"], writes=["c_on"])
                pbt_b = pbt[:].bitcast(BF16).rearrange("p (a b) -> p a b", b=128)
                for i in range(4):
                    S.op("pe", lambda e, i=i, pbt_b=pbt_b: e.transpose(pbt_b[:, i, :], on[:, i, :], C["idb"][:]), reads=["c_on", "c_idb"], writes=["c_pm1"])
                S.op("act", lambda e, pbt_b=pbt_b: e.copy(out=oT[:].rearrange("p (a b) -> p a b", a=4), in_=pbt_b[:, 0:4, :]), reads=["c_pm1"], writes=["c_oT"])
                B.finals.append(S.dma("sp", oT_ap[h * 128:(h + 1) * 128, qsl], oT[:], reads=["c_oT"], writes=["c_out"]))
        S.flush(B.es)


def _launch(build, maps):
    nc = bass.Bass("TRN2", target_bir_lowering=False)
    with ExitStack() as es:
        B = Builder(nc, es)
        build(nc, B)
    res = run_bass_kernel_spmd(nc, maps, core_ids=list(range(NCORES)))
    return res.results


def _din(nc, name, shape, dt):
    return nc.dram_tensor(name, list(shape), dt, kind="ExternalInput").ap()


def _dout(nc, name, shape, dt):
    return nc.dram_tensor(name, list(shape), dt, kind="ExternalOutput").ap()


def kernel(x, c, positions, w_mod, b_mod, ln_g, ln_b, a_w_in, a_w_conv, a_a_log, a_dt_bias, a_norm_g, a_w_out,
           b_w_in, b_q_norm_g, b_w_qb, b_kv_norm_g, b_w_kvb, b_w_out):
    f32 = np.float32
    asc = np.ascontiguousarray
    x2 = asc(np.asarray(x, f32)[0])
    R = range(NCORES)
    def bM(nc, B):
        phase_mod(B, _din(nc, "c", [1, D], F32), _din(nc, "wm", [D, 3072], F32), _din(nc, "bm", [1, 3072], F32), _dout(nc, "modrow", [1, 3072], F32))
    maps = [{"c": asc(np.asarray(c, f32)), "wm": asc(np.asarray(w_mod[r // 4][:, (r % 4) * 3072:(r % 4 + 1) * 3072], f32)),
             "bm": asc(np.asarray(b_mod[r // 4][None, (r % 4) * 3072:(r % 4 + 1) * 3072], f32))} for r in R]
    res = _launch(bM, maps)
    mod = asc(np.concatenate([res[r]["modrow"].reshape(-1) for r in R]).reshape(2, 12288))
    def bA(nc, B):
        scr = {n + "s": nc.dram_tensor("scr_" + n, [4, 128, SEQ], BF16).ap() for n in "qkvz"}
        scr["gbs"] = nc.dram_tensor("scr_gb", [SEQ, 8], F32).ap()
        y = _dout(nc, "y0T", [512, SEQ], BF16)
        phase_a1(B, _din(nc, "x", [SEQ, D], F32), _din(nc, "mod", [2, 12288], F32), _din(nc, "w4", [D, 2048], F32), _din(nc, "wab", [D, 8], F32),
                 _din(nc, "cw", [4, 1536], F32), _din(nc, "alog", [1, 4], F32), _din(nc, "dtb", [1, 4], F32), scr)
        phase_a2(B, scr, _din(nc, "ng", [1, 128], F32), y)
    W = np.asarray(a_w_in[0], f32)
    cwf = np.asarray(a_w_conv[0], f32)
    maps = []
    for r in R:
        o = 512 * r
        maps.append({"x": x2, "mod": mod,
                     "w4": asc(np.concatenate([W[:, o:o + 512], W[:, 4096 + o:4096 + o + 512], W[:, 8192 + o:8192 + o + 512], W[:, 12288 + o:12288 + o + 512]], axis=1)),
                     "wab": asc(np.concatenate([W[:, 16384 + 4 * r:16384 + 4 * r + 4], W[:, 16416 + 4 * r:16416 + 4 * r + 4]], axis=1)),
                     "cw": asc(np.concatenate([cwf[:, q + o:q + o + 512] for q in (0, 4096, 8192)], axis=1)),
                     "alog": asc(np.asarray(a_a_log, f32)[:, 4 * r:4 * r + 4]), "dtb": asc(np.asarray(a_dt_bias, f32)[:, 4 * r:4 * r + 4]),
                     "ng": asc(np.asarray(a_norm_g, f32))})
    res = _launch(bA, maps)
    Y0 = np.concatenate([res[r]["y0T"] for r in R], axis=0)
    def bB(nc, B):
        x1 = _dout(nc, "x1", [TPC, D], F32)
        modt = _din(nc, "mod", [2, 12288], F32)
        phase_out(B, _din(nc, "yT", [D, TPC], BF16), None, _din(nc, "w", [D, D], F32), _din(nc, "xr", [TPC, D], F32), modt, 0,
                  _din(nc, "lg", [1, D], F32), _din(nc, "lb", [1, D], F32), x1)
        phase_b2(B, x1, modt, _din(nc, "win", [D, 5568], F32), _din(nc, "qg", [1, 896], F32), _din(nc, "kvg", [1, 512], F32),
                 _dout(nc, "latT", [13 * 128, TPC], BF16), _dout(nc, "zsT", [D, TPC], BF16))
    maps = [{"yT": asc(Y0[:, TPC * r:TPC * (r + 1)]), "w": asc(np.asarray(a_w_out[0], f32)), "xr": asc(x2[TPC * r:TPC * (r + 1)]), "mod": mod,
             "lg": asc(np.asarray(ln_g, f32)[0:1]), "lb": asc(np.asarray(ln_b, f32)[0:1]), "win": asc(np.asarray(b_w_in[0], f32)),
             "qg": asc(np.asarray(b_q_norm_g, f32)), "kvg": asc(np.asarray(b_kv_norm_g, f32))} for r in R]
    res = _launch(bB, maps)
    x1s = [res[r]["x1"] for r in R]
    zss = [res[r]["zsT"] for r in R]
    lat = asc(np.concatenate([res[r]["latT"] for r in R], axis=1))
    def bC(nc, B):
        phase_c(B, _din(nc, "lat", [13 * 128, SEQ], BF16), _din(nc, "wq", [896, 1024], F32), _din(nc, "wkv", [512, 1024], F32),
                _din(nc, "pos", [1, SEQ], I32), _din(nc, "invf", [64, 1], F32), _din(nc, "sgn", [64, 1], F32), _dout(nc, "o1T", [512, SEQ], BF16))
    half = np.arange(32, dtype=np.float32) / 32.0
    invf = (10000.0 ** (-half)).astype(f32)
    invf = asc(np.concatenate([invf, invf])[:, None])
    sgn = asc(np.concatenate([-np.ones(32, f32), np.ones(32, f32)])[:, None])
    wqb = np.asarray(b_w_qb[0], f32).reshape(896, 32, 192)
    wkvb = np.asarray(b_w_kvb[0], f32).reshape(512, 32, 256)
    maps = []
    for r in R:
        wq = wqb[:, 4 * r:4 * r + 4]
        wq = np.concatenate([wq, wq[:, :, 160:192], wq[:, :, 128:160]], axis=2)
        maps.append({"lat": lat, "wq": asc(wq.reshape(896, 1024)), "wkv": asc(wkvb[:, 4 * r:4 * r + 4].reshape(512, 1024)),
                     "pos": asc(np.asarray(positions, np.int32)), "invf": invf, "sgn": sgn})
    res = _launch(bC, maps)
    O1 = np.concatenate([res[r]["o1T"] for r in R], axis=0)
    def bD(nc, B):
        phase_out(B, _din(nc, "yT", [D, TPC], BF16), _din(nc, "zT", [D, TPC], BF16), _din(nc, "w", [D, D], F32), _din(nc, "xr", [TPC, D], F32),
                  _din(nc, "mod", [2, 12288], F32), 1, _din(nc, "lg", [1, D], F32), _din(nc, "lb", [1, D], F32), _dout(nc, "out", [TPC, D], F32))
    maps = [{"yT": asc(O1[:, TPC * r:TPC * (r + 1)]), "zT": zss[r], "w": asc(np.asarray(b_w_out[0], f32)), "xr": x1s[r], "mod": mod,
             "lg": asc(np.asarray(ln_g, f32)[1:2]), "lb": asc(np.asarray(ln_b, f32)[1:2])} for r in R]
    res = _launch(bD, maps)
    return np.concatenate([res[r]["out"] for r in R], axis=0)[None].astype(f32)
```

```python
import numpy as np
from contextlib import ExitStack
import concourse.bass as bass
import concourse.mybir as mybir
from concourse.bass_utils import run_bass_kernel_spmd

F32 = mybir.dt.float32
BF16 = mybir.dt.bfloat16
I32 = mybir.dt.int32
AF = mybir.ActivationFunctionType
ALU = mybir.AluOpType
AX = mybir.AxisListType

NCORES = 8
SEQ = 8192
D = 4096
TPC = SEQ // NCORES
ALPHA = (2.0 * 2) ** 0.25
RMS_EPS = 1e-6
LN_EPS = 1e-5
NEG = -30000.0

ENGS = ("pe", "act", "dve", "pool", "sp")
SEM_CHUNK = 4000
STQ = "sp"
HT_ACT = False


class Sched:
    def __init__(self, nc):
        self.nc = nc
        self.ops = {e: [] for e in ENGS}
        self.writers = {}
        self.readers = {}
        self.dmacnt = {}
        self.nops = 0
        self.excl = set()

    def _collect(self, eng, reads, writes):
        deps = []
        for k in reads:
            for t in self.writers.get(k, ()):
                deps.append((t, "raw"))
            if k in self.excl:
                for t in self.readers.get(k, ()):
                    deps.append((t, "war"))
        for k in writes:
            for t in self.writers.get(k, ()):
                deps.append((t, "waw"))
            for t in self.readers.get(k, ()):
                deps.append((t, "war"))
        return deps

    def _commit(self, tok, reads, writes, partial):
        for k in writes:
            if partial:
                self.writers.setdefault(k, []).append(tok)
            else:
                self.writers[k] = [tok]
            self.readers[k] = []
        for k in reads:
            self.readers.setdefault(k, []).append(tok)

    def op(self, eng, fn, reads=(), writes=(), partial=False):
        deps = self._collect(eng, reads, writes)
        tok = ("E", eng, self.nops)
        self.ops[eng].append(dict(fn=fn, deps=deps, tok=tok, dma=None))
        self._commit(tok, reads, writes, partial)
        self.nops += 1
        return tok

    def dma(self, eng, out, in_, reads=(), writes=(), sem=None, partial=False, **kw):
        semname = sem or (writes[0] if writes else reads[0])
        deps = self._collect(eng, reads, writes)
        self.dmacnt[semname] = self.dmacnt.get(semname, 0) + 1
        tok = ("D", semname, self.dmacnt[semname] * 16)
        fn = lambda e, out=out, in_=in_, kw=kw: e.dma_start(out=out, in_=in_, **kw)
        self.ops[eng].append(dict(fn=fn, deps=deps, tok=tok, dma=semname))
        self._commit(tok, reads, writes, partial)
        self.nops += 1
        return tok

    def coll(self, kind, src, dst, reads, writes, name, inc=1):
        deps = self._collect("pool", reads, writes)
        self.dmacnt[name] = self.dmacnt.get(name, 0)
        self.collinc = getattr(self, "collinc", {})
        self.collinc[name] = self.collinc.get(name, 0) + inc
        tok = ("C", name, self.collinc[name])
        fn = lambda e: e.collective_compute(kind, ALU.bypass, replica_groups=[list(range(NCORES))], ins=[src], outs=[dst])
        self.ops["pool"].append(dict(fn=fn, deps=deps, tok=tok, dma=name, inc=inc))
        self._commit(tok, reads, writes, False)
        self.nops += 1
        return tok

    def flush(self, es, final_waits=(), barrier=True):
        nc = self.nc
        if not hasattr(self, "sems"):
            self.sems = {}
            self.nflag = {e: 0 for e in ENGS}
        sems = self.sems
        flagged = set()
        for e in ENGS:
            for o in self.ops[e]:
                keep = []
                for (t, kind) in o["deps"]:
                    if t[0] == "E" and t[1] == e and o["dma"] is None:
                        if e == "pe" or kind == "war":
                            continue
                    keep.append(t)
                    if t[0] == "E":
                        flagged.add(t)
                o["deps"] = keep
        lasts = []
        if barrier:
            for e in ENGS:
                for o in reversed(self.ops[e]):
                    if o["dma"] is None:
                        flagged.add(o["tok"])
                        lasts.append(o["tok"])
                        break
        for t in final_waits:
            if t[0] == "E":
                flagged.add(t)
        tokval = {}

        def getsem(key):
            if key not in sems:
                sems[key] = es.enter_context(nc.semaphore("s_" + "_".join(str(x) for x in key)))
            return sems[key]

        for e in ENGS:
            for o in self.ops[e]:
                if o["tok"] in flagged:
                    n = self.nflag[e]
                    tokval[o["tok"]] = ((e, n // SEM_CHUNK), n % SEM_CHUNK + 1)
                    self.nflag[e] = n + 1
                    getsem((e, n // SEM_CHUNK))
        for name in self.dmacnt:
            getsem(("D", name))

        def resolve(t):
            if t[0] == "E":
                return tokval[t]
            return (("D", t[1]), t[2])

        collinc = getattr(self, "collinc", {})

        endw = [resolve(t) for t in list(final_waits) + lasts]
        if barrier:
            endw += [(("D", name), c * 16 + collinc.get(name, 0)) for name, c in self.dmacnt.items()]
        ops = self.ops
        with nc.Block() as block:
            engobj = {"pe": block.tensor, "act": block.scalar, "dve": block.vector, "pool": block.gpsimd,
                      "sp": block.sync}

            def make(e):
                def body(eng):
                    waited = {}
                    for o in ops[e]:
                        need = {}
                        for t in o["deps"]:
                            s, v = resolve(t)
                            if waited.get(s, 0) >= v:
                                continue
                            need[s] = max(need.get(s, 0), v)
                        for s, v in need.items():
                            eng.wait_ge(sems[s], v)
                            waited[s] = v
                        ins = o["fn"](eng)
                        if o["dma"] is not None:
                            ins.then_inc(sems[("D", o["dma"])], o.get("inc", 16))
                        elif o["tok"] in tokval:
                            ins.then_inc(sems[tokval[o["tok"]][0]], 1)
                    for s, v in endw:
                        if waited.get(s, 0) < v:
                            eng.wait_ge(sems[s], v)
                            waited[s] = v
                return body

            for e in ENGS:
                engobj[e](make(e))
        self.ops = {e: [] for e in ENGS}
        self.writers = {}
        self.readers = {}


class Builder:
    def __init__(self, nc, es):
        self.nc = nc
        self.es = es
        self.S = Sched(nc)
        self.finals = []

    def sb(self, st, name, shape, dt):
        self.uid = getattr(self, "uid", 0) + 1
        return st.enter_context(self.nc.sbuf_tensor(f"{name}_u{self.uid}", list(shape), dt))

    def ps(self, st, name, shape, dt=F32):
        self.uid = getattr(self, "uid", 0) + 1
        return st.enter_context(self.nc.psum_tensor(f"{name}_u{self.uid}", list(shape), dt))


def make_consts(B, st):
    S = B.S
    C = {}
    C["idf"] = B.sb(st, "c_idf", [128, 128], F32)
    C["idb"] = B.sb(st, "c_idb", [128, 128], BF16)
    C["onesf"] = B.sb(st, "c_onesf", [128, 128], F32)
    C["onesb"] = B.sb(st, "c_onesb", [128, 128], BF16)
    S.op("pool", lambda e: e.memset(C["idf"][:], 0.0), writes=["c_idf"])
    S.op("pool", lambda e: e.affine_select(out=C["idf"][:], in_=C["idf"][:], pattern=[[-1, 128]],
                                           compare_op=ALU.not_equal, fill=1.0, base=0, channel_multiplier=1),
         reads=["c_idf"], writes=["c_idf"])
    S.op("pool", lambda e: e.tensor_copy(out=C["idb"][:], in_=C["idf"][:]), reads=["c_idf"], writes=["c_idb"])
    S.op("pool", lambda e: e.memset(C["onesf"][:], 1.0), writes=["c_onesf"])
    S.op("pool", lambda e: e.memset(C["onesb"][:], 1.0), writes=["c_onesb"])
    return C


def phase_mod(B, c_ap, wm_ap, bm_ap, out_ap):
    S, nc = B.S, B.nc
    with ExitStack() as st:
        C = make_consts(B, st)
        ct = B.sb(st, "m_ct", [32, 128], F32)
        cact = B.sb(st, "m_cact", [128, 32], F32)
        bmt = B.sb(st, "m_bm", [1, 3072], F32)
        rowt = B.sb(st, "m_row", [1, 3072], F32)
        wt = [B.sb(st, f"m_w{i}", [128, 3072], F32) for i in range(3)]
        pT = B.ps(st, "m_pT", [128, 512], F32)
        pr = [B.ps(st, f"m_pr{i}", [128, 512], F32) for i in range(6)]
        S.dma("sp", ct[:], c_ap.rearrange("o (kc p) -> (o kc) p", p=128), writes=["m_ct"])
        S.dma("sp", bmt[:], bm_ap, writes=["m_bm"])
        S.op("pe", lambda e: e.transpose(pT[:, 0:32], ct[:], C["idf"][0:32, 0:32]), reads=["m_ct", "c_idf"], writes=["m_pT"])
        S.op("act", lambda e: e.activation(out=cact[:], in_=pT[:, 0:32], func=AF.Silu), reads=["m_pT"], writes=["m_cact"])
        for kc in range(32):
            w = wt[kc % 3]
            S.dma("sp" if kc % 2 == 0 else "act", w[:], wm_ap[kc * 128:(kc + 1) * 128, :], writes=[f"m_w{kc % 3}"])
            for nb in range(6):
                S.op("pe", lambda e, w=w, nb=nb, kc=kc: e.matmul(pr[nb][0:1, :], cact[:, kc:kc + 1], w[:, nb * 512:(nb + 1) * 512],
                                                                 start=(kc == 0), stop=(kc == 31)),
                     reads=[f"m_w{kc % 3}", "m_cact"], writes=[f"m_pr{nb}"])
        for nb in range(6):
            S.op("dve", lambda e, nb=nb: e.tensor_tensor(out=rowt[0:1, nb * 512:(nb + 1) * 512], in0=pr[nb][0:1, :],
                                                         in1=bmt[0:1, nb * 512:(nb + 1) * 512], op=ALU.add),
                 reads=[f"m_pr{nb}", "m_bm"], writes=["m_row"])
        t = S.dma("sp", out_ap, rowt[:], reads=["m_row"], writes=["m_out"])
        S.flush(B.es, final_waits=[t])


def load_mod_fm(B, st, C, mod_ap, layer, pt, ptkey):
    S = B.S
    rows = B.sb(st, f"md_rows{layer}", [64, 128], F32)
    sh = B.sb(st, f"md_sh{layer}", [128, 32], F32)
    sc = B.sb(st, f"md_sc{layer}", [128, 32], F32)
    S.dma("sp", rows[:], mod_ap[layer:layer + 1, 0:8192].rearrange("o (kc p) -> (o kc) p", p=128), writes=[f"md_rows{layer}"])
    S.op("pe", lambda e: e.transpose(pt[:, 0:64], rows[:], C["idf"][0:64, 0:64]), reads=[f"md_rows{layer}", "c_idf"], writes=[ptkey])
    S.op("dve", lambda e: e.tensor_copy(out=sh[:], in_=pt[:, 0:32]), reads=[ptkey], writes=[f"md_sh{layer}"])
    S.op("dve", lambda e: e.tensor_scalar(out=sc[:], in0=pt[:, 32:64], scalar1=1.0, scalar2=None, op0=ALU.add),
         reads=[ptkey], writes=[f"md_sc{layer}"])
    return sh, sc


def make_hT(B, C, xs, xkey, hT, hkey, st_idx, sh, sc, shk, sck, ptr, ptk, cnt):
    S = B.S
    for g in range(8):
        pt = ptr[cnt[0] % len(ptr)]
        pk = ptk[cnt[0] % len(ptr)]
        cnt[0] += 1
        for j in range(4):
            kc = g * 4 + j
            S.op("pe", lambda e, pt=pt, j=j, kc=kc: e.transpose(pt[:, j, :], xs[:, kc * 128:(kc + 1) * 128], C["idf"][:]),
                 reads=[xkey, "c_idf"], writes=[pk])
        for j in range(4):
            kc = g * 4 + j
            dst = hT[:, kc, st_idx * 128:(st_idx + 1) * 128]
            if HT_ACT and kc % 2 == 0:
                S.op("act", lambda e, pt=pt, j=j, kc=kc, dst=dst: e.activation(out=dst, in_=pt[:, j, :], func=AF.Identity,
                                                                               bias=sh[:, kc:kc + 1], scale=sc[:, kc:kc + 1]),
                     reads=[pk, shk, sck], writes=[f"{hkey}{kc}"])
            else:
                S.op("dve", lambda e, pt=pt, j=j, kc=kc, dst=dst: e.scalar_tensor_tensor(out=dst, in0=pt[:, j, :], scalar=sc[:, kc:kc + 1],
                                                                                         in1=sh[:, kc:kc + 1].broadcast_to([128, 128]),
                                                                                         op0=ALU.mult, op1=ALU.add),
                     reads=[pk, shk, sck], writes=[f"{hkey}{kc}"])


def phase_a1(B, x_ap, mod_ap, w4_ap, wab_ap, cw_ap, alog_ap, dtb_ap, scr, ntiles=SEQ // 512, stage=9):
    S, nc = B.S, B.nc
    with ExitStack() as st:
        C = make_consts(B, st)
        ptr = [B.ps(st, f"a_pt{i}", [128, 4, 128], F32) for i in range(2)]
        ptk = [f"a_pt{i}" for i in range(2)]
        psp = [B.ps(st, f"a_pp{i}", [128, 512], F32) for i in range(4)]
        pgb = B.ps(st, "a_pgb", [128, 512], F32)
        pss = B.ps(st, "a_pss", [128, 512], F32)
        S.excl.update(["a_pt0", "a_pt1", "a_pp0", "a_pp1", "a_pp2", "a_pp3", "a_pgb", "a_pss"])
        sh, sc = load_mod_fm(B, st, C, mod_ap, 0, pgb, "a_pgb")
        xs = [B.sb(st, f"a_xs{i}", [128, D], F32) for i in range(2)]
        hT = B.sb(st, "a_hT", [128, 32, 512], BF16)
        Wt = [B.sb(st, f"a_W{i}", [128, 16, 512], BF16) for i in range(3)]
        pc = B.sb(st, "a_pc", [128, 12, 515], F32)
        tmp = [B.sb(st, f"a_tmp{i}", [128, 512], F32) for i in range(2)]
        sl = [B.sb(st, f"a_sl{i}", [128, 512], F32) for i in range(2)]
        sqb = B.sb(st, "a_sqb", [128, 512], BF16)
        rn = B.sb(st, "a_rn", [128, 512], F32)
        outs = {n: B.sb(st, f"a_o{n}", [128, 4, 512], BF16) for n in "qkvz"}
        wabf = B.sb(st, "a_wabf", [128, 32, 8], F32)
        wab = B.sb(st, "a_wab", [128, 32, 8], BF16)
        cwr = B.sb(st, "a_cwr", [48, 128], F32)
        cwT = B.sb(st, "a_cwT", [128, 48], F32)
        alb = B.sb(st, "a_alb", [128, 4], F32)
        negA = B.sb(st, "a_negA", [128, 4], F32)
        dtb = B.sb(st, "a_dtb", [128, 4], F32)
        gbt = B.sb(st, "a_gbt", [128, 4, 8], F32)
        t4 = [B.sb(st, f"a_t4{i}", [128, 4], F32) for i in range(2)]
        S.dma("sp", wabf[:], wab_ap.rearrange("(kc p) n -> p kc n", p=128), writes=["a_wabf"])
        S.op("dve", lambda e: e.tensor_copy(out=wab[:], in_=wabf[:]), reads=["a_wabf"], writes=["a_wab"])
        S.dma("sp", cwr[:], cw_ap.rearrange("j (b p) -> (j b) p", p=128), writes=["a_cwr"])
        S.op("pe", lambda e: e.transpose(pss[:, 0:48], cwr[:], C["idf"][0:48, 0:48]), reads=["a_cwr", "c_idf"], writes=["a_pss"])
        S.op("dve", lambda e: e.tensor_copy(out=cwT[:], in_=pss[:, 0:48]), reads=["a_pss"], writes=["a_cwT"])
        S.dma("sp", alb[:], alog_ap.partition_broadcast(128), writes=["a_alb"])
        S.dma("sp", dtb[:], dtb_ap.partition_broadcast(128), writes=["a_dtb"])
        S.op("act", lambda e: e.activation(out=negA[:], in_=alb[:], func=AF.Exp), reads=["a_alb"], writes=["a_negA"])
        S.op("dve", lambda e: e.tensor_scalar(out=negA[:], in0=negA[:], scalar1=-1.0, scalar2=None, op0=ALU.mult),
             reads=["a_negA"], writes=["a_negA"])
        S.op("pool", lambda e: e.memset(pc[:], 0.0), writes=[f"a_pc{b}" for b in range(12)])
        xv = x_ap.rearrange("(n p) d -> n p d", p=128)
        w4v = w4_ap.rearrange("(kc p) n -> p kc n", p=128)
        cnt = [0]
        wcnt = 0
        pend = []
        qk_scale = 128.0 ** -0.5
        for T in range(ntiles if stage > 0 else 0):
            for stx in range(4):
                n = T * 4 + stx
                xb = xs[n % 2]
                S.dma("sp", xb[:], xv[n], writes=[f"a_xs{n % 2}"])
                make_hT(B, C, xb, f"a_xs{n % 2}", hT, "a_hT", stx, sh, sc, "md_sh0", "md_sc0", ptr, ptk, cnt)
            for stx in range(4 if stage > 1 else 0):
                for kc in range(32):
                    S.op("pe", lambda e, kc=kc, stx=stx: e.matmul(pgb[:, 0:8], hT[:, kc, stx * 128:(stx + 1) * 128], wab[:, kc, :],
                                                                    start=(kc == 0), stop=(kc == 31)),
                         reads=[f"a_hT{kc}", "a_wab"], writes=["a_pgb"])
                S.op("act", lambda e: e.activation(out=t4[0][:], in_=pgb[:, 0:4], func=AF.Exp, scale=-1.0), reads=["a_pgb"], writes=["a_t40"])
                S.op("dve", lambda e: e.tensor_tensor(out=t4[1][:], in0=pgb[:, 4:8], in1=dtb[:], op=ALU.add), reads=["a_pgb", "a_dtb"], writes=["a_t41"])
                S.op("dve", lambda e: e.tensor_scalar(out=t4[0][:], in0=t4[0][:], scalar1=1.0, scalar2=None, op0=ALU.add), reads=["a_t40"], writes=["a_t40"])
                S.op("dve", lambda e, stx=stx: e.reciprocal(out=gbt[:, stx, 4:8], in_=t4[0][:]), reads=["a_t40"], writes=["a_gbt"])
                S.op("act", lambda e: e.activation(out=t4[1][:], in_=t4[1][:], func=AF.Exp), reads=["a_t41"], writes=["a_t41"])
                S.op("act", lambda e: e.activation(out=t4[1][:], in_=t4[1][:], func=AF.Ln, bias=1.0, scale=1.0), reads=["a_t41"], writes=["a_t41"])
                S.op("dve", lambda e, stx=stx: e.tensor_tensor(out=gbt[:, stx, 0:4], in0=t4[1][:], in1=negA[:], op=ALU.mult),
                     reads=["a_t41", "a_negA"], writes=["a_gbt"])
            if stage > 1:
                S.dma("sp", scr["gbs"].rearrange("(n p) e -> p n e", p=128)[:, T * 4:(T + 1) * 4, :], gbt[:], reads=["a_gbt"], writes=["scr_gb"])
            for cb in range(4 if stage > 2 else 0):
                for half in range(2):
                    slot = wcnt % 3
                    wcnt += 1
                    S.dma("pool", Wt[slot][:], w4v[:, half * 16:(half + 1) * 16, cb * 512:(cb + 1) * 512], writes=[f"a_W{slot}"])
                    if half == 0 and pend:
                        pend.pop()()
                    for sbk in range(4):
                        for k16 in range(16):
                            kc = half * 16 + k16
                            S.op("pe", lambda e, slot=slot, sbk=sbk, k16=k16, kc=kc, half=half: e.matmul(
                                psp[sbk][:], Wt[slot][:, k16, sbk * 128:(sbk + 1) * 128], hT[:, kc, :],
                                start=(kc == 0), stop=(kc == 31)),
                                reads=[f"a_W{slot}", f"a_hT{kc}"], writes=[f"a_pp{sbk}"])
                for sbk in range(4):
                    b = cb * 4 + sbk
                    if cb == 3:
                        S.op("act", lambda e, sbk=sbk: e.activation(out=outs["z"][:, sbk, :], in_=psp[sbk][:], func=AF.Silu),
                             reads=[f"a_pp{sbk}"], writes=["a_oz"])
                        continue
                    ev = "act" if sbk % 2 == 0 else "dve"
                    if ev == "act":
                        S.op("act", lambda e, b=b, sbk=sbk: e.copy(out=pc[:, b, 3:515], in_=psp[sbk][:]), reads=[f"a_pp{sbk}"], writes=[f"a_pc{b}"])
                    else:
                        S.op("dve", lambda e, b=b, sbk=sbk: e.tensor_copy(out=pc[:, b, 3:515], in_=psp[sbk][:]), reads=[f"a_pp{sbk}"], writes=[f"a_pc{b}"])
                    if stage < 4:
                        continue
                    ce = "dve"
                    tb = tmp[b % 2]
                    tk = f"a_tmp{b % 2}"
                    S.op(ce, lambda e, b=b, tb=tb: e.tensor_scalar(out=tb[:], in0=pc[:, b, 3:515], scalar1=cwT[:, 3 * 12 + b:3 * 12 + b + 1],
                                                                   scalar2=None, op0=ALU.mult), reads=[f"a_pc{b}", "a_cwT"], writes=[tk])
                    for j in (2, 1, 0):
                        S.op(ce, lambda e, b=b, tb=tb, j=j: e.scalar_tensor_tensor(out=tb[:], in0=pc[:, b, j:j + 512],
                                                                                 scalar=cwT[:, j * 12 + b:j * 12 + b + 1], in1=tb[:],
                                                                                 op0=ALU.mult, op1=ALU.add),
                             reads=[f"a_pc{b}", "a_cwT", tk], writes=[tk])
                    S.op(ce, lambda e, b=b: e.tensor_copy(out=pc[:, b, 0:3], in_=pc[:, b, 512:515]), reads=[f"a_pc{b}"], writes=[f"a_pc{b}"])
                    if stage < 5:
                        continue
                    if cb == 2:
                        S.op("act", lambda e, tb=tb, sbk=sbk: e.activation(out=outs["v"][:, sbk, :], in_=tb[:], func=AF.Silu),
                             reads=[tk], writes=["a_ov"])
                        continue
                    slb = sl[b % 2]
                    sk = f"a_sl{b % 2}"
                    S.op("act", lambda e, tb=tb, slb=slb: e.activation(out=slb[:], in_=tb[:], func=AF.Silu), reads=[tk], writes=[sk])
                    S.op("dve", lambda e, slb=slb: e.tensor_tensor(out=sqb[:], in0=slb[:], in1=slb[:], op=ALU.mult), reads=[sk], writes=["a_sqb"])
                    S.op("pe", lambda e: e.matmul(pss[:], C["onesb"][:], sqb[:], start=True, stop=True), reads=["c_onesb", "a_sqb"], writes=["a_pss"])
                    S.op("act", lambda e: e.activation(out=rn[:], in_=pss[:], func=AF.Sqrt, bias=RMS_EPS, scale=1.0), reads=["a_pss"], writes=["a_rn"])
                    S.op("dve", lambda e: e.reciprocal(out=rn[:], in_=rn[:]), reads=["a_rn"], writes=["a_rn"])
                    nm = "q" if cb == 0 else "k"
                    S.op("dve", lambda e, slb=slb, nm=nm, sbk=sbk, cb=cb: e.scalar_tensor_tensor(
                        out=outs[nm][:, sbk, :], in0=slb[:], scalar=(qk_scale if cb == 0 else 1.0), in1=rn[:], op0=ALU.mult, op1=ALU.mult),
                        reads=[sk, "a_rn"], writes=[f"a_o{nm}"])
                nm = "qkvz"[cb]
                if stage < 6:
                    continue
                pend.append(lambda nm=nm, T=T: S.dma(STQ, scr[nm + "s"].rearrange("h d t -> d h t")[:, :, T * 512:(T + 1) * 512], outs[nm][:],
                                                     reads=[f"a_o{nm}"], writes=[f"scr_{nm}"]))
        while pend:
            pend.pop()()
        S.flush(B.es)


def fl(ap):
    return ap.rearrange("p a b -> p (a b)")


def phase_a2(B, scr, ng_ap, y_ap, ntiles=SEQ // 512, NL=6, stage=9):
    S, nc = B.S, B.nc
    with ExitStack() as st:
        C = make_consts(B, st)
        banks = [B.ps(st, f"g_ps{i}", [128, 512], F32) for i in range(7)]
        pbt = B.ps(st, "g_pbt", [128, 8, 128], BF16)
        S.excl.update([f"g_ps{i}" for i in range(7)] + ["g_pbt"])
        bcnt = [0]

        def nb():
            i = bcnt[0] % 7
            bcnt[0] += 1
            return banks[i], f"g_ps{i}"

        def v3(t):
            return t[:].rearrange("p (a b) -> p a b", a=4)

        def T3(name, dt=F32):
            return B.sb(st, name, [128, 4, 128], dt)

        triu = B.sb(st, "g_triu", [128, 128], F32)
        su4, nm4, i4 = T3("g_su4"), T3("g_nm4"), T3("g_i4")
        ng = B.sb(st, "g_ng", [128, 1], F32)
        S.op("pool", lambda e: e.memset(triu[:], 1.0), writes=["g_triu"])
        S.op("pool", lambda e: e.affine_select(out=triu[:], in_=triu[:], pattern=[[1, 128]], compare_op=ALU.is_ge, fill=0.0,
                                               base=0, channel_multiplier=-1), reads=["g_triu"], writes=["g_triu"])
        S.op("pool", lambda e: e.memset(su4[:], 1.0), writes=["g_su4"])
        S.op("pool", lambda e: e.affine_select(out=su4[:], in_=su4[:], pattern=[[0, 4], [1, 128]], compare_op=ALU.is_ge, fill=0.0,
                                               base=-1, channel_multiplier=-1), reads=["g_su4"], writes=["g_su4"])
        S.op("pool", lambda e: e.memset(nm4[:], 0.0), writes=["g_nm4"])
        S.op("pool", lambda e: e.affine_select(out=nm4[:], in_=nm4[:], pattern=[[0, 4], [1, 128]], compare_op=ALU.is_ge, fill=NEG,
                                               base=0, channel_multiplier=-1), reads=["g_nm4"], writes=["g_nm4"])
        S.op("pool", lambda e: e.memset(i4[:], 0.0), writes=["g_i4"])
        S.op("pool", lambda e: e.affine_select(out=i4[:], in_=i4[:], pattern=[[0, 4], [-1, 128]], compare_op=ALU.not_equal, fill=1.0,
                                               base=0, channel_multiplier=1), reads=["g_i4"], writes=["g_i4"])
        S.dma("sp", ng[:], ng_ap.rearrange("o p -> p o"), writes=["g_ng"])
        S.op("dve", lambda e: e.tensor_scalar(out=ng[:], in0=ng[:], scalar1=float(128.0 ** 0.5), scalar2=None, op0=ALU.mult),
             reads=["g_ng"], writes=["g_ng"])
        S32 = T3("g_S32")
        Sbf = T3("g_Sbf", BF16)
        S.op("pool", lambda e: e.memset(S32[:], 0.0), writes=["g_S32"])
        S.op("pool", lambda e: e.memset(Sbf[:], 0.0), writes=["g_Sbf"])
        inb = {n: [B.sb(st, f"g_in{n}{i}", [128, 4, 512], BF16) for i in range(2)] for n in "qkvz"}
        gbb = [B.sb(st, f"g_gb{i}", [128, 4, 8], F32) for i in range(2)]
        ybuf = [B.sb(st, f"g_y{i}", [128, 4, 512], BF16) for i in range(2)]
        kvtm = B.sb(st, "g_kvtm", [128, 8, 128], BF16)
        gc = B.sb(st, "g_gc", [128, 2, 4], F32)
        ngc = B.sb(st, "g_ngc", [128, 4], F32)
        ed = B.sb(st, "g_ed", [128, 4], F32)
        gm, E, GM, DT, DTS, U, UT = T3("g_gm"), T3("g_E"), T3("g_GM"), T3("g_DT"), T3("g_DTS"), T3("g_U"), T3("g_UT")
        Wb = [T3(f"g_W{i}") for i in range(2)]
        WTb = [T3(f"g_WT{i}") for i in range(2)]
        Pb = [T3(f"g_P{i}") for i in range(2)]
        TpT, qkT, kgT, qdT, kdec, R, vnew = (T3(n, BF16) for n in ("g_TpT", "g_qkT", "g_kgT", "g_qdT", "g_kdec", "g_R", "g_vnew"))
        sq = T3("g_sq", BF16)
        rn, yt = T3("g_rn"), T3("g_yt")

        def bc(ap2):
            return ap2.unsqueeze(2).broadcast_to([128, 4, 128])

        for T in range(ntiles):
            bi = T % 2
            for n in "qkvz":
                S.dma("sp", inb[n][bi][:], scr[n + "s"].rearrange("h d t -> d h t")[:, :, T * 512:(T + 1) * 512],
                      reads=[f"scr_{n}"], writes=[f"g_in{n}{bi}"])
            S.dma("sp", gbb[bi][:], scr["gbs"].rearrange("(n p) e -> p n e", p=128)[:, T * 4:(T + 1) * 4, :],
                  reads=["scr_gb"], writes=[f"g_gb{bi}"])
            qT, kT, vT, zs, gb = inb["q"][bi], inb["k"][bi], inb["v"][bi], inb["z"][bi], gbb[bi]
            kq, kk, kv, kz, kgb = (f"g_inq{bi}", f"g_ink{bi}", f"g_inv{bi}", f"g_inz{bi}", f"g_gb{bi}")
            for c in range(4):
                cs = slice(c * 128, (c + 1) * 128)
                if stage < 1:
                    continue
                for h in range(4):
                    S.op("pe", lambda e, h=h, cs=cs, kT=kT: e.transpose(pbt[:, h, :], kT[:, h, cs], C["idb"][:]), reads=[kk, "c_idb"], writes=["g_pbt"])
                for h in range(4):
                    S.op("pe", lambda e, h=h, cs=cs, vT=vT: e.transpose(pbt[:, 4 + h, :], vT[:, h, cs], C["idb"][:]), reads=[kv, "c_idb"], writes=["g_pbt"])
                S.op("act", lambda e: e.copy(out=kvtm[:], in_=pbt[:]), reads=["g_pbt"], writes=["g_kvtm"])
                if stage < 2:
                    continue
                pa, pak = nb()
                S.op("pe", lambda e, pa=pa, gb=gb, c=c: e.matmul(pa[:, 0:4], triu[:], gb[:, c, 0:4], start=True, stop=True), reads=["g_triu", kgb], writes=[pak])
                S.op("pe", lambda e, pa=pa, gb=gb, c=c: e.matmul(pa[:, 4:8], C["onesf"][:], gb[:, c, 0:4], start=True, stop=True), reads=["c_onesf", kgb], writes=[pak])
                S.op("dve", lambda e, pa=pa: e.tensor_copy(out=gc[:].rearrange("p a b -> p (a b)"), in_=pa[:, 0:8]), reads=[pak], writes=["g_gc"])
                S.op("dve", lambda e: e.tensor_scalar(out=ngc[:], in0=gc[:, 0, :], scalar1=-1.0, scalar2=None, op0=ALU.mult), reads=["g_gc"], writes=["g_ngc"])
                S.op("dve", lambda e: e.tensor_tensor(out=ed[:], in0=gc[:, 1, :], in1=gc[:, 0, :], op=ALU.subtract), reads=["g_gc"], writes=["g_ed"])
                S.op("act", lambda e: e.activation(out=ed[:], in_=ed[:], func=AF.Exp), reads=["g_ed"], writes=["g_ed"])
                if stage < 3:
                    continue
                S.op("pool", lambda e, gb=gb, c=c: e.tensor_tensor(out=gm[:], in0=triu[:].unsqueeze(1).broadcast_to([128, 4, 128]),
                                                                    in1=bc(gb[:, c, 0:4]), op=ALU.mult), reads=["g_triu", kgb], writes=["g_gm"])
                pg, pgk = nb()
                S.op("pe", lambda e, pg=pg: e.matmul(pg[:], C["onesf"][:], fl(gm[:]), start=True, stop=True), reads=["c_onesf", "g_gm"], writes=[pgk])
                if stage < 3.2:
                    continue
                S.op("act", lambda e, pg=pg: e.activation(out=fl(E[:]), in_=pg[:], func=AF.Exp), reads=[pgk], writes=["g_E"])
                S.op("dve", lambda e, pg=pg: e.tensor_tensor(out=fl(GM[:]), in0=pg[:], in1=fl(nm4[:]), op=ALU.add), reads=[pgk, "g_nm4"], writes=["g_GM"])
                if stage < 3.3:
                    continue
                for h in range(4):
                    S.op("act", lambda e, h=h: e.activation(out=DT[:, h, :], in_=GM[:, h, :], func=AF.Exp, bias=ngc[:, h:h + 1], scale=1.0),
                         reads=["g_GM", "g_ngc"], writes=["g_DT"])
                if stage < 3.4:
                    continue
                S.op("pool", lambda e: e.tensor_tensor(out=DTS[:], in0=DT[:], in1=su4[:], op=ALU.mult), reads=["g_DT", "g_su4"], writes=["g_DTS"])
                S.op("pool", lambda e, gb=gb, c=c: e.tensor_tensor(out=DTS[:], in0=DTS[:], in1=bc(gb[:, c, 4:8]), op=ALU.mult), reads=["g_DTS", kgb], writes=["g_DTS"])
                if stage < 4:
                    continue
                pk, pkk = nb()
                for h in range(4):
                    S.op("pe", lambda e, h=h, cs=cs, kT=kT, pk=pk: e.matmul(v3(pk)[:, h, :], kT[:, h, cs], kT[:, h, cs], start=True, stop=True), reads=[kk], writes=[pkk])
                pq, pqk = nb()
                for h in range(4):
                    S.op("pe", lambda e, h=h, cs=cs, kT=kT, qT=qT, pq=pq: e.matmul(v3(pq)[:, h, :], kT[:, h, cs], qT[:, h, cs], start=True, stop=True), reads=[kk, kq], writes=[pqk])
                S.op("dve", lambda e, pk=pk: e.tensor_tensor(out=fl(U[:]), in0=pk[:], in1=fl(DTS[:]), op=ALU.mult), reads=[pkk, "g_DTS"], writes=["g_U"])
                S.op("dve", lambda e, pq=pq: e.tensor_tensor(out=fl(qkT[:]), in0=pq[:], in1=fl(DT[:]), op=ALU.mult), reads=[pqk, "g_DT"], writes=["g_qkT"])
                if stage < 5:
                    continue
                pu, puk = nb()
                for h in range(4):
                    S.op("pe", lambda e, h=h, pu=pu: e.transpose(v3(pu)[:, h, :], U[:, h, :], C["idf"][:]), reads=["g_U", "c_idf"], writes=[puk])
                S.op("act", lambda e, pu=pu: e.copy(out=fl(UT[:]), in_=pu[:]), reads=[puk], writes=["g_UT"])
                S.op("pool", lambda e: e.tensor_tensor(out=Pb[0][:], in0=i4[:], in1=U[:], op=ALU.subtract), reads=["g_i4", "g_U"], writes=["g_P0"])
                if stage < 6:
                    continue
                W, WT, P = U, UT, Pb[0]
                Wk, WTk, Pk = "g_U", "g_UT", "g_P0"
                for l in range(1, NL + 1):
                    Wn, WTn, Pn = Wb[l % 2], WTb[l % 2], Pb[l % 2]
                    Wnk, WTnk, Pnk = f"g_W{l % 2}", f"g_WT{l % 2}", f"g_P{l % 2}"
                    if l < NL:
                        pw, pwk = nb()
                        for h in range(4):
                            S.op("pe", lambda e, h=h, pw=pw, W=W, WT=WT: e.matmul(v3(pw)[:, h, :], WT[:, h, :], W[:, h, :], start=True, stop=True), reads=[Wk, WTk], writes=[pwk])
                    pwt, pwtk = nb()
                    for h in range(4):
                        S.op("pe", lambda e, h=h, pwt=pwt, W=W, WT=WT: e.matmul(v3(pwt)[:, h, :], W[:, h, :], WT[:, h, :], start=True, stop=True), reads=[Wk, WTk], writes=[pwtk])
                    if l < NL:
                        S.op("act", lambda e, pw=pw, Wn=Wn: e.copy(out=fl(Wn[:]), in_=pw[:]), reads=[pwk], writes=[Wnk])
                    S.op("dve", lambda e, pwt=pwt, WTn=WTn: e.tensor_copy(out=fl(WTn[:]), in_=pwt[:]), reads=[pwtk], writes=[WTnk])
                    pp, ppk = nb()
                    for h in range(4):
                        S.op("pe", lambda e, h=h, pp=pp, WTn=WTn, P=P: e.matmul(v3(pp)[:, h, :], WTn[:, h, :], P[:, h, :], start=True, stop=True), reads=[WTnk, Pk], writes=[ppk])
                    if l < NL:
                        S.op("dve", lambda e, pp=pp, P=P, Pn=Pn: e.tensor_tensor(out=fl(Pn[:]), in0=pp[:], in1=fl(P[:]), op=ALU.add), reads=[ppk, Pk], writes=[Pnk])
                    else:
                        S.op("dve", lambda e, pp=pp, P=P: e.tensor_tensor(out=fl(TpT[:]), in0=pp[:], in1=fl(P[:]), op=ALU.add), reads=[ppk, Pk], writes=["g_TpT"])
                    W, WT, P, Wk, WTk, Pk = Wn, WTn, Pn, Wnk, WTnk, Pnk
                if stage < 7:
                    continue
                S.op("pool", lambda e, kT=kT, cs=cs: e.tensor_tensor(out=kgT[:], in0=kT[:, :, cs], in1=E[:], op=ALU.mult), reads=[kk, "g_E"], writes=["g_kgT"])
                S.op("pool", lambda e, qT=qT, cs=cs: e.tensor_tensor(out=qdT[:], in0=qT[:, :, cs], in1=E[:], op=ALU.mult), reads=[kq, "g_E"], writes=["g_qdT"])
                S.op("pool", lambda e: e.tensor_tensor(out=kdec[:], in0=kvtm[:, 0:4, :], in1=bc(ed[:]), op=ALU.mult), reads=["g_kvtm", "g_ed"], writes=["g_kdec"])
                if stage < 8:
                    continue
                p1, p1k = nb()
                for h in range(4):
                    S.op("pe", lambda e, h=h, p1=p1: e.matmul(v3(p1)[:, h, :], kgT[:, h, :], Sbf[:, h, :], start=True, stop=True), reads=["g_kgT", "g_Sbf"], writes=[p1k])
                S.op("dve", lambda e, p1=p1: e.tensor_tensor(out=R[:], in0=kvtm[:, 4:8, :], in1=v3(p1), op=ALU.subtract), reads=["g_kvtm", p1k], writes=["g_R"])
                p2, p2k = nb()
                for h in range(4):
                    S.op("pe", lambda e, h=h, p2=p2: e.matmul(v3(p2)[:, h, :], TpT[:, h, :], R[:, h, :], start=True, stop=True), reads=["g_TpT", "g_R"], writes=[p2k])
                S.op("dve", lambda e, p2=p2, gb=gb, c=c: e.tensor_tensor(out=vnew[:], in0=v3(p2), in1=bc(gb[:, c, 4:8]), op=ALU.mult), reads=[p2k, kgb], writes=["g_vnew"])
                po, pok = nb()
                for h in range(4):
                    S.op("pe", lambda e, h=h, po=po: e.matmul(v3(po)[:, h, :], Sbf[:, h, :], qdT[:, h, :], start=True, stop=False), reads=["g_Sbf", "g_qdT"], writes=[pok])
                    S.op("pe", lambda e, h=h, po=po: e.matmul(v3(po)[:, h, :], vnew[:, h, :], qkT[:, h, :], start=False, stop=True), reads=["g_vnew", "g_qkT"], writes=[pok])
                p3, p3k = nb()
                for h in range(4):
                    S.op("pe", lambda e, h=h, p3=p3: e.matmul(v3(p3)[:, h, :], kdec[:, h, :], vnew[:, h, :], start=True, stop=True), reads=["g_kdec", "g_vnew"], writes=[p3k])
                for h in range(4):
                    S.op("dve", lambda e, h=h, p3=p3: e.scalar_tensor_tensor(out=S32[:, h, :], in0=S32[:, h, :], scalar=E[:, h, 127:128], in1=v3(p3)[:, h, :],
                                                                           op0=ALU.mult, op1=ALU.add), reads=["g_S32", "g_E", p3k], writes=["g_S32"])
                S.op("act", lambda e: e.copy(out=Sbf[:], in_=S32[:]), reads=["g_S32"], writes=["g_Sbf"])
                if stage < 9:
                    continue
                S.op("act", lambda e, po=po: e.activation(out=fl(sq[:]), in_=po[:], func=AF.Square), reads=[pok], writes=["g_sq"])
                pss, pssk = nb()
                S.op("pe", lambda e, pss=pss: e.matmul(pss[:], C["onesb"][:], fl(sq[:]), start=True, stop=True), reads=["c_onesb", "g_sq"], writes=[pssk])
                S.op("act", lambda e, pss=pss: e.activation(out=fl(rn[:]), in_=pss[:], func=AF.Sqrt, bias=128.0 * RMS_EPS, scale=1.0), reads=[pssk], writes=["g_rn"])
                S.op("dve", lambda e: e.reciprocal(out=rn[:], in_=rn[:]), reads=["g_rn"], writes=["g_rn"])
                S.op("dve", lambda e, po=po: e.scalar_tensor_tensor(out=fl(yt[:]), in0=po[:], scalar=ng[:, 0:1], in1=fl(rn[:]), op0=ALU.mult, op1=ALU.mult),
                     reads=[pok, "g_ng", "g_rn"], writes=["g_yt"])
                S.op("pool", lambda e, zs=zs, cs=cs, bi=bi: e.tensor_tensor(out=ybuf[bi][:, :, cs], in0=yt[:], in1=zs[:, :, cs], op=ALU.mult),
                     reads=["g_yt", kz], writes=[f"g_y{bi}"])
            B.finals.append(S.dma(STQ, y_ap.rearrange("(h d) t -> d h t", d=128)[:, :, T * 512:(T + 1) * 512], ybuf[bi][:],
                                  reads=[f"g_y{bi}"], writes=["y_out"]))
        S.flush(B.es)


def phase_out(B, yT_ap, zT_ap, w_ap, xres_ap, mod_ap, layer, lng_ap, lnb_ap, out_ap, ntok=TPC):
    S, nc = B.S, B.nc
    with ExitStack() as st:
        pp = [B.ps(st, f"o_pp{i}", [128, 512], F32) for i in range(4)]
        S.excl.update([f"o_pp{i}" for i in range(4)])
        yT = B.sb(st, "o_yT", [128, 32, 512], BF16)
        zt = B.sb(st, "o_zt", [128, 8, 512], BF16)
        gb_, lg_, lb_ = (B.sb(st, n, [128, D], F32) for n in ("o_gate", "o_lng", "o_lnb"))
        Wt = [B.sb(st, f"o_W{i}", [128, 16, 512], BF16) for i in range(3)]
        xr = B.sb(st, "o_xr", [128, D], F32)
        r = B.sb(st, "o_r", [128, D], F32)
        sm = B.sb(st, "o_sm", [128, 8], F32)
        S.dma("sp", gb_[:], mod_ap[layer:layer + 1, 8192:12288].partition_broadcast(128), writes=["o_gate"])
        S.dma("sp", lg_[:], lng_ap.partition_broadcast(128), writes=["o_lng"])
        S.dma("sp", lb_[:], lnb_ap.partition_broadcast(128), writes=["o_lnb"])
        S.op("dve", lambda e: e.tensor_scalar(out=gb_[:], in0=gb_[:], scalar1=1.0, scalar2=None, op0=ALU.add), reads=["o_gate"], writes=["o_gate"])
        yv = yT_ap.rearrange("(kc p) t -> p kc t", p=128)
        wv = w_ap.rearrange("(kc p) n -> p kc n", p=128)
        xv = xres_ap.rearrange("(n p) d -> n p d", p=128)
        ov = out_ap.rearrange("(n p) d -> n p d", p=128)
        wcnt = 0
        pcnt = 0
        for hf in range(ntok // 512):
            S.dma("sp", yT[:], yv[:, :, hf * 512:(hf + 1) * 512], writes=["o_yT"])
            if zT_ap is not None:
                zv = zT_ap.rearrange("(kc p) t -> p kc t", p=128)
                for g in range(4):
                    S.dma("sp", zt[:], zv[:, g * 8:(g + 1) * 8, hf * 512:(hf + 1) * 512], writes=["o_zt"])
                    S.op("pool", lambda e, g=g: e.tensor_tensor(out=yT[:, g * 8:(g + 1) * 8, :], in0=yT[:, g * 8:(g + 1) * 8, :], in1=zt[:], op=ALU.mult),
                         reads=["o_yT", "o_zt"], writes=["o_yT"])
            for ts in range(4):
                n = hf * 4 + ts
                S.dma("sp", xr[:], xv[n], writes=["o_xr"])
                for nb in range(8):
                    pb = pp[pcnt % 4]
                    pk = f"o_pp{pcnt % 4}"
                    pcnt += 1
                    for half in range(2):
                        slot = wcnt % 3
                        wcnt += 1
                        S.dma("pool", Wt[slot][:], wv[:, half * 16:(half + 1) * 16, nb * 512:(nb + 1) * 512], writes=[f"o_W{slot}"])
                        for k16 in range(16):
                            kc = half * 16 + k16
                            S.op("pe", lambda e, pb=pb, slot=slot, k16=k16, kc=kc, ts=ts: e.matmul(
                                pb[:], yT[:, kc, ts * 128:(ts + 1) * 128], Wt[slot][:, k16, :], start=(kc == 0), stop=(kc == 31)),
                                reads=["o_yT", f"o_W{slot}"], writes=[pk])
                    cs = slice(nb * 512, (nb + 1) * 512)
                    S.op("dve", lambda e, pb=pb, cs=cs: e.tensor_tensor(out=r[:, cs], in0=pb[:], in1=gb_[:, cs], op=ALU.mult), reads=[pk, "o_gate"], writes=["o_r"])
                    S.op("dve", lambda e, cs=cs: e.scalar_tensor_tensor(out=r[:, cs], in0=xr[:, cs], scalar=float(ALPHA), in1=r[:, cs], op0=ALU.mult, op1=ALU.add),
                         reads=["o_xr", "o_r"], writes=["o_r"])
                S.op("dve", lambda e: e.reduce_sum(out=sm[:, 0:1], in_=r[:], axis=AX.X), reads=["o_r"], writes=["o_sm"])
                S.op("pool", lambda e: e.tensor_tensor(out=xr[:], in0=r[:], in1=r[:], op=ALU.mult), reads=["o_r", "o_xr"], writes=["o_xr"])
                S.op("dve", lambda e: e.reduce_sum(out=sm[:, 1:2], in_=xr[:], axis=AX.X), reads=["o_xr", "o_sm"], writes=["o_sm"])
                S.op("dve", lambda e: e.tensor_scalar(out=sm[:, 0:2], in0=sm[:, 0:2], scalar1=1.0 / D, scalar2=None, op0=ALU.mult), reads=["o_sm"], writes=["o_sm"])
                S.op("dve", lambda e: e.tensor_tensor(out=sm[:, 2:3], in0=sm[:, 0:1], in1=sm[:, 0:1], op=ALU.mult), reads=["o_sm"], writes=["o_sm"])
                S.op("dve", lambda e: e.tensor_tensor(out=sm[:, 3:4], in0=sm[:, 1:2], in1=sm[:, 2:3], op=ALU.subtract), reads=["o_sm"], writes=["o_sm"])
                S.op("act", lambda e: e.activation(out=sm[:, 4:5], in_=sm[:, 3:4], func=AF.Sqrt, bias=LN_EPS, scale=1.0), reads=["o_sm"], writes=["o_sm"])
                S.op("dve", lambda e: e.reciprocal(out=sm[:, 5:6], in_=sm[:, 4:5]), reads=["o_sm"], writes=["o_sm"])
                S.op("dve", lambda e: e.tensor_scalar(out=r[:], in0=r[:], scalar1=sm[:, 0:1], scalar2=None, op0=ALU.subtract), reads=["o_r", "o_sm"], writes=["o_r"])
                S.op("dve", lambda e: e.scalar_tensor_tensor(out=r[:], in0=r[:], scalar=sm[:, 5:6], in1=lg_[:], op0=ALU.mult, op1=ALU.mult),
                     reads=["o_r", "o_sm", "o_lng"], writes=["o_r"])
                S.op("pool", lambda e: e.tensor_tensor(out=r[:], in0=r[:], in1=lb_[:], op=ALU.add), reads=["o_r", "o_lnb"], writes=["o_r"])
                B.finals.append(S.dma("sp", ov[n], r[:], reads=["o_r"], writes=["o_out"]))
        S.flush(B.es)


def phase_out2(B, yT_ap, zT_ap, w_ap, xres_ap, mod_ap, layer, lng_ap, lnb_ap, out_ap, rscr_ap):
    S, nc = B.S, B.nc
    NT = TPC // 128
    xv = xres_ap.rearrange("(n p) d -> n p d", p=128)
    rv = rscr_ap.rearrange("(n p) d -> n p d", p=128)
    ov = out_ap.rearrange("(n p) d -> n p d", p=128)
    with ExitStack() as st:
        pp = [B.ps(st, f"o_pp{i}", [128, 512], F32) for i in range(8)]
        S.excl.update([f"o_pp{i}" for i in range(8)])
        yT = B.sb(st, "o_yT", [128, 32, TPC], BF16)
        zt = B.sb(st, "o_zt", [128, 4, TPC], BF16)
        gb_ = B.sb(st, "o_gate", [128, D], F32)
        Wt = [B.sb(st, f"o_W{i}", [128, 16, 512], BF16) for i in range(3)]
        xc = [B.sb(st, f"o_xc{i}", [128, 512], F32) for i in range(3)]
        rc = [B.sb(st, f"o_rc{i}", [128, 512], F32) for i in range(3)]
        S.dma("sp", gb_[:], mod_ap[layer:layer + 1, 8192:12288].partition_broadcast(128), writes=["o_gate"])
        S.op("dve", lambda e: e.tensor_scalar(out=gb_[:], in0=gb_[:], scalar1=1.0, scalar2=None, op0=ALU.add), reads=["o_gate"], writes=["o_gate"])
        yv = yT_ap.rearrange("(kc p) t -> p kc t", p=128)
        wv = w_ap.rearrange("(kc p) n -> p kc n", p=128)
        for g in range(8):
            S.dma("sp", yT[:, g * 4:(g + 1) * 4, :], yv[:, g * 4:(g + 1) * 4, :], writes=[f"o_yT{g}"])
            if zT_ap is not None:
                zv = zT_ap.rearrange("(kc p) t -> p kc t", p=128)
                S.dma("sp", zt[:], zv[:, g * 4:(g + 1) * 4, :], writes=["o_zt"])
                S.op("pool", lambda e, g=g: e.tensor_tensor(out=yT[:, g * 4:(g + 1) * 4, :], in0=yT[:, g * 4:(g + 1) * 4, :], in1=zt[:], op=ALU.mult),
                     reads=[f"o_yT{g}", "o_zt"], writes=[f"o_yT{g}"])
        ykeys = [f"o_yT{g}" for g in range(8)]
        wcnt = 0
        ccnt = 0
        for nb in range(8):
            cs = slice(nb * 512, (nb + 1) * 512)
            slots = []
            for half in range(2):
                slot = wcnt % 3
                wcnt += 1
                slots.append(slot)
                S.dma("pool", Wt[slot][:], wv[:, half * 16:(half + 1) * 16, cs], writes=[f"o_W{slot}"])
            for ts in range(NT):
                pb = pp[ts]
                for kc in range(32):
                    slot = slots[kc // 16]
                    S.op("pe", lambda e, pb=pb, slot=slot, kc=kc, ts=ts: e.matmul(
                        pb[:], yT[:, kc, ts * 128:(ts + 1) * 128], Wt[slot][:, kc % 16, :], start=(kc == 0), stop=(kc == 31)),
                        reads=[ykeys[kc // 4], f"o_W{slot}"], writes=[f"o_pp{ts}"])
                ci = ccnt % 3
                ccnt += 1
                S.dma("sp", xc[ci][:], xv[ts][:, cs], writes=[f"o_xc{ci}"])
                S.op("dve", lambda e, pb=pb, ci=ci, cs=cs: e.tensor_tensor(out=rc[ci][:], in0=pb[:], in1=gb_[:, cs], op=ALU.mult), reads=[f"o_pp{ts}", "o_gate"], writes=[f"o_rc{ci}"])
                S.op("dve", lambda e, ci=ci: e.scalar_tensor_tensor(out=rc[ci][:], in0=xc[ci][:], scalar=float(ALPHA), in1=rc[ci][:], op0=ALU.mult, op1=ALU.add),
                     reads=[f"o_xc{ci}", f"o_rc{ci}"], writes=[f"o_rc{ci}"])
                S.dma("sp", rv[ts][:, cs], rc[ci][:], reads=[f"o_rc{ci}"], writes=["o_rscr"])
        S.flush(B.es)
    with ExitStack() as st:
        lg_, lb_ = (B.sb(st, n, [128, D], F32) for n in ("o_lng", "o_lnb"))
        r2 = [B.sb(st, f"o_r{i}", [128, D], F32) for i in range(2)]
        sq = B.sb(st, "o_sq", [128, D], F32)
        sm = B.sb(st, "o_sm", [128, 8], F32)
        S.dma("sp", lg_[:], lng_ap.partition_broadcast(128), writes=["o_lng"])
        S.dma("sp", lb_[:], lnb_ap.partition_broadcast(128), writes=["o_lnb"])
        for n in range(NT):
            r = r2[n % 2]
            rk = f"o_r{n % 2}"
            S.dma("sp", r[:], rv[n], writes=[rk])
            S.op("dve", lambda e, r=r: e.reduce_sum(out=sm[:, 0:1], in_=r[:], axis=AX.X), reads=[rk], writes=["o_sm"])
            S.op("pool", lambda e, r=r: e.tensor_tensor(out=sq[:], in0=r[:], in1=r[:], op=ALU.mult), reads=[rk], writes=["o_sq"])
            S.op("dve", lambda e: e.reduce_sum(out=sm[:, 1:2], in_=sq[:], axis=AX.X), reads=["o_sq", "o_sm"], writes=["o_sm"])
            S.op("dve", lambda e: e.tensor_scalar(out=sm[:, 0:2], in0=sm[:, 0:2], scalar1=1.0 / D, scalar2=None, op0=ALU.mult), reads=["o_sm"], writes=["o_sm"])
            S.op("dve", lambda e: e.tensor_tensor(out=sm[:, 2:3], in0=sm[:, 0:1], in1=sm[:, 0:1], op=ALU.mult), reads=["o_sm"], writes=["o_sm"])
            S.op("dve", lambda e: e.tensor_tensor(out=sm[:, 3:4], in0=sm[:, 1:2], in1=sm[:, 2:3], op=ALU.subtract), reads=["o_sm"], writes=["o_sm"])
            S.op("act", lambda e: e.activation(out=sm[:, 4:5], in_=sm[:, 3:4], func=AF.Sqrt, bias=LN_EPS, scale=1.0), reads=["o_sm"], writes=["o_sm"])
            S.op("dve", lambda e: e.reciprocal(out=sm[:, 5:6], in_=sm[:, 4:5]), reads=["o_sm"], writes=["o_sm"])
            S.op("dve", lambda e, r=r: e.tensor_scalar(out=r[:], in0=r[:], scalar1=sm[:, 0:1], scalar2=None, op0=ALU.subtract), reads=[rk, "o_sm"], writes=[rk])
            S.op("dve", lambda e, r=r: e.scalar_tensor_tensor(out=r[:], in0=r[:], scalar=sm[:, 5:6], in1=lg_[:], op0=ALU.mult, op1=ALU.mult),
                 reads=[rk, "o_sm", "o_lng"], writes=[rk])
            S.op("pool", lambda e, r=r: e.tensor_tensor(out=r[:], in0=r[:], in1=lb_[:], op=ALU.add), reads=[rk, "o_lnb"], writes=[rk])
            B.finals.append(S.dma("sp", ov[n], r[:], reads=[rk], writes=["o_out"]))
        S.flush(B.es)


def phase_b2(B, x1_ap, mod_ap, win_ap, qg_ap, kvg_ap, latT_ap, zsT_ap):
    S, nc = B.S, B.nc
    with ExitStack() as st:
        C = make_consts(B, st)
        ptr = [B.ps(st, f"b_pt{i}", [128, 4, 128], F32) for i in range(2)]
        ptk = [f"b_pt{i}" for i in range(2)]
        psp = [B.ps(st, f"b_pp{i}", [128, 512], F32) for i in range(4)]
        pmd = B.ps(st, "b_pmd", [128, 512], F32)
        pbt = B.ps(st, "b_pbt", [128, 8, 128], BF16)
        S.excl.update(ptk + [f"b_pp{i}" for i in range(4)] + ["b_pmd", "b_pbt"])
        sh, sc = load_mod_fm(B, st, C, mod_ap, 1, pmd, "b_pmd")
        xs = B.sb(st, "b_xs", [128, D], F32)
        hT = B.sb(st, "b_hT", [128, 32, TPC], BF16)
        Wt = [B.sb(st, f"b_W{i}", [128, 16, 512], BF16) for i in range(4)]
        zo = B.sb(st, "b_zo", [128, 4, 512], BF16)
        lt = B.sb(st, "b_lt", [128, 1472], F32)
        lsq = B.sb(st, "b_lsq", [128, 896], F32)
        lb = B.sb(st, "b_lb", [128, 13, 128], BF16)
        ltT = B.sb(st, "b_ltT", [128, 13, 128], BF16)
        qg = B.sb(st, "b_qg", [128, 896], F32)
        kvg = B.sb(st, "b_kvg", [128, 512], F32)
        sm = B.sb(st, "b_sm", [128, 8], F32)
        S.dma("sp", qg[:], qg_ap.partition_broadcast(128), writes=["b_qg"])
        S.dma("sp", kvg[:], kvg_ap.partition_broadcast(128), writes=["b_kvg"])
        S.op("pool", lambda e: e.memset(lb[:], 0.0), writes=["b_lb"])
        xv = x1_ap.rearrange("(n p) d -> n p d", p=128)
        wv = win_ap.rearrange("(kc p) n -> p kc n", p=128)
        cnt = [0]
        for ts in range(8):
            S.dma("sp", xs[:], xv[ts], writes=["b_xs"])
            make_hT(B, C, xs, "b_xs", hT, "b_hT", ts, sh, sc, "md_sh1", "md_sc1", ptr, ptk, cnt)
        hkeys = [f"b_hT{kc}" for kc in range(32)]
        wcnt = 0
        zv = zsT_ap.rearrange("(cb s p) t -> cb p s t", s=4, p=128)
        for cb in range(8):
            slots = []
            for half in range(2):
                slot = wcnt % 4
                wcnt += 1
                slots.append(slot)
                S.dma("pool", Wt[slot][:], wv[:, half * 16:(half + 1) * 16, 1472 + cb * 512:1472 + (cb + 1) * 512], writes=[f"b_W{slot}"])
            for th in range(2):
                for sbk in range(4):
                    for kc in range(32):
                        slot = slots[kc // 16]
                        S.op("pe", lambda e, slot=slot, sbk=sbk, kc=kc, th=th: e.matmul(
                            psp[sbk][:], Wt[slot][:, kc % 16, sbk * 128:(sbk + 1) * 128], hT[:, kc, th * 512:(th + 1) * 512],
                            start=(kc == 0), stop=(kc == 31)), reads=[f"b_W{slot}", hkeys[kc]], writes=[f"b_pp{sbk}"])
                    S.op("act", lambda e, sbk=sbk: e.activation(out=zo[:, sbk, :], in_=psp[sbk][:], func=AF.Silu), reads=[f"b_pp{sbk}"], writes=["b_zo"])
                S.dma("sp", zv[cb][:, :, th * 512:(th + 1) * 512], zo[:], reads=["b_zo"], writes=["b_zs"])
        lv = latT_ap.rearrange("(j p) t -> p j t", p=128)
        for ts in range(8):
            for nbk, (c0, c1) in enumerate(((0, 512), (512, 1024), (1024, 1472))):
                slots = []
                for half in range(2):
                    slot = wcnt % 4
                    wcnt += 1
                    slots.append(slot)
                    S.dma("pool", Wt[slot][:, :, 0:c1 - c0], wv[:, half * 16:(half + 1) * 16, c0:c1], writes=[f"b_W{slot}"])
                pb = psp[nbk]
                for kc in range(32):
                    slot = slots[kc // 16]
                    S.op("pe", lambda e, slot=slot, kc=kc, ts=ts, pb=pb, c0=c0, c1=c1: e.matmul(
                        pb[:, 0:c1 - c0], hT[:, kc, ts * 128:(ts + 1) * 128], Wt[slot][:, kc % 16, 0:c1 - c0],
                        start=(kc == 0), stop=(kc == 31)), reads=[f"b_W{slot}", hkeys[kc]], writes=[f"b_pp{nbk}"])
                S.op("act", lambda e, pb=pb, c0=c0, c1=c1: e.copy(out=lt[:, c0:c1], in_=pb[:, 0:c1 - c0]), reads=[f"b_pp{nbk}"], writes=["b_lt"])
            for (a0, a1, gt, gk, col) in ((0, 896, qg, "b_qg", 0), (896, 1408, kvg, "b_kvg", 1)):
                n = a1 - a0
                S.op("pool", lambda e, a0=a0, a1=a1, n=n: e.tensor_tensor(out=lsq[:, 0:n], in0=lt[:, a0:a1], in1=lt[:, a0:a1], op=ALU.mult), reads=["b_lt"], writes=["b_lsq"])
                S.op("dve", lambda e, n=n, col=col: e.reduce_sum(out=sm[:, col:col + 1], in_=lsq[:, 0:n], axis=AX.X), reads=["b_lsq"], writes=["b_sm"])
                S.op("act", lambda e, n=n, col=col: e.activation(out=sm[:, col + 2:col + 3], in_=sm[:, col:col + 1], func=AF.Sqrt, bias=RMS_EPS, scale=1.0 / n),
                     reads=["b_sm"], writes=["b_sm"])
                S.op("dve", lambda e, col=col: e.reciprocal(out=sm[:, col + 4:col + 5], in_=sm[:, col + 2:col + 3]), reads=["b_sm"], writes=["b_sm"])
                S.op("dve", lambda e, a0=a0, a1=a1, n=n, gt=gt, col=col: e.scalar_tensor_tensor(
                    out=lb[:].rearrange("p a b -> p (a b)")[:, a0:a1], in0=lt[:, a0:a1], scalar=sm[:, col + 4:col + 5], in1=gt[:, 0:n], op0=ALU.mult, op1=ALU.mult),
                    reads=["b_lt", "b_sm", gk], writes=["b_lb"])
            S.op("dve", lambda e: e.tensor_copy(out=lb[:, 11, 0:64], in_=lt[:, 1408:1472]), reads=["b_lt"], writes=["b_lb"])
            S.op("dve", lambda e: e.tensor_copy(out=lb[:, 12, 0:32], in_=lt[:, 1440:1472]), reads=["b_lt"], writes=["b_lb"])
            S.op("dve", lambda e: e.tensor_copy(out=lb[:, 12, 32:64], in_=lt[:, 1408:1440]), reads=["b_lt"], writes=["b_lb"])
            for j0 in (0, 8):
                nj = min(8, 13 - j0)
                for j in range(nj):
                    S.op("pe", lambda e, j=j, j0=j0: e.transpose(pbt[:, j, :], lb[:, j0 + j, :], C["idb"][:]), reads=["b_lb", "c_idb"], writes=["b_pbt"])
                S.op("act", lambda e, j0=j0, nj=nj: e.copy(out=ltT[:, j0:j0 + nj, :], in_=pbt[:, 0:nj, :]), reads=["b_pbt"], writes=["b_ltT"])
            S.dma("sp", lv[:, :, ts * 128:(ts + 1) * 128], ltT[:], reads=["b_ltT"], writes=["b_lat"])
        S.flush(B.es)


def phase_c(B, lat_ap, wq_ap, wkv_ap, pos_ap, invf_ap, sgn_ap, oT_ap, nq=SEQ // 512):
    S, nc = B.S, B.nc
    TWO_PI = float(2 * np.pi)
    with ExitStack() as st:
        C = make_consts(B, st)
        acc = [B.ps(st, f"c_acc{i}", [128, 512], F32) for i in range(4)]
        pst = [B.ps(st, f"c_st{i}", [128, 512], F32) for i in range(2)]
        pm = [B.ps(st, f"c_pm{i}", [128, 512], F32) for i in range(2)]
        S.excl.update([f"c_acc{i}" for i in range(4)] + ["c_st0", "c_st1", "c_pm0", "c_pm1"])
        Wq = B.sb(st, "c_Wq", [128, 7, 1024], BF16)
        Wkv = B.sb(st, "c_Wkv", [128, 4, 1024], BF16)
        S.dma("pool", Wq[:], wq_ap.rearrange("(kc p) n -> p kc n", p=128), writes=["c_Wq"])
        S.dma("pool", Wkv[:], wkv_ap.rearrange("(kc p) n -> p kc n", p=128), writes=["c_Wkv"])
        invf = B.sb(st, "c_invf", [64, 1], F32)
        sgn = B.sb(st, "c_sgn", [64, 1], F32)
        S.dma("sp", invf[:], invf_ap, writes=["c_invf"])
        S.dma("sp", sgn[:], sgn_ap, writes=["c_sgn"])
        tril = B.sb(st, "c_tril", [128, 128], BF16)
        S.op("pool", lambda e: e.memset(tril[:], 1.0), writes=["c_tril"])
        S.op("pool", lambda e: e.affine_select(out=tril[:], in_=tril[:], pattern=[[1, 128]], compare_op=ALU.is_ge, fill=0.0,
                                               base=0, channel_multiplier=-1), reads=["c_tril"], writes=["c_tril"])
        qn = B.sb(st, "c_qn", [128, SEQ], BF16)
        kn = B.sb(st, "c_kn", [128, SEQ], BF16)
        qr = B.sb(st, "c_qr", [64, SEQ], BF16)
        kr = B.sb(st, "c_kr", [64, SEQ], BF16)
        vt = B.sb(st, "c_vt", [128, SEQ // 128, 132], BF16)
        S.op("pool", lambda e: e.memset(vt[:], 1.0), writes=["c_vt"])
        lat = [B.sb(st, f"c_lat{i}", [128, 13, 512], BF16) for i in range(2)]
        posi = B.sb(st, "c_posi", [64, 512], I32)
        ang = B.sb(st, "c_ang", [64, 512], F32)
        cs_ = B.sb(st, "c_cos", [64, 512], F32)
        sn_ = B.sb(st, "c_sin", [64, 512], F32)
        t1 = B.sb(st, "c_t1", [64, 512], F32)
        t2 = B.sb(st, "c_t2", [64, 512], F32)
        pt = [B.sb(st, f"c_p{i}", [128, 512], BF16) for i in range(2)]
        rs = B.sb(st, "c_rs", [128, 4], F32)
        on = B.sb(st, "c_on", [128, 4, 128], BF16)
        oT = B.sb(st, "c_oT", [128, 512], BF16)
        pbt = pm[1]
        lv = lat_ap.rearrange("(j p) t -> p j t", p=128)
        scale = float(192.0 ** -0.5)

        def rope(dst, dkey, pa, pak, pb, pbk, sl):
            S.op("dve", lambda e: e.tensor_tensor(out=t1[:], in0=pa[0:64, :], in1=cs_[:], op=ALU.mult), reads=[pak, "c_cos"], writes=["c_t1"])
            S.op("dve", lambda e: e.tensor_tensor(out=t2[:], in0=pb[0:64, :], in1=sn_[:], op=ALU.mult), reads=[pbk, "c_sin"], writes=["c_t2"])
            S.op("pool", lambda e: e.tensor_tensor(out=dst[0:64, sl], in0=t1[:], in1=t2[:], op=ALU.add), reads=["c_t1", "c_t2"], writes=[dkey])

        for h in range(4):
            for tt in range(SEQ // 512):
                sl = slice(tt * 512, (tt + 1) * 512)
                lt_ = lat[tt % 2]
                lk = f"c_lat{tt % 2}"
                S.dma("sp", lt_[:], lv[:, :, sl], writes=[lk])
                S.dma("sp", posi[:], pos_ap[0:1, sl].partition_broadcast(64), writes=["c_posi"])
                S.op("dve", lambda e: e.tensor_copy(out=ang[:], in_=posi[:]), reads=["c_posi"], writes=["c_ang"])
                S.op("dve", lambda e: e.tensor_scalar(out=ang[:], in0=ang[:], scalar1=invf[:, 0:1], scalar2=None, op0=ALU.mult), reads=["c_ang", "c_invf"], writes=["c_ang"])
                for (dst, dkey, addc) in ((sn_, "c_sin", 0.0), (cs_, "c_cos", float(0.5 * np.pi))):
                    S.op("dve", lambda e, addc=addc: e.tensor_scalar(out=t1[:], in0=ang[:], scalar1=addc, scalar2=None, op0=ALU.add), reads=["c_ang"], writes=["c_t1"])
                    S.op("dve", lambda e: e.tensor_scalar(out=t2[:], in0=t1[:], scalar1=1.0 / TWO_PI, scalar2=None, op0=ALU.mult), reads=["c_t1"], writes=["c_t2"])
                    S.op("dve", lambda e: e.tensor_copy(out=posi[:], in_=t2[:]), reads=["c_t2"], writes=["c_posi"])
                    S.op("dve", lambda e: e.tensor_copy(out=t2[:], in_=posi[:]), reads=["c_posi"], writes=["c_t2"])
                    S.op("dve", lambda e: e.scalar_tensor_tensor(out=t1[:], in0=t2[:], scalar=-TWO_PI, in1=t1[:], op0=ALU.mult, op1=ALU.add), reads=["c_t1", "c_t2"], writes=["c_t1"])
                    S.op("dve", lambda e: e.tensor_scalar(out=t2[:], in0=t1[:], scalar1=float(np.pi), scalar2=None, op0=ALU.is_gt), reads=["c_t1"], writes=["c_t2"])
                    S.op("dve", lambda e: e.scalar_tensor_tensor(out=t1[:], in0=t2[:], scalar=-TWO_PI, in1=t1[:], op0=ALU.mult, op1=ALU.add), reads=["c_t1", "c_t2"], writes=["c_t1"])
                    S.op("dve", lambda e: e.tensor_scalar(out=t2[:], in0=t1[:], scalar1=float(-np.pi), scalar2=None, op0=ALU.is_lt), reads=["c_t1"], writes=["c_t2"])
                    S.op("dve", lambda e: e.scalar_tensor_tensor(out=t1[:], in0=t2[:], scalar=TWO_PI, in1=t1[:], op0=ALU.mult, op1=ALU.add), reads=["c_t1", "c_t2"], writes=["c_t1"])
                    S.op("act", lambda e, dst=dst: e.activation(out=dst[:], in_=t1[:], func=AF.Sin), reads=["c_t1"], writes=[dkey])
                S.op("dve", lambda e: e.tensor_scalar(out=sn_[:], in0=sn_[:], scalar1=sgn[:, 0:1], scalar2=None, op0=ALU.mult), reads=["c_sin", "c_sgn"], writes=["c_sin"])
                for kc in range(7):
                    S.op("pe", lambda e, kc=kc, lt_=lt_, h=h: e.matmul(pm[0][:], Wq[:, kc, h * 256:h * 256 + 128], lt_[:, kc, :], start=(kc == 0), stop=(kc == 6)),
                         reads=["c_Wq", lk], writes=["c_pm0"])
                S.op("act", lambda e, sl=sl: e.copy(out=qn[:, sl], in_=pm[0][:]), reads=["c_pm0"], writes=["c_qn"])
                for kc in range(7):
                    S.op("pe", lambda e, kc=kc, lt_=lt_, h=h: e.matmul(pst[0][0:64, :], Wq[:, kc, h * 256 + 128:h * 256 + 192], lt_[:, kc, :], start=(kc == 0), stop=(kc == 6)),
                         reads=["c_Wq", lk], writes=["c_st0"])
                for kc in range(7):
                    S.op("pe", lambda e, kc=kc, lt_=lt_, h=h: e.matmul(pst[1][0:64, :], Wq[:, kc, h * 256 + 192:h * 256 + 256], lt_[:, kc, :], start=(kc == 0), stop=(kc == 6)),
                         reads=["c_Wq", lk], writes=["c_st1"])
                rope(qr, "c_qr", pst[0], "c_st0", pst[1], "c_st1", sl)
                if h == 0:
                    S.op("pe", lambda e, lt_=lt_: e.matmul(pst[0][0:64, :], C["idb"][0:64, 0:64], lt_[0:64, 11, :], start=True, stop=True), reads=["c_idb", lk], writes=["c_st0"])
                    S.op("pe", lambda e, lt_=lt_: e.matmul(pst[1][0:64, :], C["idb"][0:64, 0:64], lt_[0:64, 12, :], start=True, stop=True), reads=["c_idb", lk], writes=["c_st1"])
                    rope(kr, "c_kr", pst[0], "c_st0", pst[1], "c_st1", sl)
                for kc in range(4):
                    S.op("pe", lambda e, kc=kc, lt_=lt_, h=h: e.matmul(pm[0][:], Wkv[:, kc, h * 256:h * 256 + 128], lt_[:, 7 + kc, :], start=(kc == 0), stop=(kc == 3)),
                         reads=["c_Wkv", lk], writes=["c_pm0"])
                S.op("act", lambda e, sl=sl: e.copy(out=kn[:, sl], in_=pm[0][:]), reads=["c_pm0"], writes=["c_kn"])
                for sub in range(4):
                    for kc in range(4):
                        S.op("pe", lambda e, kc=kc, lt_=lt_, h=h, sub=sub: e.matmul(pm[1][:, sub * 128:(sub + 1) * 128], lt_[:, 7 + kc, sub * 128:(sub + 1) * 128],
                                                                                   Wkv[:, kc, h * 256 + 128:h * 256 + 256], start=(kc == 0), stop=(kc == 3)),
                             reads=["c_Wkv", lk], writes=["c_pm1"])
                S.op("dve", lambda e, tt=tt: e.tensor_copy(out=vt[:, tt * 4:(tt + 1) * 4, 0:128], in_=pm[1][:].rearrange("p (a b) -> p a b", a=4)), reads=["c_pm1"], writes=["c_vt"])
            step = 0
            for qb in range(nq):
                qsl = slice(qb * 512, (qb + 1) * 512)
                nkt = 4 * qb + 4
                for kt in range(nkt):
                    ksl = slice(kt * 128, (kt + 1) * 128)
                    ps_ = pst[step % 2]
                    psk = f"c_st{step % 2}"
                    pb_ = pt[step % 2]
                    pbk_ = f"c_p{step % 2}"
                    step += 1
                    S.op("pe", lambda e, ps_=ps_, ksl=ksl, qsl=qsl: e.matmul(ps_[:], kn[:, ksl], qn[:, qsl], start=True, stop=False), reads=["c_kn", "c_qn"], writes=[psk])
                    S.op("pe", lambda e, ps_=ps_, ksl=ksl, qsl=qsl: e.matmul(ps_[:], kr[0:64, ksl], qr[0:64, qsl], start=False, stop=True), reads=["c_kr", "c_qr"], writes=[psk])
                    S.op("act", lambda e, ps_=ps_, pb_=pb_: e.activation(out=pb_[:], in_=ps_[:], func=AF.Exp, scale=scale), reads=[psk], writes=[pbk_])
                    j = kt - 4 * qb
                    if j >= 0:
                        S.op("pool", lambda e, pb_=pb_, j=j: e.tensor_tensor(out=pb_[:, j * 128:(j + 1) * 128], in0=pb_[:, j * 128:(j + 1) * 128], in1=tril[:], op=ALU.mult),
                             reads=[pbk_, "c_tril"], writes=[pbk_])
                    for i in range(4):
                        if j > i:
                            continue
                        last = (kt == 4 * qb + i)
                        S.op("pe", lambda e, pb_=pb_, i=i, kt=kt, last=last: e.matmul(acc[i][:, 0:129], pb_[:, i * 128:(i + 1) * 128], vt[:, kt, 0:129],
                                                                                       start=(kt == 0), stop=last), reads=[pbk_, "c_vt"], writes=[f"c_acc{i}"])
                for i in range(4):
                    S.op("dve", lambda e, i=i: e.reciprocal(out=rs[:, i:i + 1], in_=acc[i][:, 128:129]), reads=[f"c_acc{i}"], writes=["c_rs"])
                    S.op("dve", lambda e, i=i: e.tensor_scalar(out=on[:, i, :], in0=acc[i][:, 0:128], scalar1=rs[:, i:i + 1], scalar2=None, op0=ALU.mult),
                         reads=[f"c_acc{i}", "c_rs"], writes=["c_on"])
                pbt_b = pbt[:].bitcast(BF16).rearrange("p (a b) -> p a b", b=128)
                for i in range(4):
                    S.op("pe", lambda e, i=i, pbt_b=pbt_b: e.transpose(pbt_b[:, i, :], on[:, i, :], C["idb"][:]), reads=["c_on", "c_idb"], writes=["c_pm1"])
                S.op("act", lambda e, pbt_b=pbt_b: e.copy(out=oT[:].rearrange("p (a b) -> p a b", a=4), in_=pbt_b[:, 0:4, :]), reads=["c_pm1"], writes=["c_oT"])
                B.finals.append(S.dma("sp", oT_ap[h * 128:(h + 1) * 128, qsl], oT[:], reads=["c_oT"], writes=["c_out"]))
        S.flush(B.es)


def _launch(build, maps):
    nc = bass.Bass("TRN2", target_bir_lowering=False)
    with ExitStack() as es:
        B = Builder(nc, es)
        build(nc, B)
    res = run_bass_kernel_spmd(nc, maps, core_ids=list(range(NCORES)))
    return res.results


def _din(nc, name, shape, dt):
    return nc.dram_tensor(name, list(shape), dt, kind="ExternalInput").ap()


def _dout(nc, name, shape, dt):
    return nc.dram_tensor(name, list(shape), dt, kind="ExternalOutput").ap()


def kernel(x, c, positions, w_mod, b_mod, ln_g, ln_b, a_w_in, a_w_conv, a_a_log, a_dt_bias, a_norm_g, a_w_out,
           b_w_in, b_q_norm_g, b_w_qb, b_kv_norm_g, b_w_kvb, b_w_out):
    f32 = np.float32
    asc = np.ascontiguousarray
    x2 = asc(np.asarray(x, f32)[0])
    R = range(NCORES)
    def bM(nc, B):
        phase_mod(B, _din(nc, "c", [1, D], F32), _din(nc, "wm", [D, 3072], F32), _din(nc, "bm", [1, 3072], F32), _dout(nc, "modrow", [1, 3072], F32))
    maps = [{"c": asc(np.asarray(c, f32)), "wm": asc(np.asarray(w_mod[r // 4][:, (r % 4) * 3072:(r % 4 + 1) * 3072], f32)),
             "bm": asc(np.asarray(b_mod[r // 4][None, (r % 4) * 3072:(r % 4 + 1) * 3072], f32))} for r in R]
    res = _launch(bM, maps)
    mod = asc(np.concatenate([res[r]["modrow"].reshape(-1) for r in R]).reshape(2, 12288))
    def bA(nc, B):
        scr = {n + "s": nc.dram_tensor("scr_" + n, [4, 128, SEQ], BF16).ap() for n in "qkvz"}
        scr["gbs"] = nc.dram_tensor("scr_gb", [SEQ, 8], F32).ap()
        y = _dout(nc, "y0T", [512, SEQ], BF16)
        phase_a1(B, _din(nc, "x", [SEQ, D], F32), _din(nc, "mod", [2, 12288], F32), _din(nc, "w4", [D, 2048], F32), _din(nc, "wab", [D, 8], F32),
                 _din(nc, "cw", [4, 1536], F32), _din(nc, "alog", [1, 4], F32), _din(nc, "dtb", [1, 4], F32), scr)
        phase_a2(B, scr, _din(nc, "ng", [1, 128], F32), y)
    W = np.asarray(a_w_in[0], f32)
    cwf = np.asarray(a_w_conv[0], f32)
    maps = []
    for r in R:
        o = 512 * r
        maps.append({"x": x2, "mod": mod,
                     "w4": asc(np.concatenate([W[:, o:o + 512], W[:, 4096 + o:4096 + o + 512], W[:, 8192 + o:8192 + o + 512], W[:, 12288 + o:12288 + o + 512]], axis=1)),
                     "wab": asc(np.concatenate([W[:, 16384 + 4 * r:16384 + 4 * r + 4], W[:, 16416 + 4 * r:16416 + 4 * r + 4]], axis=1)),
                     "cw": asc(np.concatenate([cwf[:, q + o:q + o + 512] for q in (0, 4096, 8192)], axis=1)),
                     "alog": asc(np.asarray(a_a_log, f32)[:, 4 * r:4 * r + 4]), "dtb": asc(np.asarray(a_dt_bias, f32)[:, 4 * r:4 * r + 4]),
                     "ng": asc(np.asarray(a_norm_g, f32))})
    res = _launch(bA, maps)
    Y0 = np.concatenate([res[r]["y0T"] for r in R], axis=0)
    def bB(nc, B):
        x1 = _dout(nc, "x1", [TPC, D], F32)
        modt = _din(nc, "mod", [2, 12288], F32)
        phase_out2(B, _din(nc, "yT", [D, TPC], BF16), None, _din(nc, "w", [D, D], F32), _din(nc, "xr", [TPC, D], F32), modt, 0,
                   _din(nc, "lg", [1, D], F32), _din(nc, "lb", [1, D], F32), x1, nc.dram_tensor("rscr", [TPC, D], F32).ap())
        phase_b2(B, x1, modt, _din(nc, "win", [D, 5568], F32), _din(nc, "qg", [1, 896], F32), _din(nc, "kvg", [1, 512], F32),
                 _dout(nc, "latT", [13 * 128, TPC], BF16), _dout(nc, "zsT", [D, TPC], BF16))
    maps = [{"yT": asc(Y0[:, TPC * r:TPC * (r + 1)]), "w": asc(np.asarray(a_w_out[0], f32)), "xr": asc(x2[TPC * r:TPC * (r + 1)]), "mod": mod,
             "lg": asc(np.asarray(ln_g, f32)[0:1]), "lb": asc(np.asarray(ln_b, f32)[0:1]), "win": asc(np.asarray(b_w_in[0], f32)),
             "qg": asc(np.asarray(b_q_norm_g, f32)), "kvg": asc(np.asarray(b_kv_norm_g, f32))} for r in R]
    res = _launch(bB, maps)
    x1s = [res[r]["x1"] for r in R]
    zss = [res[r]["zsT"] for r in R]
    lat = asc(np.concatenate([res[r]["latT"] for r in R], axis=1))
    def bC(nc, B):
        phase_c(B, _din(nc, "lat", [13 * 128, SEQ], BF16), _din(nc, "wq", [896, 1024], F32), _din(nc, "wkv", [512, 1024], F32),
                _din(nc, "pos", [1, SEQ], I32), _din(nc, "invf", [64, 1], F32), _din(nc, "sgn", [64, 1], F32), _dout(nc, "o1T", [512, SEQ], BF16))
    half = np.arange(32, dtype=np.float32) / 32.0
    invf = (10000.0 ** (-half)).astype(f32)
    invf = asc(np.concatenate([invf, invf])[:, None])
    sgn = asc(np.concatenate([-np.ones(32, f32), np.ones(32, f32)])[:, None])
    wqb = np.asarray(b_w_qb[0], f32).reshape(896, 32, 192)
    wkvb = np.asarray(b_w_kvb[0], f32).reshape(512, 32, 256)
    maps = []
    for r in R:
        wq = wqb[:, 4 * r:4 * r + 4]
        wq = np.concatenate([wq, wq[:, :, 160:192], wq[:, :, 128:160]], axis=2)
        maps.append({"lat": lat, "wq": asc(wq.reshape(896, 1024)), "wkv": asc(wkvb[:, 4 * r:4 * r + 4].reshape(512, 1024)),
                     "pos": asc(np.asarray(positions, np.int32)), "invf": invf, "sgn": sgn})
    res = _launch(bC, maps)
    O1 = np.concatenate([res[r]["o1T"] for r in R], axis=0)
    def bD(nc, B):
        phase_out2(B, _din(nc, "yT", [D, TPC], BF16), _din(nc, "zT", [D, TPC], BF16), _din(nc, "w", [D, D], F32), _din(nc, "xr", [TPC, D], F32),
                   _din(nc, "mod", [2, 12288], F32), 1, _din(nc, "lg", [1, D], F32), _din(nc, "lb", [1, D], F32), _dout(nc, "out", [TPC, D], F32),
                   nc.dram_tensor("rscr", [TPC, D], F32).ap())
    maps = [{"yT": asc(O1[:, TPC * r:TPC * (r + 1)]), "zT": zss[r], "w": asc(np.asarray(b_w_out[0], f32)), "xr": x1s[r], "mod": mod,
             "lg": asc(np.asarray(ln_g, f32)[1:2]), "lb": asc(np.asarray(ln_b, f32)[1:2])} for r in R]
    res = _launch(bD, maps)
    return np.concatenate([res[r]["out"] for r in R], axis=0)[None].astype(f32)
```

```python
import numpy as np
from contextlib import ExitStack
import concourse.bass as bass
import concourse.mybir as mybir
from concourse.bass_utils import run_bass_kernel_spmd

F32 = mybir.dt.float32
BF16 = mybir.dt.bfloat16
I32 = mybir.dt.int32
AF = mybir.ActivationFunctionType
ALU = mybir.AluOpType
AX = mybir.AxisListType

NCORES = 8
SEQ = 8192
D = 4096
TPC = SEQ // NCORES
ALPHA = (2.0 * 2) ** 0.25
RMS_EPS = 1e-6
LN_EPS = 1e-5
NEG = -30000.0

ENGS = ("pe", "act", "dve", "pool", "sp")
SEM_CHUNK = 4000
STQ = "sp"
HT_ACT = False


class Sched:
    def __init__(self, nc):
        self.nc = nc
        self.ops = {e: [] for e in ENGS}
        self.writers = {}
        self.readers = {}
        self.dmacnt = {}
        self.nops = 0
        self.excl = set()

    def _collect(self, eng, reads, writes):
        deps = []
        for k in reads:
            for t in self.writers.get(k, ()):
                deps.append((t, "raw"))
            if k in self.excl:
                for t in self.readers.get(k, ()):
                    deps.append((t, "war"))
        for k in writes:
            for t in self.writers.get(k, ()):
                deps.append((t, "waw"))
            for t in self.readers.get(k, ()):
                deps.append((t, "war"))
        return deps

    def _commit(self, tok, reads, writes, partial):
        for k in writes:
            if partial:
                self.writers.setdefault(k, []).append(tok)
            else:
                self.writers[k] = [tok]
            self.readers[k] = []
        for k in reads:
            self.readers.setdefault(k, []).append(tok)

    def op(self, eng, fn, reads=(), writes=(), partial=False):
        deps = self._collect(eng, reads, writes)
        tok = ("E", eng, self.nops)
        self.ops[eng].append(dict(fn=fn, deps=deps, tok=tok, dma=None))
        self._commit(tok, reads, writes, partial)
        self.nops += 1
        return tok

    def dma(self, eng, out, in_, reads=(), writes=(), sem=None, partial=False, **kw):
        semname = sem or (writes[0] if writes else reads[0])
        deps = self._collect(eng, reads, writes)
        self.dmacnt[semname] = self.dmacnt.get(semname, 0) + 1
        tok = ("D", semname, self.dmacnt[semname] * 16)
        fn = lambda e, out=out, in_=in_, kw=kw: e.dma_start(out=out, in_=in_, **kw)
        self.ops[eng].append(dict(fn=fn, deps=deps, tok=tok, dma=semname))
        self._commit(tok, reads, writes, partial)
        self.nops += 1
        return tok

    def coll(self, kind, src, dst, reads, writes, name, inc=1):
        deps = self._collect("pool", reads, writes)
        self.dmacnt[name] = self.dmacnt.get(name, 0)
        self.collinc = getattr(self, "collinc", {})
        self.collinc[name] = self.collinc.get(name, 0) + inc
        tok = ("C", name, self.collinc[name])
        fn = lambda e: e.collective_compute(kind, ALU.bypass, replica_groups=[list(range(NCORES))], ins=[src], outs=[dst])
        self.ops["pool"].append(dict(fn=fn, deps=deps, tok=tok, dma=name, inc=inc))
        self._commit(tok, reads, writes, False)
        self.nops += 1
        return tok

    def flush(self, es, final_waits=(), barrier=True):
        nc = self.nc
        if not hasattr(self, "sems"):
            self.sems = {}
            self.nflag = {e: 0 for e in ENGS}
        sems = self.sems
        flagged = set()
        for e in ENGS:
            for o in self.ops[e]:
                keep = []
                for (t, kind) in o["deps"]:
                    if t[0] == "E" and t[1] == e and o["dma"] is None:
                        if e == "pe" or kind == "war":
                            continue
                    keep.append(t)
                    if t[0] == "E":
                        flagged.add(t)
                o["deps"] = keep
        lasts = []
        if barrier:
            for e in ENGS:
                for o in reversed(self.ops[e]):
                    if o["dma"] is None:
                        flagged.add(o["tok"])
                        lasts.append(o["tok"])
                        break
        for t in final_waits:
            if t[0] == "E":
                flagged.add(t)
        tokval = {}

        def getsem(key):
            if key not in sems:
                sems[key] = es.enter_context(nc.semaphore("s_" + "_".join(str(x) for x in key)))
            return sems[key]

        for e in ENGS:
            for o in self.ops[e]:
                if o["tok"] in flagged:
                    n = self.nflag[e]
                    tokval[o["tok"]] = ((e, n // SEM_CHUNK), n % SEM_CHUNK + 1)
                    self.nflag[e] = n + 1
                    getsem((e, n // SEM_CHUNK))
        for name in self.dmacnt:
            getsem(("D", name))

        def resolve(t):
            if t[0] == "E":
                return tokval[t]
            return (("D", t[1]), t[2])

        collinc = getattr(self, "collinc", {})

        endw = [resolve(t) for t in list(final_waits) + lasts]
        if barrier:
            endw += [(("D", name), c * 16 + collinc.get(name, 0)) for name, c in self.dmacnt.items()]
        ops = self.ops
        with nc.Block() as block:
            engobj = {"pe": block.tensor, "act": block.scalar, "dve": block.vector, "pool": block.gpsimd,
                      "sp": block.sync}

            def make(e):
                def body(eng):
                    waited = {}
                    for o in ops[e]:
                        need = {}
                        for t in o["deps"]:
                            s, v = resolve(t)
                            if waited.get(s, 0) >= v:
                                continue
                            need[s] = max(need.get(s, 0), v)
                        for s, v in need.items():
                            eng.wait_ge(sems[s], v)
                            waited[s] = v
                        ins = o["fn"](eng)
                        if o["dma"] is not None:
                            ins.then_inc(sems[("D", o["dma"])], o.get("inc", 16))
                        elif o["tok"] in tokval:
                            ins.then_inc(sems[tokval[o["tok"]][0]], 1)
                    for s, v in endw:
                        if waited.get(s, 0) < v:
                            eng.wait_ge(sems[s], v)
                            waited[s] = v
                return body

            for e in ENGS:
                engobj[e](make(e))
        self.ops = {e: [] for e in ENGS}
        self.writers = {}
        self.readers = {}


class Builder:
    def __init__(self, nc, es):
        self.nc = nc
        self.es = es
        self.S = Sched(nc)
        self.finals = []

    def sb(self, st, name, shape, dt):
        self.uid = getattr(self, "uid", 0) + 1
        return st.enter_context(self.nc.sbuf_tensor(f"{name}_u{self.uid}", list(shape), dt))

    def ps(self, st, name, shape, dt=F32):
        self.uid = getattr(self, "uid", 0) + 1
        return st.enter_context(self.nc.psum_tensor(f"{name}_u{self.uid}", list(shape), dt))


def make_consts(B, st):
    S = B.S
    C = {}
    C["idf"] = B.sb(st, "c_idf", [128, 128], F32)
    C["idb"] = B.sb(st, "c_idb", [128, 128], BF16)
    C["onesf"] = B.sb(st, "c_onesf", [128, 128], F32)
    C["onesb"] = B.sb(st, "c_onesb", [128, 128], BF16)
    S.op("pool", lambda e: e.memset(C["idf"][:], 0.0), writes=["c_idf"])
    S.op("pool", lambda e: e.affine_select(out=C["idf"][:], in_=C["idf"][:], pattern=[[-1, 128]],
                                           compare_op=ALU.not_equal, fill=1.0, base=0, channel_multiplier=1),
         reads=["c_idf"], writes=["c_idf"])
    S.op("pool", lambda e: e.tensor_copy(out=C["idb"][:], in_=C["idf"][:]), reads=["c_idf"], writes=["c_idb"])
    S.op("pool", lambda e: e.memset(C["onesf"][:], 1.0), writes=["c_onesf"])
    S.op("pool", lambda e: e.memset(C["onesb"][:], 1.0), writes=["c_onesb"])
    return C


def phase_mod(B, c_ap, wm_ap, bm_ap, out_ap):
    S, nc = B.S, B.nc
    with ExitStack() as st:
        C = make_consts(B, st)
        ct = B.sb(st, "m_ct", [32, 128], F32)
        cact = B.sb(st, "m_cact", [128, 32], F32)
        bmt = B.sb(st, "m_bm", [1, 3072], F32)
        rowt = B.sb(st, "m_row", [1, 3072], F32)
        wt = [B.sb(st, f"m_w{i}", [128, 3072], F32) for i in range(3)]
        pT = B.ps(st, "m_pT", [128, 512], F32)
        pr = [B.ps(st, f"m_pr{i}", [128, 512], F32) for i in range(6)]
        S.dma("sp", ct[:], c_ap.rearrange("o (kc p) -> (o kc) p", p=128), writes=["m_ct"])
        S.dma("sp", bmt[:], bm_ap, writes=["m_bm"])
        S.op("pe", lambda e: e.transpose(pT[:, 0:32], ct[:], C["idf"][0:32, 0:32]), reads=["m_ct", "c_idf"], writes=["m_pT"])
        S.op("act", lambda e: e.activation(out=cact[:], in_=pT[:, 0:32], func=AF.Silu), reads=["m_pT"], writes=["m_cact"])
        for kc in range(32):
            w = wt[kc % 3]
            S.dma("sp" if kc % 2 == 0 else "act", w[:], wm_ap[kc * 128:(kc + 1) * 128, :], writes=[f"m_w{kc % 3}"])
            for nb in range(6):
                S.op("pe", lambda e, w=w, nb=nb, kc=kc: e.matmul(pr[nb][0:1, :], cact[:, kc:kc + 1], w[:, nb * 512:(nb + 1) * 512],
                                                                 start=(kc == 0), stop=(kc == 31)),
                     reads=[f"m_w{kc % 3}", "m_cact"], writes=[f"m_pr{nb}"])
        for nb in range(6):
            S.op("dve", lambda e, nb=nb: e.tensor_tensor(out=rowt[0:1, nb * 512:(nb + 1) * 512], in0=pr[nb][0:1, :],
                                                         in1=bmt[0:1, nb * 512:(nb + 1) * 512], op=ALU.add),
                 reads=[f"m_pr{nb}", "m_bm"], writes=["m_row"])
        t = S.dma("sp", out_ap, rowt[:], reads=["m_row"], writes=["m_out"])
        S.flush(B.es, final_waits=[t])


def load_mod_fm(B, st, C, mod_ap, layer, pt, ptkey):
    S = B.S
    rows = B.sb(st, f"md_rows{layer}", [64, 128], F32)
    sh = B.sb(st, f"md_sh{layer}", [128, 32], F32)
    sc = B.sb(st, f"md_sc{layer}", [128, 32], F32)
    S.dma("sp", rows[:], mod_ap[layer:layer + 1, 0:8192].rearrange("o (kc p) -> (o kc) p", p=128), writes=[f"md_rows{layer}"])
    S.op("pe", lambda e: e.transpose(pt[:, 0:64], rows[:], C["idf"][0:64, 0:64]), reads=[f"md_rows{layer}", "c_idf"], writes=[ptkey])
    S.op("dve", lambda e: e.tensor_copy(out=sh[:], in_=pt[:, 0:32]), reads=[ptkey], writes=[f"md_sh{layer}"])
    S.op("dve", lambda e: e.tensor_scalar(out=sc[:], in0=pt[:, 32:64], scalar1=1.0, scalar2=None, op0=ALU.add),
         reads=[ptkey], writes=[f"md_sc{layer}"])
    return sh, sc


def make_hT(B, C, xs, xkey, hT, hkey, st_idx, sh, sc, shk, sck, ptr, ptk, cnt):
    S = B.S
    for g in range(8):
        pt = ptr[cnt[0] % len(ptr)]
        pk = ptk[cnt[0] % len(ptr)]
        cnt[0] += 1
        for j in range(4):
            kc = g * 4 + j
            S.op("pe", lambda e, pt=pt, j=j, kc=kc: e.transpose(pt[:, j, :], xs[:, kc * 128:(kc + 1) * 128], C["idf"][:]),
                 reads=[xkey, "c_idf"], writes=[pk])
        for j in range(4):
            kc = g * 4 + j
            dst = hT[:, kc, st_idx * 128:(st_idx + 1) * 128]
            if HT_ACT and kc % 2 == 0:
                S.op("act", lambda e, pt=pt, j=j, kc=kc, dst=dst: e.activation(out=dst, in_=pt[:, j, :], func=AF.Identity,
                                                                               bias=sh[:, kc:kc + 1], scale=sc[:, kc:kc + 1]),
                     reads=[pk, shk, sck], writes=[f"{hkey}{kc}"])
            else:
                S.op("dve", lambda e, pt=pt, j=j, kc=kc, dst=dst: e.scalar_tensor_tensor(out=dst, in0=pt[:, j, :], scalar=sc[:, kc:kc + 1],
                                                                                         in1=sh[:, kc:kc + 1].broadcast_to([128, 128]),
                                                                                         op0=ALU.mult, op1=ALU.add),
                     reads=[pk, shk, sck], writes=[f"{hkey}{kc}"])


def phase_a1(B, x_ap, mod_ap, w4_ap, wab_ap, cw_ap, alog_ap, dtb_ap, scr, ntiles=SEQ // 512, stage=9):
    S, nc = B.S, B.nc
    with ExitStack() as st:
        C = make_consts(B, st)
        ptr = [B.ps(st, f"a_pt{i}", [128, 4, 128], F32) for i in range(2)]
        ptk = [f"a_pt{i}" for i in range(2)]
        psp = [B.ps(st, f"a_pp{i}", [128, 512], F32) for i in range(4)]
        pgb = B.ps(st, "a_pgb", [128, 512], F32)
        pss = B.ps(st, "a_pss", [128, 512], F32)
        S.excl.update(["a_pt0", "a_pt1", "a_pp0", "a_pp1", "a_pp2", "a_pp3", "a_pgb", "a_pss"])
        sh, sc = load_mod_fm(B, st, C, mod_ap, 0, pgb, "a_pgb")
        xs = [B.sb(st, f"a_xs{i}", [128, D], F32) for i in range(2)]
        hT = B.sb(st, "a_hT", [128, 32, 512], BF16)
        Wt = [B.sb(st, f"a_W{i}", [128, 16, 512], BF16) for i in range(3)]
        pc = B.sb(st, "a_pc", [128, 12, 515], F32)
        tmp = [B.sb(st, f"a_tmp{i}", [128, 512], F32) for i in range(2)]
        sl = [B.sb(st, f"a_sl{i}", [128, 512], F32) for i in range(2)]
        sqb = B.sb(st, "a_sqb", [128, 512], BF16)
        rn = B.sb(st, "a_rn", [128, 512], F32)
        outs = {n: B.sb(st, f"a_o{n}", [128, 4, 512], BF16) for n in "qkvz"}
        wabf = B.sb(st, "a_wabf", [128, 32, 8], F32)
        wab = B.sb(st, "a_wab", [128, 32, 8], BF16)
        cwr = B.sb(st, "a_cwr", [48, 128], F32)
        cwT = B.sb(st, "a_cwT", [128, 48], F32)
        alb = B.sb(st, "a_alb", [128, 4], F32)
        negA = B.sb(st, "a_negA", [128, 4], F32)
        dtb = B.sb(st, "a_dtb", [128, 4], F32)
        gbt = B.sb(st, "a_gbt", [128, 4, 8], F32)
        t4 = [B.sb(st, f"a_t4{i}", [128, 4], F32) for i in range(2)]
        S.dma("sp", wabf[:], wab_ap.rearrange("(kc p) n -> p kc n", p=128), writes=["a_wabf"])
        S.op("dve", lambda e: e.tensor_copy(out=wab[:], in_=wabf[:]), reads=["a_wabf"], writes=["a_wab"])
        S.dma("sp", cwr[:], cw_ap.rearrange("j (b p) -> (j b) p", p=128), writes=["a_cwr"])
        S.op("pe", lambda e: e.transpose(pss[:, 0:48], cwr[:], C["idf"][0:48, 0:48]), reads=["a_cwr", "c_idf"], writes=["a_pss"])
        S.op("dve", lambda e: e.tensor_copy(out=cwT[:], in_=pss[:, 0:48]), reads=["a_pss"], writes=["a_cwT"])
        S.dma("sp", alb[:], alog_ap.partition_broadcast(128), writes=["a_alb"])
        S.dma("sp", dtb[:], dtb_ap.partition_broadcast(128), writes=["a_dtb"])
        S.op("act", lambda e: e.activation(out=negA[:], in_=alb[:], func=AF.Exp), reads=["a_alb"], writes=["a_negA"])
        S.op("dve", lambda e: e.tensor_scalar(out=negA[:], in0=negA[:], scalar1=-1.0, scalar2=None, op0=ALU.mult),
             reads=["a_negA"], writes=["a_negA"])
        S.op("pool", lambda e: e.memset(pc[:], 0.0), writes=[f"a_pc{b}" for b in range(12)])
        xv = x_ap.rearrange("(n p) d -> n p d", p=128)
        w4v = w4_ap.rearrange("(kc p) n -> p kc n", p=128)
        cnt = [0]
        wcnt = 0
        pend = []
        qk_scale = 128.0 ** -0.5
        for T in range(ntiles if stage > 0 else 0):
            for stx in range(4):
                n = T * 4 + stx
                xb = xs[n % 2]
                S.dma("sp", xb[:], xv[n], writes=[f"a_xs{n % 2}"])
                make_hT(B, C, xb, f"a_xs{n % 2}", hT, "a_hT", stx, sh, sc, "md_sh0", "md_sc0", ptr, ptk, cnt)
            for stx in range(4 if stage > 1 else 0):
                for kc in range(32):
                    S.op("pe", lambda e, kc=kc, stx=stx: e.matmul(pgb[:, 0:8], hT[:, kc, stx * 128:(stx + 1) * 128], wab[:, kc, :],
                                                                    start=(kc == 0), stop=(kc == 31)),
                         reads=[f"a_hT{kc}", "a_wab"], writes=["a_pgb"])
                S.op("act", lambda e: e.activation(out=t4[0][:], in_=pgb[:, 0:4], func=AF.Exp, scale=-1.0), reads=["a_pgb"], writes=["a_t40"])
                S.op("dve", lambda e: e.tensor_tensor(out=t4[1][:], in0=pgb[:, 4:8], in1=dtb[:], op=ALU.add), reads=["a_pgb", "a_dtb"], writes=["a_t41"])
                S.op("dve", lambda e: e.tensor_scalar(out=t4[0][:], in0=t4[0][:], scalar1=1.0, scalar2=None, op0=ALU.add), reads=["a_t40"], writes=["a_t40"])
                S.op("dve", lambda e, stx=stx: e.reciprocal(out=gbt[:, stx, 4:8], in_=t4[0][:]), reads=["a_t40"], writes=["a_gbt"])
                S.op("act", lambda e: e.activation(out=t4[1][:], in_=t4[1][:], func=AF.Exp), reads=["a_t41"], writes=["a_t41"])
                S.op("act", lambda e: e.activation(out=t4[1][:], in_=t4[1][:], func=AF.Ln, bias=1.0, scale=1.0), reads=["a_t41"], writes=["a_t41"])
                S.op("dve", lambda e, stx=stx: e.tensor_tensor(out=gbt[:, stx, 0:4], in0=t4[1][:], in1=negA[:], op=ALU.mult),
                     reads=["a_t41", "a_negA"], writes=["a_gbt"])
            if stage > 1:
                S.dma("sp", scr["gbs"].rearrange("(n p) e -> p n e", p=128)[:, T * 4:(T + 1) * 4, :], gbt[:], reads=["a_gbt"], writes=["scr_gb"])
            for cb in range(4 if stage > 2 else 0):
                for half in range(2):
                    slot = wcnt % 3
                    wcnt += 1
                    S.dma("pool", Wt[slot][:], w4v[:, half * 16:(half + 1) * 16, cb * 512:(cb + 1) * 512], writes=[f"a_W{slot}"])
                    if half == 0 and pend:
                        pend.pop()()
                    for sbk in range(4):
                        for k16 in range(16):
                            kc = half * 16 + k16
                            S.op("pe", lambda e, slot=slot, sbk=sbk, k16=k16, kc=kc, half=half: e.matmul(
                                psp[sbk][:], Wt[slot][:, k16, sbk * 128:(sbk + 1) * 128], hT[:, kc, :],
                                start=(kc == 0), stop=(kc == 31)),
                                reads=[f"a_W{slot}", f"a_hT{kc}"], writes=[f"a_pp{sbk}"])
                for sbk in range(4):
                    b = cb * 4 + sbk
                    if cb == 3:
                        S.op("act", lambda e, sbk=sbk: e.activation(out=outs["z"][:, sbk, :], in_=psp[sbk][:], func=AF.Silu),
                             reads=[f"a_pp{sbk}"], writes=["a_oz"])
                        continue
                    ev = "act" if sbk % 2 == 0 else "dve"
                    if ev == "act":
                        S.op("act", lambda e, b=b, sbk=sbk: e.copy(out=pc[:, b, 3:515], in_=psp[sbk][:]), reads=[f"a_pp{sbk}"], writes=[f"a_pc{b}"])
                    else:
                        S.op("dve", lambda e, b=b, sbk=sbk: e.tensor_copy(out=pc[:, b, 3:515], in_=psp[sbk][:]), reads=[f"a_pp{sbk}"], writes=[f"a_pc{b}"])
                    if stage < 4:
                        continue
                    ce = "dve"
                    tb = tmp[b % 2]
                    tk = f"a_tmp{b % 2}"
                    S.op(ce, lambda e, b=b, tb=tb: e.tensor_scalar(out=tb[:], in0=pc[:, b, 3:515], scalar1=cwT[:, 3 * 12 + b:3 * 12 + b + 1],
                                                                   scalar2=None, op0=ALU.mult), reads=[f"a_pc{b}", "a_cwT"], writes=[tk])
                    for j in (2, 1, 0):
                        S.op(ce, lambda e, b=b, tb=tb, j=j: e.scalar_tensor_tensor(out=tb[:], in0=pc[:, b, j:j + 512],
                                                                                 scalar=cwT[:, j * 12 + b:j * 12 + b + 1], in1=tb[:],
                                                                                 op0=ALU.mult, op1=ALU.add),
                             reads=[f"a_pc{b}", "a_cwT", tk], writes=[tk])
                    S.op(ce, lambda e, b=b: e.tensor_copy(out=pc[:, b, 0:3], in_=pc[:, b, 512:515]), reads=[f"a_pc{b}"], writes=[f"a_pc{b}"])
                    if stage < 5:
                        continue
                    if cb == 2:
                        S.op("act", lambda e, tb=tb, sbk=sbk: e.activation(out=outs["v"][:, sbk, :], in_=tb[:], func=AF.Silu),
                             reads=[tk], writes=["a_ov"])
                        continue
                    slb = sl[b % 2]
                    sk = f"a_sl{b % 2}"
                    S.op("act", lambda e, tb=tb, slb=slb: e.activation(out=slb[:], in_=tb[:], func=AF.Silu), reads=[tk], writes=[sk])
                    S.op("dve", lambda e, slb=slb: e.tensor_tensor(out=sqb[:], in0=slb[:], in1=slb[:], op=ALU.mult), reads=[sk], writes=["a_sqb"])
                    S.op("pe", lambda e: e.matmul(pss[:], C["onesb"][:], sqb[:], start=True, stop=True), reads=["c_onesb", "a_sqb"], writes=["a_pss"])
                    S.op("act", lambda e: e.activation(out=rn[:], in_=pss[:], func=AF.Sqrt, bias=RMS_EPS, scale=1.0), reads=["a_pss"], writes=["a_rn"])
                    S.op("dve", lambda e: e.reciprocal(out=rn[:], in_=rn[:]), reads=["a_rn"], writes=["a_rn"])
                    nm = "q" if cb == 0 else "k"
                    S.op("dve", lambda e, slb=slb, nm=nm, sbk=sbk, cb=cb: e.scalar_tensor_tensor(
                        out=outs[nm][:, sbk, :], in0=slb[:], scalar=(qk_scale if cb == 0 else 1.0), in1=rn[:], op0=ALU.mult, op1=ALU.mult),
                        reads=[sk, "a_rn"], writes=[f"a_o{nm}"])
                nm = "qkvz"[cb]
                if stage < 6:
                    continue
                pend.append(lambda nm=nm, T=T: S.dma(STQ, scr[nm + "s"].rearrange("h d t -> d h t")[:, :, T * 512:(T + 1) * 512], outs[nm][:],
                                                     reads=[f"a_o{nm}"], writes=[f"scr_{nm}"]))
        while pend:
            pend.pop()()
        S.flush(B.es)


def fl(ap):
    return ap.rearrange("p a b -> p (a b)")


def phase_a2(B, scr, ng_ap, y_ap, ntiles=SEQ // 512, NL=6, stage=9):
    S, nc = B.S, B.nc
    with ExitStack() as st:
        C = make_consts(B, st)
        banks = [B.ps(st, f"g_ps{i}", [128, 512], F32) for i in range(7)]
        pbt = B.ps(st, "g_pbt", [128, 8, 128], BF16)
        S.excl.update([f"g_ps{i}" for i in range(7)] + ["g_pbt"])
        bcnt = [0]

        def nb():
            i = bcnt[0] % 7
            bcnt[0] += 1
            return banks[i], f"g_ps{i}"

        def v3(t):
            return t[:].rearrange("p (a b) -> p a b", a=4)

        def T3(name, dt=F32):
            return B.sb(st, name, [128, 4, 128], dt)

        triu = B.sb(st, "g_triu", [128, 128], F32)
        su4, nm4, i4 = T3("g_su4"), T3("g_nm4"), T3("g_i4")
        ng = B.sb(st, "g_ng", [128, 1], F32)
        S.op("pool", lambda e: e.memset(triu[:], 1.0), writes=["g_triu"])
        S.op("pool", lambda e: e.affine_select(out=triu[:], in_=triu[:], pattern=[[1, 128]], compare_op=ALU.is_ge, fill=0.0,
                                               base=0, channel_multiplier=-1), reads=["g_triu"], writes=["g_triu"])
        S.op("pool", lambda e: e.memset(su4[:], 1.0), writes=["g_su4"])
        S.op("pool", lambda e: e.affine_select(out=su4[:], in_=su4[:], pattern=[[0, 4], [1, 128]], compare_op=ALU.is_ge, fill=0.0,
                                               base=-1, channel_multiplier=-1), reads=["g_su4"], writes=["g_su4"])
        S.op("pool", lambda e: e.memset(nm4[:], 0.0), writes=["g_nm4"])
        S.op("pool", lambda e: e.affine_select(out=nm4[:], in_=nm4[:], pattern=[[0, 4], [1, 128]], compare_op=ALU.is_ge, fill=NEG,
                                               base=0, channel_multiplier=-1), reads=["g_nm4"], writes=["g_nm4"])
        S.op("pool", lambda e: e.memset(i4[:], 0.0), writes=["g_i4"])
        S.op("pool", lambda e: e.affine_select(out=i4[:], in_=i4[:], pattern=[[0, 4], [-1, 128]], compare_op=ALU.not_equal, fill=1.0,
                                               base=0, channel_multiplier=1), reads=["g_i4"], writes=["g_i4"])
        S.dma("sp", ng[:], ng_ap.rearrange("o p -> p o"), writes=["g_ng"])
        S.op("dve", lambda e: e.tensor_scalar(out=ng[:], in0=ng[:], scalar1=float(128.0 ** 0.5), scalar2=None, op0=ALU.mult),
             reads=["g_ng"], writes=["g_ng"])
        S32 = T3("g_S32")
        Sbf = T3("g_Sbf", BF16)
        S.op("pool", lambda e: e.memset(S32[:], 0.0), writes=["g_S32"])
        S.op("pool", lambda e: e.memset(Sbf[:], 0.0), writes=["g_Sbf"])
        inb = {n: [B.sb(st, f"g_in{n}{i}", [128, 4, 512], BF16) for i in range(2)] for n in "qkvz"}
        gbb = [B.sb(st, f"g_gb{i}", [128, 4, 8], F32) for i in range(2)]
        ybuf = [B.sb(st, f"g_y{i}", [128, 4, 512], BF16) for i in range(2)]
        kvtm = B.sb(st, "g_kvtm", [128, 8, 128], BF16)
        gc = B.sb(st, "g_gc", [128, 2, 4], F32)
        ngc = B.sb(st, "g_ngc", [128, 4], F32)
        ed = B.sb(st, "g_ed", [128, 4], F32)
        gm, E, GM, DT, DTS, U, UT = T3("g_gm"), T3("g_E"), T3("g_GM"), T3("g_DT"), T3("g_DTS"), T3("g_U"), T3("g_UT")
        Wb = [T3(f"g_W{i}") for i in range(2)]
        WTb = [T3(f"g_WT{i}") for i in range(2)]
        Pb = [T3(f"g_P{i}") for i in range(2)]
        TpT, qkT, kgT, qdT, kdec, R, vnew = (T3(n, BF16) for n in ("g_TpT", "g_qkT", "g_kgT", "g_qdT", "g_kdec", "g_R", "g_vnew"))
        sq = T3("g_sq", BF16)
        rn, yt = T3("g_rn"), T3("g_yt")

        def bc(ap2):
            return ap2.unsqueeze(2).broadcast_to([128, 4, 128])

        for T in range(ntiles):
            bi = T % 2
            for n in "qkvz":
                S.dma("sp", inb[n][bi][:], scr[n + "s"].rearrange("h d t -> d h t")[:, :, T * 512:(T + 1) * 512],
                      reads=[f"scr_{n}"], writes=[f"g_in{n}{bi}"])
            S.dma("sp", gbb[bi][:], scr["gbs"].rearrange("(n p) e -> p n e", p=128)[:, T * 4:(T + 1) * 4, :],
                  reads=["scr_gb"], writes=[f"g_gb{bi}"])
            qT, kT, vT, zs, gb = inb["q"][bi], inb["k"][bi], inb["v"][bi], inb["z"][bi], gbb[bi]
            kq, kk, kv, kz, kgb = (f"g_inq{bi}", f"g_ink{bi}", f"g_inv{bi}", f"g_inz{bi}", f"g_gb{bi}")
            for c in range(4):
                cs = slice(c * 128, (c + 1) * 128)
                if stage < 1:
                    continue
                for h in range(4):
                    S.op("pe", lambda e, h=h, cs=cs, kT=kT: e.transpose(pbt[:, h, :], kT[:, h, cs], C["idb"][:]), reads=[kk, "c_idb"], writes=["g_pbt"])
                for h in range(4):
                    S.op("pe", lambda e, h=h, cs=cs, vT=vT: e.transpose(pbt[:, 4 + h, :], vT[:, h, cs], C["idb"][:]), reads=[kv, "c_idb"], writes=["g_pbt"])
                S.op("act", lambda e: e.copy(out=kvtm[:], in_=pbt[:]), reads=["g_pbt"], writes=["g_kvtm"])
                if stage < 2:
                    continue
                pa, pak = nb()
                S.op("pe", lambda e, pa=pa, gb=gb, c=c: e.matmul(pa[:, 0:4], triu[:], gb[:, c, 0:4], start=True, stop=True), reads=["g_triu", kgb], writes=[pak])
                S.op("pe", lambda e, pa=pa, gb=gb, c=c: e.matmul(pa[:, 4:8], C["onesf"][:], gb[:, c, 0:4], start=True, stop=True), reads=["c_onesf", kgb], writes=[pak])
                S.op("dve", lambda e, pa=pa: e.tensor_copy(out=gc[:].rearrange("p a b -> p (a b)"), in_=pa[:, 0:8]), reads=[pak], writes=["g_gc"])
                S.op("dve", lambda e: e.tensor_scalar(out=ngc[:], in0=gc[:, 0, :], scalar1=-1.0, scalar2=None, op0=ALU.mult), reads=["g_gc"], writes=["g_ngc"])
                S.op("dve", lambda e: e.tensor_tensor(out=ed[:], in0=gc[:, 1, :], in1=gc[:, 0, :], op=ALU.subtract), reads=["g_gc"], writes=["g_ed"])
                S.op("act", lambda e: e.activation(out=ed[:], in_=ed[:], func=AF.Exp), reads=["g_ed"], writes=["g_ed"])
                if stage < 3:
                    continue
                S.op("pool", lambda e, gb=gb, c=c: e.tensor_tensor(out=gm[:], in0=triu[:].unsqueeze(1).broadcast_to([128, 4, 128]),
                                                                    in1=bc(gb[:, c, 0:4]), op=ALU.mult), reads=["g_triu", kgb], writes=["g_gm"])
                pg, pgk = nb()
                S.op("pe", lambda e, pg=pg: e.matmul(pg[:], C["onesf"][:], fl(gm[:]), start=True, stop=True), reads=["c_onesf", "g_gm"], writes=[pgk])
                if stage < 3.2:
                    continue
                S.op("act", lambda e, pg=pg: e.activation(out=fl(E[:]), in_=pg[:], func=AF.Exp), reads=[pgk], writes=["g_E"])
                S.op("dve", lambda e, pg=pg: e.tensor_tensor(out=fl(GM[:]), in0=pg[:], in1=fl(nm4[:]), op=ALU.add), reads=[pgk, "g_nm4"], writes=["g_GM"])
                if stage < 3.3:
                    continue
                for h in range(4):
                    S.op("act", lambda e, h=h: e.activation(out=DT[:, h, :], in_=GM[:, h, :], func=AF.Exp, bias=ngc[:, h:h + 1], scale=1.0),
                         reads=["g_GM", "g_ngc"], writes=["g_DT"])
                if stage < 3.4:
                    continue
                S.op("pool", lambda e: e.tensor_tensor(out=DTS[:], in0=DT[:], in1=su4[:], op=ALU.mult), reads=["g_DT", "g_su4"], writes=["g_DTS"])
                S.op("pool", lambda e, gb=gb, c=c: e.tensor_tensor(out=DTS[:], in0=DTS[:], in1=bc(gb[:, c, 4:8]), op=ALU.mult), reads=["g_DTS", kgb], writes=["g_DTS"])
                if stage < 4:
                    continue
                pk, pkk = nb()
                for h in range(4):
                    S.op("pe", lambda e, h=h, cs=cs, kT=kT, pk=pk: e.matmul(v3(pk)[:, h, :], kT[:, h, cs], kT[:, h, cs], start=True, stop=True), reads=[kk], writes=[pkk])
                pq, pqk = nb()
                for h in range(4):
                    S.op("pe", lambda e, h=h, cs=cs, kT=kT, qT=qT, pq=pq: e.matmul(v3(pq)[:, h, :], kT[:, h, cs], qT[:, h, cs], start=True, stop=True), reads=[kk, kq], writes=[pqk])
                S.op("dve", lambda e, pk=pk: e.tensor_tensor(out=fl(U[:]), in0=pk[:], in1=fl(DTS[:]), op=ALU.mult), reads=[pkk, "g_DTS"], writes=["g_U"])
                S.op("dve", lambda e, pq=pq: e.tensor_tensor(out=fl(qkT[:]), in0=pq[:], in1=fl(DT[:]), op=ALU.mult), reads=[pqk, "g_DT"], writes=["g_qkT"])
                if stage < 5:
                    continue
                pu, puk = nb()
                for h in range(4):
                    S.op("pe", lambda e, h=h, pu=pu: e.transpose(v3(pu)[:, h, :], U[:, h, :], C["idf"][:]), reads=["g_U", "c_idf"], writes=[puk])
                S.op("act", lambda e, pu=pu: e.copy(out=fl(UT[:]), in_=pu[:]), reads=[puk], writes=["g_UT"])
                S.op("pool", lambda e: e.tensor_tensor(out=Pb[0][:], in0=i4[:], in1=U[:], op=ALU.subtract), reads=["g_i4", "g_U"], writes=["g_P0"])
                if stage < 6:
                    continue
                W, WT, P = U, UT, Pb[0]
                Wk, WTk, Pk = "g_U", "g_UT", "g_P0"
                for l in range(1, NL + 1):
                    Wn, WTn, Pn = Wb[l % 2], WTb[l % 2], Pb[l % 2]
                    Wnk, WTnk, Pnk = f"g_W{l % 2}", f"g_WT{l % 2}", f"g_P{l % 2}"
                    if l < NL:
                        pw, pwk = nb()
                        for h in range(4):
                            S.op("pe", lambda e, h=h, pw=pw, W=W, WT=WT: e.matmul(v3(pw)[:, h, :], WT[:, h, :], W[:, h, :], start=True, stop=True), reads=[Wk, WTk], writes=[pwk])
                    pwt, pwtk = nb()
                    for h in range(4):
                        S.op("pe", lambda e, h=h, pwt=pwt, W=W, WT=WT: e.matmul(v3(pwt)[:, h, :], W[:, h, :], WT[:, h, :], start=True, stop=True), reads=[Wk, WTk], writes=[pwtk])
                    if l < NL:
                        S.op("act", lambda e, pw=pw, Wn=Wn: e.copy(out=fl(Wn[:]), in_=pw[:]), reads=[pwk], writes=[Wnk])
                    S.op("dve", lambda e, pwt=pwt, WTn=WTn: e.tensor_copy(out=fl(WTn[:]), in_=pwt[:]), reads=[pwtk], writes=[WTnk])
                    pp, ppk = nb()
                    for h in range(4):
                        S.op("pe", lambda e, h=h, pp=pp, WTn=WTn, P=P: e.matmul(v3(pp)[:, h, :], WTn[:, h, :], P[:, h, :], start=True, stop=True), reads=[WTnk, Pk], writes=[ppk])
                    if l < NL:
                        S.op("dve", lambda e, pp=pp, P=P, Pn=Pn: e.tensor_tensor(out=fl(Pn[:]), in0=pp[:], in1=fl(P[:]), op=ALU.add), reads=[ppk, Pk], writes=[Pnk])
                    else:
                        S.op("dve", lambda e, pp=pp, P=P: e.tensor_tensor(out=fl(TpT[:]), in0=pp[:], in1=fl(P[:]), op=ALU.add), reads=[ppk, Pk], writes=["g_TpT"])
                    W, WT, P, Wk, WTk, Pk = Wn, WTn, Pn, Wnk, WTnk, Pnk
                if stage < 7:
                    continue
                S.op("pool", lambda e, kT=kT, cs=cs: e.tensor_tensor(out=kgT[:], in0=kT[:, :, cs], in1=E[:], op=ALU.mult), reads=[kk, "g_E"], writes=["g_kgT"])
                S.op("pool", lambda e, qT=qT, cs=cs: e.tensor_tensor(out=qdT[:], in0=qT[:, :, cs], in1=E[:], op=ALU.mult), reads=[kq, "g_E"], writes=["g_qdT"])
                S.op("pool", lambda e: e.tensor_tensor(out=kdec[:], in0=kvtm[:, 0:4, :], in1=bc(ed[:]), op=ALU.mult), reads=["g_kvtm", "g_ed"], writes=["g_kdec"])
                if stage < 8:
                    continue
                p1, p1k = nb()
                for h in range(4):
                    S.op("pe", lambda e, h=h, p1=p1: e.matmul(v3(p1)[:, h, :], kgT[:, h, :], Sbf[:, h, :], start=True, stop=True), reads=["g_kgT", "g_Sbf"], writes=[p1k])
                S.op("dve", lambda e, p1=p1: e.tensor_tensor(out=R[:], in0=kvtm[:, 4:8, :], in1=v3(p1), op=ALU.subtract), reads=["g_kvtm", p1k], writes=["g_R"])
                p2, p2k = nb()
                for h in range(4):
                    S.op("pe", lambda e, h=h, p2=p2: e.matmul(v3(p2)[:, h, :], TpT[:, h, :], R[:, h, :], start=True, stop=True), reads=["g_TpT", "g_R"], writes=[p2k])
                S.op("dve", lambda e, p2=p2, gb=gb, c=c: e.tensor_tensor(out=vnew[:], in0=v3(p2), in1=bc(gb[:, c, 4:8]), op=ALU.mult), reads=[p2k, kgb], writes=["g_vnew"])
                po, pok = nb()
                for h in range(4):
                    S.op("pe", lambda e, h=h, po=po: e.matmul(v3(po)[:, h, :], Sbf[:, h, :], qdT[:, h, :], start=True, stop=False), reads=["g_Sbf", "g_qdT"], writes=[pok])
                    S.op("pe", lambda e, h=h, po=po: e.matmul(v3(po)[:, h, :], vnew[:, h, :], qkT[:, h, :], start=False, stop=True), reads=["g_vnew", "g_qkT"], writes=[pok])
                p3, p3k = nb()
                for h in range(4):
                    S.op("pe", lambda e, h=h, p3=p3: e.matmul(v3(p3)[:, h, :], kdec[:, h, :], vnew[:, h, :], start=True, stop=True), reads=["g_kdec", "g_vnew"], writes=[p3k])
                for h in range(4):
                    S.op("dve", lambda e, h=h, p3=p3: e.scalar_tensor_tensor(out=S32[:, h, :], in0=S32[:, h, :], scalar=E[:, h, 127:128], in1=v3(p3)[:, h, :],
                                                                           op0=ALU.mult, op1=ALU.add), reads=["g_S32", "g_E", p3k], writes=["g_S32"])
                S.op("act", lambda e: e.copy(out=Sbf[:], in_=S32[:]), reads=["g_S32"], writes=["g_Sbf"])
                if stage < 9:
                    continue
                S.op("act", lambda e, po=po: e.activation(out=fl(sq[:]), in_=po[:], func=AF.Square), reads=[pok], writes=["g_sq"])
                pss, pssk = nb()
                S.op("pe", lambda e, pss=pss: e.matmul(pss[:], C["onesb"][:], fl(sq[:]), start=True, stop=True), reads=["c_onesb", "g_sq"], writes=[pssk])
                S.op("act", lambda e, pss=pss: e.activation(out=fl(rn[:]), in_=pss[:], func=AF.Sqrt, bias=128.0 * RMS_EPS, scale=1.0), reads=[pssk], writes=["g_rn"])
                S.op("dve", lambda e: e.reciprocal(out=rn[:], in_=rn[:]), reads=["g_rn"], writes=["g_rn"])
                S.op("dve", lambda e, po=po: e.scalar_tensor_tensor(out=fl(yt[:]), in0=po[:], scalar=ng[:, 0:1], in1=fl(rn[:]), op0=ALU.mult, op1=ALU.mult),
                     reads=[pok, "g_ng", "g_rn"], writes=["g_yt"])
                S.op("pool", lambda e, zs=zs, cs=cs, bi=bi: e.tensor_tensor(out=ybuf[bi][:, :, cs], in0=yt[:], in1=zs[:, :, cs], op=ALU.mult),
                     reads=["g_yt", kz], writes=[f"g_y{bi}"])
            B.finals.append(S.dma(STQ, y_ap.rearrange("(h d) t -> d h t", d=128)[:, :, T * 512:(T + 1) * 512], ybuf[bi][:],
                                  reads=[f"g_y{bi}"], writes=["y_out"]))
        S.flush(B.es)


def phase_out(B, yT_ap, zT_ap, w_ap, xres_ap, mod_ap, layer, lng_ap, lnb_ap, out_ap, ntok=TPC):
    S, nc = B.S, B.nc
    with ExitStack() as st:
        pp = [B.ps(st, f"o_pp{i}", [128, 512], F32) for i in range(4)]
        S.excl.update([f"o_pp{i}" for i in range(4)])
        yT = B.sb(st, "o_yT", [128, 32, 512], BF16)
        zt = B.sb(st, "o_zt", [128, 8, 512], BF16)
        gb_, lg_, lb_ = (B.sb(st, n, [128, D], F32) for n in ("o_gate", "o_lng", "o_lnb"))
        Wt = [B.sb(st, f"o_W{i}", [128, 16, 512], BF16) for i in range(3)]
        xr = B.sb(st, "o_xr", [128, D], F32)
        r = B.sb(st, "o_r", [128, D], F32)
        sm = B.sb(st, "o_sm", [128, 8], F32)
        S.dma("sp", gb_[:], mod_ap[layer:layer + 1, 8192:12288].partition_broadcast(128), writes=["o_gate"])
        S.dma("sp", lg_[:], lng_ap.partition_broadcast(128), writes=["o_lng"])
        S.dma("sp", lb_[:], lnb_ap.partition_broadcast(128), writes=["o_lnb"])
        S.op("dve", lambda e: e.tensor_scalar(out=gb_[:], in0=gb_[:], scalar1=1.0, scalar2=None, op0=ALU.add), reads=["o_gate"], writes=["o_gate"])
        yv = yT_ap.rearrange("(kc p) t -> p kc t", p=128)
        wv = w_ap.rearrange("(kc p) n -> p kc n", p=128)
        xv = xres_ap.rearrange("(n p) d -> n p d", p=128)
        ov = out_ap.rearrange("(n p) d -> n p d", p=128)
        wcnt = 0
        pcnt = 0
        for hf in range(ntok // 512):
            S.dma("sp", yT[:], yv[:, :, hf * 512:(hf + 1) * 512], writes=["o_yT"])
            if zT_ap is not None:
                zv = zT_ap.rearrange("(kc p) t -> p kc t", p=128)
                for g in range(4):
                    S.dma("sp", zt[:], zv[:, g * 8:(g + 1) * 8, hf * 512:(hf + 1) * 512], writes=["o_zt"])
                    S.op("pool", lambda e, g=g: e.tensor_tensor(out=yT[:, g * 8:(g + 1) * 8, :], in0=yT[:, g * 8:(g + 1) * 8, :], in1=zt[:], op=ALU.mult),
                         reads=["o_yT", "o_zt"], writes=["o_yT"])
            for ts in range(4):
                n = hf * 4 + ts
                S.dma("sp", xr[:], xv[n], writes=["o_xr"])
                for nb in range(8):
                    pb = pp[pcnt % 4]
                    pk = f"o_pp{pcnt % 4}"
                    pcnt += 1
                    for half in range(2):
                        slot = wcnt % 3
                        wcnt += 1
                        S.dma("pool", Wt[slot][:], wv[:, half * 16:(half + 1) * 16, nb * 512:(nb + 1) * 512], writes=[f"o_W{slot}"])
                        for k16 in range(16):
                            kc = half * 16 + k16
                            S.op("pe", lambda e, pb=pb, slot=slot, k16=k16, kc=kc, ts=ts: e.matmul(
                                pb[:], yT[:, kc, ts * 128:(ts + 1) * 128], Wt[slot][:, k16, :], start=(kc == 0), stop=(kc == 31)),
                                reads=["o_yT", f"o_W{slot}"], writes=[pk])
                    cs = slice(nb * 512, (nb + 1) * 512)
                    S.op("dve", lambda e, pb=pb, cs=cs: e.tensor_tensor(out=r[:, cs], in0=pb[:], in1=gb_[:, cs], op=ALU.mult), reads=[pk, "o_gate"], writes=["o_r"])
                    S.op("dve", lambda e, cs=cs: e.scalar_tensor_tensor(out=r[:, cs], in0=xr[:, cs], scalar=float(ALPHA), in1=r[:, cs], op0=ALU.mult, op1=ALU.add),
                         reads=["o_xr", "o_r"], writes=["o_r"])
                S.op("dve", lambda e: e.reduce_sum(out=sm[:, 0:1], in_=r[:], axis=AX.X), reads=["o_r"], writes=["o_sm"])
                S.op("pool", lambda e: e.tensor_tensor(out=xr[:], in0=r[:], in1=r[:], op=ALU.mult), reads=["o_r", "o_xr"], writes=["o_xr"])
                S.op("dve", lambda e: e.reduce_sum(out=sm[:, 1:2], in_=xr[:], axis=AX.X), reads=["o_xr", "o_sm"], writes=["o_sm"])
                S.op("dve", lambda e: e.tensor_scalar(out=sm[:, 0:2], in0=sm[:, 0:2], scalar1=1.0 / D, scalar2=None, op0=ALU.mult), reads=["o_sm"], writes=["o_sm"])
                S.op("dve", lambda e: e.tensor_tensor(out=sm[:, 2:3], in0=sm[:, 0:1], in1=sm[:, 0:1], op=ALU.mult), reads=["o_sm"], writes=["o_sm"])
                S.op("dve", lambda e: e.tensor_tensor(out=sm[:, 3:4], in0=sm[:, 1:2], in1=sm[:, 2:3], op=ALU.subtract), reads=["o_sm"], writes=["o_sm"])
                S.op("act", lambda e: e.activation(out=sm[:, 4:5], in_=sm[:, 3:4], func=AF.Sqrt, bias=LN_EPS, scale=1.0), reads=["o_sm"], writes=["o_sm"])
                S.op("dve", lambda e: e.reciprocal(out=sm[:, 5:6], in_=sm[:, 4:5]), reads=["o_sm"], writes=["o_sm"])
                S.op("dve", lambda e: e.tensor_scalar(out=r[:], in0=r[:], scalar1=sm[:, 0:1], scalar2=None, op0=ALU.subtract), reads=["o_r", "o_sm"], writes=["o_r"])
                S.op("dve", lambda e: e.scalar_tensor_tensor(out=r[:], in0=r[:], scalar=sm[:, 5:6], in1=lg_[:], op0=ALU.mult, op1=ALU.mult),
                     reads=["o_r", "o_sm", "o_lng"], writes=["o_r"])
                S.op("pool", lambda e: e.tensor_tensor(out=r[:], in0=r[:], in1=lb_[:], op=ALU.add), reads=["o_r", "o_lnb"], writes=["o_r"])
                B.finals.append(S.dma("sp", ov[n], r[:], reads=["o_r"], writes=["o_out"]))
        S.flush(B.es)


def phase_out2(B, yT_ap, zT_ap, w_ap, xres_ap, mod_ap, layer, lng_ap, lnb_ap, out_ap, rscr_ap):
    S, nc = B.S, B.nc
    NT = TPC // 128
    xv = xres_ap.rearrange("(n p) d -> n p d", p=128)
    rv = rscr_ap.rearrange("(n p) d -> n p d", p=128)
    ov = out_ap.rearrange("(n p) d -> n p d", p=128)
    with ExitStack() as st:
        pp = [B.ps(st, f"o_pp{i}", [128, 512], F32) for i in range(8)]
        S.excl.update([f"o_pp{i}" for i in range(8)])
        yT = B.sb(st, "o_yT", [128, 32, TPC], BF16)
        zt = B.sb(st, "o_zt", [128, 4, TPC], BF16)
        gb_ = B.sb(st, "o_gate", [128, D], F32)
        Wt = [B.sb(st, f"o_W{i}", [128, 16, 512], BF16) for i in range(3)]
        xc = [B.sb(st, f"o_xc{i}", [128, 512], F32) for i in range(3)]
        rc = [B.sb(st, f"o_rc{i}", [128, 512], F32) for i in range(3)]
        S.dma("sp", gb_[:], mod_ap[layer:layer + 1, 8192:12288].partition_broadcast(128), writes=["o_gate"])
        S.op("dve", lambda e: e.tensor_scalar(out=gb_[:], in0=gb_[:], scalar1=1.0, scalar2=None, op0=ALU.add), reads=["o_gate"], writes=["o_gate"])
        yv = yT_ap.rearrange("(kc p) t -> p kc t", p=128)
        wv = w_ap.rearrange("(kc p) n -> p kc n", p=128)
        for g in range(8):
            S.dma("sp", yT[:, g * 4:(g + 1) * 4, :], yv[:, g * 4:(g + 1) * 4, :], writes=[f"o_yT{g}"])
            if zT_ap is not None:
                zv = zT_ap.rearrange("(kc p) t -> p kc t", p=128)
                S.dma("sp", zt[:], zv[:, g * 4:(g + 1) * 4, :], writes=["o_zt"])
                S.op("pool", lambda e, g=g: e.tensor_tensor(out=yT[:, g * 4:(g + 1) * 4, :], in0=yT[:, g * 4:(g + 1) * 4, :], in1=zt[:], op=ALU.mult),
                     reads=[f"o_yT{g}", "o_zt"], writes=[f"o_yT{g}"])
        ykeys = [f"o_yT{g}" for g in range(8)]
        wcnt = 0
        ccnt = 0
        for nb in range(8):
            cs = slice(nb * 512, (nb + 1) * 512)
            slots = []
            for half in range(2):
                slot = wcnt % 3
                wcnt += 1
                slots.append(slot)
                S.dma("pool", Wt[slot][:], wv[:, half * 16:(half + 1) * 16, cs], writes=[f"o_W{slot}"])
            for ts in range(NT):
                pb = pp[ts]
                for kc in range(32):
                    slot = slots[kc // 16]
                    S.op("pe", lambda e, pb=pb, slot=slot, kc=kc, ts=ts: e.matmul(
                        pb[:], yT[:, kc, ts * 128:(ts + 1) * 128], Wt[slot][:, kc % 16, :], start=(kc == 0), stop=(kc == 31)),
                        reads=[ykeys[kc // 4], f"o_W{slot}"], writes=[f"o_pp{ts}"])
                ci = ccnt % 3
                ccnt += 1
                S.dma("sp", xc[ci][:], xv[ts][:, cs], writes=[f"o_xc{ci}"])
                S.op("dve", lambda e, pb=pb, ci=ci, cs=cs: e.tensor_tensor(out=rc[ci][:], in0=pb[:], in1=gb_[:, cs], op=ALU.mult), reads=[f"o_pp{ts}", "o_gate"], writes=[f"o_rc{ci}"])
                S.op("dve", lambda e, ci=ci: e.scalar_tensor_tensor(out=rc[ci][:], in0=xc[ci][:], scalar=float(ALPHA), in1=rc[ci][:], op0=ALU.mult, op1=ALU.add),
                     reads=[f"o_xc{ci}", f"o_rc{ci}"], writes=[f"o_rc{ci}"])
                S.dma("sp", rv[ts][:, cs], rc[ci][:], reads=[f"o_rc{ci}"], writes=["o_rscr"])
        S.flush(B.es)
    with ExitStack() as st:
        lg_, lb_ = (B.sb(st, n, [128, D], F32) for n in ("o_lng", "o_lnb"))
        r2 = [B.sb(st, f"o_r{i}", [128, D], F32) for i in range(2)]
        sq = B.sb(st, "o_sq", [128, D], F32)
        sm = B.sb(st, "o_sm", [128, 8], F32)
        S.dma("sp", lg_[:], lng_ap.partition_broadcast(128), writes=["o_lng"])
        S.dma("sp", lb_[:], lnb_ap.partition_broadcast(128), writes=["o_lnb"])
        for n in range(NT):
            r = r2[n % 2]
            rk = f"o_r{n % 2}"
            S.dma("sp", r[:], rv[n], writes=[rk])
            S.op("dve", lambda e, r=r: e.reduce_sum(out=sm[:, 0:1], in_=r[:], axis=AX.X), reads=[rk], writes=["o_sm"])
            S.op("pool", lambda e, r=r: e.tensor_tensor(out=sq[:], in0=r[:], in1=r[:], op=ALU.mult), reads=[rk], writes=["o_sq"])
            S.op("dve", lambda e: e.reduce_sum(out=sm[:, 1:2], in_=sq[:], axis=AX.X), reads=["o_sq", "o_sm"], writes=["o_sm"])
            S.op("dve", lambda e: e.tensor_scalar(out=sm[:, 0:2], in0=sm[:, 0:2], scalar1=1.0 / D, scalar2=None, op0=ALU.mult), reads=["o_sm"], writes=["o_sm"])
            S.op("dve", lambda e: e.tensor_tensor(out=sm[:, 2:3], in0=sm[:, 0:1], in1=sm[:, 0:1], op=ALU.mult), reads=["o_sm"], writes=["o_sm"])
            S.op("dve", lambda e: e.tensor_tensor(out=sm[:, 3:4], in0=sm[:, 1:2], in1=sm[:, 2:3], op=ALU.subtract), reads=["o_sm"], writes=["o_sm"])
            S.op("act", lambda e: e.activation(out=sm[:, 4:5], in_=sm[:, 3:4], func=AF.Sqrt, bias=LN_EPS, scale=1.0), reads=["o_sm"], writes=["o_sm"])
            S.op("dve", lambda e: e.reciprocal(out=sm[:, 5:6], in_=sm[:, 4:5]), reads=["o_sm"], writes=["o_sm"])
            S.op("dve", lambda e, r=r: e.tensor_scalar(out=r[:], in0=r[:], scalar1=sm[:, 0:1], scalar2=None, op0=ALU.subtract), reads=[rk, "o_sm"], writes=[rk])
            S.op("dve", lambda e, r=r: e.scalar_tensor_tensor(out=r[:], in0=r[:], scalar=sm[:, 5:6], in1=lg_[:], op0=ALU.mult, op1=ALU.mult),
                 reads=[rk, "o_sm", "o_lng"], writes=[rk])
            S.op("pool", lambda e, r=r: e.tensor_tensor(out=r[:], in0=r[:], in1=lb_[:], op=ALU.add), reads=[rk, "o_lnb"], writes=[rk])
            B.finals.append(S.dma("sp", ov[n], r[:], reads=[rk], writes=["o_out"]))
        S.flush(B.es)


def phase_b2(B, x1_ap, mod_ap, win_ap, qg_ap, kvg_ap, latT_ap, zsT_ap):
    S, nc = B.S, B.nc
    with ExitStack() as st:
        C = make_consts(B, st)
        ptr = [B.ps(st, f"b_pt{i}", [128, 4, 128], F32) for i in range(2)]
        ptk = [f"b_pt{i}" for i in range(2)]
        psp = [B.ps(st, f"b_pp{i}", [128, 512], F32) for i in range(4)]
        pmd = B.ps(st, "b_pmd", [128, 512], F32)
        pbt = B.ps(st, "b_pbt", [128, 8, 128], BF16)
        S.excl.update(ptk + [f"b_pp{i}" for i in range(4)] + ["b_pmd", "b_pbt"])
        sh, sc = load_mod_fm(B, st, C, mod_ap, 1, pmd, "b_pmd")
        xs = B.sb(st, "b_xs", [128, D], F32)
        hT = B.sb(st, "b_hT", [128, 32, TPC], BF16)
        Wt = [B.sb(st, f"b_W{i}", [128, 16, 512], BF16) for i in range(4)]
        zo = B.sb(st, "b_zo", [128, 4, 512], BF16)
        lt = B.sb(st, "b_lt", [128, 1472], F32)
        lsq = B.sb(st, "b_lsq", [128, 896], F32)
        lb = B.sb(st, "b_lb", [128, 13, 128], BF16)
        ltT = B.sb(st, "b_ltT", [128, 13, 128], BF16)
        qg = B.sb(st, "b_qg", [128, 896], F32)
        kvg = B.sb(st, "b_kvg", [128, 512], F32)
        sm = B.sb(st, "b_sm", [128, 8], F32)
        S.dma("sp", qg[:], qg_ap.partition_broadcast(128), writes=["b_qg"])
        S.dma("sp", kvg[:], kvg_ap.partition_broadcast(128), writes=["b_kvg"])
        S.op("pool", lambda e: e.memset(lb[:], 0.0), writes=["b_lb"])
        xv = x1_ap.rearrange("(n p) d -> n p d", p=128)
        wv = win_ap.rearrange("(kc p) n -> p kc n", p=128)
        cnt = [0]
        for ts in range(8):
            S.dma("sp", xs[:], xv[ts], writes=["b_xs"])
            make_hT(B, C, xs, "b_xs", hT, "b_hT", ts, sh, sc, "md_sh1", "md_sc1", ptr, ptk, cnt)
        hkeys = [f"b_hT{kc}" for kc in range(32)]
        wcnt = 0
        zv = zsT_ap.rearrange("(cb s p) t -> cb p s t", s=4, p=128)
        for cb in range(8):
            slots = []
            for half in range(2):
                slot = wcnt % 4
                wcnt += 1
                slots.append(slot)
                S.dma("pool", Wt[slot][:], wv[:, half * 16:(half + 1) * 16, 1472 + cb * 512:1472 + (cb + 1) * 512], writes=[f"b_W{slot}"])
            for th in range(2):
                for sbk in range(4):
                    for kc in range(32):
                        slot = slots[kc // 16]
                        S.op("pe", lambda e, slot=slot, sbk=sbk, kc=kc, th=th: e.matmul(
                            psp[sbk][:], Wt[slot][:, kc % 16, sbk * 128:(sbk + 1) * 128], hT[:, kc, th * 512:(th + 1) * 512],
                            start=(kc == 0), stop=(kc == 31)), reads=[f"b_W{slot}", hkeys[kc]], writes=[f"b_pp{sbk}"])
                    S.op("act", lambda e, sbk=sbk: e.activation(out=zo[:, sbk, :], in_=psp[sbk][:], func=AF.Silu), reads=[f"b_pp{sbk}"], writes=["b_zo"])
                S.dma("sp", zv[cb][:, :, th * 512:(th + 1) * 512], zo[:], reads=["b_zo"], writes=["b_zs"])
        lv = latT_ap.rearrange("(j p) t -> p j t", p=128)
        for ts in range(8):
            for nbk, (c0, c1) in enumerate(((0, 512), (512, 1024), (1024, 1472))):
                slots = []
                for half in range(2):
                    slot = wcnt % 4
                    wcnt += 1
                    slots.append(slot)
                    S.dma("pool", Wt[slot][:, :, 0:c1 - c0], wv[:, half * 16:(half + 1) * 16, c0:c1], writes=[f"b_W{slot}"])
                pb = psp[nbk]
                for kc in range(32):
                    slot = slots[kc // 16]
                    S.op("pe", lambda e, slot=slot, kc=kc, ts=ts, pb=pb, c0=c0, c1=c1: e.matmul(
                        pb[:, 0:c1 - c0], hT[:, kc, ts * 128:(ts + 1) * 128], Wt[slot][:, kc % 16, 0:c1 - c0],
                        start=(kc == 0), stop=(kc == 31)), reads=[f"b_W{slot}", hkeys[kc]], writes=[f"b_pp{nbk}"])
                S.op("act", lambda e, pb=pb, c0=c0, c1=c1: e.copy(out=lt[:, c0:c1], in_=pb[:, 0:c1 - c0]), reads=[f"b_pp{nbk}"], writes=["b_lt"])
            for (a0, a1, gt, gk, col) in ((0, 896, qg, "b_qg", 0), (896, 1408, kvg, "b_kvg", 1)):
                n = a1 - a0
                S.op("pool", lambda e, a0=a0, a1=a1, n=n: e.tensor_tensor(out=lsq[:, 0:n], in0=lt[:, a0:a1], in1=lt[:, a0:a1], op=ALU.mult), reads=["b_lt"], writes=["b_lsq"])
                S.op("dve", lambda e, n=n, col=col: e.reduce_sum(out=sm[:, col:col + 1], in_=lsq[:, 0:n], axis=AX.X), reads=["b_lsq"], writes=["b_sm"])
                S.op("act", lambda e, n=n, col=col: e.activation(out=sm[:, col + 2:col + 3], in_=sm[:, col:col + 1], func=AF.Sqrt, bias=RMS_EPS, scale=1.0 / n),
                     reads=["b_sm"], writes=["b_sm"])
                S.op("dve", lambda e, col=col: e.reciprocal(out=sm[:, col + 4:col + 5], in_=sm[:, col + 2:col + 3]), reads=["b_sm"], writes=["b_sm"])
                S.op("dve", lambda e, a0=a0, a1=a1, n=n, gt=gt, col=col: e.scalar_tensor_tensor(
                    out=lb[:].rearrange("p a b -> p (a b)")[:, a0:a1], in0=lt[:, a0:a1], scalar=sm[:, col + 4:col + 5], in1=gt[:, 0:n], op0=ALU.mult, op1=ALU.mult),
                    reads=["b_lt", "b_sm", gk], writes=["b_lb"])
            S.op("dve", lambda e: e.tensor_copy(out=lb[:, 11, 0:64], in_=lt[:, 1408:1472]), reads=["b_lt"], writes=["b_lb"])
            S.op("dve", lambda e: e.tensor_copy(out=lb[:, 12, 0:32], in_=lt[:, 1440:1472]), reads=["b_lt"], writes=["b_lb"])
            S.op("dve", lambda e: e.tensor_copy(out=lb[:, 12, 32:64], in_=lt[:, 1408:1440]), reads=["b_lt"], writes=["b_lb"])
            for j0 in (0, 8):
                nj = min(8, 13 - j0)
                for j in range(nj):
                    S.op("pe", lambda e, j=j, j0=j0: e.transpose(pbt[:, j, :], lb[:, j0 + j, :], C["idb"][:]), reads=["b_lb", "c_idb"], writes=["b_pbt"])
                S.op("act", lambda e, j0=j0, nj=nj: e.copy(out=ltT[:, j0:j0 + nj, :], in_=pbt[:, 0:nj, :]), reads=["b_pbt"], writes=["b_ltT"])
            S.dma("sp", lv[:, :, ts * 128:(ts + 1) * 128], ltT[:], reads=["b_ltT"], writes=["b_lat"])
        S.flush(B.es)


def phase_c(B, lat_ap, wq_ap, wkv_ap, pos_ap, invf_ap, sgn_ap, oT_ap, nq=SEQ // 512):
    S, nc = B.S, B.nc
    TWO_PI = float(2 * np.pi)
    with ExitStack() as st:
        C = make_consts(B, st)
        acc = [B.ps(st, f"c_acc{i}", [128, 512], F32) for i in range(4)]
        pst = [B.ps(st, f"c_st{i}", [128, 512], F32) for i in range(2)]
        pm = [B.ps(st, f"c_pm{i}", [128, 512], F32) for i in range(2)]
        S.excl.update([f"c_acc{i}" for i in range(4)] + ["c_st0", "c_st1", "c_pm0", "c_pm1"])
        Wq = B.sb(st, "c_Wq", [128, 7, 1024], BF16)
        Wkv = B.sb(st, "c_Wkv", [128, 4, 1024], BF16)
        S.dma("pool", Wq[:], wq_ap.rearrange("(kc p) n -> p kc n", p=128), writes=["c_Wq"])
        S.dma("pool", Wkv[:], wkv_ap.rearrange("(kc p) n -> p kc n", p=128), writes=["c_Wkv"])
        invf = B.sb(st, "c_invf", [64, 1], F32)
        sgn = B.sb(st, "c_sgn", [64, 1], F32)
        S.dma("sp", invf[:], invf_ap, writes=["c_invf"])
        S.dma("sp", sgn[:], sgn_ap, writes=["c_sgn"])
        tril = B.sb(st, "c_tril", [128, 128], BF16)
        S.op("pool", lambda e: e.memset(tril[:], 1.0), writes=["c_tril"])
        S.op("pool", lambda e: e.affine_select(out=tril[:], in_=tril[:], pattern=[[1, 128]], compare_op=ALU.is_ge, fill=0.0,
                                               base=0, channel_multiplier=-1), reads=["c_tril"], writes=["c_tril"])
        qn = B.sb(st, "c_qn", [128, SEQ], BF16)
        kn = B.sb(st, "c_kn", [128, SEQ], BF16)
        qr = B.sb(st, "c_qr", [64, SEQ], BF16)
        kr = B.sb(st, "c_kr", [64, SEQ], BF16)
        vt = B.sb(st, "c_vt", [128, SEQ // 128, 132], BF16)
        S.op("pool", lambda e: e.memset(vt[:], 1.0), writes=["c_vt"])
        lat = [B.sb(st, f"c_lat{i}", [128, 13, 512], BF16) for i in range(2)]
        posi = B.sb(st, "c_posi", [64, 512], I32)
        ang = B.sb(st, "c_ang", [64, 512], F32)
        cs_ = B.sb(st, "c_cos", [64, 512], F32)
        sn_ = B.sb(st, "c_sin", [64, 512], F32)
        t1 = B.sb(st, "c_t1", [64, 512], F32)
        t2 = B.sb(st, "c_t2", [64, 512], F32)
        pt = [B.sb(st, f"c_p{i}", [128, 512], BF16) for i in range(2)]
        rs = B.sb(st, "c_rs", [128, 4], F32)
        on = B.sb(st, "c_on", [128, 4, 128], BF16)
        oT = B.sb(st, "c_oT", [128, 512], BF16)
        pbt = pm[1]
        lv = lat_ap.rearrange("(j p) t -> p j t", p=128)
        scale = float(192.0 ** -0.5)

        def rope(dst, dkey, pa, pak, pb, pbk, sl):
            S.op("dve", lambda e: e.tensor_tensor(out=t1[:], in0=pa[0:64, :], in1=cs_[:], op=ALU.mult), reads=[pak, "c_cos"], writes=["c_t1"])
            S.op("dve", lambda e: e.tensor_tensor(out=t2[:], in0=pb[0:64, :], in1=sn_[:], op=ALU.mult), reads=[pbk, "c_sin"], writes=["c_t2"])
            S.op("pool", lambda e: e.tensor_tensor(out=dst[0:64, sl], in0=t1[:], in1=t2[:], op=ALU.add), reads=["c_t1", "c_t2"], writes=[dkey])

        for h in range(4):
            for tt in range(SEQ // 512):
                sl = slice(tt * 512, (tt + 1) * 512)
                lt_ = lat[tt % 2]
                lk = f"c_lat{tt % 2}"
                S.dma("sp", lt_[:], lv[:, :, sl], writes=[lk])
                S.dma("sp", posi[:], pos_ap[0:1, sl].partition_broadcast(64), writes=["c_posi"])
                S.op("dve", lambda e: e.tensor_copy(out=ang[:], in_=posi[:]), reads=["c_posi"], writes=["c_ang"])
                S.op("dve", lambda e: e.tensor_scalar(out=ang[:], in0=ang[:], scalar1=invf[:, 0:1], scalar2=None, op0=ALU.mult), reads=["c_ang", "c_invf"], writes=["c_ang"])
                for (dst, dkey, addc) in ((sn_, "c_sin", 0.0), (cs_, "c_cos", float(0.5 * np.pi))):
                    S.op("dve", lambda e, addc=addc: e.tensor_scalar(out=t1[:], in0=ang[:], scalar1=addc, scalar2=None, op0=ALU.add), reads=["c_ang"], writes=["c_t1"])
                    S.op("dve", lambda e: e.tensor_scalar(out=t2[:], in0=t1[:], scalar1=1.0 / TWO_PI, scalar2=None, op0=ALU.mult), reads=["c_t1"], writes=["c_t2"])
                    S.op("dve", lambda e: e.tensor_copy(out=posi[:], in_=t2[:]), reads=["c_t2"], writes=["c_posi"])
                    S.op("dve", lambda e: e.tensor_copy(out=t2[:], in_=posi[:]), reads=["c_posi"], writes=["c_t2"])
                    S.op("dve", lambda e: e.scalar_tensor_tensor(out=t1[:], in0=t2[:], scalar=-TWO_PI, in1=t1[:], op0=ALU.mult, op1=ALU.add), reads=["c_t1", "c_t2"], writes=["c_t1"])
                    S.op("dve", lambda e: e.tensor_scalar(out=t2[:], in0=t1[:], scalar1=float(np.pi), scalar2=None, op0=ALU.is_gt), reads=["c_t1"], writes=["c_t2"])
                    S.op("dve", lambda e: e.scalar_tensor_tensor(out=t1[:], in0=t2[:], scalar=-TWO_PI, in1=t1[:], op0=ALU.mult, op1=ALU.add), reads=["c_t1", "c_t2"], writes=["c_t1"])
                    S.op("dve", lambda e: e.tensor_scalar(out=t2[:], in0=t1[:], scalar1=float(-np.pi), scalar2=None, op0=ALU.is_lt), reads=["c_t1"], writes=["c_t2"])
                    S.op("dve", lambda e: e.scalar_tensor_tensor(out=t1[:], in0=t2[:], scalar=TWO_PI, in1=t1[:], op0=ALU.mult, op1=ALU.add), reads=["c_t1", "c_t2"], writes=["c_t1"])
                    S.op("act", lambda e, dst=dst: e.activation(out=dst[:], in_=t1[:], func=AF.Sin), reads=["c_t1"], writes=[dkey])
                S.op("dve", lambda e: e.tensor_scalar(out=sn_[:], in0=sn_[:], scalar1=sgn[:, 0:1], scalar2=None, op0=ALU.mult), reads=["c_sin", "c_sgn"], writes=["c_sin"])
                for kc in range(7):
                    S.op("pe", lambda e, kc=kc, lt_=lt_, h=h: e.matmul(pm[0][:], Wq[:, kc, h * 256:h * 256 + 128], lt_[:, kc, :], start=(kc == 0), stop=(kc == 6)),
                         reads=["c_Wq", lk], writes=["c_pm0"])
                S.op("act", lambda e, sl=sl: e.copy(out=qn[:, sl], in_=pm[0][:]), reads=["c_pm0"], writes=["c_qn"])
                for kc in range(7):
                    S.op("pe", lambda e, kc=kc, lt_=lt_, h=h: e.matmul(pst[0][0:64, :], Wq[:, kc, h * 256 + 128:h * 256 + 192], lt_[:, kc, :], start=(kc == 0), stop=(kc == 6)),
                         reads=["c_Wq", lk], writes=["c_st0"])
                for kc in range(7):
                    S.op("pe", lambda e, kc=kc, lt_=lt_, h=h: e.matmul(pst[1][0:64, :], Wq[:, kc, h * 256 + 192:h * 256 + 256], lt_[:, kc, :], start=(kc == 0), stop=(kc == 6)),
                         reads=["c_Wq", lk], writes=["c_st1"])
                rope(qr, "c_qr", pst[0], "c_st0", pst[1], "c_st1", sl)
                if h == 0:
                    S.op("pe", lambda e, lt_=lt_: e.matmul(pst[0][0:64, :], C["idb"][0:64, 0:64], lt_[0:64, 11, :], start=True, stop=True), reads=["c_idb", lk], writes=["c_st0"])
                    S.op("pe", lambda e, lt_=lt_: e.matmul(pst[1][0:64, :], C["idb"][0:64, 0:64], lt_[0:64, 12, :], start=True, stop=True), reads=["c_idb", lk], writes=["c_st1"])
                    rope(kr, "c_kr", pst[0], "c_st0", pst[1], "c_st1", sl)
                for kc in range(4):
                    S.op("pe", lambda e, kc=kc, lt_=lt_, h=h: e.matmul(pm[0][:], Wkv[:, kc, h * 256:h * 256 + 128], lt_[:, 7 + kc, :], start=(kc == 0), stop=(kc == 3)),
                         reads=["c_Wkv", lk], writes=["c_pm0"])
                S.op("act", lambda e, sl=sl: e.copy(out=kn[:, sl], in_=pm[0][:]), reads=["c_pm0"], writes=["c_kn"])
                for sub in range(4):
                    for kc in range(4):
                        S.op("pe", lambda e, kc=kc, lt_=lt_, h=h, sub=sub: e.matmul(pm[1][:, sub * 128:(sub + 1) * 128], lt_[:, 7 + kc, sub * 128:(sub + 1) * 128],
                                                                                   Wkv[:, kc, h * 256 + 128:h * 256 + 256], start=(kc == 0), stop=(kc == 3)),
                             reads=["c_Wkv", lk], writes=["c_pm1"])
                S.op("dve", lambda e, tt=tt: e.tensor_copy(out=vt[:, tt * 4:(tt + 1) * 4, 0:128], in_=pm[1][:].rearrange("p (a b) -> p a b", a=4)), reads=["c_pm1"], writes=["c_vt"])
            steps = [(qb, kt) for qb in range(nq) for kt in range(4 * qb + 4)]

            def emit_st(i):
                qb, kt = steps[i]
                qsl = slice(qb * 512, (qb + 1) * 512)
                ksl = slice(kt * 128, (kt + 1) * 128)
                ps_ = pst[i % 2]
                psk = f"c_st{i % 2}"
                pb_ = pt[i % 2]
                pbk_ = f"c_p{i % 2}"
                S.op("pe", lambda e: e.matmul(ps_[:], kn[:, ksl], qn[:, qsl], start=True, stop=False), reads=["c_kn", "c_qn"], writes=[psk])
                S.op("pe", lambda e: e.matmul(ps_[:], kr[0:64, ksl], qr[0:64, qsl], start=False, stop=True), reads=["c_kr", "c_qr"], writes=[psk])
                S.op("act", lambda e: e.activation(out=pb_[:], in_=ps_[:], func=AF.Exp, scale=scale), reads=[psk], writes=[pbk_])
                j = kt - 4 * qb
                if j >= 0:
                    S.op("pool", lambda e: e.tensor_tensor(out=pb_[:, j * 128:(j + 1) * 128], in0=pb_[:, j * 128:(j + 1) * 128], in1=tril[:], op=ALU.mult),
                         reads=[pbk_, "c_tril"], writes=[pbk_])

            def emit_pv(i):
                qb, kt = steps[i]
                qsl = slice(qb * 512, (qb + 1) * 512)
                pb_ = pt[i % 2]
                pbk_ = f"c_p{i % 2}"
                j = kt - 4 * qb
                for ii in range(4):
                    if j > ii:
                        continue
                    last = (kt == 4 * qb + ii)
                    S.op("pe", lambda e, ii=ii, last=last: e.matmul(acc[ii][:, 0:129], pb_[:, ii * 128:(ii + 1) * 128], vt[:, kt, 0:129],
                                                                     start=(kt == 0), stop=last), reads=[pbk_, "c_vt"], writes=[f"c_acc{ii}"])
                if kt != 4 * qb + 3:
                    return
                for ii in range(4):
                    S.op("dve", lambda e, ii=ii: e.reciprocal(out=rs[:, ii:ii + 1], in_=acc[ii][:, 128:129]), reads=[f"c_acc{ii}"], writes=["c_rs"])
                    S.op("dve", lambda e, ii=ii: e.tensor_scalar(out=on[:, ii, :], in0=acc[ii][:, 0:128], scalar1=rs[:, ii:ii + 1], scalar2=None, op0=ALU.mult),
                         reads=[f"c_acc{ii}", "c_rs"], writes=["c_on"])
                pbt_b = pbt[:].bitcast(BF16).rearrange("p (a b) -> p a b", b=128)
                for ii in range(4):
                    S.op("pe", lambda e, ii=ii: e.transpose(pbt_b[:, ii, :], on[:, ii, :], C["idb"][:]), reads=["c_on", "c_idb"], writes=["c_pm1"])
                S.op("act", lambda e: e.copy(out=oT[:].rearrange("p (a b) -> p a b", a=4), in_=pbt_b[:, 0:4, :]), reads=["c_pm1"], writes=["c_oT"])
                B.finals.append(S.dma("sp", oT_ap[h * 128:(h + 1) * 128, qsl], oT[:], reads=["c_oT"], writes=["c_out"]))

            emit_st(0)
            for i in range(len(steps)):
                if i + 1 < len(steps):
                    emit_st(i + 1)
                emit_pv(i)
        S.flush(B.es)


def _launch(build, maps):
    nc = bass.Bass("TRN2", target_bir_lowering=False)
    with ExitStack() as es:
        B = Builder(nc, es)
        build(nc, B)
    res = run_bass_kernel_spmd(nc, maps, core_ids=list(range(NCORES)))
    return res.results


def _din(nc, name, shape, dt):
    return nc.dram_tensor(name, list(shape), dt, kind="ExternalInput").ap()


def _dout(nc, name, shape, dt):
    return nc.dram_tensor(name, list(shape), dt, kind="ExternalOutput").ap()


def kernel(x, c, positions, w_mod, b_mod, ln_g, ln_b, a_w_in, a_w_conv, a_a_log, a_dt_bias, a_norm_g, a_w_out,
           b_w_in, b_q_norm_g, b_w_qb, b_kv_norm_g, b_w_kvb, b_w_out):
    f32 = np.float32
    asc = np.ascontiguousarray
    x2 = asc(np.asarray(x, f32)[0])
    R = range(NCORES)
    def bM(nc, B):
        phase_mod(B, _din(nc, "c", [1, D], F32), _din(nc, "wm", [D, 3072], F32), _din(nc, "bm", [1, 3072], F32), _dout(nc, "modrow", [1, 3072], F32))
    maps = [{"c": asc(np.asarray(c, f32)), "wm": asc(np.asarray(w_mod[r // 4][:, (r % 4) * 3072:(r % 4 + 1) * 3072], f32)),
             "bm": asc(np.asarray(b_mod[r // 4][None, (r % 4) * 3072:(r % 4 + 1) * 3072], f32))} for r in R]
    res = _launch(bM, maps)
    mod = asc(np.concatenate([res[r]["modrow"].reshape(-1) for r in R]).reshape(2, 12288))
    def bA(nc, B):
        scr = {n + "s": nc.dram_tensor("scr_" + n, [4, 128, SEQ], BF16).ap() for n in "qkvz"}
        scr["gbs"] = nc.dram_tensor("scr_gb", [SEQ, 8], F32).ap()
        y = _dout(nc, "y0T", [512, SEQ], BF16)
        phase_a1(B, _din(nc, "x", [SEQ, D], F32), _din(nc, "mod", [2, 12288], F32), _din(nc, "w4", [D, 2048], F32), _din(nc, "wab", [D, 8], F32),
                 _din(nc, "cw", [4, 1536], F32), _din(nc, "alog", [1, 4], F32), _din(nc, "dtb", [1, 4], F32), scr)
        phase_a2(B, scr, _din(nc, "ng", [1, 128], F32), y)
    W = np.asarray(a_w_in[0], f32)
    cwf = np.asarray(a_w_conv[0], f32)
    maps = []
    for r in R:
        o = 512 * r
        maps.append({"x": x2, "mod": mod,
                     "w4": asc(np.concatenate([W[:, o:o + 512], W[:, 4096 + o:4096 + o + 512], W[:, 8192 + o:8192 + o + 512], W[:, 12288 + o:12288 + o + 512]], axis=1)),
                     "wab": asc(np.concatenate([W[:, 16384 + 4 * r:16384 + 4 * r + 4], W[:, 16416 + 4 * r:16416 + 4 * r + 4]], axis=1)),
                     "cw": asc(np.concatenate([cwf[:, q + o:q + o + 512] for q in (0, 4096, 8192)], axis=1)),
                     "alog": asc(np.asarray(a_a_log, f32)[:, 4 * r:4 * r + 4]), "dtb": asc(np.asarray(a_dt_bias, f32)[:, 4 * r:4 * r + 4]),
                     "ng": asc(np.asarray(a_norm_g, f32))})
    res = _launch(bA, maps)
    Y0 = np.concatenate([res[r]["y0T"] for r in R], axis=0)
    def bB(nc, B):
        x1 = _dout(nc, "x1", [TPC, D], F32)
        modt = _din(nc, "mod", [2, 12288], F32)
        phase_out2(B, _din(nc, "yT", [D, TPC], BF16), None, _din(nc, "w", [D, D], F32), _din(nc, "xr", [TPC, D], F32), modt, 0,
                   _din(nc, "lg", [1, D], F32), _din(nc, "lb", [1, D], F32), x1, nc.dram_tensor("rscr", [TPC, D], F32).ap())
        phase_b2(B, x1, modt, _din(nc, "win", [D, 5568], F32), _din(nc, "qg", [1, 896], F32), _din(nc, "kvg", [1, 512], F32),
                 _dout(nc, "latT", [13 * 128, TPC], BF16), _dout(nc, "zsT", [D, TPC], BF16))
    maps = [{"yT": asc(Y0[:, TPC * r:TPC * (r + 1)]), "w": asc(np.asarray(a_w_out[0], f32)), "xr": asc(x2[TPC * r:TPC * (r + 1)]), "mod": mod,
             "lg": asc(np.asarray(ln_g, f32)[0:1]), "lb": asc(np.asarray(ln_b, f32)[0:1]), "win": asc(np.asarray(b_w_in[0], f32)),
             "qg": asc(np.asarray(b_q_norm_g, f32)), "kvg": asc(np.asarray(b_kv_norm_g, f32))} for r in R]
    res = _launch(bB, maps)
    x1s = [res[r]["x1"] for r in R]
    zss = [res[r]["zsT"] for r in R]
    lat = asc(np.concatenate([res[r]["latT"] for r in R], axis=1))
    def bC(nc, B):
        phase_c(B, _din(nc, "lat", [13 * 128, SEQ], BF16), _din(nc, "wq", [896, 1024], F32), _din(nc, "wkv", [512, 1024], F32),
                _din(nc, "pos", [1, SEQ], I32), _din(nc, "invf", [64, 1], F32), _din(nc, "sgn", [64, 1], F32), _dout(nc, "o1T", [512, SEQ], BF16))
    half = np.arange(32, dtype=np.float32) / 32.0
    invf = (10000.0 ** (-half)).astype(f32)
    invf = asc(np.concatenate([invf, invf])[:, None])
    sgn = asc(np.concatenate([-np.ones(32, f32), np.ones(32, f32)])[:, None])
    wqb = np.asarray(b_w_qb[0], f32).reshape(896, 32, 192)
    wkvb = np.asarray(b_w_kvb[0], f32).reshape(512, 32, 256)
    maps = []
    for r in R:
        wq = wqb[:, 4 * r:4 * r + 4]
        wq = np.concatenate([wq, wq[:, :, 160:192], wq[:, :, 128:160]], axis=2)
        maps.append({"lat": lat, "wq": asc(wq.reshape(896, 1024)), "wkv": asc(wkvb[:, 4 * r:4 * r + 4].reshape(512, 1024)),
                     "pos": asc(np.asarray(positions, np.int32)), "invf": invf, "sgn": sgn})
    res = _launch(bC, maps)
    O1 = np.concatenate([res[r]["o1T"] for r in R], axis=0)
    def bD(nc, B):
        phase_out2(B, _din(nc, "yT", [D, TPC], BF16), _din(nc, "zT", [D, TPC], BF16), _din(nc, "w", [D, D], F32), _din(nc, "xr", [TPC, D], F32),
                   _din(nc, "mod", [2, 12288], F32), 1, _din(nc, "lg", [1, D], F32), _din(nc, "lb", [1, D], F32), _dout(nc, "out", [TPC, D], F32),
                   nc.dram_tensor("rscr", [TPC, D], F32).ap())
    maps = [{"yT": asc(O1[:, TPC * r:TPC * (r + 1)]), "zT": zss[r], "w": asc(np.asarray(b_w_out[0], f32)), "xr": x1s[r], "mod": mod,
             "lg": asc(np.asarray(ln_g, f32)[1:2]), "lb": asc(np.asarray(ln_b, f32)[1:2])} for r in R]
    res = _launch(bD, maps)
    return np.concatenate([res[r]["out"] for r in R], axis=0)[None].astype(f32)


def phase_a2p(B, scr, ng_ap, y_ap, ntiles=SEQ // 512, NL=6):
    S, nc = B.S, B.nc
    with ExitStack() as st:
        C = make_consts(B, st)
        banks = [B.ps(st, f"g_ps{i}", [128, 512], F32) for i in range(7)]
        pbt = B.ps(st, "g_pbt", [128, 8, 128], BF16)
        S.excl.update([f"g_ps{i}" for i in range(7)] + ["g_pbt"])
        bcnt = [0]

        def nb():
            i = bcnt[0] % 7
            bcnt[0] += 1
            return banks[i], f"g_ps{i}"

        def v3(t):
            return t[:].rearrange("p (a b) -> p a b", a=4)

        def T3(name, dt=F32):
            return B.sb(st, name, [128, 4, 128], dt)

        triu = B.sb(st, "g_triu", [128, 128], F32)
        su4, nm4, i4 = T3("g_su4"), T3("g_nm4"), T3("g_i4")
        ng = B.sb(st, "g_ng", [128, 1], F32)
        S.op("pool", lambda e: e.memset(triu[:], 1.0), writes=["g_triu"])
        S.op("pool", lambda e: e.affine_select(out=triu[:], in_=triu[:], pattern=[[1, 128]], compare_op=ALU.is_ge, fill=0.0,
                                               base=0, channel_multiplier=-1), reads=["g_triu"], writes=["g_triu"])
        S.op("pool", lambda e: e.memset(su4[:], 1.0), writes=["g_su4"])
        S.op("pool", lambda e: e.affine_select(out=su4[:], in_=su4[:], pattern=[[0, 4], [1, 128]], compare_op=ALU.is_ge, fill=0.0,
                                               base=-1, channel_multiplier=-1), reads=["g_su4"], writes=["g_su4"])
        S.op("pool", lambda e: e.memset(nm4[:], 0.0), writes=["g_nm4"])
        S.op("pool", lambda e: e.affine_select(out=nm4[:], in_=nm4[:], pattern=[[0, 4], [1, 128]], compare_op=ALU.is_ge, fill=NEG,
                                               base=0, channel_multiplier=-1), reads=["g_nm4"], writes=["g_nm4"])
        S.op("pool", lambda e: e.memset(i4[:], 0.0), writes=["g_i4"])
        S.op("pool", lambda e: e.affine_select(out=i4[:], in_=i4[:], pattern=[[0, 4], [-1, 128]], compare_op=ALU.not_equal, fill=1.0,
                                               base=0, channel_multiplier=1), reads=["g_i4"], writes=["g_i4"])
        S.dma("sp", ng[:], ng_ap.rearrange("o p -> p o"), writes=["g_ng"])
        S.op("dve", lambda e: e.tensor_scalar(out=ng[:], in0=ng[:], scalar1=float(128.0 ** 0.5), scalar2=None, op0=ALU.mult),
             reads=["g_ng"], writes=["g_ng"])
        S32 = T3("g_S32")
        Sbf = T3("g_Sbf", BF16)
        S.op("pool", lambda e: e.memset(S32[:], 0.0), writes=["g_S32"])
        S.op("pool", lambda e: e.memset(Sbf[:], 0.0), writes=["g_Sbf"])
        inb = {n: [B.sb(st, f"g_in{n}{i}", [128, 4, 512], BF16) for i in range(2)] for n in "qkvz"}
        gbb = [B.sb(st, f"g_gb{i}", [128, 4, 8], F32) for i in range(2)]
        ybuf = [B.sb(st, f"g_y{i}", [128, 4, 512], BF16) for i in range(2)]
        P_ = []
        for p in range(2):
            d = dict(p=p)
            d["kvtm"] = B.sb(st, f"g_kvtm{p}", [128, 8, 128], BF16)
            d["gc"] = B.sb(st, f"g_gc{p}", [128, 2, 4], F32)
            d["ngc"] = B.sb(st, f"g_ngc{p}", [128, 4], F32)
            d["ed"] = B.sb(st, f"g_ed{p}", [128, 4], F32)
            for n in ("gm", "E", "GM", "DT", "DTS", "U", "UT", "W0", "W1", "WT0", "WT1", "P0", "P1"):
                d[n] = T3(f"g_{n}{p}")
            for n in ("TpT", "qkT", "kgT", "qdT", "kdec"):
                d[n] = T3(f"g_{n}{p}", BF16)
            P_.append(d)
        R, vnew, sq = T3("g_R", BF16), T3("g_vnew", BF16), T3("g_sq", BF16)
        rn, yt = T3("g_rn"), T3("g_yt")

        def bc(ap2):
            return ap2.unsqueeze(2).broadcast_to([128, 4, 128])

        def K(d, n):
            return f"g_{n}{d['p']}"

        def pre_a(d, T, c, qT, kT, vT, gb, kq, kk, kv, kgb):
            cs = slice(c * 128, (c + 1) * 128)
            kvtm, gc, ngc, ed, gm, E, GM, DT, DTS, U, UT = (d[n] for n in ("kvtm", "gc", "ngc", "ed", "gm", "E", "GM", "DT", "DTS", "U", "UT"))
            for h in range(4):
                S.op("pe", lambda e, h=h: e.transpose(pbt[:, h, :], kT[:, h, cs], C["idb"][:]), reads=[kk, "c_idb"], writes=["g_pbt"])
            for h in range(4):
                S.op("pe", lambda e, h=h: e.transpose(pbt[:, 4 + h, :], vT[:, h, cs], C["idb"][:]), reads=[kv, "c_idb"], writes=["g_pbt"])
            S.op("act", lambda e: e.copy(out=kvtm[:], in_=pbt[:]), reads=["g_pbt"], writes=[K(d, "kvtm")])
            pa, pak = nb()
            S.op("pe", lambda e: e.matmul(pa[:, 0:4], triu[:], gb[:, c, 0:4], start=True, stop=True), reads=["g_triu", kgb], writes=[pak])
            S.op("pe", lambda e: e.matmul(pa[:, 4:8], C["onesf"][:], gb[:, c, 0:4], start=True, stop=True), reads=["c_onesf", kgb], writes=[pak])
            S.op("dve", lambda e: e.tensor_copy(out=gc[:].rearrange("p a b -> p (a b)"), in_=pa[:, 0:8]), reads=[pak], writes=[K(d, "gc")])
            S.op("dve", lambda e: e.tensor_scalar(out=ngc[:], in0=gc[:, 0, :], scalar1=-1.0, scalar2=None, op0=ALU.mult), reads=[K(d, "gc")], writes=[K(d, "ngc")])
            S.op("dve", lambda e: e.tensor_tensor(out=ed[:], in0=gc[:, 1, :], in1=gc[:, 0, :], op=ALU.subtract), reads=[K(d, "gc")], writes=[K(d, "ed")])
            S.op("act", lambda e: e.activation(out=ed[:], in_=ed[:], func=AF.Exp), reads=[K(d, "ed")], writes=[K(d, "ed")])
            S.op("pool", lambda e: e.tensor_tensor(out=gm[:], in0=triu[:].unsqueeze(1).broadcast_to([128, 4, 128]),
                                                   in1=bc(gb[:, c, 0:4]), op=ALU.mult), reads=["g_triu", kgb], writes=[K(d, "gm")])
            pg, pgk = nb()
            S.op("pe", lambda e: e.matmul(pg[:], C["onesf"][:], fl(gm[:]), start=True, stop=True), reads=["c_onesf", K(d, "gm")], writes=[pgk])
            S.op("act", lambda e: e.activation(out=fl(E[:]), in_=pg[:], func=AF.Exp), reads=[pgk], writes=[K(d, "E")])
            S.op("dve", lambda e: e.tensor_tensor(out=fl(GM[:]), in0=pg[:], in1=fl(nm4[:]), op=ALU.add), reads=[pgk, "g_nm4"], writes=[K(d, "GM")])
            for h in range(4):
                S.op("act", lambda e, h=h: e.activation(out=DT[:, h, :], in_=GM[:, h, :], func=AF.Exp, bias=ngc[:, h:h + 1], scale=1.0),
                     reads=[K(d, "GM"), K(d, "ngc")], writes=[K(d, "DT")])
            S.op("pool", lambda e: e.tensor_tensor(out=DTS[:], in0=DT[:], in1=su4[:], op=ALU.mult), reads=[K(d, "DT"), "g_su4"], writes=[K(d, "DTS")])
            S.op("pool", lambda e: e.tensor_tensor(out=DTS[:], in0=DTS[:], in1=bc(gb[:, c, 4:8]), op=ALU.mult), reads=[K(d, "DTS"), kgb], writes=[K(d, "DTS")])
            pk, pkk = nb()
            for h in range(4):
                S.op("pe", lambda e, h=h: e.matmul(v3(pk)[:, h, :], kT[:, h, cs], kT[:, h, cs], start=True, stop=True), reads=[kk], writes=[pkk])
            pq, pqk = nb()
            for h in range(4):
                S.op("pe", lambda e, h=h: e.matmul(v3(pq)[:, h, :], kT[:, h, cs], qT[:, h, cs], start=True, stop=True), reads=[kk, kq], writes=[pqk])
            S.op("dve", lambda e: e.tensor_tensor(out=fl(U[:]), in0=pk[:], in1=fl(DTS[:]), op=ALU.mult), reads=[pkk, K(d, "DTS")], writes=[K(d, "U")])
            S.op("dve", lambda e: e.tensor_tensor(out=fl(d["qkT"][:]), in0=pq[:], in1=fl(DT[:]), op=ALU.mult), reads=[pqk, K(d, "DT")], writes=[K(d, "qkT")])
            pu, puk = nb()
            for h in range(4):
                S.op("pe", lambda e, h=h: e.transpose(v3(pu)[:, h, :], U[:, h, :], C["idf"][:]), reads=[K(d, "U"), "c_idf"], writes=[puk])
            S.op("act", lambda e: e.copy(out=fl(UT[:]), in_=pu[:]), reads=[puk], writes=[K(d, "UT")])
            S.op("pool", lambda e: e.tensor_tensor(out=d["P0"][:], in0=i4[:], in1=U[:], op=ALU.subtract), reads=["g_i4", K(d, "U")], writes=[K(d, "P0")])
            d["cur"] = (U, UT, d["P0"], K(d, "U"), K(d, "UT"), K(d, "P0"))
            S.op("pool", lambda e: e.tensor_tensor(out=d["kgT"][:], in0=kT[:, :, cs], in1=E[:], op=ALU.mult), reads=[kk, K(d, "E")], writes=[K(d, "kgT")])
            S.op("pool", lambda e: e.tensor_tensor(out=d["qdT"][:], in0=qT[:, :, cs], in1=E[:], op=ALU.mult), reads=[kq, K(d, "E")], writes=[K(d, "qdT")])
            S.op("pool", lambda e: e.tensor_tensor(out=d["kdec"][:], in0=kvtm[:, 0:4, :], in1=bc(ed[:]), op=ALU.mult), reads=[K(d, "kvtm"), K(d, "ed")], writes=[K(d, "kdec")])

        def level(d, l):
            W, WT, P, Wk, WTk, Pk = d["cur"]
            Wn, WTn, Pn = d[f"W{l % 2}"], d[f"WT{l % 2}"], d[f"P{l % 2}"]
            Wnk, WTnk, Pnk = K(d, f"W{l % 2}"), K(d, f"WT{l % 2}"), K(d, f"P{l % 2}")
            if l < NL:
                pw, pwk = nb()
                for h in range(4):
                    S.op("pe", lambda e, h=h: e.matmul(v3(pw)[:, h, :], WT[:, h, :], W[:, h, :], start=True, stop=True), reads=[Wk, WTk], writes=[pwk])
            pwt, pwtk = nb()
            for h in range(4):
                S.op("pe", lambda e, h=h: e.matmul(v3(pwt)[:, h, :], W[:, h, :], WT[:, h, :], start=True, stop=True), reads=[Wk, WTk], writes=[pwtk])
            if l < NL:
                S.op("act", lambda e: e.copy(out=fl(Wn[:]), in_=pw[:]), reads=[pwk], writes=[Wnk])
            S.op("dve", lambda e: e.tensor_copy(out=fl(WTn[:]), in_=pwt[:]), reads=[pwtk], writes=[WTnk])
            pp, ppk = nb()
            for h in range(4):
                S.op("pe", lambda e, h=h: e.matmul(v3(pp)[:, h, :], WTn[:, h, :], P[:, h, :], start=True, stop=True), reads=[WTnk, Pk], writes=[ppk])
            if l < NL:
                S.op("dve", lambda e: e.tensor_tensor(out=fl(Pn[:]), in0=pp[:], in1=fl(P[:]), op=ALU.add), reads=[ppk, Pk], writes=[Pnk])
            else:
                S.op("dve", lambda e: e.tensor_tensor(out=fl(d["TpT"][:]), in0=pp[:], in1=fl(P[:]), op=ALU.add), reads=[ppk, Pk], writes=[K(d, "TpT")])
            d["cur"] = (Wn, WTn, Pn, Wnk, WTnk, Pnk)

        def seq(d, T, c, zs, gb, kz, kgb, bi):
            cs = slice(c * 128, (c + 1) * 128)
            kvtm, E = d["kvtm"], d["E"]
            p1, p1k = nb()
            for h in range(4):
                S.op("pe", lambda e, h=h: e.matmul(v3(p1)[:, h, :], d["kgT"][:, h, :], Sbf[:, h, :], start=True, stop=True), reads=[K(d, "kgT"), "g_Sbf"], writes=[p1k])
            S.op("dve", lambda e: e.tensor_tensor(out=R[:], in0=kvtm[:, 4:8, :], in1=v3(p1), op=ALU.subtract), reads=[K(d, "kvtm"), p1k], writes=["g_R"])
            p2, p2k = nb()
            for h in range(4):
                S.op("pe", lambda e, h=h: e.matmul(v3(p2)[:, h, :], d["TpT"][:, h, :], R[:, h, :], start=True, stop=True), reads=[K(d, "TpT"), "g_R"], writes=[p2k])
            S.op("dve", lambda e: e.tensor_tensor(out=vnew[:], in0=v3(p2), in1=bc(gb[:, c, 4:8]), op=ALU.mult), reads=[p2k, kgb], writes=["g_vnew"])
            po, pok = nb()
            for h in range(4):
                S.op("pe", lambda e, h=h: e.matmul(v3(po)[:, h, :], Sbf[:, h, :], d["qdT"][:, h, :], start=True, stop=False), reads=["g_Sbf", K(d, "qdT")], writes=[pok])
                S.op("pe", lambda e, h=h: e.matmul(v3(po)[:, h, :], vnew[:, h, :], d["qkT"][:, h, :], start=False, stop=True), reads=["g_vnew", K(d, "qkT")], writes=[pok])
            p3, p3k = nb()
            for h in range(4):
                S.op("pe", lambda e, h=h: e.matmul(v3(p3)[:, h, :], d["kdec"][:, h, :], vnew[:, h, :], start=True, stop=True), reads=[K(d, "kdec"), "g_vnew"], writes=[p3k])
            for h in range(4):
                S.op("dve", lambda e, h=h: e.scalar_tensor_tensor(out=S32[:, h, :], in0=S32[:, h, :], scalar=E[:, h, 127:128], in1=v3(p3)[:, h, :],
                                                                  op0=ALU.mult, op1=ALU.add), reads=["g_S32", K(d, "E"), p3k], writes=["g_S32"])
            S.op("act", lambda e: e.copy(out=Sbf[:], in_=S32[:]), reads=["g_S32"], writes=["g_Sbf"])
            S.op("act", lambda e: e.activation(out=fl(sq[:]), in_=po[:], func=AF.Square), reads=[pok], writes=["g_sq"])
            pss, pssk = nb()
            S.op("pe", lambda e: e.matmul(pss[:], C["onesb"][:], fl(sq[:]), start=True, stop=True), reads=["c_onesb", "g_sq"], writes=[pssk])
            S.op("act", lambda e: e.activation(out=fl(rn[:]), in_=pss[:], func=AF.Sqrt, bias=128.0 * RMS_EPS, scale=1.0), reads=[pssk], writes=["g_rn"])
            S.op("dve", lambda e: e.reciprocal(out=rn[:], in_=rn[:]), reads=["g_rn"], writes=["g_rn"])
            S.op("dve", lambda e: e.scalar_tensor_tensor(out=fl(yt[:]), in0=po[:], scalar=ng[:, 0:1], in1=fl(rn[:]), op0=ALU.mult, op1=ALU.mult),
                 reads=[pok, "g_ng", "g_rn"], writes=["g_yt"])
            S.op("pool", lambda e: e.tensor_tensor(out=ybuf[bi][:, :, cs], in0=yt[:], in1=zs[:, :, cs], op=ALU.mult),
                 reads=["g_yt", kz], writes=[f"g_y{bi}"])

        for T in range(ntiles):
            bi = T % 2
            for n in "qkvz":
                S.dma("sp", inb[n][bi][:], scr[n + "s"].rearrange("h d t -> d h t")[:, :, T * 512:(T + 1) * 512],
                      reads=[f"scr_{n}"], writes=[f"g_in{n}{bi}"])
            S.dma("sp", gbb[bi][:], scr["gbs"].rearrange("(n p) e -> p n e", p=128)[:, T * 4:(T + 1) * 4, :],
                  reads=["scr_gb"], writes=[f"g_gb{bi}"])
            qT, kT, vT, zs, gb = inb["q"][bi], inb["k"][bi], inb["v"][bi], inb["z"][bi], gbb[bi]
            kq, kk, kv, kz, kgb = (f"g_inq{bi}", f"g_ink{bi}", f"g_inv{bi}", f"g_inz{bi}", f"g_gb{bi}")
            for c0 in (0, 2):
                for p in range(2):
                    pre_a(P_[p], T, c0 + p, qT, kT, vT, gb, kq, kk, kv, kgb)
                for l in range(1, NL + 1):
                    for p in range(2):
                        level(P_[p], l)
                for p in range(2):
                    seq(P_[p], T, c0 + p, zs, gb, kz, kgb, bi)
            B.finals.append(S.dma(STQ, y_ap.rearrange("(h d) t -> d h t", d=128)[:, :, T * 512:(T + 1) * 512], ybuf[bi][:],
                                  reads=[f"g_y{bi}"], writes=["y_out"]))
        S.flush(B.es)
```

```python
import numpy as np
from contextlib import ExitStack
import concourse.bass as bass
import concourse.mybir as mybir
from concourse.bass_utils import run_bass_kernel_spmd

F32 = mybir.dt.float32
BF16 = mybir.dt.bfloat16
I32 = mybir.dt.int32
AF = mybir.ActivationFunctionType
ALU = mybir.AluOpType
AX = mybir.AxisListType

NCORES = 8
SEQ = 8192
D = 4096
TPC = SEQ // NCORES
ALPHA = (2.0 * 2) ** 0.25
RMS_EPS = 1e-6
LN_EPS = 1e-5
NEG = -30000.0

ENGS = ("pe", "act", "dve", "pool", "sp")
SEM_CHUNK = 4000
STQ = "sp"
HT_ACT = False


class Sched:
    def __init__(self, nc):
        self.nc = nc
        self.ops = {e: [] for e in ENGS}
        self.writers = {}
        self.readers = {}
        self.dmacnt = {}
        self.nops = 0
        self.excl = set()

    def _collect(self, eng, reads, writes):
        deps = []
        for k in reads:
            for t in self.writers.get(k, ()):
                deps.append((t, "raw"))
            if k in self.excl:
                for t in self.readers.get(k, ()):
                    deps.append((t, "war"))
        for k in writes:
            for t in self.writers.get(k, ()):
                deps.append((t, "waw"))
            for t in self.readers.get(k, ()):
                deps.append((t, "war"))
        return deps

    def _commit(self, tok, reads, writes, partial):
        for k in writes:
            if partial:
                self.writers.setdefault(k, []).append(tok)
            else:
                self.writers[k] = [tok]
            self.readers[k] = []
        for k in reads:
            self.readers.setdefault(k, []).append(tok)

    def op(self, eng, fn, reads=(), writes=(), partial=False):
        deps = self._collect(eng, reads, writes)
        tok = ("E", eng, self.nops)
        self.ops[eng].append(dict(fn=fn, deps=deps, tok=tok, dma=None))
        self._commit(tok, reads, writes, partial)
        self.nops += 1
        return tok

    def dma(self, eng, out, in_, reads=(), writes=(), sem=None, partial=False, **kw):
        semname = sem or (writes[0] if writes else reads[0])
        deps = self._collect(eng, reads, writes)
        self.dmacnt[semname] = self.dmacnt.get(semname, 0) + 1
        tok = ("D", semname, self.dmacnt[semname] * 16)
        fn = lambda e, out=out, in_=in_, kw=kw: e.dma_start(out=out, in_=in_, **kw)
        self.ops[eng].append(dict(fn=fn, deps=deps, tok=tok, dma=semname))
        self._commit(tok, reads, writes, partial)
        self.nops += 1
        return tok

    def coll(self, kind, src, dst, reads, writes, name, inc=1):
        deps = self._collect("pool", reads, writes)
        self.dmacnt[name] = self.dmacnt.get(name, 0)
        self.collinc = getattr(self, "collinc", {})
        self.collinc[name] = self.collinc.get(name, 0) + inc
        tok = ("C", name, self.collinc[name])
        fn = lambda e: e.collective_compute(kind, ALU.bypass, replica_groups=[list(range(NCORES))], ins=[src], outs=[dst])
        self.ops["pool"].append(dict(fn=fn, deps=deps, tok=tok, dma=name, inc=inc))
        self._commit(tok, reads, writes, False)
        self.nops += 1
        return tok

    def flush(self, es, final_waits=(), barrier=True):
        nc = self.nc
        if not hasattr(self, "sems"):
            self.sems = {}
            self.nflag = {e: 0 for e in ENGS}
        sems = self.sems
        flagged = set()
        for e in ENGS:
            for o in self.ops[e]:
                keep = []
                for (t, kind) in o["deps"]:
                    if t[0] == "E" and t[1] == e and o["dma"] is None:
                        if e == "pe" or kind == "war":
                            continue
                    keep.append(t)
                    if t[0] == "E":
                        flagged.add(t)
                o["deps"] = keep
        lasts = []
        if barrier:
            for e in ENGS:
                for o in reversed(self.ops[e]):
                    if o["dma"] is None:
                        flagged.add(o["tok"])
                        lasts.append(o["tok"])
                        break
        for t in final_waits:
            if t[0] == "E":
                flagged.add(t)
        tokval = {}

        def getsem(key):
            if key not in sems:
                sems[key] = es.enter_context(nc.semaphore("s_" + "_".join(str(x) for x in key)))
            return sems[key]

        for e in ENGS:
            for o in self.ops[e]:
                if o["tok"] in flagged:
                    n = self.nflag[e]
                    tokval[o["tok"]] = ((e, n // SEM_CHUNK), n % SEM_CHUNK + 1)
                    self.nflag[e] = n + 1
                    getsem((e, n // SEM_CHUNK))
        for name in self.dmacnt:
            getsem(("D", name))

        def resolve(t):
            if t[0] == "E":
                return tokval[t]
            return (("D", t[1]), t[2])

        collinc = getattr(self, "collinc", {})

        endw = [resolve(t) for t in list(final_waits) + lasts]
        if barrier:
            endw += [(("D", name), c * 16 + collinc.get(name, 0)) for name, c in self.dmacnt.items()]
        ops = self.ops
        with nc.Block() as block:
            engobj = {"pe": block.tensor, "act": block.scalar, "dve": block.vector, "pool": block.gpsimd,
                      "sp": block.sync}

            def make(e):
                def body(eng):
                    waited = {}
                    for o in ops[e]:
                        need = {}
                        for t in o["deps"]:
                            s, v = resolve(t)
                            if waited.get(s, 0) >= v:
                                continue
                            need[s] = max(need.get(s, 0), v)
                        for s, v in need.items():
                            eng.wait_ge(sems[s], v)
                            waited[s] = v
                        ins = o["fn"](eng)
                        if o["dma"] is not None:
                            ins.then_inc(sems[("D", o["dma"])], o.get("inc", 16))
                        elif o["tok"] in tokval:
                            ins.then_inc(sems[tokval[o["tok"]][0]], 1)
                    for s, v in endw:
                        if waited.get(s, 0) < v:
                            eng.wait_ge(sems[s], v)
                            waited[s] = v
                return body

            for e in ENGS:
                engobj[e](make(e))
        self.ops = {e: [] for e in ENGS}
        self.writers = {}
        self.readers = {}


class Builder:
    def __init__(self, nc, es):
        self.nc = nc
        self.es = es
        self.S = Sched(nc)
        self.finals = []

    def sb(self, st, name, shape, dt):
        self.uid = getattr(self, "uid", 0) + 1
        return st.enter_context(self.nc.sbuf_tensor(f"{name}_u{self.uid}", list(shape), dt))

    def ps(self, st, name, shape, dt=F32):
        self.uid = getattr(self, "uid", 0) + 1
        return st.enter_context(self.nc.psum_tensor(f"{name}_u{self.uid}", list(shape), dt))


def make_consts(B, st):
    S = B.S
    C = {}
    C["idf"] = B.sb(st, "c_idf", [128, 128], F32)
    C["idb"] = B.sb(st, "c_idb", [128, 128], BF16)
    C["onesf"] = B.sb(st, "c_onesf", [128, 128], F32)
    C["onesb"] = B.sb(st, "c_onesb", [128, 128], BF16)
    S.op("pool", lambda e: e.memset(C["idf"][:], 0.0), writes=["c_idf"])
    S.op("pool", lambda e: e.affine_select(out=C["idf"][:], in_=C["idf"][:], pattern=[[-1, 128]],
                                           compare_op=ALU.not_equal, fill=1.0, base=0, channel_multiplier=1),
         reads=["c_idf"], writes=["c_idf"])
    S.op("pool", lambda e: e.tensor_copy(out=C["idb"][:], in_=C["idf"][:]), reads=["c_idf"], writes=["c_idb"])
    S.op("pool", lambda e: e.memset(C["onesf"][:], 1.0), writes=["c_onesf"])
    S.op("pool", lambda e: e.memset(C["onesb"][:], 1.0), writes=["c_onesb"])
    return C


def phase_mod(B, c_ap, wm_ap, bm_ap, out_ap):
    S, nc = B.S, B.nc
    with ExitStack() as st:
        C = make_consts(B, st)
        ct = B.sb(st, "m_ct", [32, 128], F32)
        cact = B.sb(st, "m_cact", [128, 32], F32)
        bmt = B.sb(st, "m_bm", [1, 3072], F32)
        rowt = B.sb(st, "m_row", [1, 3072], F32)
        wt = [B.sb(st, f"m_w{i}", [128, 3072], F32) for i in range(3)]
        pT = B.ps(st, "m_pT", [128, 512], F32)
        pr = [B.ps(st, f"m_pr{i}", [128, 512], F32) for i in range(6)]
        S.dma("sp", ct[:], c_ap.rearrange("o (kc p) -> (o kc) p", p=128), writes=["m_ct"])
        S.dma("sp", bmt[:], bm_ap, writes=["m_bm"])
        S.op("pe", lambda e: e.transpose(pT[:, 0:32], ct[:], C["idf"][0:32, 0:32]), reads=["m_ct", "c_idf"], writes=["m_pT"])
        S.op("act", lambda e: e.activation(out=cact[:], in_=pT[:, 0:32], func=AF.Silu), reads=["m_pT"], writes=["m_cact"])
        for kc in range(32):
            w = wt[kc % 3]
            S.dma("sp" if kc % 2 == 0 else "act", w[:], wm_ap[kc * 128:(kc + 1) * 128, :], writes=[f"m_w{kc % 3}"])
            for nb in range(6):
                S.op("pe", lambda e, w=w, nb=nb, kc=kc: e.matmul(pr[nb][0:1, :], cact[:, kc:kc + 1], w[:, nb * 512:(nb + 1) * 512],
                                                                 start=(kc == 0), stop=(kc == 31)),
                     reads=[f"m_w{kc % 3}", "m_cact"], writes=[f"m_pr{nb}"])
        for nb in range(6):
            S.op("dve", lambda e, nb=nb: e.tensor_tensor(out=rowt[0:1, nb * 512:(nb + 1) * 512], in0=pr[nb][0:1, :],
                                                         in1=bmt[0:1, nb * 512:(nb + 1) * 512], op=ALU.add),
                 reads=[f"m_pr{nb}", "m_bm"], writes=["m_row"])
        t = S.dma("sp", out_ap, rowt[:], reads=["m_row"], writes=["m_out"])
        S.flush(B.es, final_waits=[t])


def load_mod_fm(B, st, C, mod_ap, layer, pt, ptkey):
    S = B.S
    rows = B.sb(st, f"md_rows{layer}", [64, 128], F32)
    sh = B.sb(st, f"md_sh{layer}", [128, 32], F32)
    sc = B.sb(st, f"md_sc{layer}", [128, 32], F32)
    S.dma("sp", rows[:], mod_ap[layer:layer + 1, 0:8192].rearrange("o (kc p) -> (o kc) p", p=128), writes=[f"md_rows{layer}"])
    S.op("pe", lambda e: e.transpose(pt[:, 0:64], rows[:], C["idf"][0:64, 0:64]), reads=[f"md_rows{layer}", "c_idf"], writes=[ptkey])
    S.op("dve", lambda e: e.tensor_copy(out=sh[:], in_=pt[:, 0:32]), reads=[ptkey], writes=[f"md_sh{layer}"])
    S.op("dve", lambda e: e.tensor_scalar(out=sc[:], in0=pt[:, 32:64], scalar1=1.0, scalar2=None, op0=ALU.add),
         reads=[ptkey], writes=[f"md_sc{layer}"])
    return sh, sc


def make_hT(B, C, xs, xkey, hT, hkey, st_idx, sh, sc, shk, sck, ptr, ptk, cnt):
    S = B.S
    for g in range(8):
        pt = ptr[cnt[0] % len(ptr)]
        pk = ptk[cnt[0] % len(ptr)]
        cnt[0] += 1
        for j in range(4):
            kc = g * 4 + j
            S.op("pe", lambda e, pt=pt, j=j, kc=kc: e.transpose(pt[:, j, :], xs[:, kc * 128:(kc + 1) * 128], C["idf"][:]),
                 reads=[xkey, "c_idf"], writes=[pk])
        for j in range(4):
            kc = g * 4 + j
            dst = hT[:, kc, st_idx * 128:(st_idx + 1) * 128]
            if HT_ACT and kc % 2 == 0:
                S.op("act", lambda e, pt=pt, j=j, kc=kc, dst=dst: e.activation(out=dst, in_=pt[:, j, :], func=AF.Identity,
                                                                               bias=sh[:, kc:kc + 1], scale=sc[:, kc:kc + 1]),
                     reads=[pk, shk, sck], writes=[f"{hkey}{kc}"])
            else:
                S.op("dve", lambda e, pt=pt, j=j, kc=kc, dst=dst: e.scalar_tensor_tensor(out=dst, in0=pt[:, j, :], scalar=sc[:, kc:kc + 1],
                                                                                         in1=sh[:, kc:kc + 1].broadcast_to([128, 128]),
                                                                                         op0=ALU.mult, op1=ALU.add),
                     reads=[pk, shk, sck], writes=[f"{hkey}{kc}"])


def phase_a1(B, x_ap, mod_ap, w4_ap, wab_ap, cw_ap, alog_ap, dtb_ap, scr, ntiles=SEQ // 512, stage=9):
    S, nc = B.S, B.nc
    with ExitStack() as st:
        C = make_consts(B, st)
        ptr = [B.ps(st, f"a_pt{i}", [128, 4, 128], F32) for i in range(2)]
        ptk = [f"a_pt{i}" for i in range(2)]
        psp = [B.ps(st, f"a_pp{i}", [128, 512], F32) for i in range(4)]
        pgb = B.ps(st, "a_pgb", [128, 512], F32)
        pss = B.ps(st, "a_pss", [128, 512], F32)
        S.excl.update(["a_pt0", "a_pt1", "a_pp0", "a_pp1", "a_pp2", "a_pp3", "a_pgb", "a_pss"])
        sh, sc = load_mod_fm(B, st, C, mod_ap, 0, pgb, "a_pgb")
        xs = [B.sb(st, f"a_xs{i}", [128, D], F32) for i in range(2)]
        hT = B.sb(st, "a_hT", [128, 32, 512], BF16)
        Wt = [B.sb(st, f"a_W{i}", [128, 16, 512], BF16) for i in range(3)]
        pc = B.sb(st, "a_pc", [128, 12, 515], F32)
        tmp = [B.sb(st, f"a_tmp{i}", [128, 512], F32) for i in range(2)]
        sl = [B.sb(st, f"a_sl{i}", [128, 512], F32) for i in range(4)]
        sqb4 = [B.sb(st, f"a_sqb{i}", [128, 512], BF16) for i in range(4)]
        rn = B.sb(st, "a_rn", [128, 512], F32)
        outs = {n: B.sb(st, f"a_o{n}", [128, 4, 512], BF16) for n in "qkvz"}
        wabf = B.sb(st, "a_wabf", [128, 32, 8], F32)
        wab = B.sb(st, "a_wab", [128, 32, 8], BF16)
        cwr = B.sb(st, "a_cwr", [48, 128], F32)
        cwT = B.sb(st, "a_cwT", [128, 48], F32)
        alb = B.sb(st, "a_alb", [128, 4], F32)
        negA = B.sb(st, "a_negA", [128, 4], F32)
        dtb = B.sb(st, "a_dtb", [128, 4], F32)
        gbt = B.sb(st, "a_gbt", [128, 4, 8], F32)
        t4 = [B.sb(st, f"a_t4{i}", [128, 4], F32) for i in range(2)]
        S.dma("sp", wabf[:], wab_ap.rearrange("(kc p) n -> p kc n", p=128), writes=["a_wabf"])
        S.op("dve", lambda e: e.tensor_copy(out=wab[:], in_=wabf[:]), reads=["a_wabf"], writes=["a_wab"])
        S.dma("sp", cwr[:], cw_ap.rearrange("j (b p) -> (j b) p", p=128), writes=["a_cwr"])
        S.op("pe", lambda e: e.transpose(pss[:, 0:48], cwr[:], C["idf"][0:48, 0:48]), reads=["a_cwr", "c_idf"], writes=["a_pss"])
        S.op("dve", lambda e: e.tensor_copy(out=cwT[:], in_=pss[:, 0:48]), reads=["a_pss"], writes=["a_cwT"])
        S.dma("sp", alb[:], alog_ap.partition_broadcast(128), writes=["a_alb"])
        S.dma("sp", dtb[:], dtb_ap.partition_broadcast(128), writes=["a_dtb"])
        S.op("act", lambda e: e.activation(out=negA[:], in_=alb[:], func=AF.Exp), reads=["a_alb"], writes=["a_negA"])
        S.op("dve", lambda e: e.tensor_scalar(out=negA[:], in0=negA[:], scalar1=-1.0, scalar2=None, op0=ALU.mult),
             reads=["a_negA"], writes=["a_negA"])
        S.op("pool", lambda e: e.memset(pc[:], 0.0), writes=[f"a_pc{b}" for b in range(12)])
        xv = x_ap.rearrange("(n p) d -> n p d", p=128)
        w4v = w4_ap.rearrange("(kc p) n -> p kc n", p=128)
        cnt = [0]
        wcnt = 0
        pend = []
        pend2 = []
        qk_scale = 128.0 ** -0.5
        for T in range(ntiles if stage > 0 else 0):
            for stx in range(4):
                n = T * 4 + stx
                xb = xs[n % 2]
                S.dma("sp", xb[:], xv[n], writes=[f"a_xs{n % 2}"])
                make_hT(B, C, xb, f"a_xs{n % 2}", hT, "a_hT", stx, sh, sc, "md_sh0", "md_sc0", ptr, ptk, cnt)
            for stx in range(4 if stage > 1 else 0):
                for kc in range(32):
                    S.op("pe", lambda e, kc=kc, stx=stx: e.matmul(pgb[:, 0:8], hT[:, kc, stx * 128:(stx + 1) * 128], wab[:, kc, :],
                                                                    start=(kc == 0), stop=(kc == 31)),
                         reads=[f"a_hT{kc}", "a_wab"], writes=["a_pgb"])
                S.op("act", lambda e: e.activation(out=t4[0][:], in_=pgb[:, 0:4], func=AF.Exp, scale=-1.0), reads=["a_pgb"], writes=["a_t40"])
                S.op("dve", lambda e: e.tensor_tensor(out=t4[1][:], in0=pgb[:, 4:8], in1=dtb[:], op=ALU.add), reads=["a_pgb", "a_dtb"], writes=["a_t41"])
                S.op("dve", lambda e: e.tensor_scalar(out=t4[0][:], in0=t4[0][:], scalar1=1.0, scalar2=None, op0=ALU.add), reads=["a_t40"], writes=["a_t40"])
                S.op("dve", lambda e, stx=stx: e.reciprocal(out=gbt[:, stx, 4:8], in_=t4[0][:]), reads=["a_t40"], writes=["a_gbt"])
                S.op("act", lambda e: e.activation(out=t4[1][:], in_=t4[1][:], func=AF.Exp), reads=["a_t41"], writes=["a_t41"])
                S.op("act", lambda e: e.activation(out=t4[1][:], in_=t4[1][:], func=AF.Ln, bias=1.0, scale=1.0), reads=["a_t41"], writes=["a_t41"])
                S.op("dve", lambda e, stx=stx: e.tensor_tensor(out=gbt[:, stx, 0:4], in0=t4[1][:], in1=negA[:], op=ALU.mult),
                     reads=["a_t41", "a_negA"], writes=["a_gbt"])
            if stage > 1:
                S.dma("sp", scr["gbs"].rearrange("(n p) e -> p n e", p=128)[:, T * 4:(T + 1) * 4, :], gbt[:], reads=["a_gbt"], writes=["scr_gb"])
            for cb in range(4 if stage > 2 else 0):
                for half in range(2):
                    slot = wcnt % 3
                    wcnt += 1
                    S.dma("pool", Wt[slot][:], w4v[:, half * 16:(half + 1) * 16, cb * 512:(cb + 1) * 512], writes=[f"a_W{slot}"])
                    for sbk in range(4):
                        for k16 in range(16):
                            kc = half * 16 + k16
                            S.op("pe", lambda e, slot=slot, sbk=sbk, k16=k16, kc=kc, half=half: e.matmul(
                                psp[sbk][:], Wt[slot][:, k16, sbk * 128:(sbk + 1) * 128], hT[:, kc, :],
                                start=(kc == 0), stop=(kc == 31)),
                                reads=[f"a_W{slot}", f"a_hT{kc}"], writes=[f"a_pp{sbk}"])
                while pend2:
                    pend2.pop(0)()
                while pend:
                    pend.pop(0)()
                for sbk in range(4):
                    b = cb * 4 + sbk
                    if cb == 3:
                        S.op("act", lambda e, sbk=sbk: e.activation(out=outs["z"][:, sbk, :], in_=psp[sbk][:], func=AF.Silu),
                             reads=[f"a_pp{sbk}"], writes=["a_oz"])
                        continue
                    ev = "act" if sbk % 2 == 0 else "dve"
                    if ev == "act":
                        S.op("act", lambda e, b=b, sbk=sbk: e.copy(out=pc[:, b, 3:515], in_=psp[sbk][:]), reads=[f"a_pp{sbk}"], writes=[f"a_pc{b}"])
                    else:
                        S.op("dve", lambda e, b=b, sbk=sbk: e.tensor_copy(out=pc[:, b, 3:515], in_=psp[sbk][:]), reads=[f"a_pp{sbk}"], writes=[f"a_pc{b}"])
                    if stage < 4:
                        continue
                    ce = "dve"
                    tb = tmp[b % 2]
                    tk = f"a_tmp{b % 2}"
                    S.op(ce, lambda e, b=b, tb=tb: e.tensor_scalar(out=tb[:], in0=pc[:, b, 3:515], scalar1=cwT[:, 3 * 12 + b:3 * 12 + b + 1],
                                                                   scalar2=None, op0=ALU.mult), reads=[f"a_pc{b}", "a_cwT"], writes=[tk])
                    for j in (2, 1, 0):
                        S.op(ce, lambda e, b=b, tb=tb, j=j: e.scalar_tensor_tensor(out=tb[:], in0=pc[:, b, j:j + 512],
                                                                                 scalar=cwT[:, j * 12 + b:j * 12 + b + 1], in1=tb[:],
                                                                                 op0=ALU.mult, op1=ALU.add),
                             reads=[f"a_pc{b}", "a_cwT", tk], writes=[tk])
                    S.op(ce, lambda e, b=b: e.tensor_copy(out=pc[:, b, 0:3], in_=pc[:, b, 512:515]), reads=[f"a_pc{b}"], writes=[f"a_pc{b}"])
                    if stage < 5:
                        continue
                    if cb == 2:
                        S.op("act", lambda e, tb=tb, sbk=sbk: e.activation(out=outs["v"][:, sbk, :], in_=tb[:], func=AF.Silu),
                             reads=[tk], writes=["a_ov"])
                        continue
                    slb = sl[sbk]
                    sk = f"a_sl{sbk}"
                    sqb = sqb4[sbk]
                    sqk = f"a_sqb{sbk}"
                    S.op("act", lambda e, tb=tb, slb=slb: e.activation(out=slb[:], in_=tb[:], func=AF.Silu), reads=[tk], writes=[sk])
                    S.op("dve", lambda e, slb=slb, sqb=sqb: e.tensor_tensor(out=sqb[:], in0=slb[:], in1=slb[:], op=ALU.mult), reads=[sk], writes=[sqk])
                    nm = "q" if cb == 0 else "k"

                    def part2(slb=slb, sk=sk, sqb=sqb, sqk=sqk, nm=nm, sbk=sbk, cb=cb):
                        S.op("pe", lambda e: e.matmul(pss[:], C["onesb"][:], sqb[:], start=True, stop=True), reads=["c_onesb", sqk], writes=["a_pss"])
                        S.op("act", lambda e: e.activation(out=rn[:], in_=pss[:], func=AF.Sqrt, bias=RMS_EPS, scale=1.0), reads=["a_pss"], writes=["a_rn"])
                        S.op("dve", lambda e: e.reciprocal(out=rn[:], in_=rn[:]), reads=["a_rn"], writes=["a_rn"])
                        S.op("dve", lambda e: e.scalar_tensor_tensor(
                            out=outs[nm][:, sbk, :], in0=slb[:], scalar=(qk_scale if cb == 0 else 1.0), in1=rn[:], op0=ALU.mult, op1=ALU.mult),
                            reads=[sk, "a_rn"], writes=[f"a_o{nm}"])
                    pend2.append(part2)
                nm = "qkvz"[cb]
                if stage < 6:
                    continue
                pend.append(lambda nm=nm, T=T: S.dma(STQ, scr[nm + "s"].rearrange("h d t -> d h t")[:, :, T * 512:(T + 1) * 512], outs[nm][:],
                                                     reads=[f"a_o{nm}"], writes=[f"scr_{nm}"]))
        while pend2:
            pend2.pop(0)()
        while pend:
            pend.pop(0)()
        S.flush(B.es)


def fl(ap):
    return ap.rearrange("p a b -> p (a b)")


def phase_a2(B, scr, ng_ap, y_ap, ntiles=SEQ // 512, NL=6, stage=9):
    S, nc = B.S, B.nc
    with ExitStack() as st:
        C = make_consts(B, st)
        banks = [B.ps(st, f"g_ps{i}", [128, 512], F32) for i in range(7)]
        pbt = B.ps(st, "g_pbt", [128, 8, 128], BF16)
        S.excl.update([f"g_ps{i}" for i in range(7)] + ["g_pbt"])
        bcnt = [0]

        def nb():
            i = bcnt[0] % 7
            bcnt[0] += 1
            return banks[i], f"g_ps{i}"

        def v3(t):
            return t[:].rearrange("p (a b) -> p a b", a=4)

        def T3(name, dt=F32):
            return B.sb(st, name, [128, 4, 128], dt)

        triu = B.sb(st, "g_triu", [128, 128], F32)
        su4, nm4, i4 = T3("g_su4"), T3("g_nm4"), T3("g_i4")
        ng = B.sb(st, "g_ng", [128, 1], F32)
        S.op("pool", lambda e: e.memset(triu[:], 1.0), writes=["g_triu"])
        S.op("pool", lambda e: e.affine_select(out=triu[:], in_=triu[:], pattern=[[1, 128]], compare_op=ALU.is_ge, fill=0.0,
                                               base=0, channel_multiplier=-1), reads=["g_triu"], writes=["g_triu"])
        S.op("pool", lambda e: e.memset(su4[:], 1.0), writes=["g_su4"])
        S.op("pool", lambda e: e.affine_select(out=su4[:], in_=su4[:], pattern=[[0, 4], [1, 128]], compare_op=ALU.is_ge, fill=0.0,
                                               base=-1, channel_multiplier=-1), reads=["g_su4"], writes=["g_su4"])
        S.op("pool", lambda e: e.memset(nm4[:], 0.0), writes=["g_nm4"])
        S.op("pool", lambda e: e.affine_select(out=nm4[:], in_=nm4[:], pattern=[[0, 4], [1, 128]], compare_op=ALU.is_ge, fill=NEG,
                                               base=0, channel_multiplier=-1), reads=["g_nm4"], writes=["g_nm4"])
        S.op("pool", lambda e: e.memset(i4[:], 0.0), writes=["g_i4"])
        S.op("pool", lambda e: e.affine_select(out=i4[:], in_=i4[:], pattern=[[0, 4], [-1, 128]], compare_op=ALU.not_equal, fill=1.0,
                                               base=0, channel_multiplier=1), reads=["g_i4"], writes=["g_i4"])
        S.dma("sp", ng[:], ng_ap.rearrange("o p -> p o"), writes=["g_ng"])
        S.op("dve", lambda e: e.tensor_scalar(out=ng[:], in0=ng[:], scalar1=float(128.0 ** 0.5), scalar2=None, op0=ALU.mult),
             reads=["g_ng"], writes=["g_ng"])
        S32 = T3("g_S32")
        Sbf = T3("g_Sbf", BF16)
        S.op("pool", lambda e: e.memset(S32[:], 0.0), writes=["g_S32"])
        S.op("pool", lambda e: e.memset(Sbf[:], 0.0), writes=["g_Sbf"])
        inb = {n: [B.sb(st, f"g_in{n}{i}", [128, 4, 512], BF16) for i in range(2)] for n in "qkvz"}
        gbb = [B.sb(st, f"g_gb{i}", [128, 4, 8], F32) for i in range(2)]
        ybuf = [B.sb(st, f"g_y{i}", [128, 4, 512], BF16) for i in range(2)]
        kvtm = B.sb(st, "g_kvtm", [128, 8, 128], BF16)
        gc = B.sb(st, "g_gc", [128, 2, 4], F32)
        ngc = B.sb(st, "g_ngc", [128, 4], F32)
        ed = B.sb(st, "g_ed", [128, 4], F32)
        gm, E, GM, DT, DTS, U, UT = T3("g_gm"), T3("g_E"), T3("g_GM"), T3("g_DT"), T3("g_DTS"), T3("g_U"), T3("g_UT")
        Wb = [T3(f"g_W{i}") for i in range(2)]
        WTb = [T3(f"g_WT{i}") for i in range(2)]
        Pb = [T3(f"g_P{i}") for i in range(2)]
        TpT, qkT, kgT, qdT, kdec, R, vnew = (T3(n, BF16) for n in ("g_TpT", "g_qkT", "g_kgT", "g_qdT", "g_kdec", "g_R", "g_vnew"))
        sq = T3("g_sq", BF16)
        rn, yt = T3("g_rn"), T3("g_yt")

        def bc(ap2):
            return ap2.unsqueeze(2).broadcast_to([128, 4, 128])

        for T in range(ntiles):
            bi = T % 2
            for n in "qkvz":
                S.dma("sp", inb[n][bi][:], scr[n + "s"].rearrange("h d t -> d h t")[:, :, T * 512:(T + 1) * 512],
                      reads=[f"scr_{n}"], writes=[f"g_in{n}{bi}"])
            S.dma("sp", gbb[bi][:], scr["gbs"].rearrange("(n p) e -> p n e", p=128)[:, T * 4:(T + 1) * 4, :],
                  reads=["scr_gb"], writes=[f"g_gb{bi}"])
            qT, kT, vT, zs, gb = inb["q"][bi], inb["k"][bi], inb["v"][bi], inb["z"][bi], gbb[bi]
            kq, kk, kv, kz, kgb = (f"g_inq{bi}", f"g_ink{bi}", f"g_inv{bi}", f"g_inz{bi}", f"g_gb{bi}")
            for c in range(4):
                cs = slice(c * 128, (c + 1) * 128)
                if stage < 1:
                    continue
                for h in range(4):
                    S.op("pe", lambda e, h=h, cs=cs, kT=kT: e.transpose(pbt[:, h, :], kT[:, h, cs], C["idb"][:]), reads=[kk, "c_idb"], writes=["g_pbt"])
                for h in range(4):
                    S.op("pe", lambda e, h=h, cs=cs, vT=vT: e.transpose(pbt[:, 4 + h, :], vT[:, h, cs], C["idb"][:]), reads=[kv, "c_idb"], writes=["g_pbt"])
                S.op("act", lambda e: e.copy(out=kvtm[:], in_=pbt[:]), reads=["g_pbt"], writes=["g_kvtm"])
                if stage < 2:
                    continue
                pa, pak = nb()
                S.op("pe", lambda e, pa=pa, gb=gb, c=c: e.matmul(pa[:, 0:4], triu[:], gb[:, c, 0:4], start=True, stop=True), reads=["g_triu", kgb], writes=[pak])
                S.op("pe", lambda e, pa=pa, gb=gb, c=c: e.matmul(pa[:, 4:8], C["onesf"][:], gb[:, c, 0:4], start=True, stop=True), reads=["c_onesf", kgb], writes=[pak])
                S.op("dve", lambda e, pa=pa: e.tensor_copy(out=gc[:].rearrange("p a b -> p (a b)"), in_=pa[:, 0:8]), reads=[pak], writes=["g_gc"])
                S.op("dve", lambda e: e.tensor_scalar(out=ngc[:], in0=gc[:, 0, :], scalar1=-1.0, scalar2=None, op0=ALU.mult), reads=["g_gc"], writes=["g_ngc"])
                S.op("dve", lambda e: e.tensor_tensor(out=ed[:], in0=gc[:, 1, :], in1=gc[:, 0, :], op=ALU.subtract), reads=["g_gc"], writes=["g_ed"])
                S.op("act", lambda e: e.activation(out=ed[:], in_=ed[:], func=AF.Exp), reads=["g_ed"], writes=["g_ed"])
                if stage < 3:
                    continue
                S.op("pool", lambda e, gb=gb, c=c: e.tensor_tensor(out=gm[:], in0=triu[:].unsqueeze(1).broadcast_to([128, 4, 128]),
                                                                    in1=bc(gb[:, c, 0:4]), op=ALU.mult), reads=["g_triu", kgb], writes=["g_gm"])
                pg, pgk = nb()
                S.op("pe", lambda e, pg=pg: e.matmul(pg[:], C["onesf"][:], fl(gm[:]), start=True, stop=True), reads=["c_onesf", "g_gm"], writes=[pgk])
                if stage < 3.2:
                    continue
                S.op("act", lambda e, pg=pg: e.activation(out=fl(E[:]), in_=pg[:], func=AF.Exp), reads=[pgk], writes=["g_E"])
                S.op("dve", lambda e, pg=pg: e.tensor_tensor(out=fl(GM[:]), in0=pg[:], in1=fl(nm4[:]), op=ALU.add), reads=[pgk, "g_nm4"], writes=["g_GM"])
                if stage < 3.3:
                    continue
                for h in range(4):
                    S.op("act", lambda e, h=h: e.activation(out=DT[:, h, :], in_=GM[:, h, :], func=AF.Exp, bias=ngc[:, h:h + 1], scale=1.0),
                         reads=["g_GM", "g_ngc"], writes=["g_DT"])
                if stage < 3.4:
                    continue
                S.op("pool", lambda e: e.tensor_tensor(out=DTS[:], in0=DT[:], in1=su4[:], op=ALU.mult), reads=["g_DT", "g_su4"], writes=["g_DTS"])
                S.op("pool", lambda e, gb=gb, c=c: e.tensor_tensor(out=DTS[:], in0=DTS[:], in1=bc(gb[:, c, 4:8]), op=ALU.mult), reads=["g_DTS", kgb], writes=["g_DTS"])
                if stage < 4:
                    continue
                pk, pkk = nb()
                for h in range(4):
                    S.op("pe", lambda e, h=h, cs=cs, kT=kT, pk=pk: e.matmul(v3(pk)[:, h, :], kT[:, h, cs], kT[:, h, cs], start=True, stop=True), reads=[kk], writes=[pkk])
                pq, pqk = nb()
                for h in range(4):
                    S.op("pe", lambda e, h=h, cs=cs, kT=kT, qT=qT, pq=pq: e.matmul(v3(pq)[:, h, :], kT[:, h, cs], qT[:, h, cs], start=True, stop=True), reads=[kk, kq], writes=[pqk])
                S.op("dve", lambda e, pk=pk: e.tensor_tensor(out=fl(U[:]), in0=pk[:], in1=fl(DTS[:]), op=ALU.mult), reads=[pkk, "g_DTS"], writes=["g_U"])
                S.op("dve", lambda e, pq=pq: e.tensor_tensor(out=fl(qkT[:]), in0=pq[:], in1=fl(DT[:]), op=ALU.mult), reads=[pqk, "g_DT"], writes=["g_qkT"])
                if stage < 5:
                    continue
                pu, puk = nb()
                for h in range(4):
                    S.op("pe", lambda e, h=h, pu=pu: e.transpose(v3(pu)[:, h, :], U[:, h, :], C["idf"][:]), reads=["g_U", "c_idf"], writes=[puk])
                S.op("act", lambda e, pu=pu: e.copy(out=fl(UT[:]), in_=pu[:]), reads=[puk], writes=["g_UT"])
                S.op("pool", lambda e: e.tensor_tensor(out=Pb[0][:], in0=i4[:], in1=U[:], op=ALU.subtract), reads=["g_i4", "g_U"], writes=["g_P0"])
                if stage < 6:
                    continue
                W, WT, P = U, UT, Pb[0]
                Wk, WTk, Pk = "g_U", "g_UT", "g_P0"
                for l in range(1, NL + 1):
                    Wn, WTn, Pn = Wb[l % 2], WTb[l % 2], Pb[l % 2]
                    Wnk, WTnk, Pnk = f"g_W{l % 2}", f"g_WT{l % 2}", f"g_P{l % 2}"
                    if l < NL:
                        pw, pwk = nb()
                        for h in range(4):
                            S.op("pe", lambda e, h=h, pw=pw, W=W, WT=WT: e.matmul(v3(pw)[:, h, :], WT[:, h, :], W[:, h, :], start=True, stop=True), reads=[Wk, WTk], writes=[pwk])
                    pwt, pwtk = nb()
                    for h in range(4):
                        S.op("pe", lambda e, h=h, pwt=pwt, W=W, WT=WT: e.matmul(v3(pwt)[:, h, :], W[:, h, :], WT[:, h, :], start=True, stop=True), reads=[Wk, WTk], writes=[pwtk])
                    if l < NL:
                        S.op("act", lambda e, pw=pw, Wn=Wn: e.copy(out=fl(Wn[:]), in_=pw[:]), reads=[pwk], writes=[Wnk])
                    S.op("dve", lambda e, pwt=pwt, WTn=WTn: e.tensor_copy(out=fl(WTn[:]), in_=pwt[:]), reads=[pwtk], writes=[WTnk])
                    pp, ppk = nb()
                    for h in range(4):
                        S.op("pe", lambda e, h=h, pp=pp, WTn=WTn, P=P: e.matmul(v3(pp)[:, h, :], WTn[:, h, :], P[:, h, :], start=True, stop=True), reads=[WTnk, Pk], writes=[ppk])
                    if l < NL:
                        S.op("dve", lambda e, pp=pp, P=P, Pn=Pn: e.tensor_tensor(out=fl(Pn[:]), in0=pp[:], in1=fl(P[:]), op=ALU.add), reads=[ppk, Pk], writes=[Pnk])
                    else:
                        S.op("dve", lambda e, pp=pp, P=P: e.tensor_tensor(out=fl(TpT[:]), in0=pp[:], in1=fl(P[:]), op=ALU.add), reads=[ppk, Pk], writes=["g_TpT"])
                    W, WT, P, Wk, WTk, Pk = Wn, WTn, Pn, Wnk, WTnk, Pnk
                if stage < 7:
                    continue
                S.op("pool", lambda e, kT=kT, cs=cs: e.tensor_tensor(out=kgT[:], in0=kT[:, :, cs], in1=E[:], op=ALU.mult), reads=[kk, "g_E"], writes=["g_kgT"])
                S.op("pool", lambda e, qT=qT, cs=cs: e.tensor_tensor(out=qdT[:], in0=qT[:, :, cs], in1=E[:], op=ALU.mult), reads=[kq, "g_E"], writes=["g_qdT"])
                S.op("pool", lambda e: e.tensor_tensor(out=kdec[:], in0=kvtm[:, 0:4, :], in1=bc(ed[:]), op=ALU.mult), reads=["g_kvtm", "g_ed"], writes=["g_kdec"])
                if stage < 8:
                    continue
                p1, p1k = nb()
                for h in range(4):
                    S.op("pe", lambda e, h=h, p1=p1: e.matmul(v3(p1)[:, h, :], kgT[:, h, :], Sbf[:, h, :], start=True, stop=True), reads=["g_kgT", "g_Sbf"], writes=[p1k])
                S.op("dve", lambda e, p1=p1: e.tensor_tensor(out=R[:], in0=kvtm[:, 4:8, :], in1=v3(p1), op=ALU.subtract), reads=["g_kvtm", p1k], writes=["g_R"])
                p2, p2k = nb()
                for h in range(4):
                    S.op("pe", lambda e, h=h, p2=p2: e.matmul(v3(p2)[:, h, :], TpT[:, h, :], R[:, h, :], start=True, stop=True), reads=["g_TpT", "g_R"], writes=[p2k])
                S.op("dve", lambda e, p2=p2, gb=gb, c=c: e.tensor_tensor(out=vnew[:], in0=v3(p2), in1=bc(gb[:, c, 4:8]), op=ALU.mult), reads=[p2k, kgb], writes=["g_vnew"])
                po, pok = nb()
                for h in range(4):
                    S.op("pe", lambda e, h=h, po=po: e.matmul(v3(po)[:, h, :], Sbf[:, h, :], qdT[:, h, :], start=True, stop=False), reads=["g_Sbf", "g_qdT"], writes=[pok])
                    S.op("pe", lambda e, h=h, po=po: e.matmul(v3(po)[:, h, :], vnew[:, h, :], qkT[:, h, :], start=False, stop=True), reads=["g_vnew", "g_qkT"], writes=[pok])
                p3, p3k = nb()
                for h in range(4):
                    S.op("pe", lambda e, h=h, p3=p3: e.matmul(v3(p3)[:, h, :], kdec[:, h, :], vnew[:, h, :], start=True, stop=True), reads=["g_kdec", "g_vnew"], writes=[p3k])
                for h in range(4):
                    S.op("dve", lambda e, h=h, p3=p3: e.scalar_tensor_tensor(out=S32[:, h, :], in0=S32[:, h, :], scalar=E[:, h, 127:128], in1=v3(p3)[:, h, :],
                                                                           op0=ALU.mult, op1=ALU.add), reads=["g_S32", "g_E", p3k], writes=["g_S32"])
                S.op("act", lambda e: e.copy(out=Sbf[:], in_=S32[:]), reads=["g_S32"], writes=["g_Sbf"])
                if stage < 9:
                    continue
                S.op("act", lambda e, po=po: e.activation(out=fl(sq[:]), in_=po[:], func=AF.Square), reads=[pok], writes=["g_sq"])
                pss, pssk = nb()
                S.op("pe", lambda e, pss=pss: e.matmul(pss[:], C["onesb"][:], fl(sq[:]), start=True, stop=True), reads=["c_onesb", "g_sq"], writes=[pssk])
                S.op("act", lambda e, pss=pss: e.activation(out=fl(rn[:]), in_=pss[:], func=AF.Sqrt, bias=128.0 * RMS_EPS, scale=1.0), reads=[pssk], writes=["g_rn"])
                S.op("dve", lambda e: e.reciprocal(out=rn[:], in_=rn[:]), reads=["g_rn"], writes=["g_rn"])
                S.op("dve", lambda e, po=po: e.scalar_tensor_tensor(out=fl(yt[:]), in0=po[:], scalar=ng[:, 0:1], in1=fl(rn[:]), op0=ALU.mult, op1=ALU.mult),
                     reads=[pok, "g_ng", "g_rn"], writes=["g_yt"])
                S.op("pool", lambda e, zs=zs, cs=cs, bi=bi: e.tensor_tensor(out=ybuf[bi][:, :, cs], in0=yt[:], in1=zs[:, :, cs], op=ALU.mult),
                     reads=["g_yt", kz], writes=[f"g_y{bi}"])
            B.finals.append(S.dma(STQ, y_ap.rearrange("(h d) t -> d h t", d=128)[:, :, T * 512:(T + 1) * 512], ybuf[bi][:],
                                  reads=[f"g_y{bi}"], writes=["y_out"]))
        S.flush(B.es)


def phase_out(B, yT_ap, zT_ap, w_ap, xres_ap, mod_ap, layer, lng_ap, lnb_ap, out_ap, ntok=TPC):
    S, nc = B.S, B.nc
    with ExitStack() as st:
        pp = [B.ps(st, f"o_pp{i}", [128, 512], F32) for i in range(4)]
        S.excl.update([f"o_pp{i}" for i in range(4)])
        yT = B.sb(st, "o_yT", [128, 32, 512], BF16)
        zt = B.sb(st, "o_zt", [128, 8, 512], BF16)
        gb_, lg_, lb_ = (B.sb(st, n, [128, D], F32) for n in ("o_gate", "o_lng", "o_lnb"))
        Wt = [B.sb(st, f"o_W{i}", [128, 16, 512], BF16) for i in range(3)]
        xr = B.sb(st, "o_xr", [128, D], F32)
        r = B.sb(st, "o_r", [128, D], F32)
        sm = B.sb(st, "o_sm", [128, 8], F32)
        S.dma("sp", gb_[:], mod_ap[layer:layer + 1, 8192:12288].partition_broadcast(128), writes=["o_gate"])
        S.dma("sp", lg_[:], lng_ap.partition_broadcast(128), writes=["o_lng"])
        S.dma("sp", lb_[:], lnb_ap.partition_broadcast(128), writes=["o_lnb"])
        S.op("dve", lambda e: e.tensor_scalar(out=gb_[:], in0=gb_[:], scalar1=1.0, scalar2=None, op0=ALU.add), reads=["o_gate"], writes=["o_gate"])
        yv = yT_ap.rearrange("(kc p) t -> p kc t", p=128)
        wv = w_ap.rearrange("(kc p) n -> p kc n", p=128)
        xv = xres_ap.rearrange("(n p) d -> n p d", p=128)
        ov = out_ap.rearrange("(n p) d -> n p d", p=128)
        wcnt = 0
        pcnt = 0
        for hf in range(ntok // 512):
            S.dma("sp", yT[:], yv[:, :, hf * 512:(hf + 1) * 512], writes=["o_yT"])
            if zT_ap is not None:
                zv = zT_ap.rearrange("(kc p) t -> p kc t", p=128)
                for g in range(4):
                    S.dma("sp", zt[:], zv[:, g * 8:(g + 1) * 8, hf * 512:(hf + 1) * 512], writes=["o_zt"])
                    S.op("pool", lambda e, g=g: e.tensor_tensor(out=yT[:, g * 8:(g + 1) * 8, :], in0=yT[:, g * 8:(g + 1) * 8, :], in1=zt[:], op=ALU.mult),
                         reads=["o_yT", "o_zt"], writes=["o_yT"])
            for ts in range(4):
                n = hf * 4 + ts
                S.dma("sp", xr[:], xv[n], writes=["o_xr"])
                for nb in range(8):
                    pb = pp[pcnt % 4]
                    pk = f"o_pp{pcnt % 4}"
                    pcnt += 1
                    for half in range(2):
                        slot = wcnt % 3
                        wcnt += 1
                        S.dma("pool", Wt[slot][:], wv[:, half * 16:(half + 1) * 16, nb * 512:(nb + 1) * 512], writes=[f"o_W{slot}"])
                        for k16 in range(16):
                            kc = half * 16 + k16
                            S.op("pe", lambda e, pb=pb, slot=slot, k16=k16, kc=kc, ts=ts: e.matmul(
                                pb[:], yT[:, kc, ts * 128:(ts + 1) * 128], Wt[slot][:, k16, :], start=(kc == 0), stop=(kc == 31)),
                                reads=["o_yT", f"o_W{slot}"], writes=[pk])
                    cs = slice(nb * 512, (nb + 1) * 512)
                    S.op("dve", lambda e, pb=pb, cs=cs: e.tensor_tensor(out=r[:, cs], in0=pb[:], in1=gb_[:, cs], op=ALU.mult), reads=[pk, "o_gate"], writes=["o_r"])
                    S.op("dve", lambda e, cs=cs: e.scalar_tensor_tensor(out=r[:, cs], in0=xr[:, cs], scalar=float(ALPHA), in1=r[:, cs], op0=ALU.mult, op1=ALU.add),
                         reads=["o_xr", "o_r"], writes=["o_r"])
                S.op("dve", lambda e: e.reduce_sum(out=sm[:, 0:1], in_=r[:], axis=AX.X), reads=["o_r"], writes=["o_sm"])
                S.op("pool", lambda e: e.tensor_tensor(out=xr[:], in0=r[:], in1=r[:], op=ALU.mult), reads=["o_r", "o_xr"], writes=["o_xr"])
                S.op("dve", lambda e: e.reduce_sum(out=sm[:, 1:2], in_=xr[:], axis=AX.X), reads=["o_xr", "o_sm"], writes=["o_sm"])
                S.op("dve", lambda e: e.tensor_scalar(out=sm[:, 0:2], in0=sm[:, 0:2], scalar1=1.0 / D, scalar2=None, op0=ALU.mult), reads=["o_sm"], writes=["o_sm"])
                S.op("dve", lambda e: e.tensor_tensor(out=sm[:, 2:3], in0=sm[:, 0:1], in1=sm[:, 0:1], op=ALU.mult), reads=["o_sm"], writes=["o_sm"])
                S.op("dve", lambda e: e.tensor_tensor(out=sm[:, 3:4], in0=sm[:, 1:2], in1=sm[:, 2:3], op=ALU.subtract), reads=["o_sm"], writes=["o_sm"])
                S.op("act", lambda e: e.activation(out=sm[:, 4:5], in_=sm[:, 3:4], func=AF.Sqrt, bias=LN_EPS, scale=1.0), reads=["o_sm"], writes=["o_sm"])
                S.op("dve", lambda e: e.reciprocal(out=sm[:, 5:6], in_=sm[:, 4:5]), reads=["o_sm"], writes=["o_sm"])
                S.op("dve", lambda e: e.tensor_scalar(out=r[:], in0=r[:], scalar1=sm[:, 0:1], scalar2=None, op0=ALU.subtract), reads=["o_r", "o_sm"], writes=["o_r"])
                S.op("dve", lambda e: e.scalar_tensor_tensor(out=r[:], in0=r[:], scalar=sm[:, 5:6], in1=lg_[:], op0=ALU.mult, op1=ALU.mult),
                     reads=["o_r", "o_sm", "o_lng"], writes=["o_r"])
                S.op("pool", lambda e: e.tensor_tensor(out=r[:], in0=r[:], in1=lb_[:], op=ALU.add), reads=["o_r", "o_lnb"], writes=["o_r"])
                B.finals.append(S.dma("sp", ov[n], r[:], reads=["o_r"], writes=["o_out"]))
        S.flush(B.es)


def phase_out2(B, yT_ap, zT_ap, w_ap, xres_ap, mod_ap, layer, lng_ap, lnb_ap, out_ap, rscr_ap):
    S, nc = B.S, B.nc
    NT = TPC // 128
    xv = xres_ap.rearrange("(n p) d -> n p d", p=128)
    rv = rscr_ap.rearrange("(n p) d -> n p d", p=128)
    ov = out_ap.rearrange("(n p) d -> n p d", p=128)
    with ExitStack() as st:
        pp = [B.ps(st, f"o_pp{i}", [128, 512], F32) for i in range(8)]
        S.excl.update([f"o_pp{i}" for i in range(8)])
        yT = B.sb(st, "o_yT", [128, 32, TPC], BF16)
        zt = B.sb(st, "o_zt", [128, 4, TPC], BF16)
        gb_ = B.sb(st, "o_gate", [128, D], F32)
        Wt = [B.sb(st, f"o_W{i}", [128, 16, 512], BF16) for i in range(3)]
        xc = [B.sb(st, f"o_xc{i}", [128, 512], F32) for i in range(3)]
        rc = [B.sb(st, f"o_rc{i}", [128, 512], F32) for i in range(3)]
        S.dma("sp", gb_[:], mod_ap[layer:layer + 1, 8192:12288].partition_broadcast(128), writes=["o_gate"])
        S.op("dve", lambda e: e.tensor_scalar(out=gb_[:], in0=gb_[:], scalar1=1.0, scalar2=None, op0=ALU.add), reads=["o_gate"], writes=["o_gate"])
        yv = yT_ap.rearrange("(kc p) t -> p kc t", p=128)
        wv = w_ap.rearrange("(kc p) n -> p kc n", p=128)
        for g in range(8):
            S.dma("sp", yT[:, g * 4:(g + 1) * 4, :], yv[:, g * 4:(g + 1) * 4, :], writes=[f"o_yT{g}"])
            if zT_ap is not None:
                zv = zT_ap.rearrange("(kc p) t -> p kc t", p=128)
                S.dma("sp", zt[:], zv[:, g * 4:(g + 1) * 4, :], writes=["o_zt"])
                S.op("pool", lambda e, g=g: e.tensor_tensor(out=yT[:, g * 4:(g + 1) * 4, :], in0=yT[:, g * 4:(g + 1) * 4, :], in1=zt[:], op=ALU.mult),
                     reads=[f"o_yT{g}", "o_zt"], writes=[f"o_yT{g}"])
        ykeys = [f"o_yT{g}" for g in range(8)]
        wcnt = 0
        ccnt = 0
        for nb in range(8):
            cs = slice(nb * 512, (nb + 1) * 512)
            slots = []
            for half in range(2):
                slot = wcnt % 3
                wcnt += 1
                slots.append(slot)
                S.dma("pool", Wt[slot][:], wv[:, half * 16:(half + 1) * 16, cs], writes=[f"o_W{slot}"])
            for ts in range(NT):
                pb = pp[ts]
                for kc in range(32):
                    slot = slots[kc // 16]
                    S.op("pe", lambda e, pb=pb, slot=slot, kc=kc, ts=ts: e.matmul(
                        pb[:], yT[:, kc, ts * 128:(ts + 1) * 128], Wt[slot][:, kc % 16, :], start=(kc == 0), stop=(kc == 31)),
                        reads=[ykeys[kc // 4], f"o_W{slot}"], writes=[f"o_pp{ts}"])
                ci = ccnt % 3
                ccnt += 1
                S.dma("sp", xc[ci][:], xv[ts][:, cs], writes=[f"o_xc{ci}"])
                S.op("dve", lambda e, pb=pb, ci=ci, cs=cs: e.tensor_tensor(out=rc[ci][:], in0=pb[:], in1=gb_[:, cs], op=ALU.mult), reads=[f"o_pp{ts}", "o_gate"], writes=[f"o_rc{ci}"])
                S.op("dve", lambda e, ci=ci: e.scalar_tensor_tensor(out=rc[ci][:], in0=xc[ci][:], scalar=float(ALPHA), in1=rc[ci][:], op0=ALU.mult, op1=ALU.add),
                     reads=[f"o_xc{ci}", f"o_rc{ci}"], writes=[f"o_rc{ci}"])
                S.dma("sp", rv[ts][:, cs], rc[ci][:], reads=[f"o_rc{ci}"], writes=["o_rscr"])
        S.flush(B.es)
    with ExitStack() as st:
        lg_, lb_ = (B.sb(st, n, [128, D], F32) for n in ("o_lng", "o_lnb"))
        r2 = [B.sb(st, f"o_r{i}", [128, D], F32) for i in range(2)]
        sq = B.sb(st, "o_sq", [128, D], F32)
        sm = B.sb(st, "o_sm", [128, 8], F32)
        S.dma("sp", lg_[:], lng_ap.partition_broadcast(128), writes=["o_lng"])
        S.dma("sp", lb_[:], lnb_ap.partition_broadcast(128), writes=["o_lnb"])
        for n in range(NT):
            r = r2[n % 2]
            rk = f"o_r{n % 2}"
            S.dma("sp", r[:], rv[n], writes=[rk])
            S.op("dve", lambda e, r=r: e.reduce_sum(out=sm[:, 0:1], in_=r[:], axis=AX.X), reads=[rk], writes=["o_sm"])
            S.op("pool", lambda e, r=r: e.tensor_tensor(out=sq[:], in0=r[:], in1=r[:], op=ALU.mult), reads=[rk], writes=["o_sq"])
            S.op("dve", lambda e: e.reduce_sum(out=sm[:, 1:2], in_=sq[:], axis=AX.X), reads=["o_sq", "o_sm"], writes=["o_sm"])
            S.op("dve", lambda e: e.tensor_scalar(out=sm[:, 0:2], in0=sm[:, 0:2], scalar1=1.0 / D, scalar2=None, op0=ALU.mult), reads=["o_sm"], writes=["o_sm"])
            S.op("dve", lambda e: e.tensor_tensor(out=sm[:, 2:3], in0=sm[:, 0:1], in1=sm[:, 0:1], op=ALU.mult), reads=["o_sm"], writes=["o_sm"])
            S.op("dve", lambda e: e.tensor_tensor(out=sm[:, 3:4], in0=sm[:, 1:2], in1=sm[:, 2:3], op=ALU.subtract), reads=["o_sm"], writes=["o_sm"])
            S.op("act", lambda e: e.activation(out=sm[:, 4:5], in_=sm[:, 3:4], func=AF.Sqrt, bias=LN_EPS, scale=1.0), reads=["o_sm"], writes=["o_sm"])
            S.op("dve", lambda e: e.reciprocal(out=sm[:, 5:6], in_=sm[:, 4:5]), reads=["o_sm"], writes=["o_sm"])
            S.op("dve", lambda e, r=r: e.tensor_scalar(out=r[:], in0=r[:], scalar1=sm[:, 0:1], scalar2=None, op0=ALU.subtract), reads=[rk, "o_sm"], writes=[rk])
            S.op("dve", lambda e, r=r: e.scalar_tensor_tensor(out=r[:], in0=r[:], scalar=sm[:, 5:6], in1=lg_[:], op0=ALU.mult, op1=ALU.mult),
                 reads=[rk, "o_sm", "o_lng"], writes=[rk])
            S.op("pool", lambda e, r=r: e.tensor_tensor(out=r[:], in0=r[:], in1=lb_[:], op=ALU.add), reads=[rk, "o_lnb"], writes=[rk])
            B.finals.append(S.dma("sp", ov[n], r[:], reads=[rk], writes=["o_out"]))
        S.flush(B.es)


def phase_b2(B, x1_ap, mod_ap, win_ap, qg_ap, kvg_ap, latT_ap, zsT_ap):
    S, nc = B.S, B.nc
    with ExitStack() as st:
        C = make_consts(B, st)
        ptr = [B.ps(st, f"b_pt{i}", [128, 4, 128], F32) for i in range(2)]
        ptk = [f"b_pt{i}" for i in range(2)]
        psp = [B.ps(st, f"b_pp{i}", [128, 512], F32) for i in range(4)]
        pmd = B.ps(st, "b_pmd", [128, 512], F32)
        pbt = B.ps(st, "b_pbt", [128, 8, 128], BF16)
        S.excl.update(ptk + [f"b_pp{i}" for i in range(4)] + ["b_pmd", "b_pbt"])
        sh, sc = load_mod_fm(B, st, C, mod_ap, 1, pmd, "b_pmd")
        xs = B.sb(st, "b_xs", [128, D], F32)
        hT = B.sb(st, "b_hT", [128, 32, TPC], BF16)
        Wt = [B.sb(st, f"b_W{i}", [128, 16, 512], BF16) for i in range(4)]
        zo = B.sb(st, "b_zo", [128, 4, 512], BF16)
        lt = B.sb(st, "b_lt", [128, 1472], F32)
        lsq = B.sb(st, "b_lsq", [128, 896], F32)
        lb = B.sb(st, "b_lb", [128, 13, 128], BF16)
        ltT = B.sb(st, "b_ltT", [128, 13, 128], BF16)
        qg = B.sb(st, "b_qg", [128, 896], F32)
        kvg = B.sb(st, "b_kvg", [128, 512], F32)
        sm = B.sb(st, "b_sm", [128, 8], F32)
        S.dma("sp", qg[:], qg_ap.partition_broadcast(128), writes=["b_qg"])
        S.dma("sp", kvg[:], kvg_ap.partition_broadcast(128), writes=["b_kvg"])
        S.op("pool", lambda e: e.memset(lb[:], 0.0), writes=["b_lb"])
        xv = x1_ap.rearrange("(n p) d -> n p d", p=128)
        wv = win_ap.rearrange("(kc p) n -> p kc n", p=128)
        cnt = [0]
        for ts in range(8):
            S.dma("sp", xs[:], xv[ts], writes=["b_xs"])
            make_hT(B, C, xs, "b_xs", hT, "b_hT", ts, sh, sc, "md_sh1", "md_sc1", ptr, ptk, cnt)
        hkeys = [f"b_hT{kc}" for kc in range(32)]
        wcnt = 0
        zv = zsT_ap.rearrange("(cb s p) t -> cb p s t", s=4, p=128)
        for cb in range(8):
            slots = []
            for half in range(2):
                slot = wcnt % 4
                wcnt += 1
                slots.append(slot)
                S.dma("pool", Wt[slot][:], wv[:, half * 16:(half + 1) * 16, 1472 + cb * 512:1472 + (cb + 1) * 512], writes=[f"b_W{slot}"])
            for th in range(2):
                for sbk in range(4):
                    for kc in range(32):
                        slot = slots[kc // 16]
                        S.op("pe", lambda e, slot=slot, sbk=sbk, kc=kc, th=th: e.matmul(
                            psp[sbk][:], Wt[slot][:, kc % 16, sbk * 128:(sbk + 1) * 128], hT[:, kc, th * 512:(th + 1) * 512],
                            start=(kc == 0), stop=(kc == 31)), reads=[f"b_W{slot}", hkeys[kc]], writes=[f"b_pp{sbk}"])
                    S.op("act", lambda e, sbk=sbk: e.activation(out=zo[:, sbk, :], in_=psp[sbk][:], func=AF.Silu), reads=[f"b_pp{sbk}"], writes=["b_zo"])
                S.dma("sp", zv[cb][:, :, th * 512:(th + 1) * 512], zo[:], reads=["b_zo"], writes=["b_zs"])
        lv = latT_ap.rearrange("(j p) t -> p j t", p=128)
        for ts in range(8):
            for nbk, (c0, c1) in enumerate(((0, 512), (512, 1024), (1024, 1472))):
                slots = []
                for half in range(2):
                    slot = wcnt % 4
                    wcnt += 1
                    slots.append(slot)
                    S.dma("pool", Wt[slot][:, :, 0:c1 - c0], wv[:, half * 16:(half + 1) * 16, c0:c1], writes=[f"b_W{slot}"])
                pb = psp[nbk]
                for kc in range(32):
                    slot = slots[kc // 16]
                    S.op("pe", lambda e, slot=slot, kc=kc, ts=ts, pb=pb, c0=c0, c1=c1: e.matmul(
                        pb[:, 0:c1 - c0], hT[:, kc, ts * 128:(ts + 1) * 128], Wt[slot][:, kc % 16, 0:c1 - c0],
                        start=(kc == 0), stop=(kc == 31)), reads=[f"b_W{slot}", hkeys[kc]], writes=[f"b_pp{nbk}"])
                S.op("act", lambda e, pb=pb, c0=c0, c1=c1: e.copy(out=lt[:, c0:c1], in_=pb[:, 0:c1 - c0]), reads=[f"b_pp{nbk}"], writes=["b_lt"])
            for (a0, a1, gt, gk, col) in ((0, 896, qg, "b_qg", 0), (896, 1408, kvg, "b_kvg", 1)):
                n = a1 - a0
                S.op("pool", lambda e, a0=a0, a1=a1, n=n: e.tensor_tensor(out=lsq[:, 0:n], in0=lt[:, a0:a1], in1=lt[:, a0:a1], op=ALU.mult), reads=["b_lt"], writes=["b_lsq"])
                S.op("dve", lambda e, n=n, col=col: e.reduce_sum(out=sm[:, col:col + 1], in_=lsq[:, 0:n], axis=AX.X), reads=["b_lsq"], writes=["b_sm"])
                S.op("act", lambda e, n=n, col=col: e.activation(out=sm[:, col + 2:col + 3], in_=sm[:, col:col + 1], func=AF.Sqrt, bias=RMS_EPS, scale=1.0 / n),
                     reads=["b_sm"], writes=["b_sm"])
                S.op("dve", lambda e, col=col: e.reciprocal(out=sm[:, col + 4:col + 5], in_=sm[:, col + 2:col + 3]), reads=["b_sm"], writes=["b_sm"])
                S.op("dve", lambda e, a0=a0, a1=a1, n=n, gt=gt, col=col: e.scalar_tensor_tensor(
                    out=lb[:].rearrange("p a b -> p (a b)")[:, a0:a1], in0=lt[:, a0:a1], scalar=sm[:, col + 4:col + 5], in1=gt[:, 0:n], op0=ALU.mult, op1=ALU.mult),
                    reads=["b_lt", "b_sm", gk], writes=["b_lb"])
            S.op("dve", lambda e: e.tensor_copy(out=lb[:, 11, 0:64], in_=lt[:, 1408:1472]), reads=["b_lt"], writes=["b_lb"])
            S.op("dve", lambda e: e.tensor_copy(out=lb[:, 12, 0:32], in_=lt[:, 1440:1472]), reads=["b_lt"], writes=["b_lb"])
            S.op("dve", lambda e: e.tensor_copy(out=lb[:, 12, 32:64], in_=lt[:, 1408:1440]), reads=["b_lt"], writes=["b_lb"])
            for j0 in (0, 8):
                nj = min(8, 13 - j0)
                for j in range(nj):
                    S.op("pe", lambda e, j=j, j0=j0: e.transpose(pbt[:, j, :], lb[:, j0 + j, :], C["idb"][:]), reads=["b_lb", "c_idb"], writes=["b_pbt"])
                S.op("act", lambda e, j0=j0, nj=nj: e.copy(out=ltT[:, j0:j0 + nj, :], in_=pbt[:, 0:nj, :]), reads=["b_pbt"], writes=["b_ltT"])
            S.dma("sp", lv[:, :, ts * 128:(ts + 1) * 128], ltT[:], reads=["b_ltT"], writes=["b_lat"])
        S.flush(B.es)


def phase_c(B, lat_ap, wq_ap, wkv_ap, pos_ap, invf_ap, sgn_ap, oT_ap, nq=SEQ // 512):
    S, nc = B.S, B.nc
    TWO_PI = float(2 * np.pi)
    with ExitStack() as st:
        C = make_consts(B, st)
        acc = [B.ps(st, f"c_acc{i}", [128, 512], F32) for i in range(4)]
        pst = [B.ps(st, f"c_st{i}", [128, 512], F32) for i in range(2)]
        pm = [B.ps(st, f"c_pm{i}", [128, 512], F32) for i in range(2)]
        S.excl.update([f"c_acc{i}" for i in range(4)] + ["c_st0", "c_st1", "c_pm0", "c_pm1"])
        Wq = B.sb(st, "c_Wq", [128, 7, 1024], BF16)
        Wkv = B.sb(st, "c_Wkv", [128, 4, 1024], BF16)
        S.dma("pool", Wq[:], wq_ap.rearrange("(kc p) n -> p kc n", p=128), writes=["c_Wq"])
        S.dma("pool", Wkv[:], wkv_ap.rearrange("(kc p) n -> p kc n", p=128), writes=["c_Wkv"])
        invf = B.sb(st, "c_invf", [64, 1], F32)
        sgn = B.sb(st, "c_sgn", [64, 1], F32)
        S.dma("sp", invf[:], invf_ap, writes=["c_invf"])
        S.dma("sp", sgn[:], sgn_ap, writes=["c_sgn"])
        tril = B.sb(st, "c_tril", [128, 128], BF16)
        S.op("pool", lambda e: e.memset(tril[:], 1.0), writes=["c_tril"])
        S.op("pool", lambda e: e.affine_select(out=tril[:], in_=tril[:], pattern=[[1, 128]], compare_op=ALU.is_ge, fill=0.0,
                                               base=0, channel_multiplier=-1), reads=["c_tril"], writes=["c_tril"])
        qn = B.sb(st, "c_qn", [128, SEQ], BF16)
        kn = B.sb(st, "c_kn", [128, SEQ], BF16)
        qr = B.sb(st, "c_qr", [64, SEQ], BF16)
        kr = B.sb(st, "c_kr", [64, SEQ], BF16)
        vt = B.sb(st, "c_vt", [128, SEQ // 128, 132], BF16)
        S.op("pool", lambda e: e.memset(vt[:], 1.0), writes=["c_vt"])
        lat = [B.sb(st, f"c_lat{i}", [128, 13, 512], BF16) for i in range(2)]
        posi = B.sb(st, "c_posi", [64, 512], I32)
        ang = B.sb(st, "c_ang", [64, 512], F32)
        cs_ = B.sb(st, "c_cos", [64, 512], F32)
        sn_ = B.sb(st, "c_sin", [64, 512], F32)
        t1 = B.sb(st, "c_t1", [64, 512], F32)
        t2 = B.sb(st, "c_t2", [64, 512], F32)
        pt = [B.sb(st, f"c_p{i}", [128, 512], BF16) for i in range(2)]
        rs = B.sb(st, "c_rs", [128, 4], F32)
        on = B.sb(st, "c_on", [128, 4, 128], BF16)
        oT = B.sb(st, "c_oT", [128, 512], BF16)
        pbt = pm[1]
        lv = lat_ap.rearrange("(j p) t -> p j t", p=128)
        scale = float(192.0 ** -0.5)

        def rope(dst, dkey, pa, pak, pb, pbk, sl):
            S.op("dve", lambda e: e.tensor_tensor(out=t1[:], in0=pa[0:64, :], in1=cs_[:], op=ALU.mult), reads=[pak, "c_cos"], writes=["c_t1"])
            S.op("dve", lambda e: e.tensor_tensor(out=t2[:], in0=pb[0:64, :], in1=sn_[:], op=ALU.mult), reads=[pbk, "c_sin"], writes=["c_t2"])
            S.op("pool", lambda e: e.tensor_tensor(out=dst[0:64, sl], in0=t1[:], in1=t2[:], op=ALU.add), reads=["c_t1", "c_t2"], writes=[dkey])

        for h in range(4):
            for tt in range(SEQ // 512):
                sl = slice(tt * 512, (tt + 1) * 512)
                lt_ = lat[tt % 2]
                lk = f"c_lat{tt % 2}"
                S.dma("sp", lt_[:], lv[:, :, sl], writes=[lk])
                S.dma("sp", posi[:], pos_ap[0:1, sl].partition_broadcast(64), writes=["c_posi"])
                S.op("dve", lambda e: e.tensor_copy(out=ang[:], in_=posi[:]), reads=["c_posi"], writes=["c_ang"])
                S.op("dve", lambda e: e.tensor_scalar(out=ang[:], in0=ang[:], scalar1=invf[:, 0:1], scalar2=None, op0=ALU.mult), reads=["c_ang", "c_invf"], writes=["c_ang"])
                for (dst, dkey, addc) in ((sn_, "c_sin", 0.0), (cs_, "c_cos", float(0.5 * np.pi))):
                    S.op("dve", lambda e, addc=addc: e.tensor_scalar(out=t1[:], in0=ang[:], scalar1=addc, scalar2=None, op0=ALU.add), reads=["c_ang"], writes=["c_t1"])
                    S.op("dve", lambda e: e.tensor_scalar(out=t2[:], in0=t1[:], scalar1=1.0 / TWO_PI, scalar2=None, op0=ALU.mult), reads=["c_t1"], writes=["c_t2"])
                    S.op("dve", lambda e: e.tensor_copy(out=posi[:], in_=t2[:]), reads=["c_t2"], writes=["c_posi"])
                    S.op("dve", lambda e: e.tensor_copy(out=t2[:], in_=posi[:]), reads=["c_posi"], writes=["c_t2"])
                    S.op("dve", lambda e: e.scalar_tensor_tensor(out=t1[:], in0=t2[:], scalar=-TWO_PI, in1=t1[:], op0=ALU.mult, op1=ALU.add), reads=["c_t1", "c_t2"], writes=["c_t1"])
                    S.op("dve", lambda e: e.tensor_scalar(out=t2[:], in0=t1[:], scalar1=float(np.pi), scalar2=None, op0=ALU.is_gt), reads=["c_t1"], writes=["c_t2"])
                    S.op("dve", lambda e: e.scalar_tensor_tensor(out=t1[:], in0=t2[:], scalar=-TWO_PI, in1=t1[:], op0=ALU.mult, op1=ALU.add), reads=["c_t1", "c_t2"], writes=["c_t1"])
                    S.op("dve", lambda e: e.tensor_scalar(out=t2[:], in0=t1[:], scalar1=float(-np.pi), scalar2=None, op0=ALU.is_lt), reads=["c_t1"], writes=["c_t2"])
                    S.op("dve", lambda e: e.scalar_tensor_tensor(out=t1[:], in0=t2[:], scalar=TWO_PI, in1=t1[:], op0=ALU.mult, op1=ALU.add), reads=["c_t1", "c_t2"], writes=["c_t1"])
                    S.op("act", lambda e, dst=dst: e.activation(out=dst[:], in_=t1[:], func=AF.Sin), reads=["c_t1"], writes=[dkey])
                S.op("dve", lambda e: e.tensor_scalar(out=sn_[:], in0=sn_[:], scalar1=sgn[:, 0:1], scalar2=None, op0=ALU.mult), reads=["c_sin", "c_sgn"], writes=["c_sin"])
                for kc in range(7):
                    S.op("pe", lambda e, kc=kc, lt_=lt_, h=h: e.matmul(pm[0][:], Wq[:, kc, h * 256:h * 256 + 128], lt_[:, kc, :], start=(kc == 0), stop=(kc == 6)),
                         reads=["c_Wq", lk], writes=["c_pm0"])
                S.op("act", lambda e, sl=sl: e.copy(out=qn[:, sl], in_=pm[0][:]), reads=["c_pm0"], writes=["c_qn"])
                for kc in range(7):
                    S.op("pe", lambda e, kc=kc, lt_=lt_, h=h: e.matmul(pst[0][0:64, :], Wq[:, kc, h * 256 + 128:h * 256 + 192], lt_[:, kc, :], start=(kc == 0), stop=(kc == 6)),
                         reads=["c_Wq", lk], writes=["c_st0"])
                for kc in range(7):
                    S.op("pe", lambda e, kc=kc, lt_=lt_, h=h: e.matmul(pst[1][0:64, :], Wq[:, kc, h * 256 + 192:h * 256 + 256], lt_[:, kc, :], start=(kc == 0), stop=(kc == 6)),
                         reads=["c_Wq", lk], writes=["c_st1"])
                rope(qr, "c_qr", pst[0], "c_st0", pst[1], "c_st1", sl)
                if h == 0:
                    S.op("pe", lambda e, lt_=lt_: e.matmul(pst[0][0:64, :], C["idb"][0:64, 0:64], lt_[0:64, 11, :], start=True, stop=True), reads=["c_idb", lk], writes=["c_st0"])
                    S.op("pe", lambda e, lt_=lt_: e.matmul(pst[1][0:64, :], C["idb"][0:64, 0:64], lt_[0:64, 12, :], start=True, stop=True), reads=["c_idb", lk], writes=["c_st1"])
                    rope(kr, "c_kr", pst[0], "c_st0", pst[1], "c_st1", sl)
                for kc in range(4):
                    S.op("pe", lambda e, kc=kc, lt_=lt_, h=h: e.matmul(pm[0][:], Wkv[:, kc, h * 256:h * 256 + 128], lt_[:, 7 + kc, :], start=(kc == 0), stop=(kc == 3)),
                         reads=["c_Wkv", lk], writes=["c_pm0"])
                S.op("act", lambda e, sl=sl: e.copy(out=kn[:, sl], in_=pm[0][:]), reads=["c_pm0"], writes=["c_kn"])
                for sub in range(4):
                    for kc in range(4):
                        S.op("pe", lambda e, kc=kc, lt_=lt_, h=h, sub=sub: e.matmul(pm[1][:, sub * 128:(sub + 1) * 128], lt_[:, 7 + kc, sub * 128:(sub + 1) * 128],
                                                                                   Wkv[:, kc, h * 256 + 128:h * 256 + 256], start=(kc == 0), stop=(kc == 3)),
                             reads=["c_Wkv", lk], writes=["c_pm1"])
                S.op("dve", lambda e, tt=tt: e.tensor_copy(out=vt[:, tt * 4:(tt + 1) * 4, 0:128], in_=pm[1][:].rearrange("p (a b) -> p a b", a=4)), reads=["c_pm1"], writes=["c_vt"])
            steps = [(qb, kt) for qb in range(nq) for kt in range(4 * qb + 4)]

            def emit_st(i):
                qb, kt = steps[i]
                qsl = slice(qb * 512, (qb + 1) * 512)
                ksl = slice(kt * 128, (kt + 1) * 128)
                ps_ = pst[i % 2]
                psk = f"c_st{i % 2}"
                pb_ = pt[i % 2]
                pbk_ = f"c_p{i % 2}"
                S.op("pe", lambda e: e.matmul(ps_[:], kn[:, ksl], qn[:, qsl], start=True, stop=False), reads=["c_kn", "c_qn"], writes=[psk])
                S.op("pe", lambda e: e.matmul(ps_[:], kr[0:64, ksl], qr[0:64, qsl], start=False, stop=True), reads=["c_kr", "c_qr"], writes=[psk])
                S.op("act", lambda e: e.activation(out=pb_[:], in_=ps_[:], func=AF.Exp, scale=scale), reads=[psk], writes=[pbk_])
                j = kt - 4 * qb
                if j >= 0:
                    S.op("pool", lambda e: e.tensor_tensor(out=pb_[:, j * 128:(j + 1) * 128], in0=pb_[:, j * 128:(j + 1) * 128], in1=tril[:], op=ALU.mult),
                         reads=[pbk_, "c_tril"], writes=[pbk_])

            def emit_pv(i):
                qb, kt = steps[i]
                qsl = slice(qb * 512, (qb + 1) * 512)
                pb_ = pt[i % 2]
                pbk_ = f"c_p{i % 2}"
                j = kt - 4 * qb
                for ii in range(4):
                    if j > ii:
                        continue
                    last = (kt == 4 * qb + ii)
                    S.op("pe", lambda e, ii=ii, last=last: e.matmul(acc[ii][:, 0:129], pb_[:, ii * 128:(ii + 1) * 128], vt[:, kt, 0:129],
                                                                     start=(kt == 0), stop=last), reads=[pbk_, "c_vt"], writes=[f"c_acc{ii}"])
                if kt != 4 * qb + 3:
                    return
                for ii in range(4):
                    S.op("dve", lambda e, ii=ii: e.reciprocal(out=rs[:, ii:ii + 1], in_=acc[ii][:, 128:129]), reads=[f"c_acc{ii}"], writes=["c_rs"])
                    S.op("dve", lambda e, ii=ii: e.tensor_scalar(out=on[:, ii, :], in0=acc[ii][:, 0:128], scalar1=rs[:, ii:ii + 1], scalar2=None, op0=ALU.mult),
                         reads=[f"c_acc{ii}", "c_rs"], writes=["c_on"])
                pbt_b = pbt[:].bitcast(BF16).rearrange("p (a b) -> p a b", b=128)
                for ii in range(4):
                    S.op("pe", lambda e, ii=ii: e.transpose(pbt_b[:, ii, :], on[:, ii, :], C["idb"][:]), reads=["c_on", "c_idb"], writes=["c_pm1"])
                S.op("act", lambda e: e.copy(out=oT[:].rearrange("p (a b) -> p a b", a=4), in_=pbt_b[:, 0:4, :]), reads=["c_pm1"], writes=["c_oT"])
                B.finals.append(S.dma("sp", oT_ap[h * 128:(h + 1) * 128, qsl], oT[:], reads=["c_oT"], writes=["c_out"]))

            emit_st(0)
            for i in range(len(steps)):
                if i + 1 < len(steps):
                    emit_st(i + 1)
                emit_pv(i)
        S.flush(B.es)


def _launch(build, maps):
    nc = bass.Bass("TRN2", target_bir_lowering=False)
    with ExitStack() as es:
        B = Builder(nc, es)
        build(nc, B)
    res = run_bass_kernel_spmd(nc, maps, core_ids=list(range(NCORES)))
    return res.results


def _din(nc, name, shape, dt):
    return nc.dram_tensor(name, list(shape), dt, kind="ExternalInput").ap()


def _dout(nc, name, shape, dt):
    return nc.dram_tensor(name, list(shape), dt, kind="ExternalOutput").ap()


def kernel(x, c, positions, w_mod, b_mod, ln_g, ln_b, a_w_in, a_w_conv, a_a_log, a_dt_bias, a_norm_g, a_w_out,
           b_w_in, b_q_norm_g, b_w_qb, b_kv_norm_g, b_w_kvb, b_w_out):
    f32 = np.float32
    asc = np.ascontiguousarray
    x2 = asc(np.asarray(x, f32)[0])
    R = range(NCORES)
    def bM(nc, B):
        phase_mod(B, _din(nc, "c", [1, D], F32), _din(nc, "wm", [D, 3072], F32), _din(nc, "bm", [1, 3072], F32), _dout(nc, "modrow", [1, 3072], F32))
    maps = [{"c": asc(np.asarray(c, f32)), "wm": asc(np.asarray(w_mod[r // 4][:, (r % 4) * 3072:(r % 4 + 1) * 3072], f32)),
             "bm": asc(np.asarray(b_mod[r // 4][None, (r % 4) * 3072:(r % 4 + 1) * 3072], f32))} for r in R]
    res = _launch(bM, maps)
    mod = asc(np.concatenate([res[r]["modrow"].reshape(-1) for r in R]).reshape(2, 12288))
    def bA(nc, B):
        scr = {n + "s": nc.dram_tensor("scr_" + n, [4, 128, SEQ], BF16).ap() for n in "qkvz"}
        scr["gbs"] = nc.dram_tensor("scr_gb", [SEQ, 8], F32).ap()
        y = _dout(nc, "y0T", [512, SEQ], BF16)
        phase_a1(B, _din(nc, "x", [SEQ, D], F32), _din(nc, "mod", [2, 12288], F32), _din(nc, "w4", [D, 2048], F32), _din(nc, "wab", [D, 8], F32),
                 _din(nc, "cw", [4, 1536], F32), _din(nc, "alog", [1, 4], F32), _din(nc, "dtb", [1, 4], F32), scr)
        phase_a2(B, scr, _din(nc, "ng", [1, 128], F32), y)
    W = np.asarray(a_w_in[0], f32)
    cwf = np.asarray(a_w_conv[0], f32)
    maps = []
    for r in R:
        o = 512 * r
        maps.append({"x": x2, "mod": mod,
                     "w4": asc(np.concatenate([W[:, o:o + 512], W[:, 4096 + o:4096 + o + 512], W[:, 8192 + o:8192 + o + 512], W[:, 12288 + o:12288 + o + 512]], axis=1)),
                     "wab": asc(np.concatenate([W[:, 16384 + 4 * r:16384 + 4 * r + 4], W[:, 16416 + 4 * r:16416 + 4 * r + 4]], axis=1)),
                     "cw": asc(np.concatenate([cwf[:, q + o:q + o + 512] for q in (0, 4096, 8192)], axis=1)),
                     "alog": asc(np.asarray(a_a_log, f32)[:, 4 * r:4 * r + 4]), "dtb": asc(np.asarray(a_dt_bias, f32)[:, 4 * r:4 * r + 4]),
                     "ng": asc(np.asarray(a_norm_g, f32))})
    res = _launch(bA, maps)
    Y0 = np.concatenate([res[r]["y0T"] for r in R], axis=0)
    def bB(nc, B):
        x1 = _dout(nc, "x1", [TPC, D], F32)
        modt = _din(nc, "mod", [2, 12288], F32)
        phase_out2(B, _din(nc, "yT", [D, TPC], BF16), None, _din(nc, "w", [D, D], F32), _din(nc, "xr", [TPC, D], F32), modt, 0,
                   _din(nc, "lg", [1, D], F32), _din(nc, "lb", [1, D], F32), x1, nc.dram_tensor("rscr", [TPC, D], F32).ap())
        phase_b2(B, x1, modt, _din(nc, "win", [D, 5568], F32), _din(nc, "qg", [1, 896], F32), _din(nc, "kvg", [1, 512], F32),
                 _dout(nc, "latT", [13 * 128, TPC], BF16), _dout(nc, "zsT", [D, TPC], BF16))
    maps = [{"yT": asc(Y0[:, TPC * r:TPC * (r + 1)]), "w": asc(np.asarray(a_w_out[0], f32)), "xr": asc(x2[TPC * r:TPC * (r + 1)]), "mod": mod,
             "lg": asc(np.asarray(ln_g, f32)[0:1]), "lb": asc(np.asarray(ln_b, f32)[0:1]), "win": asc(np.asarray(b_w_in[0], f32)),
             "qg": asc(np.asarray(b_q_norm_g, f32)), "kvg": asc(np.asarray(b_kv_norm_g, f32))} for r in R]
    res = _launch(bB, maps)
    x1s = [res[r]["x1"] for r in R]
    zss = [res[r]["zsT"] for r in R]
    lat = asc(np.concatenate([res[r]["latT"] for r in R], axis=1))
    def bC(nc, B):
        phase_c(B, _din(nc, "lat", [13 * 128, SEQ], BF16), _din(nc, "wq", [896, 1024], F32), _din(nc, "wkv", [512, 1024], F32),
                _din(nc, "pos", [1, SEQ], I32), _din(nc, "invf", [64, 1], F32), _din(nc, "sgn", [64, 1], F32), _dout(nc, "o1T", [512, SEQ], BF16))
    half = np.arange(32, dtype=np.float32) / 32.0
    invf = (10000.0 ** (-half)).astype(f32)
    invf = asc(np.concatenate([invf, invf])[:, None])
    sgn = asc(np.concatenate([-np.ones(32, f32), np.ones(32, f32)])[:, None])
    wqb = np.asarray(b_w_qb[0], f32).reshape(896, 32, 192)
    wkvb = np.asarray(b_w_kvb[0], f32).reshape(512, 32, 256)
    maps = []
    for r in R:
        wq = wqb[:, 4 * r:4 * r + 4]
        wq = np.concatenate([wq, wq[:, :, 160:192], wq[:, :, 128:160]], axis=2)
        maps.append({"lat": lat, "wq": asc(wq.reshape(896, 1024)), "wkv": asc(wkvb[:, 4 * r:4 * r + 4].reshape(512, 1024)),
                     "pos": asc(np.asarray(positions, np.int32)), "invf": invf, "sgn": sgn})
    res = _launch(bC, maps)
    O1 = np.concatenate([res[r]["o1T"] for r in R], axis=0)
    def bD(nc, B):
        phase_out2(B, _din(nc, "yT", [D, TPC], BF16), _din(nc, "zT", [D, TPC], BF16), _din(nc, "w", [D, D], F32), _din(nc, "xr", [TPC, D], F32),
                   _din(nc, "mod", [2, 12288], F32), 1, _din(nc, "lg", [1, D], F32), _din(nc, "lb", [1, D], F32), _dout(nc, "out", [TPC, D], F32),
                   nc.dram_tensor("rscr", [TPC, D], F32).ap())
    maps = [{"yT": asc(O1[:, TPC * r:TPC * (r + 1)]), "zT": zss[r], "w": asc(np.asarray(b_w_out[0], f32)), "xr": x1s[r], "mod": mod,
             "lg": asc(np.asarray(ln_g, f32)[1:2]), "lb": asc(np.asarray(ln_b, f32)[1:2])} for r in R]
    res = _launch(bD, maps)
    return np.concatenate([res[r]["out"] for r in R], axis=0)[None].astype(f32)


def phase_a2p(B, scr, ng_ap, y_ap, ntiles=SEQ // 512, NL=6):
    S, nc = B.S, B.nc
    with ExitStack() as st:
        C = make_consts(B, st)
        banks = [B.ps(st, f"g_ps{i}", [128, 512], F32) for i in range(7)]
        pbt = B.ps(st, "g_pbt", [128, 8, 128], BF16)
        S.excl.update([f"g_ps{i}" for i in range(7)] + ["g_pbt"])
        bcnt = [0]

        def nb():
            i = bcnt[0] % 7
            bcnt[0] += 1
            return banks[i], f"g_ps{i}"

        def v3(t):
            return t[:].rearrange("p (a b) -> p a b", a=4)

        def T3(name, dt=F32):
            return B.sb(st, name, [128, 4, 128], dt)

        triu = B.sb(st, "g_triu", [128, 128], F32)
        su4, nm4, i4 = T3("g_su4"), T3("g_nm4"), T3("g_i4")
        ng = B.sb(st, "g_ng", [128, 1], F32)
        S.op("pool", lambda e: e.memset(triu[:], 1.0), writes=["g_triu"])
        S.op("pool", lambda e: e.affine_select(out=triu[:], in_=triu[:], pattern=[[1, 128]], compare_op=ALU.is_ge, fill=0.0,
                                               base=0, channel_multiplier=-1), reads=["g_triu"], writes=["g_triu"])
        S.op("pool", lambda e: e.memset(su4[:], 1.0), writes=["g_su4"])
        S.op("pool", lambda e: e.affine_select(out=su4[:], in_=su4[:], pattern=[[0, 4], [1, 128]], compare_op=ALU.is_ge, fill=0.0,
                                               base=-1, channel_multiplier=-1), reads=["g_su4"], writes=["g_su4"])
        S.op("pool", lambda e: e.memset(nm4[:], 0.0), writes=["g_nm4"])
        S.op("pool", lambda e: e.affine_select(out=nm4[:], in_=nm4[:], pattern=[[0, 4], [1, 128]], compare_op=ALU.is_ge, fill=NEG,
                                               base=0, channel_multiplier=-1), reads=["g_nm4"], writes=["g_nm4"])
        S.op("pool", lambda e: e.memset(i4[:], 0.0), writes=["g_i4"])
        S.op("pool", lambda e: e.affine_select(out=i4[:], in_=i4[:], pattern=[[0, 4], [-1, 128]], compare_op=ALU.not_equal, fill=1.0,
                                               base=0, channel_multiplier=1), reads=["g_i4"], writes=["g_i4"])
        S.dma("sp", ng[:], ng_ap.rearrange("o p -> p o"), writes=["g_ng"])
        S.op("dve", lambda e: e.tensor_scalar(out=ng[:], in0=ng[:], scalar1=float(128.0 ** 0.5), scalar2=None, op0=ALU.mult),
             reads=["g_ng"], writes=["g_ng"])
        S32 = T3("g_S32")
        Sbf = T3("g_Sbf", BF16)
        S.op("pool", lambda e: e.memset(S32[:], 0.0), writes=["g_S32"])
        S.op("pool", lambda e: e.memset(Sbf[:], 0.0), writes=["g_Sbf"])
        inb = {n: [B.sb(st, f"g_in{n}{i}", [128, 4, 512], BF16) for i in range(2)] for n in "qkvz"}
        gbb = [B.sb(st, f"g_gb{i}", [128, 4, 8], F32) for i in range(2)]
        ybuf = [B.sb(st, f"g_y{i}", [128, 4, 512], BF16) for i in range(2)]
        P_ = []
        for p in range(2):
            d = dict(p=p)
            d["kvtm"] = B.sb(st, f"g_kvtm{p}", [128, 8, 128], BF16)
            d["gc"] = B.sb(st, f"g_gc{p}", [128, 2, 4], F32)
            d["ngc"] = B.sb(st, f"g_ngc{p}", [128, 4], F32)
            d["ed"] = B.sb(st, f"g_ed{p}", [128, 4], F32)
            for n in ("gm", "E", "GM", "DT", "DTS", "U", "UT", "W0", "W1", "WT0", "WT1", "P0", "P1"):
                d[n] = T3(f"g_{n}{p}")
            for n in ("TpT", "qkT", "kgT", "qdT", "kdec"):
                d[n] = T3(f"g_{n}{p}", BF16)
            P_.append(d)
        R, vnew, sq = T3("g_R", BF16), T3("g_vnew", BF16), T3("g_sq", BF16)
        rn, yt = T3("g_rn"), T3("g_yt")

        def bc(ap2):
            return ap2.unsqueeze(2).broadcast_to([128, 4, 128])

        def K(d, n):
            return f"g_{n}{d['p']}"

        def pre_a(d, T, c, qT, kT, vT, gb, kq, kk, kv, kgb):
            cs = slice(c * 128, (c + 1) * 128)
            kvtm, gc, ngc, ed, gm, E, GM, DT, DTS, U, UT = (d[n] for n in ("kvtm", "gc", "ngc", "ed", "gm", "E", "GM", "DT", "DTS", "U", "UT"))
            for h in range(4):
                S.op("pe", lambda e, h=h: e.transpose(pbt[:, h, :], kT[:, h, cs], C["idb"][:]), reads=[kk, "c_idb"], writes=["g_pbt"])
            for h in range(4):
                S.op("pe", lambda e, h=h: e.transpose(pbt[:, 4 + h, :], vT[:, h, cs], C["idb"][:]), reads=[kv, "c_idb"], writes=["g_pbt"])
            S.op("act", lambda e: e.copy(out=kvtm[:], in_=pbt[:]), reads=["g_pbt"], writes=[K(d, "kvtm")])
            pa, pak = nb()
            S.op("pe", lambda e: e.matmul(pa[:, 0:4], triu[:], gb[:, c, 0:4], start=True, stop=True), reads=["g_triu", kgb], writes=[pak])
            S.op("pe", lambda e: e.matmul(pa[:, 4:8], C["onesf"][:], gb[:, c, 0:4], start=True, stop=True), reads=["c_onesf", kgb], writes=[pak])
            S.op("dve", lambda e: e.tensor_copy(out=gc[:].rearrange("p a b -> p (a b)"), in_=pa[:, 0:8]), reads=[pak], writes=[K(d, "gc")])
            S.op("dve", lambda e: e.tensor_scalar(out=ngc[:], in0=gc[:, 0, :], scalar1=-1.0, scalar2=None, op0=ALU.mult), reads=[K(d, "gc")], writes=[K(d, "ngc")])
            S.op("dve", lambda e: e.tensor_tensor(out=ed[:], in0=gc[:, 1, :], in1=gc[:, 0, :], op=ALU.subtract), reads=[K(d, "gc")], writes=[K(d, "ed")])
            S.op("act", lambda e: e.activation(out=ed[:], in_=ed[:], func=AF.Exp), reads=[K(d, "ed")], writes=[K(d, "ed")])
            S.op("pool", lambda e: e.tensor_tensor(out=gm[:], in0=triu[:].unsqueeze(1).broadcast_to([128, 4, 128]),
                                                   in1=bc(gb[:, c, 0:4]), op=ALU.mult), reads=["g_triu", kgb], writes=[K(d, "gm")])
            pg, pgk = nb()
            S.op("pe", lambda e: e.matmul(pg[:], C["onesf"][:], fl(gm[:]), start=True, stop=True), reads=["c_onesf", K(d, "gm")], writes=[pgk])
            S.op("act", lambda e: e.activation(out=fl(E[:]), in_=pg[:], func=AF.Exp), reads=[pgk], writes=[K(d, "E")])
            S.op("dve", lambda e: e.tensor_tensor(out=fl(GM[:]), in0=pg[:], in1=fl(nm4[:]), op=ALU.add), reads=[pgk, "g_nm4"], writes=[K(d, "GM")])
            for h in range(4):
                S.op("act", lambda e, h=h: e.activation(out=DT[:, h, :], in_=GM[:, h, :], func=AF.Exp, bias=ngc[:, h:h + 1], scale=1.0),
                     reads=[K(d, "GM"), K(d, "ngc")], writes=[K(d, "DT")])
            S.op("pool", lambda e: e.tensor_tensor(out=DTS[:], in0=DT[:], in1=su4[:], op=ALU.mult), reads=[K(d, "DT"), "g_su4"], writes=[K(d, "DTS")])
            S.op("pool", lambda e: e.tensor_tensor(out=DTS[:], in0=DTS[:], in1=bc(gb[:, c, 4:8]), op=ALU.mult), reads=[K(d, "DTS"), kgb], writes=[K(d, "DTS")])
            pk, pkk = nb()
            for h in range(4):
                S.op("pe", lambda e, h=h: e.matmul(v3(pk)[:, h, :], kT[:, h, cs], kT[:, h, cs], start=True, stop=True), reads=[kk], writes=[pkk])
            pq, pqk = nb()
            for h in range(4):
                S.op("pe", lambda e, h=h: e.matmul(v3(pq)[:, h, :], kT[:, h, cs], qT[:, h, cs], start=True, stop=True), reads=[kk, kq], writes=[pqk])
            S.op("dve", lambda e: e.tensor_tensor(out=fl(U[:]), in0=pk[:], in1=fl(DTS[:]), op=ALU.mult), reads=[pkk, K(d, "DTS")], writes=[K(d, "U")])
            S.op("dve", lambda e: e.tensor_tensor(out=fl(d["qkT"][:]), in0=pq[:], in1=fl(DT[:]), op=ALU.mult), reads=[pqk, K(d, "DT")], writes=[K(d, "qkT")])
            pu, puk = nb()
            for h in range(4):
                S.op("pe", lambda e, h=h: e.transpose(v3(pu)[:, h, :], U[:, h, :], C["idf"][:]), reads=[K(d, "U"), "c_idf"], writes=[puk])
            S.op("act", lambda e: e.copy(out=fl(UT[:]), in_=pu[:]), reads=[puk], writes=[K(d, "UT")])
            S.op("pool", lambda e: e.tensor_tensor(out=d["P0"][:], in0=i4[:], in1=U[:], op=ALU.subtract), reads=["g_i4", K(d, "U")], writes=[K(d, "P0")])
            d["cur"] = (U, UT, d["P0"], K(d, "U"), K(d, "UT"), K(d, "P0"))
            S.op("pool", lambda e: e.tensor_tensor(out=d["kgT"][:], in0=kT[:, :, cs], in1=E[:], op=ALU.mult), reads=[kk, K(d, "E")], writes=[K(d, "kgT")])
            S.op("pool", lambda e: e.tensor_tensor(out=d["qdT"][:], in0=qT[:, :, cs], in1=E[:], op=ALU.mult), reads=[kq, K(d, "E")], writes=[K(d, "qdT")])
            S.op("pool", lambda e: e.tensor_tensor(out=d["kdec"][:], in0=kvtm[:, 0:4, :], in1=bc(ed[:]), op=ALU.mult), reads=[K(d, "kvtm"), K(d, "ed")], writes=[K(d, "kdec")])

        def level(d, l):
            W, WT, P, Wk, WTk, Pk = d["cur"]
            Wn, WTn, Pn = d[f"W{l % 2}"], d[f"WT{l % 2}"], d[f"P{l % 2}"]
            Wnk, WTnk, Pnk = K(d, f"W{l % 2}"), K(d, f"WT{l % 2}"), K(d, f"P{l % 2}")
            if l < NL:
                pw, pwk = nb()
                for h in range(4):
                    S.op("pe", lambda e, h=h: e.matmul(v3(pw)[:, h, :], WT[:, h, :], W[:, h, :], start=True, stop=True), reads=[Wk, WTk], writes=[pwk])
            pwt, pwtk = nb()
            for h in range(4):
                S.op("pe", lambda e, h=h: e.matmul(v3(pwt)[:, h, :], W[:, h, :], WT[:, h, :], start=True, stop=True), reads=[Wk, WTk], writes=[pwtk])
            if l < NL:
                S.op("act", lambda e: e.copy(out=fl(Wn[:]), in_=pw[:]), reads=[pwk], writes=[Wnk])
            S.op("dve", lambda e: e.tensor_copy(out=fl(WTn[:]), in_=pwt[:]), reads=[pwtk], writes=[WTnk])
            pp, ppk = nb()
            for h in range(4):
                S.op("pe", lambda e, h=h: e.matmul(v3(pp)[:, h, :], WTn[:, h, :], P[:, h, :], start=True, stop=True), reads=[WTnk, Pk], writes=[ppk])
            if l < NL:
                S.op("dve", lambda e: e.tensor_tensor(out=fl(Pn[:]), in0=pp[:], in1=fl(P[:]), op=ALU.add), reads=[ppk, Pk], writes=[Pnk])
            else:
                S.op("dve", lambda e: e.tensor_tensor(out=fl(d["TpT"][:]), in0=pp[:], in1=fl(P[:]), op=ALU.add), reads=[ppk, Pk], writes=[K(d, "TpT")])
            d["cur"] = (Wn, WTn, Pn, Wnk, WTnk, Pnk)

        def seq(d, T, c, zs, gb, kz, kgb, bi):
            cs = slice(c * 128, (c + 1) * 128)
            kvtm, E = d["kvtm"], d["E"]
            p1, p1k = nb()
            for h in range(4):
                S.op("pe", lambda e, h=h: e.matmul(v3(p1)[:, h, :], d["kgT"][:, h, :], Sbf[:, h, :], start=True, stop=True), reads=[K(d, "kgT"), "g_Sbf"], writes=[p1k])
            S.op("dve", lambda e: e.tensor_tensor(out=R[:], in0=kvtm[:, 4:8, :], in1=v3(p1), op=ALU.subtract), reads=[K(d, "kvtm"), p1k], writes=["g_R"])
            p2, p2k = nb()
            for h in range(4):
                S.op("pe", lambda e, h=h: e.matmul(v3(p2)[:, h, :], d["TpT"][:, h, :], R[:, h, :], start=True, stop=True), reads=[K(d, "TpT"), "g_R"], writes=[p2k])
            S.op("dve", lambda e: e.tensor_tensor(out=vnew[:], in0=v3(p2), in1=bc(gb[:, c, 4:8]), op=ALU.mult), reads=[p2k, kgb], writes=["g_vnew"])
            po, pok = nb()
            for h in range(4):
                S.op("pe", lambda e, h=h: e.matmul(v3(po)[:, h, :], Sbf[:, h, :], d["qdT"][:, h, :], start=True, stop=False), reads=["g_Sbf", K(d, "qdT")], writes=[pok])
                S.op("pe", lambda e, h=h: e.matmul(v3(po)[:, h, :], vnew[:, h, :], d["qkT"][:, h, :], start=False, stop=True), reads=["g_vnew", K(d, "qkT")], writes=[pok])
            p3, p3k = nb()
            for h in range(4):
                S.op("pe", lambda e, h=h: e.matmul(v3(p3)[:, h, :], d["kdec"][:, h, :], vnew[:, h, :], start=True, stop=True), reads=[K(d, "kdec"), "g_vnew"], writes=[p3k])
            for h in range(4):
                S.op("dve", lambda e, h=h: e.scalar_tensor_tensor(out=S32[:, h, :], in0=S32[:, h, :], scalar=E[:, h, 127:128], in1=v3(p3)[:, h, :],
                                                                  op0=ALU.mult, op1=ALU.add), reads=["g_S32", K(d, "E"), p3k], writes=["g_S32"])
            S.op("act", lambda e: e.copy(out=Sbf[:], in_=S32[:]), reads=["g_S32"], writes=["g_Sbf"])
            S.op("act", lambda e: e.activation(out=fl(sq[:]), in_=po[:], func=AF.Square), reads=[pok], writes=["g_sq"])
            pss, pssk = nb()
            S.op("pe", lambda e: e.matmul(pss[:], C["onesb"][:], fl(sq[:]), start=True, stop=True), reads=["c_onesb", "g_sq"], writes=[pssk])
            S.op("act", lambda e: e.activation(out=fl(rn[:]), in_=pss[:], func=AF.Sqrt, bias=128.0 * RMS_EPS, scale=1.0), reads=[pssk], writes=["g_rn"])
            S.op("dve", lambda e: e.reciprocal(out=rn[:], in_=rn[:]), reads=["g_rn"], writes=["g_rn"])
            S.op("dve", lambda e: e.scalar_tensor_tensor(out=fl(yt[:]), in0=po[:], scalar=ng[:, 0:1], in1=fl(rn[:]), op0=ALU.mult, op1=ALU.mult),
                 reads=[pok, "g_ng", "g_rn"], writes=["g_yt"])
            S.op("pool", lambda e: e.tensor_tensor(out=ybuf[bi][:, :, cs], in0=yt[:], in1=zs[:, :, cs], op=ALU.mult),
                 reads=["g_yt", kz], writes=[f"g_y{bi}"])

        for T in range(ntiles):
            bi = T % 2
            for n in "qkvz":
                S.dma("sp", inb[n][bi][:], scr[n + "s"].rearrange("h d t -> d h t")[:, :, T * 512:(T + 1) * 512],
                      reads=[f"scr_{n}"], writes=[f"g_in{n}{bi}"])
            S.dma("sp", gbb[bi][:], scr["gbs"].rearrange("(n p) e -> p n e", p=128)[:, T * 4:(T + 1) * 4, :],
                  reads=["scr_gb"], writes=[f"g_gb{bi}"])
            qT, kT, vT, zs, gb = inb["q"][bi], inb["k"][bi], inb["v"][bi], inb["z"][bi], gbb[bi]
            kq, kk, kv, kz, kgb = (f"g_inq{bi}", f"g_ink{bi}", f"g_inv{bi}", f"g_inz{bi}", f"g_gb{bi}")
            for c0 in (0, 2):
                for p in range(2):
                    pre_a(P_[p], T, c0 + p, qT, kT, vT, gb, kq, kk, kv, kgb)
                for l in range(1, NL + 1):
                    for p in range(2):
                        level(P_[p], l)
                for p in range(2):
                    seq(P_[p], T, c0 + p, zs, gb, kz, kgb, bi)
            B.finals.append(S.dma(STQ, y_ap.rearrange("(h d) t -> d h t", d=128)[:, :, T * 512:(T + 1) * 512], ybuf[bi][:],
                                  reads=[f"g_y{bi}"], writes=["y_out"]))
        S.flush(B.es)
```

```python
import numpy as np
from contextlib import ExitStack
import concourse.bass as bass
import concourse.mybir as mybir
from concourse.bass_utils import run_bass_kernel_spmd

F32 = mybir.dt.float32
BF16 = mybir.dt.bfloat16
I32 = mybir.dt.int32
AF = mybir.ActivationFunctionType
ALU = mybir.AluOpType
AX = mybir.AxisListType

NCORES = 8
SEQ = 8192
D = 4096
TPC = SEQ // NCORES
ALPHA = (2.0 * 2) ** 0.25
RMS_EPS = 1e-6
LN_EPS = 1e-5
NEG = -30000.0

ENGS = ("pe", "act", "dve", "pool", "sp")
SEM_CHUNK = 4000
STQ = "sp"
HT_ACT = False


class Sched:
    def __init__(self, nc):
        self.nc = nc
        self.ops = {e: [] for e in ENGS}
        self.writers = {}
        self.readers = {}
        self.dmacnt = {}
        self.nops = 0
        self.excl = set()

    def _collect(self, eng, reads, writes):
        deps = []
        for k in reads:
            for t in self.writers.get(k, ()):
                deps.append((t, "raw"))
            if k in self.excl:
                for t in self.readers.get(k, ()):
                    deps.append((t, "war"))
        for k in writes:
            for t in self.writers.get(k, ()):
                deps.append((t, "waw"))
            for t in self.readers.get(k, ()):
                deps.append((t, "war"))
        return deps

    def _commit(self, tok, reads, writes, partial):
        for k in writes:
            if partial:
                self.writers.setdefault(k, []).append(tok)
            else:
                self.writers[k] = [tok]
            self.readers[k] = []
        for k in reads:
            self.readers.setdefault(k, []).append(tok)

    def op(self, eng, fn, reads=(), writes=(), partial=False):
        deps = self._collect(eng, reads, writes)
        tok = ("E", eng, self.nops)
        self.ops[eng].append(dict(fn=fn, deps=deps, tok=tok, dma=None))
        self._commit(tok, reads, writes, partial)
        self.nops += 1
        return tok

    def dma(self, eng, out, in_, reads=(), writes=(), sem=None, partial=False, **kw):
        semname = sem or (writes[0] if writes else reads[0])
        deps = self._collect(eng, reads, writes)
        self.dmacnt[semname] = self.dmacnt.get(semname, 0) + 1
        tok = ("D", semname, self.dmacnt[semname] * 16)
        fn = lambda e, out=out, in_=in_, kw=kw: e.dma_start(out=out, in_=in_, **kw)
        self.ops[eng].append(dict(fn=fn, deps=deps, tok=tok, dma=semname))
        self._commit(tok, reads, writes, partial)
        self.nops += 1
        return tok

    def coll(self, kind, src, dst, reads, writes, name, inc=1):
        deps = self._collect("pool", reads, writes)
        self.dmacnt[name] = self.dmacnt.get(name, 0)
        self.collinc = getattr(self, "collinc", {})
        self.collinc[name] = self.collinc.get(name, 0) + inc
        tok = ("C", name, self.collinc[name])
        fn = lambda e: e.collective_compute(kind, ALU.bypass, replica_groups=[list(range(NCORES))], ins=[src], outs=[dst])
        self.ops["pool"].append(dict(fn=fn, deps=deps, tok=tok, dma=name, inc=inc))
        self._commit(tok, reads, writes, False)
        self.nops += 1
        return tok

    def flush(self, es, final_waits=(), barrier=True):
        nc = self.nc
        if not hasattr(self, "sems"):
            self.sems = {}
            self.nflag = {e: 0 for e in ENGS}
        sems = self.sems
        flagged = set()
        for e in ENGS:
            for o in self.ops[e]:
                keep = []
                for (t, kind) in o["deps"]:
                    if t[0] == "E" and t[1] == e and o["dma"] is None:
                        if e == "pe" or kind == "war":
                            continue
                    keep.append(t)
                    if t[0] == "E":
                        flagged.add(t)
                o["deps"] = keep
        lasts = []
        if barrier:
            for e in ENGS:
                for o in reversed(self.ops[e]):
                    if o["dma"] is None:
                        flagged.add(o["tok"])
                        lasts.append(o["tok"])
                        break
        for t in final_waits:
            if t[0] == "E":
                flagged.add(t)
        tokval = {}

        def getsem(key):
            if key not in sems:
                sems[key] = es.enter_context(nc.semaphore("s_" + "_".join(str(x) for x in key)))
            return sems[key]

        for e in ENGS:
            for o in self.ops[e]:
                if o["tok"] in flagged:
                    n = self.nflag[e]
                    tokval[o["tok"]] = ((e, n // SEM_CHUNK), n % SEM_CHUNK + 1)
                    self.nflag[e] = n + 1
                    getsem((e, n // SEM_CHUNK))
        for name in self.dmacnt:
            getsem(("D", name))

        def resolve(t):
            if t[0] == "E":
                return tokval[t]
            return (("D", t[1]), t[2])

        collinc = getattr(self, "collinc", {})

        endw = [resolve(t) for t in list(final_waits) + lasts]
        if barrier:
            endw += [(("D", name), c * 16 + collinc.get(name, 0)) for name, c in self.dmacnt.items()]
        ops = self.ops
        with nc.Block() as block:
            engobj = {"pe": block.tensor, "act": block.scalar, "dve": block.vector, "pool": block.gpsimd,
                      "sp": block.sync}

            def make(e):
                def body(eng):
                    waited = {}
                    for o in ops[e]:
                        need = {}
                        for t in o["deps"]:
                            s, v = resolve(t)
                            if waited.get(s, 0) >= v:
                                continue
                            need[s] = max(need.get(s, 0), v)
                        for s, v in need.items():
                            eng.wait_ge(sems[s], v)
                            waited[s] = v
                        ins = o["fn"](eng)
                        if o["dma"] is not None:
                            ins.then_inc(sems[("D", o["dma"])], o.get("inc", 16))
                        elif o["tok"] in tokval:
                            ins.then_inc(sems[tokval[o["tok"]][0]], 1)
                    for s, v in endw:
                        if waited.get(s, 0) < v:
                            eng.wait_ge(sems[s], v)
                            waited[s] = v
                return body

            for e in ENGS:
                engobj[e](make(e))
        self.ops = {e: [] for e in ENGS}
        self.writers = {}
        self.readers = {}


class Builder:
    def __init__(self, nc, es):
        self.nc = nc
        self.es = es
        self.S = Sched(nc)
        self.finals = []

    def sb(self, st, name, shape, dt):
        self.uid = getattr(self, "uid", 0) + 1
        return st.enter_context(self.nc.sbuf_tensor(f"{name}_u{self.uid}", list(shape), dt))

    def ps(self, st, name, shape, dt=F32):
        self.uid = getattr(self, "uid", 0) + 1
        return st.enter_context(self.nc.psum_tensor(f"{name}_u{self.uid}", list(shape), dt))


def make_consts(B, st):
    S = B.S
    C = {}
    C["idf"] = B.sb(st, "c_idf", [128, 128], F32)
    C["idb"] = B.sb(st, "c_idb", [128, 128], BF16)
    C["onesf"] = B.sb(st, "c_onesf", [128, 128], F32)
    C["onesb"] = B.sb(st, "c_onesb", [128, 128], BF16)
    S.op("pool", lambda e: e.memset(C["idf"][:], 0.0), writes=["c_idf"])
    S.op("pool", lambda e: e.affine_select(out=C["idf"][:], in_=C["idf"][:], pattern=[[-1, 128]],
                                           compare_op=ALU.not_equal, fill=1.0, base=0, channel_multiplier=1),
         reads=["c_idf"], writes=["c_idf"])
    S.op("pool", lambda e: e.tensor_copy(out=C["idb"][:], in_=C["idf"][:]), reads=["c_idf"], writes=["c_idb"])
    S.op("pool", lambda e: e.memset(C["onesf"][:], 1.0), writes=["c_onesf"])
    S.op("pool", lambda e: e.memset(C["onesb"][:], 1.0), writes=["c_onesb"])
    return C


def phase_mod(B, c_ap, wm_ap, bm_ap, out_ap):
    S, nc = B.S, B.nc
    with ExitStack() as st:
        C = make_consts(B, st)
        ct = B.sb(st, "m_ct", [32, 128], F32)
        cact = B.sb(st, "m_cact", [128, 32], F32)
        bmt = B.sb(st, "m_bm", [1, 3072], F32)
        rowt = B.sb(st, "m_row", [1, 3072], F32)
        wt = [B.sb(st, f"m_w{i}", [128, 3072], F32) for i in range(3)]
        pT = B.ps(st, "m_pT", [128, 512], F32)
        pr = [B.ps(st, f"m_pr{i}", [128, 512], F32) for i in range(6)]
        S.dma("sp", ct[:], c_ap.rearrange("o (kc p) -> (o kc) p", p=128), writes=["m_ct"])
        S.dma("sp", bmt[:], bm_ap, writes=["m_bm"])
        S.op("pe", lambda e: e.transpose(pT[:, 0:32], ct[:], C["idf"][0:32, 0:32]), reads=["m_ct", "c_idf"], writes=["m_pT"])
        S.op("act", lambda e: e.activation(out=cact[:], in_=pT[:, 0:32], func=AF.Silu), reads=["m_pT"], writes=["m_cact"])
        for kc in range(32):
            w = wt[kc % 3]
            S.dma("sp" if kc % 2 == 0 else "act", w[:], wm_ap[kc * 128:(kc + 1) * 128, :], writes=[f"m_w{kc % 3}"])
            for nb in range(6):
                S.op("pe", lambda e, w=w, nb=nb, kc=kc: e.matmul(pr[nb][0:1, :], cact[:, kc:kc + 1], w[:, nb * 512:(nb + 1) * 512],
                                                                 start=(kc == 0), stop=(kc == 31)),
                     reads=[f"m_w{kc % 3}", "m_cact"], writes=[f"m_pr{nb}"])
        for nb in range(6):
            S.op("dve", lambda e, nb=nb: e.tensor_tensor(out=rowt[0:1, nb * 512:(nb + 1) * 512], in0=pr[nb][0:1, :],
                                                         in1=bmt[0:1, nb * 512:(nb + 1) * 512], op=ALU.add),
                 reads=[f"m_pr{nb}", "m_bm"], writes=["m_row"])
        t = S.dma("sp", out_ap, rowt[:], reads=["m_row"], writes=["m_out"])
        S.flush(B.es, final_waits=[t])


def load_mod_fm(B, st, C, mod_ap, layer, pt, ptkey):
    S = B.S
    rows = B.sb(st, f"md_rows{layer}", [64, 128], F32)
    sh = B.sb(st, f"md_sh{layer}", [128, 32], F32)
    sc = B.sb(st, f"md_sc{layer}", [128, 32], F32)
    S.dma("sp", rows[:], mod_ap[layer:layer + 1, 0:8192].rearrange("o (kc p) -> (o kc) p", p=128), writes=[f"md_rows{layer}"])
    S.op("pe", lambda e: e.transpose(pt[:, 0:64], rows[:], C["idf"][0:64, 0:64]), reads=[f"md_rows{layer}", "c_idf"], writes=[ptkey])
    S.op("dve", lambda e: e.tensor_copy(out=sh[:], in_=pt[:, 0:32]), reads=[ptkey], writes=[f"md_sh{layer}"])
    S.op("dve", lambda e: e.tensor_scalar(out=sc[:], in0=pt[:, 32:64], scalar1=1.0, scalar2=None, op0=ALU.add),
         reads=[ptkey], writes=[f"md_sc{layer}"])
    return sh, sc


def make_hT(B, C, xs, xkey, hT, hkey, st_idx, sh, sc, shk, sck, ptr, ptk, cnt):
    S = B.S
    for g in range(8):
        pt = ptr[cnt[0] % len(ptr)]
        pk = ptk[cnt[0] % len(ptr)]
        cnt[0] += 1
        for j in range(4):
            kc = g * 4 + j
            S.op("pe", lambda e, pt=pt, j=j, kc=kc: e.transpose(pt[:, j, :], xs[:, kc * 128:(kc + 1) * 128], C["idf"][:]),
                 reads=[xkey, "c_idf"], writes=[pk])
        for j in range(4):
            kc = g * 4 + j
            dst = hT[:, kc, st_idx * 128:(st_idx + 1) * 128]
            if HT_ACT and kc % 2 == 0:
                S.op("act", lambda e, pt=pt, j=j, kc=kc, dst=dst: e.activation(out=dst, in_=pt[:, j, :], func=AF.Identity,
                                                                               bias=sh[:, kc:kc + 1], scale=sc[:, kc:kc + 1]),
                     reads=[pk, shk, sck], writes=[f"{hkey}{kc}"])
            else:
                S.op("dve", lambda e, pt=pt, j=j, kc=kc, dst=dst: e.scalar_tensor_tensor(out=dst, in0=pt[:, j, :], scalar=sc[:, kc:kc + 1],
                                                                                         in1=sh[:, kc:kc + 1].broadcast_to([128, 128]),
                                                                                         op0=ALU.mult, op1=ALU.add),
                     reads=[pk, shk, sck], writes=[f"{hkey}{kc}"])


def phase_a1(B, x_ap, mod_ap, w4_ap, wab_ap, cw_ap, alog_ap, dtb_ap, scr, ntiles=SEQ // 512, stage=9):
    S, nc = B.S, B.nc
    with ExitStack() as st:
        C = make_consts(B, st)
        ptr = [B.ps(st, f"a_pt{i}", [128, 4, 128], F32) for i in range(2)]
        ptk = [f"a_pt{i}" for i in range(2)]
        psp = [B.ps(st, f"a_pp{i}", [128, 512], F32) for i in range(4)]
        pgb = B.ps(st, "a_pgb", [128, 512], F32)
        pss = B.ps(st, "a_pss", [128, 512], F32)
        S.excl.update(["a_pt0", "a_pt1", "a_pp0", "a_pp1", "a_pp2", "a_pp3", "a_pgb", "a_pss"])
        sh, sc = load_mod_fm(B, st, C, mod_ap, 0, pgb, "a_pgb")
        xs = [B.sb(st, f"a_xs{i}", [128, D], F32) for i in range(2)]
        hT = B.sb(st, "a_hT", [128, 32, 512], BF16)
        Wt = [B.sb(st, f"a_W{i}", [128, 16, 512], BF16) for i in range(3)]
        pc = B.sb(st, "a_pc", [128, 12, 515], F32)
        tmp = [B.sb(st, f"a_tmp{i}", [128, 512], F32) for i in range(2)]
        sl = [B.sb(st, f"a_sl{i}", [128, 512], F32) for i in range(4)]
        sqb4 = [B.sb(st, f"a_sqb{i}", [128, 512], BF16) for i in range(4)]
        rn = B.sb(st, "a_rn", [128, 512], F32)
        outs = {n: B.sb(st, f"a_o{n}", [128, 4, 512], BF16) for n in "qkvz"}
        wabf = B.sb(st, "a_wabf", [128, 32, 8], F32)
        wab = B.sb(st, "a_wab", [128, 32, 8], BF16)
        cwr = B.sb(st, "a_cwr", [48, 128], F32)
        cwT = B.sb(st, "a_cwT", [128, 48], F32)
        alb = B.sb(st, "a_alb", [128, 4], F32)
        negA = B.sb(st, "a_negA", [128, 4], F32)
        dtb = B.sb(st, "a_dtb", [128, 4], F32)
        gbt = B.sb(st, "a_gbt", [128, 4, 8], F32)
        t4 = [B.sb(st, f"a_t4{i}", [128, 4], F32) for i in range(2)]
        S.dma("sp", wabf[:], wab_ap.rearrange("(kc p) n -> p kc n", p=128), writes=["a_wabf"])
        S.op("dve", lambda e: e.tensor_copy(out=wab[:], in_=wabf[:]), reads=["a_wabf"], writes=["a_wab"])
        S.dma("sp", cwr[:], cw_ap.rearrange("j (b p) -> (j b) p", p=128), writes=["a_cwr"])
        S.op("pe", lambda e: e.transpose(pss[:, 0:48], cwr[:], C["idf"][0:48, 0:48]), reads=["a_cwr", "c_idf"], writes=["a_pss"])
        S.op("dve", lambda e: e.tensor_copy(out=cwT[:], in_=pss[:, 0:48]), reads=["a_pss"], writes=["a_cwT"])
        S.dma("sp", alb[:], alog_ap.partition_broadcast(128), writes=["a_alb"])
        S.dma("sp", dtb[:], dtb_ap.partition_broadcast(128), writes=["a_dtb"])
        S.op("act", lambda e: e.activation(out=negA[:], in_=alb[:], func=AF.Exp), reads=["a_alb"], writes=["a_negA"])
        S.op("dve", lambda e: e.tensor_scalar(out=negA[:], in0=negA[:], scalar1=-1.0, scalar2=None, op0=ALU.mult),
             reads=["a_negA"], writes=["a_negA"])
        S.op("pool", lambda e: e.memset(pc[:], 0.0), writes=[f"a_pc{b}" for b in range(12)])
        xv = x_ap.rearrange("(n p) d -> n p d", p=128)
        w4v = w4_ap.rearrange("(kc p) n -> p kc n", p=128)
        cnt = [0]
        wcnt = 0
        pend = []
        pend2 = []
        qk_scale = 128.0 ** -0.5
        for T in range(ntiles if stage > 0 else 0):
            for stx in range(4):
                n = T * 4 + stx
                xb = xs[n % 2]
                S.dma("sp", xb[:], xv[n], writes=[f"a_xs{n % 2}"])
                make_hT(B, C, xb, f"a_xs{n % 2}", hT, "a_hT", stx, sh, sc, "md_sh0", "md_sc0", ptr, ptk, cnt)
            for stx in range(4 if stage > 1 else 0):
                for kc in range(32):
                    S.op("pe", lambda e, kc=kc, stx=stx: e.matmul(pgb[:, 0:8], hT[:, kc, stx * 128:(stx + 1) * 128], wab[:, kc, :],
                                                                    start=(kc == 0), stop=(kc == 31)),
                         reads=[f"a_hT{kc}", "a_wab"], writes=["a_pgb"])
                S.op("act", lambda e: e.activation(out=t4[0][:], in_=pgb[:, 0:4], func=AF.Exp, scale=-1.0), reads=["a_pgb"], writes=["a_t40"])
                S.op("dve", lambda e: e.tensor_tensor(out=t4[1][:], in0=pgb[:, 4:8], in1=dtb[:], op=ALU.add), reads=["a_pgb", "a_dtb"], writes=["a_t41"])
                S.op("dve", lambda e: e.tensor_scalar(out=t4[0][:], in0=t4[0][:], scalar1=1.0, scalar2=None, op0=ALU.add), reads=["a_t40"], writes=["a_t40"])
                S.op("dve", lambda e, stx=stx: e.reciprocal(out=gbt[:, stx, 4:8], in_=t4[0][:]), reads=["a_t40"], writes=["a_gbt"])
                S.op("act", lambda e: e.activation(out=t4[1][:], in_=t4[1][:], func=AF.Exp), reads=["a_t41"], writes=["a_t41"])
                S.op("act", lambda e: e.activation(out=t4[1][:], in_=t4[1][:], func=AF.Ln, bias=1.0, scale=1.0), reads=["a_t41"], writes=["a_t41"])
                S.op("dve", lambda e, stx=stx: e.tensor_tensor(out=gbt[:, stx, 0:4], in0=t4[1][:], in1=negA[:], op=ALU.mult),
                     reads=["a_t41", "a_negA"], writes=["a_gbt"])
            if stage > 1:
                S.dma("sp", scr["gbs"].rearrange("(n p) e -> p n e", p=128)[:, T * 4:(T + 1) * 4, :], gbt[:], reads=["a_gbt"], writes=["scr_gb"])
            for cb in range(4 if stage > 2 else 0):
                for half in range(2):
                    slot = wcnt % 3
                    wcnt += 1
                    S.dma("pool", Wt[slot][:], w4v[:, half * 16:(half + 1) * 16, cb * 512:(cb + 1) * 512], writes=[f"a_W{slot}"])
                    for sbk in range(4):
                        for k16 in range(16):
                            kc = half * 16 + k16
                            S.op("pe", lambda e, slot=slot, sbk=sbk, k16=k16, kc=kc, half=half: e.matmul(
                                psp[sbk][:], Wt[slot][:, k16, sbk * 128:(sbk + 1) * 128], hT[:, kc, :],
                                start=(kc == 0), stop=(kc == 31)),
                                reads=[f"a_W{slot}", f"a_hT{kc}"], writes=[f"a_pp{sbk}"])
                while pend2:
                    pend2.pop(0)()
                while pend:
                    pend.pop(0)()
                for sbk in range(4):
                    b = cb * 4 + sbk
                    if cb == 3:
                        S.op("act", lambda e, sbk=sbk: e.activation(out=outs["z"][:, sbk, :], in_=psp[sbk][:], func=AF.Silu),
                             reads=[f"a_pp{sbk}"], writes=["a_oz"])
                        continue
                    ev = "act" if sbk % 2 == 0 else "dve"
                    if ev == "act":
                        S.op("act", lambda e, b=b, sbk=sbk: e.copy(out=pc[:, b, 3:515], in_=psp[sbk][:]), reads=[f"a_pp{sbk}"], writes=[f"a_pc{b}"])
                    else:
                        S.op("dve", lambda e, b=b, sbk=sbk: e.tensor_copy(out=pc[:, b, 3:515], in_=psp[sbk][:]), reads=[f"a_pp{sbk}"], writes=[f"a_pc{b}"])
                    if stage < 4:
                        continue
                    ce = "dve"
                    tb = tmp[b % 2]
                    tk = f"a_tmp{b % 2}"
                    S.op(ce, lambda e, b=b, tb=tb: e.tensor_scalar(out=tb[:], in0=pc[:, b, 3:515], scalar1=cwT[:, 3 * 12 + b:3 * 12 + b + 1],
                                                                   scalar2=None, op0=ALU.mult), reads=[f"a_pc{b}", "a_cwT"], writes=[tk])
                    for j in (2, 1, 0):
                        S.op(ce, lambda e, b=b, tb=tb, j=j: e.scalar_tensor_tensor(out=tb[:], in0=pc[:, b, j:j + 512],
                                                                                 scalar=cwT[:, j * 12 + b:j * 12 + b + 1], in1=tb[:],
                                                                                 op0=ALU.mult, op1=ALU.add),
                             reads=[f"a_pc{b}", "a_cwT", tk], writes=[tk])
                    S.op(ce, lambda e, b=b: e.tensor_copy(out=pc[:, b, 0:3], in_=pc[:, b, 512:515]), reads=[f"a_pc{b}"], writes=[f"a_pc{b}"])
                    if stage < 5:
                        continue
                    if cb == 2:
                        S.op("act", lambda e, tb=tb, sbk=sbk: e.activation(out=outs["v"][:, sbk, :], in_=tb[:], func=AF.Silu),
                             reads=[tk], writes=["a_ov"])
                        continue
                    slb = sl[sbk]
                    sk = f"a_sl{sbk}"
                    sqb = sqb4[sbk]
                    sqk = f"a_sqb{sbk}"
                    S.op("act", lambda e, tb=tb, slb=slb: e.activation(out=slb[:], in_=tb[:], func=AF.Silu), reads=[tk], writes=[sk])
                    S.op("dve", lambda e, slb=slb, sqb=sqb: e.tensor_tensor(out=sqb[:], in0=slb[:], in1=slb[:], op=ALU.mult), reads=[sk], writes=[sqk])
                    nm = "q" if cb == 0 else "k"

                    def part2(slb=slb, sk=sk, sqb=sqb, sqk=sqk, nm=nm, sbk=sbk, cb=cb):
                        S.op("pe", lambda e: e.matmul(pss[:], C["onesb"][:], sqb[:], start=True, stop=True), reads=["c_onesb", sqk], writes=["a_pss"])
                        S.op("act", lambda e: e.activation(out=rn[:], in_=pss[:], func=AF.Sqrt, bias=RMS_EPS, scale=1.0), reads=["a_pss"], writes=["a_rn"])
                        S.op("dve", lambda e: e.reciprocal(out=rn[:], in_=rn[:]), reads=["a_rn"], writes=["a_rn"])
                        S.op("dve", lambda e: e.scalar_tensor_tensor(
                            out=outs[nm][:, sbk, :], in0=slb[:], scalar=(qk_scale if cb == 0 else 1.0), in1=rn[:], op0=ALU.mult, op1=ALU.mult),
                            reads=[sk, "a_rn"], writes=[f"a_o{nm}"])
                    pend2.append(part2)
                nm = "qkvz"[cb]
                if stage < 6:
                    continue
                pend.append(lambda nm=nm, T=T: S.dma(STQ, scr[nm + "s"].rearrange("h d t -> d h t")[:, :, T * 512:(T + 1) * 512], outs[nm][:],
                                                     reads=[f"a_o{nm}"], writes=[f"scr_{nm}"]))
        while pend2:
            pend2.pop(0)()
        while pend:
            pend.pop(0)()
        S.flush(B.es)


def fl(ap):
    return ap.rearrange("p a b -> p (a b)")


def phase_a2(B, scr, ng_ap, y_ap, ntiles=SEQ // 512, NL=6, stage=9):
    S, nc = B.S, B.nc
    with ExitStack() as st:
        C = make_consts(B, st)
        banks = [B.ps(st, f"g_ps{i}", [128, 512], F32) for i in range(7)]
        pbt = B.ps(st, "g_pbt", [128, 8, 128], BF16)
        S.excl.update([f"g_ps{i}" for i in range(7)] + ["g_pbt"])
        bcnt = [0]

        def nb():
            i = bcnt[0] % 7
            bcnt[0] += 1
            return banks[i], f"g_ps{i}"

        def v3(t):
            return t[:].rearrange("p (a b) -> p a b", a=4)

        def T3(name, dt=F32):
            return B.sb(st, name, [128, 4, 128], dt)

        triu = B.sb(st, "g_triu", [128, 128], F32)
        su4, nm4, i4 = T3("g_su4"), T3("g_nm4"), T3("g_i4")
        ng = B.sb(st, "g_ng", [128, 1], F32)
        S.op("pool", lambda e: e.memset(triu[:], 1.0), writes=["g_triu"])
        S.op("pool", lambda e: e.affine_select(out=triu[:], in_=triu[:], pattern=[[1, 128]], compare_op=ALU.is_ge, fill=0.0,
                                               base=0, channel_multiplier=-1), reads=["g_triu"], writes=["g_triu"])
        S.op("pool", lambda e: e.memset(su4[:], 1.0), writes=["g_su4"])
        S.op("pool", lambda e: e.affine_select(out=su4[:], in_=su4[:], pattern=[[0, 4], [1, 128]], compare_op=ALU.is_ge, fill=0.0,
                                               base=-1, channel_multiplier=-1), reads=["g_su4"], writes=["g_su4"])
        S.op("pool", lambda e: e.memset(nm4[:], 0.0), writes=["g_nm4"])
        S.op("pool", lambda e: e.affine_select(out=nm4[:], in_=nm4[:], pattern=[[0, 4], [1, 128]], compare_op=ALU.is_ge, fill=NEG,
                                               base=0, channel_multiplier=-1), reads=["g_nm4"], writes=["g_nm4"])
        S.op("pool", lambda e: e.memset(i4[:], 0.0), writes=["g_i4"])
        S.op("pool", lambda e: e.affine_select(out=i4[:], in_=i4[:], pattern=[[0, 4], [-1, 128]], compare_op=ALU.not_equal, fill=1.0,
                                               base=0, channel_multiplier=1), reads=["g_i4"], writes=["g_i4"])
        S.dma("sp", ng[:], ng_ap.rearrange("o p -> p o"), writes=["g_ng"])
        S.op("dve", lambda e: e.tensor_scalar(out=ng[:], in0=ng[:], scalar1=float(128.0 ** 0.5), scalar2=None, op0=ALU.mult),
             reads=["g_ng"], writes=["g_ng"])
        S32 = T3("g_S32")
        Sbf = T3("g_Sbf", BF16)
        S.op("pool", lambda e: e.memset(S32[:], 0.0), writes=["g_S32"])
        S.op("pool", lambda e: e.memset(Sbf[:], 0.0), writes=["g_Sbf"])
        inb = {n: [B.sb(st, f"g_in{n}{i}", [128, 4, 512], BF16) for i in range(2)] for n in "qkvz"}
        gbb = [B.sb(st, f"g_gb{i}", [128, 4, 8], F32) for i in range(2)]
        ybuf = [B.sb(st, f"g_y{i}", [128, 4, 512], BF16) for i in range(2)]
        kvtm = B.sb(st, "g_kvtm", [128, 8, 128], BF16)
        gc = B.sb(st, "g_gc", [128, 2, 4], F32)
        ngc = B.sb(st, "g_ngc", [128, 4], F32)
        ed = B.sb(st, "g_ed", [128, 4], F32)
        gm, E, GM, DT, DTS, U, UT = T3("g_gm"), T3("g_E"), T3("g_GM"), T3("g_DT"), T3("g_DTS"), T3("g_U"), T3("g_UT")
        Wb = [T3(f"g_W{i}") for i in range(2)]
        WTb = [T3(f"g_WT{i}") for i in range(2)]
        Pb = [T3(f"g_P{i}") for i in range(2)]
        TpT, qkT, kgT, qdT, kdec, R, vnew = (T3(n, BF16) for n in ("g_TpT", "g_qkT", "g_kgT", "g_qdT", "g_kdec", "g_R", "g_vnew"))
        sq = T3("g_sq", BF16)
        rn, yt = T3("g_rn"), T3("g_yt")

        def bc(ap2):
            return ap2.unsqueeze(2).broadcast_to([128, 4, 128])

        for T in range(ntiles):
            bi = T % 2
            for n in "qkvz":
                S.dma("sp", inb[n][bi][:], scr[n + "s"].rearrange("h d t -> d h t")[:, :, T * 512:(T + 1) * 512],
                      reads=[f"scr_{n}"], writes=[f"g_in{n}{bi}"])
            S.dma("sp", gbb[bi][:], scr["gbs"].rearrange("(n p) e -> p n e", p=128)[:, T * 4:(T + 1) * 4, :],
                  reads=["scr_gb"], writes=[f"g_gb{bi}"])
            qT, kT, vT, zs, gb = inb["q"][bi], inb["k"][bi], inb["v"][bi], inb["z"][bi], gbb[bi]
            kq, kk, kv, kz, kgb = (f"g_inq{bi}", f"g_ink{bi}", f"g_inv{bi}", f"g_inz{bi}", f"g_gb{bi}")
            for c in range(4):
                cs = slice(c * 128, (c + 1) * 128)
                if stage < 1:
                    continue
                for h in range(4):
                    S.op("pe", lambda e, h=h, cs=cs, kT=kT: e.transpose(pbt[:, h, :], kT[:, h, cs], C["idb"][:]), reads=[kk, "c_idb"], writes=["g_pbt"])
                for h in range(4):
                    S.op("pe", lambda e, h=h, cs=cs, vT=vT: e.transpose(pbt[:, 4 + h, :], vT[:, h, cs], C["idb"][:]), reads=[kv, "c_idb"], writes=["g_pbt"])
                S.op("act", lambda e: e.copy(out=kvtm[:], in_=pbt[:]), reads=["g_pbt"], writes=["g_kvtm"])
                if stage < 2:
                    continue
                pa, pak = nb()
                S.op("pe", lambda e, pa=pa, gb=gb, c=c: e.matmul(pa[:, 0:4], triu[:], gb[:, c, 0:4], start=True, stop=True), reads=["g_triu", kgb], writes=[pak])
                S.op("pe", lambda e, pa=pa, gb=gb, c=c: e.matmul(pa[:, 4:8], C["onesf"][:], gb[:, c, 0:4], start=True, stop=True), reads=["c_onesf", kgb], writes=[pak])
                S.op("dve", lambda e, pa=pa: e.tensor_copy(out=gc[:].rearrange("p a b -> p (a b)"), in_=pa[:, 0:8]), reads=[pak], writes=["g_gc"])
                S.op("dve", lambda e: e.tensor_scalar(out=ngc[:], in0=gc[:, 0, :], scalar1=-1.0, scalar2=None, op0=ALU.mult), reads=["g_gc"], writes=["g_ngc"])
                S.op("dve", lambda e: e.tensor_tensor(out=ed[:], in0=gc[:, 1, :], in1=gc[:, 0, :], op=ALU.subtract), reads=["g_gc"], writes=["g_ed"])
                S.op("act", lambda e: e.activation(out=ed[:], in_=ed[:], func=AF.Exp), reads=["g_ed"], writes=["g_ed"])
                if stage < 3:
                    continue
                S.op("pool", lambda e, gb=gb, c=c: e.tensor_tensor(out=gm[:], in0=triu[:].unsqueeze(1).broadcast_to([128, 4, 128]),
                                                                    in1=bc(gb[:, c, 0:4]), op=ALU.mult), reads=["g_triu", kgb], writes=["g_gm"])
                pg, pgk = nb()
                S.op("pe", lambda e, pg=pg: e.matmul(pg[:], C["onesf"][:], fl(gm[:]), start=True, stop=True), reads=["c_onesf", "g_gm"], writes=[pgk])
                if stage < 3.2:
                    continue
                S.op("act", lambda e, pg=pg: e.activation(out=fl(E[:]), in_=pg[:], func=AF.Exp), reads=[pgk], writes=["g_E"])
                S.op("dve", lambda e, pg=pg: e.tensor_tensor(out=fl(GM[:]), in0=pg[:], in1=fl(nm4[:]), op=ALU.add), reads=[pgk, "g_nm4"], writes=["g_GM"])
                if stage < 3.3:
                    continue
                for h in range(4):
                    S.op("act", lambda e, h=h: e.activation(out=DT[:, h, :], in_=GM[:, h, :], func=AF.Exp, bias=ngc[:, h:h + 1], scale=1.0),
                         reads=["g_GM", "g_ngc"], writes=["g_DT"])
                if stage < 3.4:
                    continue
                S.op("pool", lambda e: e.tensor_tensor(out=DTS[:], in0=DT[:], in1=su4[:], op=ALU.mult), reads=["g_DT", "g_su4"], writes=["g_DTS"])
                S.op("pool", lambda e, gb=gb, c=c: e.tensor_tensor(out=DTS[:], in0=DTS[:], in1=bc(gb[:, c, 4:8]), op=ALU.mult), reads=["g_DTS", kgb], writes=["g_DTS"])
                if stage < 4:
                    continue
                pk, pkk = nb()
                for h in range(4):
                    S.op("pe", lambda e, h=h, cs=cs, kT=kT, pk=pk: e.matmul(v3(pk)[:, h, :], kT[:, h, cs], kT[:, h, cs], start=True, stop=True), reads=[kk], writes=[pkk])
                pq, pqk = nb()
                for h in range(4):
                    S.op("pe", lambda e, h=h, cs=cs, kT=kT, qT=qT, pq=pq: e.matmul(v3(pq)[:, h, :], kT[:, h, cs], qT[:, h, cs], start=True, stop=True), reads=[kk, kq], writes=[pqk])
                S.op("dve", lambda e, pk=pk: e.tensor_tensor(out=fl(U[:]), in0=pk[:], in1=fl(DTS[:]), op=ALU.mult), reads=[pkk, "g_DTS"], writes=["g_U"])
                S.op("dve", lambda e, pq=pq: e.tensor_tensor(out=fl(qkT[:]), in0=pq[:], in1=fl(DT[:]), op=ALU.mult), reads=[pqk, "g_DT"], writes=["g_qkT"])
                if stage < 5:
                    continue
                pu, puk = nb()
                for h in range(4):
                    S.op("pe", lambda e, h=h, pu=pu: e.transpose(v3(pu)[:, h, :], U[:, h, :], C["idf"][:]), reads=["g_U", "c_idf"], writes=[puk])
                S.op("act", lambda e, pu=pu: e.copy(out=fl(UT[:]), in_=pu[:]), reads=[puk], writes=["g_UT"])
                S.op("pool", lambda e: e.tensor_tensor(out=Pb[0][:], in0=i4[:], in1=U[:], op=ALU.subtract), reads=["g_i4", "g_U"], writes=["g_P0"])
                if stage < 6:
                    continue
                W, WT, P = U, UT, Pb[0]
                Wk, WTk, Pk = "g_U", "g_UT", "g_P0"
                for l in range(1, NL + 1):
                    Wn, WTn, Pn = Wb[l % 2], WTb[l % 2], Pb[l % 2]
                    Wnk, WTnk, Pnk = f"g_W{l % 2}", f"g_WT{l % 2}", f"g_P{l % 2}"
                    if l < NL:
                        pw, pwk = nb()
                        for h in range(4):
                            S.op("pe", lambda e, h=h, pw=pw, W=W, WT=WT: e.matmul(v3(pw)[:, h, :], WT[:, h, :], W[:, h, :], start=True, stop=True), reads=[Wk, WTk], writes=[pwk])
                    pwt, pwtk = nb()
                    for h in range(4):
                        S.op("pe", lambda e, h=h, pwt=pwt, W=W, WT=WT: e.matmul(v3(pwt)[:, h, :], W[:, h, :], WT[:, h, :], start=True, stop=True), reads=[Wk, WTk], writes=[pwtk])
                    if l < NL:
                        S.op("act", lambda e, pw=pw, Wn=Wn: e.copy(out=fl(Wn[:]), in_=pw[:]), reads=[pwk], writes=[Wnk])
                    S.op("dve", lambda e, pwt=pwt, WTn=WTn: e.tensor_copy(out=fl(WTn[:]), in_=pwt[:]), reads=[pwtk], writes=[WTnk])
                    pp, ppk = nb()
                    for h in range(4):
                        S.op("pe", lambda e, h=h, pp=pp, WTn=WTn, P=P: e.matmul(v3(pp)[:, h, :], WTn[:, h, :], P[:, h, :], start=True, stop=True), reads=[WTnk, Pk], writes=[ppk])
                    if l < NL:
                        S.op("dve", lambda e, pp=pp, P=P, Pn=Pn: e.tensor_tensor(out=fl(Pn[:]), in0=pp[:], in1=fl(P[:]), op=ALU.add), reads=[ppk, Pk], writes=[Pnk])
                    else:
                        S.op("dve", lambda e, pp=pp, P=P: e.tensor_tensor(out=fl(TpT[:]), in0=pp[:], in1=fl(P[:]), op=ALU.add), reads=[ppk, Pk], writes=["g_TpT"])
                    W, WT, P, Wk, WTk, Pk = Wn, WTn, Pn, Wnk, WTnk, Pnk
                if stage < 7:
                    continue
                S.op("pool", lambda e, kT=kT, cs=cs: e.tensor_tensor(out=kgT[:], in0=kT[:, :, cs], in1=E[:], op=ALU.mult), reads=[kk, "g_E"], writes=["g_kgT"])
                S.op("pool", lambda e, qT=qT, cs=cs: e.tensor_tensor(out=qdT[:], in0=qT[:, :, cs], in1=E[:], op=ALU.mult), reads=[kq, "g_E"], writes=["g_qdT"])
                S.op("pool", lambda e: e.tensor_tensor(out=kdec[:], in0=kvtm[:, 0:4, :], in1=bc(ed[:]), op=ALU.mult), reads=["g_kvtm", "g_ed"], writes=["g_kdec"])
                if stage < 8:
                    continue
                p1, p1k = nb()
                for h in range(4):
                    S.op("pe", lambda e, h=h, p1=p1: e.matmul(v3(p1)[:, h, :], kgT[:, h, :], Sbf[:, h, :], start=True, stop=True), reads=["g_kgT", "g_Sbf"], writes=[p1k])
                S.op("dve", lambda e, p1=p1: e.tensor_tensor(out=R[:], in0=kvtm[:, 4:8, :], in1=v3(p1), op=ALU.subtract), reads=["g_kvtm", p1k], writes=["g_R"])
                p2, p2k = nb()
                for h in range(4):
                    S.op("pe", lambda e, h=h, p2=p2: e.matmul(v3(p2)[:, h, :], TpT[:, h, :], R[:, h, :], start=True, stop=True), reads=["g_TpT", "g_R"], writes=[p2k])
                S.op("dve", lambda e, p2=p2, gb=gb, c=c: e.tensor_tensor(out=vnew[:], in0=v3(p2), in1=bc(gb[:, c, 4:8]), op=ALU.mult), reads=[p2k, kgb], writes=["g_vnew"])
                po, pok = nb()
                for h in range(4):
                    S.op("pe", lambda e, h=h, po=po: e.matmul(v3(po)[:, h, :], Sbf[:, h, :], qdT[:, h, :], start=True, stop=False), reads=["g_Sbf", "g_qdT"], writes=[pok])
                    S.op("pe", lambda e, h=h, po=po: e.matmul(v3(po)[:, h, :], vnew[:, h, :], qkT[:, h, :], start=False, stop=True), reads=["g_vnew", "g_qkT"], writes=[pok])
                p3, p3k = nb()
                for h in range(4):
                    S.op("pe", lambda e, h=h, p3=p3: e.matmul(v3(p3)[:, h, :], kdec[:, h, :], vnew[:, h, :], start=True, stop=True), reads=["g_kdec", "g_vnew"], writes=[p3k])
                for h in range(4):
                    S.op("dve", lambda e, h=h, p3=p3: e.scalar_tensor_tensor(out=S32[:, h, :], in0=S32[:, h, :], scalar=E[:, h, 127:128], in1=v3(p3)[:, h, :],
                                                                           op0=ALU.mult, op1=ALU.add), reads=["g_S32", "g_E", p3k], writes=["g_S32"])
                S.op("act", lambda e: e.copy(out=Sbf[:], in_=S32[:]), reads=["g_S32"], writes=["g_Sbf"])
                if stage < 9:
                    continue
                S.op("act", lambda e, po=po: e.activation(out=fl(sq[:]), in_=po[:], func=AF.Square), reads=[pok], writes=["g_sq"])
                pss, pssk = nb()
                S.op("pe", lambda e, pss=pss: e.matmul(pss[:], C["onesb"][:], fl(sq[:]), start=True, stop=True), reads=["c_onesb", "g_sq"], writes=[pssk])
                S.op("act", lambda e, pss=pss: e.activation(out=fl(rn[:]), in_=pss[:], func=AF.Sqrt, bias=128.0 * RMS_EPS, scale=1.0), reads=[pssk], writes=["g_rn"])
                S.op("dve", lambda e: e.reciprocal(out=rn[:], in_=rn[:]), reads=["g_rn"], writes=["g_rn"])
                S.op("dve", lambda e, po=po: e.scalar_tensor_tensor(out=fl(yt[:]), in0=po[:], scalar=ng[:, 0:1], in1=fl(rn[:]), op0=ALU.mult, op1=ALU.mult),
                     reads=[pok, "g_ng", "g_rn"], writes=["g_yt"])
                S.op("pool", lambda e, zs=zs, cs=cs, bi=bi: e.tensor_tensor(out=ybuf[bi][:, :, cs], in0=yt[:], in1=zs[:, :, cs], op=ALU.mult),
                     reads=["g_yt", kz], writes=[f"g_y{bi}"])
            B.finals.append(S.dma(STQ, y_ap.rearrange("(h d) t -> d h t", d=128)[:, :, T * 512:(T + 1) * 512], ybuf[bi][:],
                                  reads=[f"g_y{bi}"], writes=["y_out"]))
        S.flush(B.es)


def phase_out(B, yT_ap, zT_ap, w_ap, xres_ap, mod_ap, layer, lng_ap, lnb_ap, out_ap, ntok=TPC):
    S, nc = B.S, B.nc
    with ExitStack() as st:
        pp = [B.ps(st, f"o_pp{i}", [128, 512], F32) for i in range(4)]
        S.excl.update([f"o_pp{i}" for i in range(4)])
        yT = B.sb(st, "o_yT", [128, 32, 512], BF16)
        zt = B.sb(st, "o_zt", [128, 8, 512], BF16)
        gb_, lg_, lb_ = (B.sb(st, n, [128, D], F32) for n in ("o_gate", "o_lng", "o_lnb"))
        Wt = [B.sb(st, f"o_W{i}", [128, 16, 512], BF16) for i in range(3)]
        xr = B.sb(st, "o_xr", [128, D], F32)
        r = B.sb(st, "o_r", [128, D], F32)
        sm = B.sb(st, "o_sm", [128, 8], F32)
        S.dma("sp", gb_[:], mod_ap[layer:layer + 1, 8192:12288].partition_broadcast(128), writes=["o_gate"])
        S.dma("sp", lg_[:], lng_ap.partition_broadcast(128), writes=["o_lng"])
        S.dma("sp", lb_[:], lnb_ap.partition_broadcast(128), writes=["o_lnb"])
        S.op("dve", lambda e: e.tensor_scalar(out=gb_[:], in0=gb_[:], scalar1=1.0, scalar2=None, op0=ALU.add), reads=["o_gate"], writes=["o_gate"])
        yv = yT_ap.rearrange("(kc p) t -> p kc t", p=128)
        wv = w_ap.rearrange("(kc p) n -> p kc n", p=128)
        xv = xres_ap.rearrange("(n p) d -> n p d", p=128)
        ov = out_ap.rearrange("(n p) d -> n p d", p=128)
        wcnt = 0
        pcnt = 0
        for hf in range(ntok // 512):
            S.dma("sp", yT[:], yv[:, :, hf * 512:(hf + 1) * 512], writes=["o_yT"])
            if zT_ap is not None:
                zv = zT_ap.rearrange("(kc p) t -> p kc t", p=128)
                for g in range(4):
                    S.dma("sp", zt[:], zv[:, g * 8:(g + 1) * 8, hf * 512:(hf + 1) * 512], writes=["o_zt"])
                    S.op("pool", lambda e, g=g: e.tensor_tensor(out=yT[:, g * 8:(g + 1) * 8, :], in0=yT[:, g * 8:(g + 1) * 8, :], in1=zt[:], op=ALU.mult),
                         reads=["o_yT", "o_zt"], writes=["o_yT"])
            for ts in range(4):
                n = hf * 4 + ts
                S.dma("sp", xr[:], xv[n], writes=["o_xr"])
                for nb in range(8):
                    pb = pp[pcnt % 4]
                    pk = f"o_pp{pcnt % 4}"
                    pcnt += 1
                    for half in range(2):
                        slot = wcnt % 3
                        wcnt += 1
                        S.dma("pool", Wt[slot][:], wv[:, half * 16:(half + 1) * 16, nb * 512:(nb + 1) * 512], writes=[f"o_W{slot}"])
                        for k16 in range(16):
                            kc = half * 16 + k16
                            S.op("pe", lambda e, pb=pb, slot=slot, k16=k16, kc=kc, ts=ts: e.matmul(
                                pb[:], yT[:, kc, ts * 128:(ts + 1) * 128], Wt[slot][:, k16, :], start=(kc == 0), stop=(kc == 31)),
                                reads=["o_yT", f"o_W{slot}"], writes=[pk])
                    cs = slice(nb * 512, (nb + 1) * 512)
                    S.op("dve", lambda e, pb=pb, cs=cs: e.tensor_tensor(out=r[:, cs], in0=pb[:], in1=gb_[:, cs], op=ALU.mult), reads=[pk, "o_gate"], writes=["o_r"])
                    S.op("dve", lambda e, cs=cs: e.scalar_tensor_tensor(out=r[:, cs], in0=xr[:, cs], scalar=float(ALPHA), in1=r[:, cs], op0=ALU.mult, op1=ALU.add),
                         reads=["o_xr", "o_r"], writes=["o_r"])
                S.op("dve", lambda e: e.reduce_sum(out=sm[:, 0:1], in_=r[:], axis=AX.X), reads=["o_r"], writes=["o_sm"])
                S.op("pool", lambda e: e.tensor_tensor(out=xr[:], in0=r[:], in1=r[:], op=ALU.mult), reads=["o_r", "o_xr"], writes=["o_xr"])
                S.op("dve", lambda e: e.reduce_sum(out=sm[:, 1:2], in_=xr[:], axis=AX.X), reads=["o_xr", "o_sm"], writes=["o_sm"])
                S.op("dve", lambda e: e.tensor_scalar(out=sm[:, 0:2], in0=sm[:, 0:2], scalar1=1.0 / D, scalar2=None, op0=ALU.mult), reads=["o_sm"], writes=["o_sm"])
                S.op("dve", lambda e: e.tensor_tensor(out=sm[:, 2:3], in0=sm[:, 0:1], in1=sm[:, 0:1], op=ALU.mult), reads=["o_sm"], writes=["o_sm"])
                S.op("dve", lambda e: e.tensor_tensor(out=sm[:, 3:4], in0=sm[:, 1:2], in1=sm[:, 2:3], op=ALU.subtract), reads=["o_sm"], writes=["o_sm"])
                S.op("act", lambda e: e.activation(out=sm[:, 4:5], in_=sm[:, 3:4], func=AF.Sqrt, bias=LN_EPS, scale=1.0), reads=["o_sm"], writes=["o_sm"])
                S.op("dve", lambda e: e.reciprocal(out=sm[:, 5:6], in_=sm[:, 4:5]), reads=["o_sm"], writes=["o_sm"])
                S.op("dve", lambda e: e.tensor_scalar(out=r[:], in0=r[:], scalar1=sm[:, 0:1], scalar2=None, op0=ALU.subtract), reads=["o_r", "o_sm"], writes=["o_r"])
                S.op("dve", lambda e: e.scalar_tensor_tensor(out=r[:], in0=r[:], scalar=sm[:, 5:6], in1=lg_[:], op0=ALU.mult, op1=ALU.mult),
                     reads=["o_r", "o_sm", "o_lng"], writes=["o_r"])
                S.op("pool", lambda e: e.tensor_tensor(out=r[:], in0=r[:], in1=lb_[:], op=ALU.add), reads=["o_r", "o_lnb"], writes=["o_r"])
                B.finals.append(S.dma("sp", ov[n], r[:], reads=["o_r"], writes=["o_out"]))
        S.flush(B.es)


def phase_out2(B, yT_ap, zT_ap, w_ap, xres_ap, mod_ap, layer, lng_ap, lnb_ap, out_ap, rscr_ap):
    S, nc = B.S, B.nc
    NT = TPC // 128
    xv = xres_ap.rearrange("(n p) d -> n p d", p=128)
    rv = rscr_ap.rearrange("(n p) d -> n p d", p=128)
    ov = out_ap.rearrange("(n p) d -> n p d", p=128)
    with ExitStack() as st:
        pp = [B.ps(st, f"o_pp{i}", [128, 512], F32) for i in range(8)]
        S.excl.update([f"o_pp{i}" for i in range(8)])
        yT = B.sb(st, "o_yT", [128, 32, TPC], BF16)
        zt = B.sb(st, "o_zt", [128, 4, TPC], BF16)
        gb_ = B.sb(st, "o_gate", [128, D], F32)
        Wt = [B.sb(st, f"o_W{i}", [128, 16, 512], BF16) for i in range(3)]
        xc = [B.sb(st, f"o_xc{i}", [128, 512], F32) for i in range(3)]
        rc = [B.sb(st, f"o_rc{i}", [128, 512], F32) for i in range(3)]
        S.dma("sp", gb_[:], mod_ap[layer:layer + 1, 8192:12288].partition_broadcast(128), writes=["o_gate"])
        S.op("dve", lambda e: e.tensor_scalar(out=gb_[:], in0=gb_[:], scalar1=1.0, scalar2=None, op0=ALU.add), reads=["o_gate"], writes=["o_gate"])
        yv = yT_ap.rearrange("(kc p) t -> p kc t", p=128)
        wv = w_ap.rearrange("(kc p) n -> p kc n", p=128)
        for g in range(8):
            S.dma("sp", yT[:, g * 4:(g + 1) * 4, :], yv[:, g * 4:(g + 1) * 4, :], writes=[f"o_yT{g}"])
            if zT_ap is not None:
                zv = zT_ap.rearrange("(kc p) t -> p kc t", p=128)
                S.dma("sp", zt[:], zv[:, g * 4:(g + 1) * 4, :], writes=["o_zt"])
                S.op("pool", lambda e, g=g: e.tensor_tensor(out=yT[:, g * 4:(g + 1) * 4, :], in0=yT[:, g * 4:(g + 1) * 4, :], in1=zt[:], op=ALU.mult),
                     reads=[f"o_yT{g}", "o_zt"], writes=[f"o_yT{g}"])
        ykeys = [f"o_yT{g}" for g in range(8)]
        wcnt = 0
        ccnt = 0
        for nb in range(8):
            cs = slice(nb * 512, (nb + 1) * 512)
            slots = []
            for half in range(2):
                slot = wcnt % 3
                wcnt += 1
                slots.append(slot)
                S.dma("pool", Wt[slot][:], wv[:, half * 16:(half + 1) * 16, cs], writes=[f"o_W{slot}"])
            for ts in range(NT):
                pb = pp[ts]
                for kc in range(32):
                    slot = slots[kc // 16]
                    S.op("pe", lambda e, pb=pb, slot=slot, kc=kc, ts=ts: e.matmul(
                        pb[:], yT[:, kc, ts * 128:(ts + 1) * 128], Wt[slot][:, kc % 16, :], start=(kc == 0), stop=(kc == 31)),
                        reads=[ykeys[kc // 4], f"o_W{slot}"], writes=[f"o_pp{ts}"])
                ci = ccnt % 3
                ccnt += 1
                S.dma("sp", xc[ci][:], xv[ts][:, cs], writes=[f"o_xc{ci}"])
                S.op("dve", lambda e, pb=pb, ci=ci, cs=cs: e.tensor_tensor(out=rc[ci][:], in0=pb[:], in1=gb_[:, cs], op=ALU.mult), reads=[f"o_pp{ts}", "o_gate"], writes=[f"o_rc{ci}"])
                S.op("dve", lambda e, ci=ci: e.scalar_tensor_tensor(out=rc[ci][:], in0=xc[ci][:], scalar=float(ALPHA), in1=rc[ci][:], op0=ALU.mult, op1=ALU.add),
                     reads=[f"o_xc{ci}", f"o_rc{ci}"], writes=[f"o_rc{ci}"])
                S.dma("sp", rv[ts][:, cs], rc[ci][:], reads=[f"o_rc{ci}"], writes=["o_rscr"])
        S.flush(B.es)
    with ExitStack() as st:
        lg_, lb_ = (B.sb(st, n, [128, D], F32) for n in ("o_lng", "o_lnb"))
        r2 = [B.sb(st, f"o_r{i}", [128, D], F32) for i in range(2)]
        sq = B.sb(st, "o_sq", [128, D], F32)
        sm = B.sb(st, "o_sm", [128, 8], F32)
        S.dma("sp", lg_[:], lng_ap.partition_broadcast(128), writes=["o_lng"])
        S.dma("sp", lb_[:], lnb_ap.partition_broadcast(128), writes=["o_lnb"])
        for n in range(NT):
            r = r2[n % 2]
            rk = f"o_r{n % 2}"
            S.dma("sp", r[:], rv[n], writes=[rk])
            S.op("dve", lambda e, r=r: e.reduce_sum(out=sm[:, 0:1], in_=r[:], axis=AX.X), reads=[rk], writes=["o_sm"])
            S.op("pool", lambda e, r=r: e.tensor_tensor(out=sq[:], in0=r[:], in1=r[:], op=ALU.mult), reads=[rk], writes=["o_sq"])
            S.op("dve", lambda e: e.reduce_sum(out=sm[:, 1:2], in_=sq[:], axis=AX.X), reads=["o_sq", "o_sm"], writes=["o_sm"])
            S.op("dve", lambda e: e.tensor_scalar(out=sm[:, 0:2], in0=sm[:, 0:2], scalar1=1.0 / D, scalar2=None, op0=ALU.mult), reads=["o_sm"], writes=["o_sm"])
            S.op("dve", lambda e: e.tensor_tensor(out=sm[:, 2:3], in0=sm[:, 0:1], in1=sm[:, 0:1], op=ALU.mult), reads=["o_sm"], writes=["o_sm"])
            S.op("dve", lambda e: e.tensor_tensor(out=sm[:, 3:4], in0=sm[:, 1:2], in1=sm[:, 2:3], op=ALU.subtract), reads=["o_sm"], writes=["o_sm"])
            S.op("act", lambda e: e.activation(out=sm[:, 4:5], in_=sm[:, 3:4], func=AF.Sqrt, bias=LN_EPS, scale=1.0), reads=["o_sm"], writes=["o_sm"])
            S.op("dve", lambda e: e.reciprocal(out=sm[:, 5:6], in_=sm[:, 4:5]), reads=["o_sm"], writes=["o_sm"])
            S.op("dve", lambda e, r=r: e.tensor_scalar(out=r[:], in0=r[:], scalar1=sm[:, 0:1], scalar2=None, op0=ALU.subtract), reads=[rk, "o_sm"], writes=[rk])
            S.op("dve", lambda e, r=r: e.scalar_tensor_tensor(out=r[:], in0=r[:], scalar=sm[:, 5:6], in1=lg_[:], op0=ALU.mult, op1=ALU.mult),
                 reads=[rk, "o_sm", "o_lng"], writes=[rk])
            S.op("pool", lambda e, r=r: e.tensor_tensor(out=r[:], in0=r[:], in1=lb_[:], op=ALU.add), reads=[rk, "o_lnb"], writes=[rk])
            B.finals.append(S.dma("sp", ov[n], r[:], reads=[rk], writes=["o_out"]))
        S.flush(B.es)


def phase_b2(B, x1_ap, mod_ap, win_ap, qg_ap, kvg_ap, latT_ap, zsT_ap):
    S, nc = B.S, B.nc
    with ExitStack() as st:
        C = make_consts(B, st)
        ptr = [B.ps(st, f"b_pt{i}", [128, 4, 128], F32) for i in range(2)]
        ptk = [f"b_pt{i}" for i in range(2)]
        psp = [B.ps(st, f"b_pp{i}", [128, 512], F32) for i in range(4)]
        pmd = B.ps(st, "b_pmd", [128, 512], F32)
        pbt = B.ps(st, "b_pbt", [128, 8, 128], BF16)
        S.excl.update(ptk + [f"b_pp{i}" for i in range(4)] + ["b_pmd", "b_pbt"])
        sh, sc = load_mod_fm(B, st, C, mod_ap, 1, pmd, "b_pmd")
        xs = B.sb(st, "b_xs", [128, D], F32)
        hT = B.sb(st, "b_hT", [128, 32, TPC], BF16)
        Wt = [B.sb(st, f"b_W{i}", [128, 16, 512], BF16) for i in range(4)]
        zo = B.sb(st, "b_zo", [128, 4, 512], BF16)
        lt = B.sb(st, "b_lt", [128, 1472], F32)
        lsq = B.sb(st, "b_lsq", [128, 896], F32)
        lb = B.sb(st, "b_lb", [128, 13, 128], BF16)
        ltT = B.sb(st, "b_ltT", [128, 13, 128], BF16)
        qg = B.sb(st, "b_qg", [128, 896], F32)
        kvg = B.sb(st, "b_kvg", [128, 512], F32)
        sm = B.sb(st, "b_sm", [128, 8], F32)
        S.dma("sp", qg[:], qg_ap.partition_broadcast(128), writes=["b_qg"])
        S.dma("sp", kvg[:], kvg_ap.partition_broadcast(128), writes=["b_kvg"])
        S.op("pool", lambda e: e.memset(lb[:], 0.0), writes=["b_lb"])
        xv = x1_ap.rearrange("(n p) d -> n p d", p=128)
        wv = win_ap.rearrange("(kc p) n -> p kc n", p=128)
        cnt = [0]
        for ts in range(8):
            S.dma("sp", xs[:], xv[ts], writes=["b_xs"])
            make_hT(B, C, xs, "b_xs", hT, "b_hT", ts, sh, sc, "md_sh1", "md_sc1", ptr, ptk, cnt)
        hkeys = [f"b_hT{kc}" for kc in range(32)]
        wcnt = 0
        zv = zsT_ap.rearrange("(cb s p) t -> cb p s t", s=4, p=128)
        for cb in range(8):
            slots = []
            for half in range(2):
                slot = wcnt % 4
                wcnt += 1
                slots.append(slot)
                S.dma("pool", Wt[slot][:], wv[:, half * 16:(half + 1) * 16, 1472 + cb * 512:1472 + (cb + 1) * 512], writes=[f"b_W{slot}"])
            for th in range(2):
                for sbk in range(4):
                    for kc in range(32):
                        slot = slots[kc // 16]
                        S.op("pe", lambda e, slot=slot, sbk=sbk, kc=kc, th=th: e.matmul(
                            psp[sbk][:], Wt[slot][:, kc % 16, sbk * 128:(sbk + 1) * 128], hT[:, kc, th * 512:(th + 1) * 512],
                            start=(kc == 0), stop=(kc == 31)), reads=[f"b_W{slot}", hkeys[kc]], writes=[f"b_pp{sbk}"])
                    S.op("act", lambda e, sbk=sbk: e.activation(out=zo[:, sbk, :], in_=psp[sbk][:], func=AF.Silu), reads=[f"b_pp{sbk}"], writes=["b_zo"])
                S.dma("sp", zv[cb][:, :, th * 512:(th + 1) * 512], zo[:], reads=["b_zo"], writes=["b_zs"])
        lv = latT_ap.rearrange("(j p) t -> p j t", p=128)
        for ts in range(8):
            for nbk, (c0, c1) in enumerate(((0, 512), (512, 1024), (1024, 1472))):
                slots = []
                for half in range(2):
                    slot = wcnt % 4
                    wcnt += 1
                    slots.append(slot)
                    S.dma("pool", Wt[slot][:, :, 0:c1 - c0], wv[:, half * 16:(half + 1) * 16, c0:c1], writes=[f"b_W{slot}"])
                pb = psp[nbk]
                for kc in range(32):
                    slot = slots[kc // 16]
                    S.op("pe", lambda e, slot=slot, kc=kc, ts=ts, pb=pb, c0=c0, c1=c1: e.matmul(
                        pb[:, 0:c1 - c0], hT[:, kc, ts * 128:(ts + 1) * 128], Wt[slot][:, kc % 16, 0:c1 - c0],
                        start=(kc == 0), stop=(kc == 31)), reads=[f"b_W{slot}", hkeys[kc]], writes=[f"b_pp{nbk}"])
                S.op("act", lambda e, pb=pb, c0=c0, c1=c1: e.copy(out=lt[:, c0:c1], in_=pb[:, 0:c1 - c0]), reads=[f"b_pp{nbk}"], writes=["b_lt"])
            for (a0, a1, gt, gk, col) in ((0, 896, qg, "b_qg", 0), (896, 1408, kvg, "b_kvg", 1)):
                n = a1 - a0
                S.op("pool", lambda e, a0=a0, a1=a1, n=n: e.tensor_tensor(out=lsq[:, 0:n], in0=lt[:, a0:a1], in1=lt[:, a0:a1], op=ALU.mult), reads=["b_lt"], writes=["b_lsq"])
                S.op("dve", lambda e, n=n, col=col: e.reduce_sum(out=sm[:, col:col + 1], in_=lsq[:, 0:n], axis=AX.X), reads=["b_lsq"], writes=["b_sm"])
                S.op("act", lambda e, n=n, col=col: e.activation(out=sm[:, col + 2:col + 3], in_=sm[:, col:col + 1], func=AF.Sqrt, bias=RMS_EPS, scale=1.0 / n),
                     reads=["b_sm"], writes=["b_sm"])
                S.op("dve", lambda e, col=col: e.reciprocal(out=sm[:, col + 4:col + 5], in_=sm[:, col + 2:col + 3]), reads=["b_sm"], writes=["b_sm"])
                S.op("dve", lambda e, a0=a0, a1=a1, n=n, gt=gt, col=col: e.scalar_tensor_tensor(
                    out=lb[:].rearrange("p a b -> p (a b)")[:, a0:a1], in0=lt[:, a0:a1], scalar=sm[:, col + 4:col + 5], in1=gt[:, 0:n], op0=ALU.mult, op1=ALU.mult),
                    reads=["b_lt", "b_sm", gk], writes=["b_lb"])
            S.op("dve", lambda e: e.tensor_copy(out=lb[:, 11, 0:64], in_=lt[:, 1408:1472]), reads=["b_lt"], writes=["b_lb"])
            S.op("dve", lambda e: e.tensor_copy(out=lb[:, 12, 0:32], in_=lt[:, 1440:1472]), reads=["b_lt"], writes=["b_lb"])
            S.op("dve", lambda e: e.tensor_copy(out=lb[:, 12, 32:64], in_=lt[:, 1408:1440]), reads=["b_lt"], writes=["b_lb"])
            for j0 in (0, 8):
                nj = min(8, 13 - j0)
                for j in range(nj):
                    S.op("pe", lambda e, j=j, j0=j0: e.transpose(pbt[:, j, :], lb[:, j0 + j, :], C["idb"][:]), reads=["b_lb", "c_idb"], writes=["b_pbt"])
                S.op("act", lambda e, j0=j0, nj=nj: e.copy(out=ltT[:, j0:j0 + nj, :], in_=pbt[:, 0:nj, :]), reads=["b_pbt"], writes=["b_ltT"])
            S.dma("sp", lv[:, :, ts * 128:(ts + 1) * 128], ltT[:], reads=["b_ltT"], writes=["b_lat"])
        S.flush(B.es)


def phase_c(B, lat_ap, wq_ap, wkv_ap, pos_ap, invf_ap, sgn_ap, oT_ap, nq=SEQ // 512):
    S, nc = B.S, B.nc
    TWO_PI = float(2 * np.pi)
    with ExitStack() as st:
        C = make_consts(B, st)
        acc = [B.ps(st, f"c_acc{i}", [128, 512], F32) for i in range(4)]
        pst = [B.ps(st, f"c_st{i}", [128, 512], F32) for i in range(2)]
        pm = [B.ps(st, f"c_pm{i}", [128, 512], F32) for i in range(2)]
        S.excl.update([f"c_acc{i}" for i in range(4)] + ["c_st0", "c_st1", "c_pm0", "c_pm1"])
        Wq = B.sb(st, "c_Wq", [128, 7, 1024], BF16)
        Wkv = B.sb(st, "c_Wkv", [128, 4, 1024], BF16)
        S.dma("pool", Wq[:], wq_ap.rearrange("(kc p) n -> p kc n", p=128), writes=["c_Wq"])
        S.dma("pool", Wkv[:], wkv_ap.rearrange("(kc p) n -> p kc n", p=128), writes=["c_Wkv"])
        invf = B.sb(st, "c_invf", [64, 1], F32)
        sgn = B.sb(st, "c_sgn", [64, 1], F32)
        S.dma("sp", invf[:], invf_ap, writes=["c_invf"])
        S.dma("sp", sgn[:], sgn_ap, writes=["c_sgn"])
        tril = B.sb(st, "c_tril", [128, 128], BF16)
        S.op("pool", lambda e: e.memset(tril[:], 1.0), writes=["c_tril"])
        S.op("pool", lambda e: e.affine_select(out=tril[:], in_=tril[:], pattern=[[1, 128]], compare_op=ALU.is_ge, fill=0.0,
                                               base=0, channel_multiplier=-1), reads=["c_tril"], writes=["c_tril"])
        qn = B.sb(st, "c_qn", [128, SEQ], BF16)
        kn = B.sb(st, "c_kn", [128, SEQ], BF16)
        qr = B.sb(st, "c_qr", [64, SEQ], BF16)
        kr = B.sb(st, "c_kr", [64, SEQ], BF16)
        vt = B.sb(st, "c_vt", [128, SEQ // 128, 132], BF16)
        S.op("pool", lambda e: e.memset(vt[:], 1.0), writes=["c_vt"])
        lat = [B.sb(st, f"c_lat{i}", [128, 13, 512], BF16) for i in range(2)]
        posi = B.sb(st, "c_posi", [64, 512], I32)
        ang = B.sb(st, "c_ang", [64, 512], F32)
        cs_ = B.sb(st, "c_cos", [64, 512], F32)
        sn_ = B.sb(st, "c_sin", [64, 512], F32)
        t1 = B.sb(st, "c_t1", [64, 512], F32)
        t2 = B.sb(st, "c_t2", [64, 512], F32)
        pt = [B.sb(st, f"c_p{i}", [128, 512], BF16) for i in range(2)]
        rs = B.sb(st, "c_rs", [128, 4], F32)
        on = B.sb(st, "c_on", [128, 4, 128], BF16)
        oT = B.sb(st, "c_oT", [128, 512], BF16)
        pbt = pm[1]
        lv = lat_ap.rearrange("(j p) t -> p j t", p=128)
        scale = float(192.0 ** -0.5)

        def rope(dst, dkey, pa, pak, pb, pbk, sl):
            S.op("dve", lambda e: e.tensor_tensor(out=t1[:], in0=pa[0:64, :], in1=cs_[:], op=ALU.mult), reads=[pak, "c_cos"], writes=["c_t1"])
            S.op("dve", lambda e: e.tensor_tensor(out=t2[:], in0=pb[0:64, :], in1=sn_[:], op=ALU.mult), reads=[pbk, "c_sin"], writes=["c_t2"])
            S.op("pool", lambda e: e.tensor_tensor(out=dst[0:64, sl], in0=t1[:], in1=t2[:], op=ALU.add), reads=["c_t1", "c_t2"], writes=[dkey])

        for h in range(4):
            for tt in range(SEQ // 512):
                sl = slice(tt * 512, (tt + 1) * 512)
                lt_ = lat[tt % 2]
                lk = f"c_lat{tt % 2}"
                S.dma("sp", lt_[:], lv[:, :, sl], writes=[lk])
                S.dma("sp", posi[:], pos_ap[0:1, sl].partition_broadcast(64), writes=["c_posi"])
                S.op("dve", lambda e: e.tensor_copy(out=ang[:], in_=posi[:]), reads=["c_posi"], writes=["c_ang"])
                S.op("dve", lambda e: e.tensor_scalar(out=ang[:], in0=ang[:], scalar1=invf[:, 0:1], scalar2=None, op0=ALU.mult), reads=["c_ang", "c_invf"], writes=["c_ang"])
                for (dst, dkey, addc) in ((sn_, "c_sin", 0.0), (cs_, "c_cos", float(0.5 * np.pi))):
                    S.op("dve", lambda e, addc=addc: e.tensor_scalar(out=t1[:], in0=ang[:], scalar1=addc, scalar2=None, op0=ALU.add), reads=["c_ang"], writes=["c_t1"])
                    S.op("dve", lambda e: e.tensor_scalar(out=t2[:], in0=t1[:], scalar1=1.0 / TWO_PI, scalar2=None, op0=ALU.mult), reads=["c_t1"], writes=["c_t2"])
                    S.op("dve", lambda e: e.tensor_copy(out=posi[:], in_=t2[:]), reads=["c_t2"], writes=["c_posi"])
                    S.op("dve", lambda e: e.tensor_copy(out=t2[:], in_=posi[:]), reads=["c_posi"], writes=["c_t2"])
                    S.op("dve", lambda e: e.scalar_tensor_tensor(out=t1[:], in0=t2[:], scalar=-TWO_PI, in1=t1[:], op0=ALU.mult, op1=ALU.add), reads=["c_t1", "c_t2"], writes=["c_t1"])
                    S.op("dve", lambda e: e.tensor_scalar(out=t2[:], in0=t1[:], scalar1=float(np.pi), scalar2=None, op0=ALU.is_gt), reads=["c_t1"], writes=["c_t2"])
                    S.op("dve", lambda e: e.scalar_tensor_tensor(out=t1[:], in0=t2[:], scalar=-TWO_PI, in1=t1[:], op0=ALU.mult, op1=ALU.add), reads=["c_t1", "c_t2"], writes=["c_t1"])
                    S.op("dve", lambda e: e.tensor_scalar(out=t2[:], in0=t1[:], scalar1=float(-np.pi), scalar2=None, op0=ALU.is_lt), reads=["c_t1"], writes=["c_t2"])
                    S.op("dve", lambda e: e.scalar_tensor_tensor(out=t1[:], in0=t2[:], scalar=TWO_PI, in1=t1[:], op0=ALU.mult, op1=ALU.add), reads=["c_t1", "c_t2"], writes=["c_t1"])
                    S.op("act", lambda e, dst=dst: e.activation(out=dst[:], in_=t1[:], func=AF.Sin), reads=["c_t1"], writes=[dkey])
                S.op("dve", lambda e: e.tensor_scalar(out=sn_[:], in0=sn_[:], scalar1=sgn[:, 0:1], scalar2=None, op0=ALU.mult), reads=["c_sin", "c_sgn"], writes=["c_sin"])
                for kc in range(7):
                    S.op("pe", lambda e, kc=kc, lt_=lt_, h=h: e.matmul(pm[0][:], Wq[:, kc, h * 256:h * 256 + 128], lt_[:, kc, :], start=(kc == 0), stop=(kc == 6)),
                         reads=["c_Wq", lk], writes=["c_pm0"])
                S.op("act", lambda e, sl=sl: e.copy(out=qn[:, sl], in_=pm[0][:]), reads=["c_pm0"], writes=["c_qn"])
                for kc in range(7):
                    S.op("pe", lambda e, kc=kc, lt_=lt_, h=h: e.matmul(pst[0][0:64, :], Wq[:, kc, h * 256 + 128:h * 256 + 192], lt_[:, kc, :], start=(kc == 0), stop=(kc == 6)),
                         reads=["c_Wq", lk], writes=["c_st0"])
                for kc in range(7):
                    S.op("pe", lambda e, kc=kc, lt_=lt_, h=h: e.matmul(pst[1][0:64, :], Wq[:, kc, h * 256 + 192:h * 256 + 256], lt_[:, kc, :], start=(kc == 0), stop=(kc == 6)),
                         reads=["c_Wq", lk], writes=["c_st1"])
                rope(qr, "c_qr", pst[0], "c_st0", pst[1], "c_st1", sl)
                if h == 0:
                    S.op("pe", lambda e, lt_=lt_: e.matmul(pst[0][0:64, :], C["idb"][0:64, 0:64], lt_[0:64, 11, :], start=True, stop=True), reads=["c_idb", lk], writes=["c_st0"])
                    S.op("pe", lambda e, lt_=lt_: e.matmul(pst[1][0:64, :], C["idb"][0:64, 0:64], lt_[0:64, 12, :], start=True, stop=True), reads=["c_idb", lk], writes=["c_st1"])
                    rope(kr, "c_kr", pst[0], "c_st0", pst[1], "c_st1", sl)
                for kc in range(4):
                    S.op("pe", lambda e, kc=kc, lt_=lt_, h=h: e.matmul(pm[0][:], Wkv[:, kc, h * 256:h * 256 + 128], lt_[:, 7 + kc, :], start=(kc == 0), stop=(kc == 3)),
                         reads=["c_Wkv", lk], writes=["c_pm0"])
                S.op("act", lambda e, sl=sl: e.copy(out=kn[:, sl], in_=pm[0][:]), reads=["c_pm0"], writes=["c_kn"])
                for sub in range(4):
                    for kc in range(4):
                        S.op("pe", lambda e, kc=kc, lt_=lt_, h=h, sub=sub: e.matmul(pm[1][:, sub * 128:(sub + 1) * 128], lt_[:, 7 + kc, sub * 128:(sub + 1) * 128],
                                                                                   Wkv[:, kc, h * 256 + 128:h * 256 + 256], start=(kc == 0), stop=(kc == 3)),
                             reads=["c_Wkv", lk], writes=["c_pm1"])
                S.op("dve", lambda e, tt=tt: e.tensor_copy(out=vt[:, tt * 4:(tt + 1) * 4, 0:128], in_=pm[1][:].rearrange("p (a b) -> p a b", a=4)), reads=["c_pm1"], writes=["c_vt"])
            steps = [(qb, kt) for qb in range(nq) for kt in range(4 * qb + 4)]

            def emit_st(i):
                qb, kt = steps[i]
                qsl = slice(qb * 512, (qb + 1) * 512)
                ksl = slice(kt * 128, (kt + 1) * 128)
                ps_ = pst[i % 2]
                psk = f"c_st{i % 2}"
                pb_ = pt[i % 2]
                pbk_ = f"c_p{i % 2}"
                S.op("pe", lambda e: e.matmul(ps_[:], kn[:, ksl], qn[:, qsl], start=True, stop=False), reads=["c_kn", "c_qn"], writes=[psk])
                S.op("pe", lambda e: e.matmul(ps_[:], kr[0:64, ksl], qr[0:64, qsl], start=False, stop=True), reads=["c_kr", "c_qr"], writes=[psk])
                S.op("act", lambda e: e.activation(out=pb_[:], in_=ps_[:], func=AF.Exp, scale=scale), reads=[psk], writes=[pbk_])
                j = kt - 4 * qb
                if j >= 0:
                    S.op("pool", lambda e: e.tensor_tensor(out=pb_[:, j * 128:(j + 1) * 128], in0=pb_[:, j * 128:(j + 1) * 128], in1=tril[:], op=ALU.mult),
                         reads=[pbk_, "c_tril"], writes=[pbk_])

            def emit_pv(i):
                qb, kt = steps[i]
                qsl = slice(qb * 512, (qb + 1) * 512)
                pb_ = pt[i % 2]
                pbk_ = f"c_p{i % 2}"
                j = kt - 4 * qb
                for ii in range(4):
                    if j > ii:
                        continue
                    last = (kt == 4 * qb + ii)
                    S.op("pe", lambda e, ii=ii, last=last: e.matmul(acc[ii][:, 0:129], pb_[:, ii * 128:(ii + 1) * 128], vt[:, kt, 0:129],
                                                                     start=(kt == 0), stop=last), reads=[pbk_, "c_vt"], writes=[f"c_acc{ii}"])
                if kt != 4 * qb + 3:
                    return
                for ii in range(4):
                    S.op("dve", lambda e, ii=ii: e.reciprocal(out=rs[:, ii:ii + 1], in_=acc[ii][:, 128:129]), reads=[f"c_acc{ii}"], writes=["c_rs"])
                    S.op("dve", lambda e, ii=ii: e.tensor_scalar(out=on[:, ii, :], in0=acc[ii][:, 0:128], scalar1=rs[:, ii:ii + 1], scalar2=None, op0=ALU.mult),
                         reads=[f"c_acc{ii}", "c_rs"], writes=["c_on"])
                pbt_b = pbt[:].bitcast(BF16).rearrange("p (a b) -> p a b", b=128)
                for ii in range(4):
                    S.op("pe", lambda e, ii=ii: e.transpose(pbt_b[:, ii, :], on[:, ii, :], C["idb"][:]), reads=["c_on", "c_idb"], writes=["c_pm1"])
                S.op("act", lambda e: e.copy(out=oT[:].rearrange("p (a b) -> p a b", a=4), in_=pbt_b[:, 0:4, :]), reads=["c_pm1"], writes=["c_oT"])
                B.finals.append(S.dma("sp", oT_ap[h * 128:(h + 1) * 128, qsl], oT[:], reads=["c_oT"], writes=["c_out"]))

            emit_st(0)
            for i in range(len(steps)):
                if i + 1 < len(steps):
                    emit_st(i + 1)
                emit_pv(i)
        S.flush(B.es)


def _launch(build, maps):
    nc = bass.Bass("TRN2", target_bir_lowering=False)
    with ExitStack() as es:
        B = Builder(nc, es)
        build(nc, B)
    res = run_bass_kernel_spmd(nc, maps, core_ids=list(range(NCORES)))
    return res.results


def _din(nc, name, shape, dt):
    return nc.dram_tensor(name, list(shape), dt, kind="ExternalInput").ap()


def _dout(nc, name, shape, dt):
    return nc.dram_tensor(name, list(shape), dt, kind="ExternalOutput").ap()


def kernel(x, c, positions, w_mod, b_mod, ln_g, ln_b, a_w_in, a_w_conv, a_a_log, a_dt_bias, a_norm_g, a_w_out,
           b_w_in, b_q_norm_g, b_w_qb, b_kv_norm_g, b_w_kvb, b_w_out):
    f32 = np.float32
    asc = np.ascontiguousarray
    x2 = asc(np.asarray(x, f32)[0])
    R = range(NCORES)
    def bM(nc, B):
        phase_mod(B, _din(nc, "c", [1, D], F32), _din(nc, "wm", [D, 3072], F32), _din(nc, "bm", [1, 3072], F32), _dout(nc, "modrow", [1, 3072], F32))
    maps = [{"c": asc(np.asarray(c, f32)), "wm": asc(np.asarray(w_mod[r // 4][:, (r % 4) * 3072:(r % 4 + 1) * 3072], f32)),
             "bm": asc(np.asarray(b_mod[r // 4][None, (r % 4) * 3072:(r % 4 + 1) * 3072], f32))} for r in R]
    res = _launch(bM, maps)
    mod = asc(np.concatenate([res[r]["modrow"].reshape(-1) for r in R]).reshape(2, 12288))
    def bA(nc, B):
        scr = {n + "s": nc.dram_tensor("scr_" + n, [4, 128, SEQ], BF16).ap() for n in "qkvz"}
        scr["gbs"] = nc.dram_tensor("scr_gb", [SEQ, 8], F32).ap()
        y = _dout(nc, "y0T", [512, SEQ], BF16)
        phase_a1(B, _din(nc, "x", [SEQ, D], F32), _din(nc, "mod", [2, 12288], F32), _din(nc, "w4", [D, 2048], F32), _din(nc, "wab", [D, 8], F32),
                 _din(nc, "cw", [4, 1536], F32), _din(nc, "alog", [1, 4], F32), _din(nc, "dtb", [1, 4], F32), scr)
        phase_a2p(B, scr, _din(nc, "ng", [1, 128], F32), y)
    W = np.asarray(a_w_in[0], f32)
    cwf = np.asarray(a_w_conv[0], f32)
    maps = []
    for r in R:
        o = 512 * r
        maps.append({"x": x2, "mod": mod,
                     "w4": asc(np.concatenate([W[:, o:o + 512], W[:, 4096 + o:4096 + o + 512], W[:, 8192 + o:8192 + o + 512], W[:, 12288 + o:12288 + o + 512]], axis=1)),
                     "wab": asc(np.concatenate([W[:, 16384 + 4 * r:16384 + 4 * r + 4], W[:, 16416 + 4 * r:16416 + 4 * r + 4]], axis=1)),
                     "cw": asc(np.concatenate([cwf[:, q + o:q + o + 512] for q in (0, 4096, 8192)], axis=1)),
                     "alog": asc(np.asarray(a_a_log, f32)[:, 4 * r:4 * r + 4]), "dtb": asc(np.asarray(a_dt_bias, f32)[:, 4 * r:4 * r + 4]),
                     "ng": asc(np.asarray(a_norm_g, f32))})
    res = _launch(bA, maps)
    Y0 = np.concatenate([res[r]["y0T"] for r in R], axis=0)
    def bB(nc, B):
        x1 = _dout(nc, "x1", [TPC, D], F32)
        modt = _din(nc, "mod", [2, 12288], F32)
        phase_out2(B, _din(nc, "yT", [D, TPC], BF16), None, _din(nc, "w", [D, D], F32), _din(nc, "xr", [TPC, D], F32), modt, 0,
                   _din(nc, "lg", [1, D], F32), _din(nc, "lb", [1, D], F32), x1, nc.dram_tensor("rscr", [TPC, D], F32).ap())
        phase_b2(B, x1, modt, _din(nc, "win", [D, 5568], F32), _din(nc, "qg", [1, 896], F32), _din(nc, "kvg", [1, 512], F32),
                 _dout(nc, "latT", [13 * 128, TPC], BF16), _dout(nc, "zsT", [D, TPC], BF16))
    maps = [{"yT": asc(Y0[:, TPC * r:TPC * (r + 1)]), "w": asc(np.asarray(a_w_out[0], f32)), "xr": asc(x2[TPC * r:TPC * (r + 1)]), "mod": mod,
             "lg": asc(np.asarray(ln_g, f32)[0:1]), "lb": asc(np.asarray(ln_b, f32)[0:1]), "win": asc(np.asarray(b_w_in[0], f32)),
             "qg": asc(np.asarray(b_q_norm_g, f32)), "kvg": asc(np.asarray(b_kv_norm_g, f32))} for r in R]
    res = _launch(bB, maps)
    x1s = [res[r]["x1"] for r in R]
    zss = [res[r]["zsT"] for r in R]
    lat = asc(np.concatenate([res[r]["latT"] for r in R], axis=1))
    def bC(nc, B):
        phase_c(B, _din(nc, "lat", [13 * 128, SEQ], BF16), _din(nc, "wq", [896, 1024], F32), _din(nc, "wkv", [512, 1024], F32),
                _din(nc, "pos", [1, SEQ], I32), _din(nc, "invf", [64, 1], F32), _din(nc, "sgn", [64, 1], F32), _dout(nc, "o1T", [512, SEQ], BF16))
    half = np.arange(32, dtype=np.float32) / 32.0
    invf = (10000.0 ** (-half)).astype(f32)
    invf = asc(np.concatenate([invf, invf])[:, None])
    sgn = asc(np.concatenate([-np.ones(32, f32), np.ones(32, f32)])[:, None])
    wqb = np.asarray(b_w_qb[0], f32).reshape(896, 32, 192)
    wkvb = np.asarray(b_w_kvb[0], f32).reshape(512, 32, 256)
    maps = []
    for r in R:
        wq = wqb[:, 4 * r:4 * r + 4]
        wq = np.concatenate([wq, wq[:, :, 160:192], wq[:, :, 128:160]], axis=2)
        maps.append({"lat": lat, "wq": asc(wq.reshape(896, 1024)), "wkv": asc(wkvb[:, 4 * r:4 * r + 4].reshape(512, 1024)),
                     "pos": asc(np.asarray(positions, np.int32)), "invf": invf, "sgn": sgn})
    res = _launch(bC, maps)
    O1 = np.concatenate([res[r]["o1T"] for r in R], axis=0)
    def bD(nc, B):
        phase_out2(B, _din(nc, "yT", [D, TPC], BF16), _din(nc, "zT", [D, TPC], BF16), _din(nc, "w", [D, D], F32), _din(nc, "xr", [TPC, D], F32),
                   _din(nc, "mod", [2, 12288], F32), 1, _din(nc, "lg", [1, D], F32), _din(nc, "lb", [1, D], F32), _dout(nc, "out", [TPC, D], F32),
                   nc.dram_tensor("rscr", [TPC, D], F32).ap())
    maps = [{"yT": asc(O1[:, TPC * r:TPC * (r + 1)]), "zT": zss[r], "w": asc(np.asarray(b_w_out[0], f32)), "xr": x1s[r], "mod": mod,
             "lg": asc(np.asarray(ln_g, f32)[1:2]), "lb": asc(np.asarray(ln_b, f32)[1:2])} for r in R]
    res = _launch(bD, maps)
    return np.concatenate([res[r]["out"] for r in R], axis=0)[None].astype(f32)


def phase_a2p(B, scr, ng_ap, y_ap, ntiles=SEQ // 512, NL=6):
    S, nc = B.S, B.nc
    with ExitStack() as st:
        C = make_consts(B, st)
        banks = [B.ps(st, f"g_ps{i}", [128, 512], F32) for i in range(7)]
        pbt = B.ps(st, "g_pbt", [128, 8, 128], BF16)
        S.excl.update([f"g_ps{i}" for i in range(7)] + ["g_pbt"])
        bcnt = [0]

        def nb():
            i = bcnt[0] % 7
            bcnt[0] += 1
            return banks[i], f"g_ps{i}"

        def v3(t):
            return t[:].rearrange("p (a b) -> p a b", a=4)

        def T3(name, dt=F32):
            return B.sb(st, name, [128, 4, 128], dt)

        triu = B.sb(st, "g_triu", [128, 128], F32)
        su4, nm4, i4 = T3("g_su4"), T3("g_nm4"), T3("g_i4")
        ng = B.sb(st, "g_ng", [128, 1], F32)
        S.op("pool", lambda e: e.memset(triu[:], 1.0), writes=["g_triu"])
        S.op("pool", lambda e: e.affine_select(out=triu[:], in_=triu[:], pattern=[[1, 128]], compare_op=ALU.is_ge, fill=0.0,
                                               base=0, channel_multiplier=-1), reads=["g_triu"], writes=["g_triu"])
        S.op("pool", lambda e: e.memset(su4[:], 1.0), writes=["g_su4"])
        S.op("pool", lambda e: e.affine_select(out=su4[:], in_=su4[:], pattern=[[0, 4], [1, 128]], compare_op=ALU.is_ge, fill=0.0,
                                               base=-1, channel_multiplier=-1), reads=["g_su4"], writes=["g_su4"])
        S.op("pool", lambda e: e.memset(nm4[:], 0.0), writes=["g_nm4"])
        S.op("pool", lambda e: e.affine_select(out=nm4[:], in_=nm4[:], pattern=[[0, 4], [1, 128]], compare_op=ALU.is_ge, fill=NEG,
                                               base=0, channel_multiplier=-1), reads=["g_nm4"], writes=["g_nm4"])
        S.op("pool", lambda e: e.memset(i4[:], 0.0), writes=["g_i4"])
        S.op("pool", lambda e: e.affine_select(out=i4[:], in_=i4[:], pattern=[[0, 4], [-1, 128]], compare_op=ALU.not_equal, fill=1.0,
                                               base=0, channel_multiplier=1), reads=["g_i4"], writes=["g_i4"])
        S.dma("sp", ng[:], ng_ap.rearrange("o p -> p o"), writes=["g_ng"])
        S.op("dve", lambda e: e.tensor_scalar(out=ng[:], in0=ng[:], scalar1=float(128.0 ** 0.5), scalar2=None, op0=ALU.mult),
             reads=["g_ng"], writes=["g_ng"])
        S32 = T3("g_S32")
        Sbf = T3("g_Sbf", BF16)
        S.op("pool", lambda e: e.memset(S32[:], 0.0), writes=["g_S32"])
        S.op("pool", lambda e: e.memset(Sbf[:], 0.0), writes=["g_Sbf"])
        inb = {n: [B.sb(st, f"g_in{n}{i}", [128, 4, 512], BF16) for i in range(2)] for n in "qkvz"}
        gbb = [B.sb(st, f"g_gb{i}", [128, 4, 8], F32) for i in range(2)]
        ybuf = [B.sb(st, f"g_y{i}", [128, 4, 512], BF16) for i in range(2)]
        P_ = []
        for p in range(2):
            d = dict(p=p)
            d["kvtm"] = B.sb(st, f"g_kvtm{p}", [128, 8, 128], BF16)
            d["gc"] = B.sb(st, f"g_gc{p}", [128, 2, 4], F32)
            d["ngc"] = B.sb(st, f"g_ngc{p}", [128, 4], F32)
            d["ed"] = B.sb(st, f"g_ed{p}", [128, 4], F32)
            for n in ("gm", "E", "GM", "DT", "DTS", "U", "UT", "W0", "W1", "WT0", "WT1", "P0", "P1"):
                d[n] = T3(f"g_{n}{p}")
            for n in ("TpT", "qkT", "kgT", "qdT", "kdec"):
                d[n] = T3(f"g_{n}{p}", BF16)
            P_.append(d)
        R, vnew, sq = T3("g_R", BF16), T3("g_vnew", BF16), T3("g_sq", BF16)
        rn, yt = T3("g_rn"), T3("g_yt")

        def bc(ap2):
            return ap2.unsqueeze(2).broadcast_to([128, 4, 128])

        def K(d, n):
            return f"g_{n}{d['p']}"

        def pre_a(d, T, c, qT, kT, vT, gb, kq, kk, kv, kgb):
            cs = slice(c * 128, (c + 1) * 128)
            kvtm, gc, ngc, ed, gm, E, GM, DT, DTS, U, UT = (d[n] for n in ("kvtm", "gc", "ngc", "ed", "gm", "E", "GM", "DT", "DTS", "U", "UT"))
            for h in range(4):
                S.op("pe", lambda e, h=h: e.transpose(pbt[:, h, :], kT[:, h, cs], C["idb"][:]), reads=[kk, "c_idb"], writes=["g_pbt"])
            for h in range(4):
                S.op("pe", lambda e, h=h: e.transpose(pbt[:, 4 + h, :], vT[:, h, cs], C["idb"][:]), reads=[kv, "c_idb"], writes=["g_pbt"])
            S.op("act", lambda e: e.copy(out=kvtm[:], in_=pbt[:]), reads=["g_pbt"], writes=[K(d, "kvtm")])
            pa, pak = nb()
            S.op("pe", lambda e: e.matmul(pa[:, 0:4], triu[:], gb[:, c, 0:4], start=True, stop=True), reads=["g_triu", kgb], writes=[pak])
            S.op("pe", lambda e: e.matmul(pa[:, 4:8], C["onesf"][:], gb[:, c, 0:4], start=True, stop=True), reads=["c_onesf", kgb], writes=[pak])
            S.op("dve", lambda e: e.tensor_copy(out=gc[:].rearrange("p a b -> p (a b)"), in_=pa[:, 0:8]), reads=[pak], writes=[K(d, "gc")])
            S.op("dve", lambda e: e.tensor_scalar(out=ngc[:], in0=gc[:, 0, :], scalar1=-1.0, scalar2=None, op0=ALU.mult), reads=[K(d, "gc")], writes=[K(d, "ngc")])
            S.op("dve", lambda e: e.tensor_tensor(out=ed[:], in0=gc[:, 1, :], in1=gc[:, 0, :], op=ALU.subtract), reads=[K(d, "gc")], writes=[K(d, "ed")])
            S.op("act", lambda e: e.activation(out=ed[:], in_=ed[:], func=AF.Exp), reads=[K(d, "ed")], writes=[K(d, "ed")])
            S.op("pool", lambda e: e.tensor_tensor(out=gm[:], in0=triu[:].unsqueeze(1).broadcast_to([128, 4, 128]),
                                                   in1=bc(gb[:, c, 0:4]), op=ALU.mult), reads=["g_triu", kgb], writes=[K(d, "gm")])
            pg, pgk = nb()
            S.op("pe", lambda e: e.matmul(pg[:], C["onesf"][:], fl(gm[:]), start=True, stop=True), reads=["c_onesf", K(d, "gm")], writes=[pgk])
            S.op("act", lambda e: e.activation(out=fl(E[:]), in_=pg[:], func=AF.Exp), reads=[pgk], writes=[K(d, "E")])
            S.op("dve", lambda e: e.tensor_tensor(out=fl(GM[:]), in0=pg[:], in1=fl(nm4[:]), op=ALU.add), reads=[pgk, "g_nm4"], writes=[K(d, "GM")])
            for h in range(4):
                S.op("act", lambda e, h=h: e.activation(out=DT[:, h, :], in_=GM[:, h, :], func=AF.Exp, bias=ngc[:, h:h + 1], scale=1.0),
                     reads=[K(d, "GM"), K(d, "ngc")], writes=[K(d, "DT")])
            S.op("pool", lambda e: e.tensor_tensor(out=DTS[:], in0=DT[:], in1=su4[:], op=ALU.mult), reads=[K(d, "DT"), "g_su4"], writes=[K(d, "DTS")])
            S.op("pool", lambda e: e.tensor_tensor(out=DTS[:], in0=DTS[:], in1=bc(gb[:, c, 4:8]), op=ALU.mult), reads=[K(d, "DTS"), kgb], writes=[K(d, "DTS")])
            pk, pkk = nb()
            for h in range(4):
                S.op("pe", lambda e, h=h: e.matmul(v3(pk)[:, h, :], kT[:, h, cs], kT[:, h, cs], start=True, stop=True), reads=[kk], writes=[pkk])
            pq, pqk = nb()
            for h in range(4):
                S.op("pe", lambda e, h=h: e.matmul(v3(pq)[:, h, :], kT[:, h, cs], qT[:, h, cs], start=True, stop=True), reads=[kk, kq], writes=[pqk])
            S.op("dve", lambda e: e.tensor_tensor(out=fl(U[:]), in0=pk[:], in1=fl(DTS[:]), op=ALU.mult), reads=[pkk, K(d, "DTS")], writes=[K(d, "U")])
            S.op("dve", lambda e: e.tensor_tensor(out=fl(d["qkT"][:]), in0=pq[:], in1=fl(DT[:]), op=ALU.mult), reads=[pqk, K(d, "DT")], writes=[K(d, "qkT")])
            pu, puk = nb()
            for h in range(4):
                S.op("pe", lambda e, h=h: e.transpose(v3(pu)[:, h, :], U[:, h, :], C["idf"][:]), reads=[K(d, "U"), "c_idf"], writes=[puk])
            S.op("act", lambda e: e.copy(out=fl(UT[:]), in_=pu[:]), reads=[puk], writes=[K(d, "UT")])
            S.op("pool", lambda e: e.tensor_tensor(out=d["P0"][:], in0=i4[:], in1=U[:], op=ALU.subtract), reads=["g_i4", K(d, "U")], writes=[K(d, "P0")])
            d["cur"] = (U, UT, d["P0"], K(d, "U"), K(d, "UT"), K(d, "P0"))
            S.op("pool", lambda e: e.tensor_tensor(out=d["kgT"][:], in0=kT[:, :, cs], in1=E[:], op=ALU.mult), reads=[kk, K(d, "E")], writes=[K(d, "kgT")])
            S.op("pool", lambda e: e.tensor_tensor(out=d["qdT"][:], in0=qT[:, :, cs], in1=E[:], op=ALU.mult), reads=[kq, K(d, "E")], writes=[K(d, "qdT")])
            S.op("pool", lambda e: e.tensor_tensor(out=d["kdec"][:], in0=kvtm[:, 0:4, :], in1=bc(ed[:]), op=ALU.mult), reads=[K(d, "kvtm"), K(d, "ed")], writes=[K(d, "kdec")])

        def level(d, l):
            W, WT, P, Wk, WTk, Pk = d["cur"]
            Wn, WTn, Pn = d[f"W{l % 2}"], d[f"WT{l % 2}"], d[f"P{l % 2}"]
            Wnk, WTnk, Pnk = K(d, f"W{l % 2}"), K(d, f"WT{l % 2}"), K(d, f"P{l % 2}")
            if l < NL:
                pw, pwk = nb()
                for h in range(4):
                    S.op("pe", lambda e, h=h: e.matmul(v3(pw)[:, h, :], WT[:, h, :], W[:, h, :], start=True, stop=True), reads=[Wk, WTk], writes=[pwk])
            pwt, pwtk = nb()
            for h in range(4):
                S.op("pe", lambda e, h=h: e.matmul(v3(pwt)[:, h, :], W[:, h, :], WT[:, h, :], start=True, stop=True), reads=[Wk, WTk], writes=[pwtk])
            if l < NL:
                S.op("act", lambda e: e.copy(out=fl(Wn[:]), in_=pw[:]), reads=[pwk], writes=[Wnk])
            S.op("dve", lambda e: e.tensor_copy(out=fl(WTn[:]), in_=pwt[:]), reads=[pwtk], writes=[WTnk])
            pp, ppk = nb()
            for h in range(4):
                S.op("pe", lambda e, h=h: e.matmul(v3(pp)[:, h, :], WTn[:, h, :], P[:, h, :], start=True, stop=True), reads=[WTnk, Pk], writes=[ppk])
            if l < NL:
                S.op("dve", lambda e: e.tensor_tensor(out=fl(Pn[:]), in0=pp[:], in1=fl(P[:]), op=ALU.add), reads=[ppk, Pk], writes=[Pnk])
            else:
                S.op("dve", lambda e: e.tensor_tensor(out=fl(d["TpT"][:]), in0=pp[:], in1=fl(P[:]), op=ALU.add), reads=[ppk, Pk], writes=[K(d, "TpT")])
            d["cur"] = (Wn, WTn, Pn, Wnk, WTnk, Pnk)

        def seq(d, T, c, zs, gb, kz, kgb, bi):
            cs = slice(c * 128, (c + 1) * 128)
            kvtm, E = d["kvtm"], d["E"]
            p1, p1k = nb()
            for h in range(4):
                S.op("pe", lambda e, h=h: e.matmul(v3(p1)[:, h, :], d["kgT"][:, h, :], Sbf[:, h, :], start=True, stop=True), reads=[K(d, "kgT"), "g_Sbf"], writes=[p1k])
            S.op("dve", lambda e: e.tensor_tensor(out=R[:], in0=kvtm[:, 4:8, :], in1=v3(p1), op=ALU.subtract), reads=[K(d, "kvtm"), p1k], writes=["g_R"])
            p2, p2k = nb()
            for h in range(4):
                S.op("pe", lambda e, h=h: e.matmul(v3(p2)[:, h, :], d["TpT"][:, h, :], R[:, h, :], start=True, stop=True), reads=[K(d, "TpT"), "g_R"], writes=[p2k])
            S.op("dve", lambda e: e.tensor_tensor(out=vnew[:], in0=v3(p2), in1=bc(gb[:, c, 4:8]), op=ALU.mult), reads=[p2k, kgb], writes=["g_vnew"])
            po, pok = nb()
            for h in range(4):
                S.op("pe", lambda e, h=h: e.matmul(v3(po)[:, h, :], Sbf[:, h, :], d["qdT"][:, h, :], start=True, stop=False), reads=["g_Sbf", K(d, "qdT")], writes=[pok])
                S.op("pe", lambda e, h=h: e.matmul(v3(po)[:, h, :], vnew[:, h, :], d["qkT"][:, h, :], start=False, stop=True), reads=["g_vnew", K(d, "qkT")], writes=[pok])
            p3, p3k = nb()
            for h in range(4):
                S.op("pe", lambda e, h=h: e.matmul(v3(p3)[:, h, :], d["kdec"][:, h, :], vnew[:, h, :], start=True, stop=True), reads=[K(d, "kdec"), "g_vnew"], writes=[p3k])
            for h in range(4):
                S.op("dve", lambda e, h=h: e.scalar_tensor_tensor(out=S32[:, h, :], in0=S32[:, h, :], scalar=E[:, h, 127:128], in1=v3(p3)[:, h, :],
                                                                  op0=ALU.mult, op1=ALU.add), reads=["g_S32", K(d, "E"), p3k], writes=["g_S32"])
            S.op("act", lambda e: e.copy(out=Sbf[:], in_=S32[:]), reads=["g_S32"], writes=["g_Sbf"])
            S.op("act", lambda e: e.activation(out=fl(sq[:]), in_=po[:], func=AF.Square), reads=[pok], writes=["g_sq"])
            pss, pssk = nb()
            S.op("pe", lambda e: e.matmul(pss[:], C["onesb"][:], fl(sq[:]), start=True, stop=True), reads=["c_onesb", "g_sq"], writes=[pssk])
            S.op("act", lambda e: e.activation(out=fl(rn[:]), in_=pss[:], func=AF.Sqrt, bias=128.0 * RMS_EPS, scale=1.0), reads=[pssk], writes=["g_rn"])
            S.op("dve", lambda e: e.reciprocal(out=rn[:], in_=rn[:]), reads=["g_rn"], writes=["g_rn"])
            S.op("dve", lambda e: e.scalar_tensor_tensor(out=fl(yt[:]), in0=po[:], scalar=ng[:, 0:1], in1=fl(rn[:]), op0=ALU.mult, op1=ALU.mult),
                 reads=[pok, "g_ng", "g_rn"], writes=["g_yt"])
            S.op("pool", lambda e: e.tensor_tensor(out=ybuf[bi][:, :, cs], in0=yt[:], in1=zs[:, :, cs], op=ALU.mult),
                 reads=["g_yt", kz], writes=[f"g_y{bi}"])

        for T in range(ntiles):
            bi = T % 2
            for n in "qkvz":
                S.dma("sp", inb[n][bi][:], scr[n + "s"].rearrange("h d t -> d h t")[:, :, T * 512:(T + 1) * 512],
                      reads=[f"scr_{n}"], writes=[f"g_in{n}{bi}"])
            S.dma("sp", gbb[bi][:], scr["gbs"].rearrange("(n p) e -> p n e", p=128)[:, T * 4:(T + 1) * 4, :],
                  reads=["scr_gb"], writes=[f"g_gb{bi}"])
            qT, kT, vT, zs, gb = inb["q"][bi], inb["k"][bi], inb["v"][bi], inb["z"][bi], gbb[bi]
            kq, kk, kv, kz, kgb = (f"g_inq{bi}", f"g_ink{bi}", f"g_inv{bi}", f"g_inz{bi}", f"g_gb{bi}")
            for c0 in (0, 2):
                for p in range(2):
                    pre_a(P_[p], T, c0 + p, qT, kT, vT, gb, kq, kk, kv, kgb)
                for l in range(1, NL + 1):
                    for p in range(2):
                        level(P_[p], l)
                for p in range(2):
                    seq(P_[p], T, c0 + p, zs, gb, kz, kgb, bi)
            B.finals.append(S.dma(STQ, y_ap.rearrange("(h d) t -> d h t", d=128)[:, :, T * 512:(T + 1) * 512], ybuf[bi][:],
                                  reads=[f"g_y{bi}"], writes=["y_out"]))
        S.flush(B.es)
```
